# Optimizing a Trainium2 kernel written in Bass

```python
import math
import jax
import jax.numpy as jnp
from jax import lax
import numpy as np

D_MODEL = 1024
BATCH = 16
SEQ = 2048
DEPTH = 2

ATT_WIDTH = D_MODEL // 2
HEAD_DIM = 128
ATT_HEADS = ATT_WIDTH // HEAD_DIM
ROT_DIM = HEAD_DIM // 4
ROPE_THETA = 500000.0
MOBA_BLOCK = 256
MOBA_TOPK = 3
Q_CHUNK = 128
POOL_WIDTH = D_MODEL // 2
POOL_WINDOWS = (2, 4, 8, 16)
POOL_GROUPS = len(POOL_WINDOWS)
POOL_GROUP_W = POOL_WIDTH // POOL_GROUPS
EVEN_IN = 4 * ATT_WIDTH + 2 * POOL_WIDTH

S5_WIDTH = D_MODEL // 2
S5_GROUP_IN = 16
S5_GROUPS = S5_WIDTH // S5_GROUP_IN
S5_STATE = 64
CONV_WIDTH = D_MODEL // 2
CONV_K = 31
ODD_IN = 2 * S5_WIDTH + 3 * CONV_WIDTH

N_EVEN = (DEPTH + 1) // 2
N_ODD = DEPTH // 2
EPS = 1e-6
NEG = -1e30

kernel_name = "hybrid_moba_pool_s5_conformer"


def rms_norm(x, g):
    xf = x.astype(jnp.float32)
    y = xf * lax.rsqrt(jnp.mean(xf * xf, axis=-1, keepdims=True) + EPS) * g.astype(jnp.float32)
    return y.astype(x.dtype)


def rope_partial(x):
    s = x.shape[1]
    pos = jnp.arange(s, dtype=jnp.float32)
    inv = jnp.power(ROPE_THETA, -jnp.arange(0, ROT_DIM, 2, dtype=jnp.float32) / ROT_DIM)
    ang = pos[:, None] * inv[None, :]
    cos = jnp.cos(ang)[None, :, None, :]
    sin = jnp.sin(ang)[None, :, None, :]
    xf = x.astype(jnp.float32)
    x1 = xf[..., : ROT_DIM // 2]
    x2 = xf[..., ROT_DIM // 2: ROT_DIM]
    out = jnp.concatenate([x1 * cos - x2 * sin, x2 * cos + x1 * sin, xf[..., ROT_DIM:]], axis=-1)
    return out.astype(x.dtype)


def moba_attention(q, k, v):
    b, s, h, hd = q.shape
    nb = -(-s // MOBA_BLOCK)
    pad = nb * MOBA_BLOCK - s
    kp = jnp.pad(k, ((0, 0), (0, pad), (0, 0), (0, 0)))
    vp = jnp.pad(v, ((0, 0), (0, pad), (0, 0), (0, 0)))
    kb = kp.reshape(b, nb, MOBA_BLOCK, h, hd)
    vb = vp.reshape(b, nb, MOBA_BLOCK, h, hd)
    kbar = jnp.mean(kb.astype(jnp.float32), axis=2)
    gate = jnp.einsum('bshd,bnhd->bshn', q.astype(jnp.float32), kbar)
    qblk = jnp.arange(s) // MOBA_BLOCK
    past = jnp.arange(nb)[None, :] < qblk[:, None]
    gate = jnp.where(past[None, :, None, :], gate, -jnp.inf)
    n_sel = min(MOBA_TOPK, nb)
    _, idx = lax.top_k(gate, n_sel)
    valid = idx < qblk[None, :, None, None]

    nc = s // Q_CHUNK
    q_c = q.reshape(b, nc, Q_CHUNK, h, hd)
    idx_c = idx.reshape(b, nc, Q_CHUNK, h, n_sel)
    valid_c = valid.reshape(b, nc, Q_CHUNK, h, n_sel)
    kh = kb.transpose(0, 3, 1, 2, 4)
    vh = vb.transpose(0, 3, 1, 2, 4)
    scale = 1.0 / math.sqrt(hd)
    h_ar = jnp.arange(h)[None, :, None]

    def per_batch(args):
        qb, ib, mb, khb, vhb = args

        def per_chunk(cargs):
            c, qc, ic, mc = cargs
            kg = khb[h_ar, ic]
            vg = vhb[h_ar, ic]
            blk = (c * Q_CHUNK) // MOBA_BLOCK
            ko = lax.dynamic_index_in_dim(khb, blk, axis=1, keepdims=False)
            vo = lax.dynamic_index_in_dim(vhb, blk, axis=1, keepdims=False)
            s_sel = jnp.einsum('qhd,qhjkd->qhjk', qc, kg).astype(jnp.float32) * scale
            s_sel = jnp.where(mc[..., None], s_sel, NEG)
            s_own = jnp.einsum('qhd,hkd->qhk', qc, ko).astype(jnp.float32) * scale
            qpos = c * Q_CHUNK + jnp.arange(Q_CHUNK)
            kpos = blk * MOBA_BLOCK + jnp.arange(MOBA_BLOCK)
            causal = (kpos[None, :] <= qpos[:, None])[:, None, :]
            s_own = jnp.where(causal, s_own, NEG)
            sc = jnp.concatenate([s_sel.reshape(Q_CHUNK, h, n_sel * MOBA_BLOCK), s_own], axis=-1)
            p = jax.nn.softmax(sc, axis=-1).astype(vhb.dtype)
            p_sel = p[..., : n_sel * MOBA_BLOCK].reshape(Q_CHUNK, h, n_sel, MOBA_BLOCK)
            p_own = p[..., n_sel * MOBA_BLOCK:]
            return (jnp.einsum('qhjk,qhjkd->qhd', p_sel, vg)
                    + jnp.einsum('qhk,hkd->qhd', p_own, vo))

        return lax.map(per_chunk, (jnp.arange(nc), qb, ib, mb))

    out = lax.map(per_batch, (q_c, idx_c, valid_c, kh, vh))
    return out.reshape(b, s, h, hd)


def multiscale_pool(u, pool_w, pool_scale):
    b, s, _ = u.shape
    ug = u.reshape(b, s, POOL_GROUPS, POOL_GROUP_W).astype(jnp.float32)
    cs = jnp.concatenate([jnp.zeros((b, 1, POOL_GROUPS, POOL_GROUP_W), jnp.float32),
                          jnp.cumsum(ug, axis=1)], axis=1)
    t = jnp.arange(s)
    outs = []
    for g, w in enumerate(POOL_WINDOWS):
        lo = jnp.maximum(t + 1 - w, 0)
        win = cs[:, 1:, g] - cs[:, lo, g]
        cnt = (t + 1 - lo).astype(jnp.float32)[None, :, None]
        outs.append(win / cnt - ug[:, :, g])
    m = jnp.stack(outs, axis=2)
    y = jnp.einsum('bsgc,gcd->bsgd', m, pool_w.astype(jnp.float32)).reshape(b, s, POOL_WIDTH)
    return (y * pool_scale.astype(jnp.float32)).astype(u.dtype)


def s5_ssm(u, lam_re, lam_im, log_dt, b_re, b_im, c_re, c_im, d):
    bsz, s, _ = u.shape
    f32 = jnp.float32
    uf = u.reshape(bsz, s, S5_GROUPS, S5_GROUP_IN).astype(f32)
    lr, li = lam_re.astype(f32), lam_im.astype(f32)
    dt = jnp.exp(log_dt.astype(f32))[:, None]
    mag = jnp.exp(lr * dt)
    ab_re = mag * jnp.cos(li * dt)
    ab_im = mag * jnp.sin(li * dt)
    den = lr * lr + li * li
    num_re = ab_re - 1.0
    f_re = (num_re * lr + ab_im * li) / den
    f_im = (ab_im * lr - num_re * li) / den
    br, bi = b_re.astype(f32), b_im.astype(f32)
    bb_re = f_re[..., None] * br - f_im[..., None] * bi
    bb_im = f_re[..., None] * bi + f_im[..., None] * br
    bu_re = jnp.einsum('bsgi,gpi->bsgp', uf, bb_re)
    bu_im = jnp.einsum('bsgi,gpi->bsgp', uf, bb_im)
    a_re = jnp.broadcast_to(ab_re, bu_re.shape)
    a_im = jnp.broadcast_to(ab_im, bu_im.shape)

    def combine(e1, e2):
        a1r, a1i, b1r, b1i = e1
        a2r, a2i, b2r, b2i = e2
        return (a2r * a1r - a2i * a1i,
                a2r * a1i + a2i * a1r,
                a2r * b1r - a2i * b1i + b2r,
                a2r * b1i + a2i * b1r + b2i)

    _, _, xr, xi = lax.associative_scan(combine, (a_re, a_im, bu_re, bu_im), axis=1)
    y = (jnp.einsum('bsgp,gip->bsgi', xr, c_re.astype(f32))
         - jnp.einsum('bsgp,gip->bsgi', xi, c_im.astype(f32)))
    y = y.reshape(bsz, s, S5_WIDTH) + d.astype(f32) * u.astype(f32)
    return y.astype(u.dtype)


def conformer_conv(a, gl, dw, ln_g, ln_b, pw):
    g = a * jax.nn.sigmoid(gl)
    kern = dw.reshape(CONV_K, 1, CONV_WIDTH).astype(g.dtype)
    c = lax.conv_general_dilated(g, kern, window_strides=(1,), padding=[(CONV_K - 1, 0)],
                                 dimension_numbers=('NWC', 'WIO', 'NWC'),
                                 feature_group_count=CONV_WIDTH)
    cf = c.astype(jnp.float32)
    mu = jnp.mean(cf, axis=-1, keepdims=True)
    var = jnp.mean(jnp.square(cf - mu), axis=-1, keepdims=True)
    n = (cf - mu) * lax.rsqrt(var + EPS) * ln_g.astype(jnp.float32) + ln_b.astype(jnp.float32)
    return (jax.nn.silu(n).astype(a.dtype)) @ pw


def even_mixer(h, w_in, pool_w, pool_scale, w_out):
    b, s, _ = h.shape
    z = h @ w_in
    A = ATT_WIDTH
    q, k, v, ga, pu, gb = jnp.split(z, [A, 2 * A, 3 * A, 4 * A, 4 * A + POOL_WIDTH], axis=-1)
    q = rope_partial(q.reshape(b, s, ATT_HEADS, HEAD_DIM))
    k = rope_partial(k.reshape(b, s, ATT_HEADS, HEAD_DIM))
    v = v.reshape(b, s, ATT_HEADS, HEAD_DIM)
    att = moba_attention(q, k, v).reshape(b, s, A) * jax.nn.silu(ga)
    pool = multiscale_pool(pu, pool_w, pool_scale) * jax.nn.silu(gb)
    return jnp.concatenate([att, pool], axis=-1) @ w_out


def odd_mixer(h, w_in, lam_re, lam_im, log_dt, b_re, b_im, c_re, c_im, d,
              glu_w, glu_b, dw, ln_g, ln_b, pw, w_out):
    z = h @ w_in
    Wc, Wd = S5_WIDTH, CONV_WIDTH
    su, gc, ca, cb, gd = jnp.split(z, [Wc, 2 * Wc, 2 * Wc + Wd, 2 * Wc + 2 * Wd], axis=-1)
    y = s5_ssm(su, lam_re, lam_im, log_dt, b_re, b_im, c_re, c_im, d)
    ya, yb = jnp.split(y @ glu_w + glu_b, 2, axis=-1)
    ssm_out = ya * jax.nn.sigmoid(yb) * jax.nn.silu(gc)
    conv_out = conformer_conv(ca, cb, dw, ln_g, ln_b, pw) * jax.nn.silu(gd)
    return jnp.concatenate([ssm_out, conv_out], axis=-1) @ w_out


def setup_inputs(seed: int = 0) -> dict:
    key = jax.random.key(seed)
    ks = jax.random.split(key, 24)
    nrm = jax.random.normal
    f32 = jnp.float32
    D = D_MODEL
    x = nrm(ks[0], (BATCH, SEQ, D), f32)
    pre_norm_g = 1.0 + 0.05 * nrm(ks[1], (DEPTH, D), f32)
    post_norm_g = 1.0 + 0.05 * nrm(ks[2], (DEPTH, D), f32)
    e_w_in = nrm(ks[3], (N_EVEN, D, EVEN_IN), f32) * D ** -0.5
    e_pool_w = nrm(ks[4], (N_EVEN, POOL_GROUPS, POOL_GROUP_W, POOL_GROUP_W), f32) * POOL_GROUP_W ** -0.5
    e_pool_scale = 1.0 + 0.05 * nrm(ks[5], (N_EVEN, POOL_WIDTH), f32)
    e_w_out = nrm(ks[6], (N_EVEN, ATT_WIDTH + POOL_WIDTH, D), f32) * (ATT_WIDTH + POOL_WIDTH) ** -0.5
    o_w_in = nrm(ks[7], (N_ODD, D, ODD_IN), f32) * D ** -0.5
    n_idx = jnp.arange(S5_STATE, dtype=f32)
    o_lam_re = -0.5 + 0.01 * nrm(ks[8], (N_ODD, S5_GROUPS, S5_STATE), f32)
    o_lam_im = math.pi * n_idx[None, None, :] + 0.01 * nrm(ks[9], (N_ODD, S5_GROUPS, S5_STATE), f32)
    o_log_dt = jax.random.uniform(ks[10], (N_ODD, S5_GROUPS), f32, math.log(1e-3), math.log(1e-1))
    o_b_re = nrm(ks[11], (N_ODD, S5_GROUPS, S5_STATE, S5_GROUP_IN), f32) * (2.0 * S5_GROUP_IN) ** -0.5
    o_b_im = nrm(ks[12], (N_ODD, S5_GROUPS, S5_STATE, S5_GROUP_IN), f32) * (2.0 * S5_GROUP_IN) ** -0.5
    o_c_re = nrm(ks[13], (N_ODD, S5_GROUPS, S5_GROUP_IN, S5_STATE), f32) * (2.0 * S5_STATE) ** -0.5
    o_c_im = nrm(ks[14], (N_ODD, S5_GROUPS, S5_GROUP_IN, S5_STATE), f32) * (2.0 * S5_STATE) ** -0.5
    o_d = nrm(ks[15], (N_ODD, S5_WIDTH), f32)
    o_glu_w = nrm(ks[16], (N_ODD, S5_WIDTH, 2 * S5_WIDTH), f32) * S5_WIDTH ** -0.5
    o_glu_b = 0.01 * nrm(ks[17], (N_ODD, 2 * S5_WIDTH), f32)
    o_dw = nrm(ks[18], (N_ODD, CONV_K, CONV_WIDTH), f32) * CONV_K ** -0.5
    o_ln_g = 1.0 + 0.05 * nrm(ks[19], (N_ODD, CONV_WIDTH), f32)
    o_ln_b = 0.01 * nrm(ks[20], (N_ODD, CONV_WIDTH), f32)
    o_pw = nrm(ks[21], (N_ODD, CONV_WIDTH, CONV_WIDTH), f32) * CONV_WIDTH ** -0.5
    o_w_out = nrm(ks[22], (N_ODD, S5_WIDTH + CONV_WIDTH, D), f32) * (S5_WIDTH + CONV_WIDTH) ** -0.5
    return {"x": x, "pre_norm_g": pre_norm_g, "post_norm_g": post_norm_g,
            "e_w_in": e_w_in, "e_pool_w": e_pool_w, "e_pool_scale": e_pool_scale, "e_w_out": e_w_out,
            "o_w_in": o_w_in, "o_lam_re": o_lam_re, "o_lam_im": o_lam_im, "o_log_dt": o_log_dt,
            "o_b_re": o_b_re, "o_b_im": o_b_im, "o_c_re": o_c_re, "o_c_im": o_c_im, "o_d": o_d,
            "o_glu_w": o_glu_w, "o_glu_b": o_glu_b, "o_dw": o_dw, "o_ln_g": o_ln_g, "o_ln_b": o_ln_b,
            "o_pw": o_pw, "o_w_out": o_w_out}


def reference(x, pre_norm_g, post_norm_g, e_w_in, e_pool_w, e_pool_scale, e_w_out,
              o_w_in, o_lam_re, o_lam_im, o_log_dt, o_b_re, o_b_im, o_c_re, o_c_im, o_d,
              o_glu_w, o_glu_b, o_dw, o_ln_g, o_ln_b, o_pw, o_w_out):
    for i in range(DEPTH):
        h = rms_norm(x, pre_norm_g[i])
        j = i // 2
        if i % 2 == 0:
            y = even_mixer(h, e_w_in[j], e_pool_w[j], e_pool_scale[j], e_w_out[j])
        else:
            y = odd_mixer(h, o_w_in[j], o_lam_re[j], o_lam_im[j], o_log_dt[j], o_b_re[j], o_b_im[j],
                          o_c_re[j], o_c_im[j], o_d[j], o_glu_w[j], o_glu_b[j], o_dw[j],
                          o_ln_g[j], o_ln_b[j], o_pw[j], o_w_out[j])
        x = x + rms_norm(y, post_norm_g[i])
    return x
```

```python
import math
import os
from contextlib import ExitStack
import numpy as np
import concourse.bass as bass
import concourse.mybir as mybir
from concourse.bass_utils import run_bass_kernel_spmd

F32 = mybir.dt.float32
BF16 = mybir.dt.bfloat16
I32 = mybir.dt.int32
ALU = mybir.AluOpType
AF = mybir.ActivationFunctionType
AX = mybir.AxisListType

D = 1024
SEQ = 2048
NB = 2
HD = 128
EPS = 1e-6
NEGB = -30000.0
S5L = 8
NCH = SEQ // S5L


class T:
    __slots__ = ("w", "r")

    def __init__(self):
        self.w = None
        self.r = {}


def TL(n):
    return [T() for _ in range(n)]


class Sched:
    ENG = ("pe", "act", "dve", "pool", "sp")

    def __init__(self, nc, es, n_dma=24):
        self.nc = nc
        self.ops = {e: [] for e in self.ENG}
        self.cnt = {e: 0 for e in self.ENG}
        self.seen = {e: {} for e in self.ENG}
        self.n_dma = n_dma
        self.dma_cnt = [0] * n_dma
        self.dma_rr2 = {"sp": 0, "pool": 0}
        self.semh = {}
        for e in self.ENG:
            self.semh[("c", e)] = es.enter_context(nc.semaphore(f"s_{e}"))
        for i in range(n_dma):
            self.semh[("d", i)] = es.enter_context(nc.semaphore(f"s_d{i}"))

    def _deps(self, eng, reads, writes):
        need = {}
        for t in reads:
            if t.w is not None:
                k, v = t.w
                if need.get(k, 0) < v:
                    need[k] = v
        for t in writes:
            if t.w is not None:
                k, v = t.w
                if need.get(k, 0) < v:
                    need[k] = v
            for k, v in t.r.items():
                if need.get(k, 0) < v:
                    need[k] = v
        waits = []
        sn = self.seen[eng]
        for k, v in need.items():
            if eng == "pe" and k == ("c", "pe"):
                continue
            if sn.get(k, 0) < v:
                waits.append((k, v))
                sn[k] = v
        return waits

    def _record(self, key, v, reads, writes):
        for t in reads:
            if t.r.get(key, 0) < v:
                t.r[key] = v
        for t in writes:
            t.w = (key, v)
            t.r = {}

    def op(self, eng, fn, reads=(), writes=(), inc=True):
        waits = self._deps(eng, reads, writes)
        key = ("c", eng)
        if inc:
            self.cnt[eng] += 1
            v = self.cnt[eng]
        else:
            v = self.cnt[eng] + 1
        self.ops[eng].append([waits, fn, key, 1 if inc else 0])
        self._record(key, v, reads, writes)

    def dma(self, eng, fn, reads=(), writes=()):
        half = self.n_dma // 2
        base = 0 if eng == "sp" else half
        j = self.dma_rr2[eng]
        self.dma_rr2[eng] = (j + 1) % half
        i = base + j
        waits = self._deps(eng, reads, writes)
        key = ("d", i)
        prev = self.dma_cnt[i]
        if prev > 0 and self.seen[eng].get(key, 0) < prev:
            waits.append((key, prev))
            self.seen[eng][key] = prev
        self.dma_cnt[i] += 16
        self.ops[eng].append([waits, fn, key, 16])
        self._record(key, self.dma_cnt[i], reads, writes)

    def flush(self):
        nc = self.nc
        for e in ("pe", "act", "dve", "pool"):
            if self.ops[e] and self.ops[e][-1][3] == 0:
                self.ops[e][-1][3] = 1
                self.cnt[e] += 1
        fin = [(("d", i), self.dma_cnt[i]) for i in range(self.n_dma) if self.dma_cnt[i] > 0]
        fin += [(("c", e), self.cnt[e]) for e in ("pe", "act", "dve", "pool") if self.cnt[e] > 0]
        ops = self.ops
        semh = self.semh
        with nc.Block() as block:
            def run(engname, eng):
                for waits, fn, key, inc in ops[engname]:
                    for k, v in waits:
                        eng.wait_ge(semh[k], v)
                    ins = fn(eng)
                    if inc:
                        ins.then_inc(semh[key], inc)
                if engname == "sp":
                    for k, v in fin:
                        eng.wait_ge(semh[k], v)

            @block.tensor
            def _(e):
                run("pe", e)

            @block.scalar
            def _(e):
                run("act", e)

            @block.vector
            def _(e):
                run("dve", e)

            @block.gpsimd
            def _(e):
                run("pool", e)

            @block.sync
            def _(e):
                run("sp", e)
        self.ops = {e: [] for e in self.ENG}
        for e in self.ENG:
            for k, v in fin:
                self.seen[e][k] = v


def host_consts():
    c = {}
    c["ident"] = np.eye(128, dtype=np.float32)
    c["ones"] = np.ones((128, 128), np.float32)
    kk = np.arange(128)[:, None]
    qq = np.arange(128)[None, :]
    c["trineg"] = np.where(kk <= qq, 0.0, NEGB).astype(np.float32)
    e8 = np.zeros((8, 8, 128), np.float32)
    for n in range(8):
        e8[n, n, :] = 1.0
    c["e8"] = e8.reshape(8, 1024)
    p32 = np.zeros((128, 128), np.float32)
    for m in range(32):
        p32[(m + 16) % 32, m] = 1.0
    c["p32"] = p32
    inv = np.power(500000.0, -np.arange(0, 32, 2, dtype=np.float32) / 32.0).astype(np.float32)
    pos = np.arange(SEQ, dtype=np.float32)
    ang = (pos[None, :] * inv[:, None]).astype(np.float32)
    cs = np.cos(ang).astype(np.float32)
    sn = np.sin(ang).astype(np.float32)
    c["ropeC"] = np.concatenate([cs, cs, np.ones((96, SEQ), np.float32)], 0)
    c["ropeS"] = np.concatenate([-sn, sn, np.zeros((96, SEQ), np.float32)], 0)
    negm = np.zeros((128, 8, 8), np.float32)
    bfix = np.zeros((128, 8, 8), np.float32)
    for i in range(8):
        B = 4 + i // 2
        for n in range(8):
            negm[:, i, n] = 0.0 if n < B else -1e30
            bfix[:, i, n] = 0.0 if n == B else NEGB
    c["negmask"] = negm.reshape(128, 64)
    c["biasfix"] = bfix.reshape(128, 64)
    rc = np.zeros((128, 4, 16), np.float32)
    for g, w in enumerate((2, 4, 8, 16)):
        for t in range(16):
            rc[:, g, t] = 1.0 / min(t + 1, w)
    c["rcnt"] = rc.reshape(128, 64)
    pp = np.arange(128)
    c["maskA"] = np.stack([((pp // 16) % 2 == 0), ((pp // 16) % 2 == 1)], 1).astype(np.float32)
    c["maskB"] = np.stack([(pp // 64 == 0), (pp // 64 == 1)], 1).astype(np.float32)
    return c


CONST_SHAPES = {k: v.shape for k, v in host_consts().items()}


def build(n_layers=2):
    nc = bass.Bass("TRN2", target_bir_lowering=False)

    def dr(name, shape, kind="ExternalInput", dt=F32):
        return nc.dram_tensor(name, list(shape), dt, kind=kind).ap()

    x_d = dr("x", [NB, SEQ, D])
    out_d = dr("out", [NB, SEQ, D], "ExternalOutput")
    pre_g = dr("pre_norm_g", [2, D])
    post_g = dr("post_norm_g", [2, D])
    e_w_in = dr("e_w_in", [D, 3072])
    e_pool_w = dr("e_pool_w", [4, 128, 128])
    e_pool_scale = dr("e_pool_scale", [128, 4])
    e_w_out = dr("e_w_out", [D, D])
    o_w_in = dr("o_w_in", [D, 2560])
    o_lamre_A = dr("o_lamre_A", [128, 256])
    o_lamim_A = dr("o_lamim_A", [128, 256])
    o_dt_A = dr("o_dt_A", [128, 256])
    o_bre_A = dr("o_bre_A", [128, 256])
    o_bim_A = dr("o_bim_A", [128, 256])
    o_bre_B = dr("o_bre_B", [128, 256])
    o_bim_B = dr("o_bim_B", [128, 256])
    o_lamre_B = dr("o_lamre_B", [128, 16])
    o_lamim_B = dr("o_lamim_B", [128, 16])
    o_dt_B = dr("o_dt_B", [128, 16])
    o_cre_B = dr("o_cre_B", [128, 256])
    o_cim_B = dr("o_cim_B", [128, 256])
    o_dA = dr("o_dA", [128, 4])
    o_glu_w = dr("o_glu_w", [512, 1024])
    o_glub = dr("o_glub", [128, 8])
    o_dwT = dr("o_dwT", [128, 4, 31])
    o_lng = dr("o_lng", [128, 4])
    o_lnb = dr("o_lnb", [128, 4])
    o_pw = dr("o_pw", [512, 512])
    o_w_out = dr("o_w_out", [D, D])
    dbg_d = dr("dbg", [128, 4, SEQ], "ExternalOutput") if os.environ.get("KDBG") else None
    cd = {k: dr("c_" + k, list(s)) for k, s in CONST_SHAPES.items()}

    with ExitStack() as top:
        S = Sched(nc, top)
        uid = [0]

        def sbt(es, name, shape, dt):
            uid[0] += 1
            return es.enter_context(nc.sbuf_tensor(f"{name}_{uid[0]}", list(shape), dt))
        psf = [top.enter_context(nc.psum_tensor(f"psf{i}", [128, 512], F32)) for i in range(7)]
        psb = top.enter_context(nc.psum_tensor("psb", [128, 1024], BF16))
        pb = TL(8)

        identb = sbt(top, "identb", [128, 128], BF16)
        identf = sbt(top, "identf", [128, 128], F32)
        onesb = sbt(top, "onesb", [128, 128], BF16)
        t_const = T()
        S.dma("pool", lambda e: e.dma_start(out=identb[:], in_=cd["ident"][:, :]), writes=[t_const])
        S.dma("sp", lambda e: e.dma_start(out=identf[:], in_=cd["ident"][:, :]), writes=[t_const])
        S.dma("pool", lambda e: e.dma_start(out=onesb[:], in_=cd["ones"][:, :]), writes=[t_const])

        def rms_stats(es_tag, src_aps, st, col0, reads, t_st, junk):
            for i, ap in enumerate(src_aps):
                S.op("act", lambda e, ap=ap, i=i: e.activation(out=junk[:, 0:ap.shape[1]], in_=ap, func=AF.Square,
                                                             accum_out=st[:, col0 + i:col0 + i + 1]),
                     reads=reads, writes=[t_st])
            if len(src_aps) == 2:
                S.op("dve", lambda e: e.tensor_tensor(out=st[:, col0:col0 + 1], in0=st[:, col0:col0 + 1],
                                                      in1=st[:, col0 + 1:col0 + 2], op=ALU.add), reads=[t_st], writes=[t_st])
            S.op("act", lambda e: e.activation(out=st[:, col0 + 2:col0 + 3], in_=st[:, col0:col0 + 1], func=AF.Sqrt,
                                               scale=1.0 / D, bias=EPS), reads=[t_st], writes=[t_st])
            S.op("dve", lambda e: e.reciprocal(out=st[:, col0 + 3:col0 + 4], in_=st[:, col0 + 2:col0 + 3]),
                 reads=[t_st], writes=[t_st])

        def phase_norm_T(es, src_d, b, gvec_d, hT, t_hT, nxb=2):
            xb = [sbt(es, f"xb{i}", [128, D], F32) for i in range(nxb)]
            hb = [sbt(es, f"hb{i}", [128, D], BF16) for i in range(2)]
            junk = sbt(es, "junkA", [128, D], BF16)
            gt = sbt(es, "gtA", [128, D], F32)
            st = sbt(es, "stA", [128, 16 * 4], F32)
            t_xb, t_hb, t_g = TL(nxb), TL(2), T()
            t_st = TL(16)
            S.dma("sp", lambda e: e.dma_start(out=gt[:], in_=gvec_d.partition_broadcast(128)), writes=[t_g])
            def stats(tt):
                i = tt % 2
                xi = tt % nxb
                S.dma("sp", lambda e: e.dma_start(out=xb[xi][:], in_=src_d[b, tt * 128:(tt + 1) * 128, :]),
                      writes=[t_xb[xi]])
                rms_stats(es, [xb[xi][:]], st, tt * 4, [t_xb[xi]], t_st[tt], junk)
                S.op("dve", lambda e: e.scalar_tensor_tensor(out=hb[i][:], in0=xb[xi][:], scalar=st[:, tt * 4 + 3:tt * 4 + 4],
                                                             in1=gt[:], op0=ALU.mult, op1=ALU.mult),
                     reads=[t_xb[xi], t_st[tt], t_g], writes=[t_hb[i]])

            def trans(tt):
                i = tt % 2
                for k in range(8):
                    S.op("pe", lambda e, k=k: e.transpose(out=psb[:, k * 128:(k + 1) * 128], in_=hb[i][:, k * 128:(k + 1) * 128],
                                                          identity=identb[:]),
                         reads=[t_hb[i], t_const], writes=[pb[7]], inc=(k == 7))
                S.op("act", lambda e: e.activation(out=hT[:, :, tt * 128:(tt + 1) * 128],
                                                   in_=psb[:, :].rearrange("p (k t) -> p k t", k=8), func=AF.Copy),
                     reads=[pb[7]], writes=[t_hT[tt]])

            stats(0)
            for tt in range(16):
                if tt + 1 < 16:
                    stats(tt + 1)
                trans(tt)

        def load_wo(wo, t_wo, w_out_d):
            for hf in range(2):
                S.dma("pool", lambda e, hf=hf: e.dma_start(out=wo[:, :, hf * 512:(hf + 1) * 512],
                                                          in_=w_out_d[:, hf * 512:(hf + 1) * 512].rearrange("(k p) n -> p k n", p=128)),
                      writes=[t_wo])

        def phase_outproj(es, w_out_d, gvec_d, res_d, b, gA, gB, t_gA, t_gB, wo, t_wo):
            gt = sbt(es, "gtF", [128, D], F32)
            xr = [sbt(es, f"xr{i}", [128, D], F32) for i in range(2)]
            tm = [sbt(es, f"tmF{i}", [128, D], F32) for i in range(2)]
            junk = sbt(es, "junkF", [128, 512], BF16)
            st = sbt(es, "stF", [128, 16 * 4], F32)
            t_g = T()
            t_xr, t_tm, t_st = TL(2), TL(2), TL(16)
            S.dma("sp", lambda e: e.dma_start(out=gt[:], in_=gvec_d.partition_broadcast(128)), writes=[t_g])
            for tt in range(16):
                i = tt % 2
                bk = [psf[2 * i], psf[2 * i + 1]]
                tbk = [pb[2 * i], pb[2 * i + 1]]
                S.dma("sp", lambda e, tt=tt, i=i: e.dma_start(out=xr[i][:], in_=res_d[b, tt * 128:(tt + 1) * 128, :]),
                      writes=[t_xr[i]])
                for hf in range(2):
                    for k in range(8):
                        g_ap = (gA if k < 4 else gB)
                        S.op("pe", lambda e, hf=hf, k=k, g_ap=g_ap, tt=tt, bk=bk: e.matmul(
                            bk[hf][:], lhsT=g_ap[:, k % 4, tt * 128:(tt + 1) * 128], rhs=wo[:, k, hf * 512:(hf + 1) * 512],
                            start=(k == 0), stop=(k == 7)),
                            reads=[(t_gA if k < 4 else t_gB)[tt // 4], t_wo], writes=[tbk[hf]], inc=(k == 7))
                rms_stats(es, [bk[0][:], bk[1][:]], st, tt * 4, tbk, t_st[tt], junk)
                for hf in range(2):
                    S.op("dve", lambda e, hf=hf, tt=tt, i=i, bk=bk: e.scalar_tensor_tensor(
                        out=tm[i][:, hf * 512:(hf + 1) * 512], in0=bk[hf][:], scalar=st[:, tt * 4 + 3:tt * 4 + 4],
                        in1=gt[:, hf * 512:(hf + 1) * 512], op0=ALU.mult, op1=ALU.mult),
                        reads=[tbk[hf], t_st[tt], t_g], writes=[t_tm[i]])
                S.op("pool", lambda e, i=i: e.tensor_tensor(out=tm[i][:], in0=tm[i][:], in1=xr[i][:], op=ALU.add),
                     reads=[t_tm[i], t_xr[i]], writes=[t_tm[i]])
                S.dma("sp", lambda e, tt=tt, i=i: e.dma_start(out=out_d[b, tt * 128:(tt + 1) * 128, :], in_=tm[i][:]),
                      reads=[t_tm[i]])

        def layer0(b):
            with ExitStack() as L:
                qkT = sbt(L, "qkT", [128, 8, SEQ], BF16)
                Vt = sbt(L, "Vt", [128, 16, 512], BF16)
                gaT = sbt(L, "gaT", [128, 4, SEQ], BF16)
                gbT = sbt(L, "gbT", [128, 4, SEQ], BF16)
                t_qk = [TL(4) for _ in range(8)]
                t_V = TL(16)
                t_ga = [TL(4) for _ in range(4)]
                t_gb = [TL(4) for _ in range(4)]
                wo0 = sbt(L, "wo0", [128, 8, D], BF16)
                t_wo0 = T()

                with ExitStack() as es:
                    hT = sbt(es, "hT", [128, 8, SEQ], BF16)
                    t_hT = TL(16)
                    phase_norm_T(es, x_d, b, pre_g[0:1, :], hT, t_hT, nxb=4)
                    wb = [sbt(es, f"wb{i}", [128, 8, 512], BF16) for i in range(2)]
                    t_wb = TL(2)
                    ropeC = sbt(es, "ropeC", [128, SEQ], F32)
                    ropeS = sbt(es, "ropeS", [128, SEQ], F32)
                    p32 = sbt(es, "p32", [128, 128], BF16)
                    poolw = sbt(es, "poolw", [128, 4, 128], BF16)
                    pscale = sbt(es, "pscale", [128, 4], F32)
                    rcnt = sbt(es, "rcnt", [128, 64], F32)
                    t_c2 = T()
                    S.dma("sp", lambda e: e.dma_start(out=ropeC[:], in_=cd["ropeC"][:, :]), writes=[t_c2])
                    S.dma("sp", lambda e: e.dma_start(out=ropeS[:], in_=cd["ropeS"][:, :]), writes=[t_c2])
                    S.dma("pool", lambda e: e.dma_start(out=p32[:], in_=cd["p32"][:, :]), writes=[t_c2])
                    S.dma("pool", lambda e: e.dma_start(out=poolw[:], in_=e_pool_w.rearrange("g c d -> c g d")), writes=[t_c2])
                    S.dma("sp", lambda e: e.dma_start(out=pscale[:], in_=e_pool_scale[:, :]), writes=[t_c2])
                    S.dma("sp", lambda e: e.dma_start(out=rcnt[:], in_=cd["rcnt"][:, :]), writes=[t_c2])
                    r1 = [sbt(es, f"r1_{i}", [128, 512], F32) for i in range(2)]
                    r2 = [sbt(es, f"r2_{i}", [128, 512], F32) for i in range(2)]
                    t_r1, t_r2 = TL(2), TL(2)
                    ub = [sbt(es, f"ub{i}", [128, 528], F32) for i in range(2)]
                    sa = sbt(es, "sa", [128, 528], F32)
                    sb_ = sbt(es, "sb", [128, 528], F32)
                    mt = [sbt(es, f"mt{i}", [128, 512], BF16) for i in range(2)]
                    t_ub, t_mt = TL(2), TL(2)
                    t_sa, t_sb = T(), T()

                    order = [0, 1, 2, 3, 5, 4]
                    cnt = [0]
                    for oi in range(2):
                        S.dma("pool", lambda e, oi=oi: e.dma_start(out=wb[oi][:], in_=e_w_in[:, order[oi] * 512:(order[oi] + 1) * 512].rearrange("(k p) n -> p k n", p=128)),
                              writes=[t_wb[oi]])
                    ri = [0]
                    ksub = os.environ.get("KSUB", "")
                    for oi, gi in enumerate(order):
                        if ksub and oi >= int(ksub):
                            break
                        wbi = wb[oi % 2]
                        t_wbi = t_wb[oi % 2]
                        if gi in (0, 1):
                            pend = [None]

                            def rope_tail(j, c, r, pbk):
                                def f():
                                    S.op("pe", lambda e: e.matmul(psf[pbk][:], lhsT=p32[:], rhs=qkT[:, j, c * 512:(c + 1) * 512], start=True, stop=True),
                                         reads=[t_qk[j][c], t_c2], writes=[pb[pbk]])
                                    S.op("dve", lambda e: e.tensor_tensor(out=r2[r][:], in0=psf[pbk][:], in1=ropeS[:, c * 512:(c + 1) * 512], op=ALU.mult),
                                         reads=[pb[pbk], t_c2], writes=[t_r2[r]])
                                    S.op("dve", lambda e: e.tensor_tensor(out=qkT[:, j, c * 512:(c + 1) * 512], in0=r1[r][:], in1=r2[r][:], op=ALU.add),
                                         reads=[t_r1[r], t_r2[r]], writes=[t_qk[j][c]])
                                return f
                            for m in range(4):
                                j = gi * 4 + m
                                for c in range(4):
                                    bi = cnt[0] % 3
                                    cnt[0] += 1
                                    for k in range(8):
                                        S.op("pe", lambda e, k=k, bi=bi, m=m, c=c, wbi=wbi: e.matmul(
                                            psf[bi][:], lhsT=wbi[:, k, m * 128:(m + 1) * 128], rhs=hT[:, k, c * 512:(c + 1) * 512],
                                            start=(k == 0), stop=(k == 7)),
                                            reads=[t_wbi] + t_hT[4 * c:4 * c + 4], writes=[pb[bi]], inc=(k == 7))
                                    if pend[0] is not None:
                                        pend[0]()
                                    S.op("act", lambda e, bi=bi, j=j, c=c: e.activation(out=qkT[:, j, c * 512:(c + 1) * 512], in_=psf[bi][:], func=AF.Copy),
                                         reads=[pb[bi]], writes=[t_qk[j][c]])
                                    r = ri[0] % 2
                                    ri[0] += 1
                                    pbk = 3 + r
                                    S.op("dve", lambda e, bi=bi, c=c, r=r: e.tensor_tensor(out=r1[r][:], in0=psf[bi][:], in1=ropeC[:, c * 512:(c + 1) * 512], op=ALU.mult),
                                         reads=[pb[bi], t_c2, t_qk[j][c]], writes=[t_r1[r]])
                                    pend[0] = rope_tail(j, c, r, pbk)
                            pend[0]()
                        elif gi == 2:
                            for tt in range(16):
                                bi = cnt[0] % 3
                                cnt[0] += 1
                                for k in range(8):
                                    S.op("pe", lambda e, k=k, bi=bi, tt=tt, wbi=wbi: e.matmul(
                                        psf[bi][:], lhsT=hT[:, k, tt * 128:(tt + 1) * 128], rhs=wbi[:, k, :], start=(k == 0), stop=(k == 7)),
                                        reads=[t_wbi, t_hT[tt]], writes=[pb[bi]], inc=(k == 7))
                                S.op("dve", lambda e, bi=bi, tt=tt: e.tensor_copy(out=Vt[:, tt, :], in_=psf[bi][:]), reads=[pb[bi]], writes=[t_V[tt]])
                        elif gi in (3, 5):
                            dst, t_dst = (gaT, t_ga) if gi == 3 else (gbT, t_gb)
                            for m in range(4):
                                for c in range(4):
                                    bi = cnt[0] % 3
                                    cnt[0] += 1
                                    for k in range(8):
                                        S.op("pe", lambda e, k=k, bi=bi, m=m, c=c, wbi=wbi: e.matmul(
                                            psf[bi][:], lhsT=wbi[:, k, m * 128:(m + 1) * 128], rhs=hT[:, k, c * 512:(c + 1) * 512],
                                            start=(k == 0), stop=(k == 7)),
                                            reads=[t_wbi] + t_hT[4 * c:4 * c + 4], writes=[pb[bi]], inc=(k == 7))
                                    S.op("act", lambda e, bi=bi, m=m, c=c, dst=dst: e.activation(out=dst[:, m, c * 512:(c + 1) * 512], in_=psf[bi][:], func=AF.Silu),
                                         reads=[pb[bi]], writes=[t_dst[m][c]])
                        else:
                            ppend = [None]
                            for g in range(4):
                                w = (2, 4, 8, 16)[g]
                                nlev = g + 1
                                for c in range(4):
                                    bi = cnt[0] % 3
                                    cnt[0] += 1
                                    for k in range(8):
                                        S.op("pe", lambda e, k=k, bi=bi, g=g, c=c, wbi=wbi: e.matmul(
                                            psf[bi][:], lhsT=wbi[:, k, g * 128:(g + 1) * 128], rhs=hT[:, k, c * 512:(c + 1) * 512],
                                            start=(k == 0), stop=(k == 7)),
                                            reads=[t_wbi] + t_hT[4 * c:4 * c + 4], writes=[pb[bi]], inc=(k == 7))
                                    if ppend[0] is not None:
                                        ppend[0]()
                                        ppend[0] = None
                                    u = ub[c % 2]
                                    up = ub[(c + 1) % 2]
                                    if c == 0:
                                        S.op("pool", lambda e, u=u: e.memset(u[:, 0:16], 0.0), writes=[t_ub[c % 2]])
                                    else:
                                        S.op("pool", lambda e, u=u, up=up: e.tensor_copy(out=u[:, 0:16], in_=up[:, 512:528]),
                                             reads=[t_ub[(c + 1) % 2]], writes=[t_ub[c % 2]])
                                    S.op("act", lambda e, bi=bi, u=u: e.activation(out=u[:, 16:528], in_=psf[bi][:], func=AF.Copy),
                                         reads=[pb[bi]], writes=[t_ub[c % 2]])
                                    src, t_src = u, t_ub[c % 2]
                                    for lv in range(nlev):
                                        sh = 1 << lv
                                        lo = 2 * sh
                                        dstb, t_d = (sa, t_sa) if lv % 2 == 0 else (sb_, t_sb)
                                        S.op("dve", lambda e, src=src, dstb=dstb, lo=lo, sh=sh: e.tensor_tensor(
                                            out=dstb[:, lo:528], in0=src[:, lo:528], in1=src[:, lo - sh:528 - sh], op=ALU.add),
                                            reads=[t_src], writes=[t_d])
                                        src, t_src = dstb, t_d
                                    mi = c % 2
                                    S.op("dve", lambda e, src=src, u=u, mi=mi, w=w: e.scalar_tensor_tensor(
                                        out=mt[mi][:], in0=src[:, 16:528], scalar=1.0 / w, in1=u[:, 16:528], op0=ALU.mult, op1=ALU.subtract),
                                        reads=[t_src, t_ub[c % 2]], writes=[t_mt[mi]])
                                    if c == 0:
                                        S.op("dve", lambda e, src=src, g=g: e.tensor_tensor(out=src[:, 0:16], in0=src[:, 16:32], in1=rcnt[:, g * 16:(g + 1) * 16], op=ALU.mult),
                                             reads=[t_src, t_c2], writes=[t_src])
                                        S.op("dve", lambda e, src=src, u=u, mi=mi: e.tensor_tensor(out=mt[mi][:, 0:16], in0=src[:, 0:16], in1=u[:, 16:32], op=ALU.subtract),
                                             reads=[t_src, t_ub[c % 2]], writes=[t_mt[mi]])
                                    pbk = 3 + (c % 2)

                                    def pool_tail(g=g, c=c, mi=mi, pbk=pbk):
                                        S.op("pe", lambda e: e.matmul(psf[pbk][:], lhsT=poolw[:, g, :], rhs=mt[mi][:], start=True, stop=True),
                                             reads=[t_mt[mi], t_c2], writes=[pb[pbk]])
                                        S.op("dve", lambda e: e.scalar_tensor_tensor(
                                            out=gbT[:, g, c * 512:(c + 1) * 512], in0=psf[pbk][:], scalar=pscale[:, g:g + 1],
                                            in1=gbT[:, g, c * 512:(c + 1) * 512], op0=ALU.mult, op1=ALU.mult),
                                            reads=[pb[pbk], t_c2, t_gb[g][c]], writes=[t_gb[g][c]])
                                    ppend[0] = pool_tail
                        if gi == 4 and ppend[0] is not None:
                            ppend[0]()
                            ppend[0] = None
                        if oi + 2 < len(order) and not ksub:
                            gn = order[oi + 2]
                            S.dma("pool", lambda e, gn=gn, wbi=wbi: e.dma_start(out=wbi[:], in_=e_w_in[:, gn * 512:(gn + 1) * 512].rearrange("(k p) n -> p k n", p=128)),
                                  writes=[t_wbi])
                    S.flush()
                if os.environ.get("KSTOP") == "AB":
                    return

                with ExitStack() as es:
                    load_wo(wo0, t_wo0, e_w_out)
                    e8 = sbt(es, "e8", [8, 1024], BF16)
                    trineg = sbt(es, "trineg", [128, 128], BF16)
                    negmask = sbt(es, "negmask", [128, 64], F32)
                    biasfix = sbt(es, "biasfix", [128, 64], F32)
                    t_c3 = T()
                    S.dma("pool", lambda e: e.dma_start(out=e8[:], in_=cd["e8"][:, :]), writes=[t_c3])
                    S.dma("pool", lambda e: e.dma_start(out=trineg[:], in_=cd["trineg"][:, :]), writes=[t_c3])
                    S.dma("sp", lambda e: e.dma_start(out=negmask[:], in_=cd["negmask"][:, :]), writes=[t_c3])
                    S.dma("sp", lambda e: e.dma_start(out=biasfix[:], in_=cd["biasfix"][:, :]), writes=[t_c3])
                    Mrow = sbt(es, "Mrow", [8, 4, SEQ], BF16)
                    t_M = TL(4)
                    stab = sbt(es, "stab", [8, SEQ], F32)
                    sqt = [sbt(es, f"sqt{i}", [128, 512], BF16) for i in range(2)]
                    t_sq = TL(2)
                    kmx = sbt(es, "kmx", [8, 8], F32)
                    kb32 = sbt(es, "kb32", [128, 8], F32)
                    kbar = sbt(es, "kbar", [128, 8], BF16)
                    gm = sbt(es, "gm", [128, 64], F32)
                    top8 = sbt(es, "top8", [128, 64], F32)
                    sel = sbt(es, "sel", [128, 64], F32)
                    t_stab, t_kmx, t_kb, t_gm, t_top, t_sel = T(), T(), T(), T(), T(), T()
                    si = [0]
                    for h in range(4):
                        for c in range(4):
                            i = si[0] % 2
                            si[0] += 1
                            S.op("act", lambda e, i=i, h=h, c=c: e.activation(out=sqt[i][:], in_=qkT[:, 4 + h, c * 512:(c + 1) * 512], func=AF.Square),
                                 reads=[t_qk[4 + h][c]], writes=[t_sq[i]])
                            S.op("pe", lambda e, i=i: e.matmul(psf[6][0:8, :], lhsT=onesb[:, 0:8], rhs=sqt[i][:], start=True, stop=True),
                                 reads=[t_sq[i], t_const], writes=[pb[6]])
                            S.op("dve", lambda e, c=c: e.tensor_reduce(out=kmx[:, c:c + 1], in_=psf[6][0:8, :], axis=AX.X, op=ALU.max),
                                 reads=[pb[6]], writes=[t_kmx])
                        S.op("dve", lambda e: e.tensor_reduce(out=kmx[:, 4:5], in_=kmx[:, 0:4], axis=AX.X, op=ALU.max), reads=[t_kmx], writes=[t_kmx])
                        for c in range(4):
                            i = si[0] % 2
                            si[0] += 1
                            S.op("act", lambda e, i=i, h=h, c=c: e.activation(out=sqt[i][:], in_=qkT[:, h, c * 512:(c + 1) * 512], func=AF.Square),
                                 reads=[t_qk[h][c]], writes=[t_sq[i]])
                            S.op("pe", lambda e, i=i: e.matmul(psf[6][0:8, :], lhsT=onesb[:, 0:8], rhs=sqt[i][:], start=True, stop=True),
                                 reads=[t_sq[i], t_const], writes=[pb[6]])
                            S.op("act", lambda e, c=c: e.activation(out=stab[:, c * 512:(c + 1) * 512], in_=psf[6][0:8, :], func=AF.Sqrt, scale=kmx[:, 4:5]),
                                 reads=[pb[6], t_kmx], writes=[t_stab])
                        S.op("dve", lambda e, h=h: e.tensor_reduce(out=kb32[:], in_=qkT[:, 4 + h, :].rearrange("p (n s) -> p n s", s=256), axis=AX.X, op=ALU.add),
                             reads=t_qk[4 + h], writes=[t_kb])
                        S.op("dve", lambda e: e.tensor_scalar(out=kbar[:], in0=kb32[:], scalar1=1.0 / 256, scalar2=None, op0=ALU.mult),
                             reads=[t_kb], writes=[t_kb])
                        for i8 in range(8):
                            S.op("pe", lambda e, h=h, i8=i8: e.matmul(psf[5][:, i8 * 8:(i8 + 1) * 8], lhsT=qkT[:, h, (8 + i8) * 128:(9 + i8) * 128], rhs=kbar[:], start=True, stop=True),
                                 reads=[t_qk[h][2 + i8 // 4], t_kb], writes=[pb[5]])
                        S.op("dve", lambda e: e.tensor_tensor(out=gm[:], in0=psf[5][:, 0:64], in1=negmask[:], op=ALU.add), reads=[pb[5], t_c3], writes=[t_gm])
                        for i8 in range(8):
                            S.op("dve", lambda e, i8=i8: e.max(out=top8[:, i8 * 8:(i8 + 1) * 8], in_=gm[:, i8 * 8:(i8 + 1) * 8]), reads=[t_gm], writes=[t_top])
                        for i8 in range(8):
                            S.op("dve", lambda e, i8=i8: e.tensor_scalar(out=sel[:, i8 * 8:(i8 + 1) * 8], in0=gm[:, i8 * 8:(i8 + 1) * 8],
                                                                       scalar1=top8[:, i8 * 8 + 2:i8 * 8 + 3], scalar2=None, op0=ALU.is_ge),
                                 reads=[t_gm, t_top], writes=[t_sel])
                        S.op("dve", lambda e: e.scalar_tensor_tensor(out=sel[:], in0=sel[:], scalar=-NEGB, in1=biasfix[:], op0=ALU.mult, op1=ALU.add),
                             reads=[t_sel, t_c3], writes=[t_sel])
                        for i8 in range(8):
                            bk = 3 + i8 // 4
                            S.op("pe", lambda e, i8=i8, bk=bk: e.transpose(out=psf[bk][0:8, (i8 % 4) * 128:(i8 % 4 + 1) * 128], in_=sel[:, i8 * 8:(i8 + 1) * 8], identity=identf[:]),
                                 reads=[t_sel, t_const], writes=[pb[bk]])
                        S.op("act", lambda e, h=h: e.activation(out=Mrow[:, h, 0:1024], in_=stab[:, 0:1024], func=AF.Copy, scale=-1.0),
                             reads=[t_stab], writes=[t_M[h]])
                        for hf in range(2):
                            S.op("dve", lambda e, h=h, hf=hf: e.tensor_tensor(out=Mrow[:, h, 1024 + hf * 512:1536 + hf * 512], in0=psf[3 + hf][0:8, :],
                                                                            in1=stab[:, 1024 + hf * 512:1536 + hf * 512], op=ALU.subtract),
                                 reads=[pb[3 + hf], t_stab], writes=[t_M[h]])

                    PT = [sbt(es, f"PT{i}", [128, 512], BF16) for i in range(3)]
                    t_PT = TL(3)
                    lns = sbt(es, "lns", [128, 512], F32)
                    rinv = sbt(es, "rinv", [128, 512], F32)
                    ot = sbt(es, "ot", [128, 512], F32)
                    t_lns, t_rinv, t_ot = T(), T(), T()
                    scale = 1.0 / math.sqrt(HD)
                    items = [(h, qc, kt) for h in range(4) for qc in range(4) for kt in range(4 * qc + 4)]

                    def emit_S(idx):
                        h, qc, kt = items[idx]
                        sb_i = idx % 2
                        off = max(0, kt * 128 - qc * 512)
                        q0 = qc * 512 + off
                        q1 = (qc + 1) * 512
                        n = kt // 2
                        diag = kt >= 4 * qc
                        S.op("pe", lambda e: e.matmul(psf[sb_i][:, off:512], lhsT=qkT[:, 4 + h, kt * 128:(kt + 1) * 128], rhs=qkT[:, h, q0:q1], start=True, stop=False),
                             reads=[t_qk[4 + h][kt // 4], t_qk[h][qc]], writes=[pb[sb_i]])
                        S.op("pe", lambda e: e.matmul(psf[sb_i][:, off:512], lhsT=e8[:, n * 128:(n + 1) * 128], rhs=Mrow[:, h, q0:q1], start=False, stop=(not diag)),
                             reads=[t_M[h], t_c3], writes=[pb[sb_i]])
                        if diag:
                            S.op("pe", lambda e: e.matmul(psf[sb_i][:, off:off + 128], lhsT=identb[:], rhs=trineg[:], start=False, stop=True),
                                 reads=[t_c3, t_const], writes=[pb[sb_i]])
                        pi = idx % 3
                        S.op("act", lambda e: e.activation(out=PT[pi][:, off:512], in_=psf[sb_i][:, off:512], func=AF.Exp, scale=scale),
                             reads=[pb[sb_i]], writes=[t_PT[pi]])

                    def emit_PV(idx):
                        h, qc, kt = items[idx]
                        off = max(0, kt * 128 - qc * 512)
                        pi = idx % 3
                        par = (h * 4 + qc) % 2
                        ob, sbk = 2 + par, 4 + par
                        last = (kt == 4 * qc + 3)
                        S.op("pe", lambda e: e.matmul(psf[ob][:, off:512], lhsT=Vt[:, kt, h * 128:(h + 1) * 128], rhs=PT[pi][:, off:512], start=(kt == 0), stop=last),
                             reads=[t_V[kt], t_PT[pi]], writes=[pb[ob]])
                        S.op("pe", lambda e: e.matmul(psf[sbk][:, off:512], lhsT=onesb[:], rhs=PT[pi][:, off:512], start=(kt == 0), stop=last),
                             reads=[t_const, t_PT[pi]], writes=[pb[sbk]])
                        if last:
                            S.op("act", lambda e: e.activation(out=lns[:], in_=psf[sbk][:], func=AF.Ln), reads=[pb[sbk]], writes=[t_lns])
                            S.op("act", lambda e: e.activation(out=rinv[:], in_=lns[:], func=AF.Exp, scale=-1.0), reads=[t_lns], writes=[t_rinv])
                            S.op("dve", lambda e: e.tensor_tensor(out=ot[:], in0=psf[ob][:], in1=rinv[:], op=ALU.mult), reads=[pb[ob], t_rinv], writes=[t_ot])
                            S.op("pool", lambda e: e.tensor_tensor(out=gaT[:, h, qc * 512:(qc + 1) * 512], in0=ot[:], in1=gaT[:, h, qc * 512:(qc + 1) * 512], op=ALU.mult),
                                 reads=[t_ot, t_ga[h][qc]], writes=[t_ga[h][qc]])

                    emit_S(0)
                    for idx in range(len(items)):
                        if idx + 1 < len(items):
                            emit_S(idx + 1)
                        emit_PV(idx)
                    S.flush()
                if os.environ.get("KSTOP") == "CD":
                    return

                with ExitStack() as es:
                    t_gA = [T() for _ in range(4)]
                    t_gB = [T() for _ in range(4)]
                    phase_outproj(es, e_w_out, post_g[0:1, :], x_d, b, gaT, gbT, t_gA, t_gB, wo0, t_wo0)
                    S.flush()

        L1 = top

        def layer1_all(nb):
            with ExitStack() as P1:
                wv_sb = sbt(P1, "wv_sb", [128, 4, 8, 2, 128], BF16)
                toep_sb = sbt(P1, "toep_sb", [128, 4, 8, 256], BF16)
                w3_sb = sbt(P1, "w3_sb", [128, 16, 2, 256], BF16)
                pw_tab = sbt(P1, "pw_tab", [128, 16, 8, 3], F32)
                dA = sbt(P1, "dA", [128, 4], F32)
                glub = sbt(P1, "glub", [128, 8], F32)
                lng = sbt(P1, "lng", [128, 4], F32)
                lnb = sbt(P1, "lnb", [128, 4], F32)
                dwT = sbt(P1, "dwT", [128, 4, 31], F32)
                ones512 = sbt(P1, "ones512", [128, 128], BF16)
                t_par = T()
                for dst, src in ((dA, o_dA), (glub, o_glub), (lng, o_lng), (lnb, o_lnb)):
                    S.dma("sp", lambda e, dst=dst, src=src: e.dma_start(out=dst[:], in_=src[:, :]), writes=[t_par])
                S.dma("sp", lambda e: e.dma_start(out=dwT[:], in_=o_dwT[:, :, :]), writes=[t_par])
                S.op("act", lambda e: e.activation(out=ones512[:], in_=onesb[:], func=AF.Copy, scale=1.0 / 512), reads=[t_const], writes=[t_par])

                with ExitStack() as es:
                    tkA, tkB, t_m = T(), T(), T()
                    cur = {"eng": "pool", "tk": tkA}

                    def vop(fn, eng=None):
                        S.op(eng or cur["eng"], fn, reads=[cur["tk"], t_par, t_m], writes=[cur["tk"]])

                    def pop(fn, eng=None):
                        S.op(eng or cur["eng"], fn, reads=[cur["tk"], t_par, t_m], writes=[T()])

                    def tt(o, a, b_, op, eng=None):
                        vop(lambda e: e.tensor_tensor(out=o, in0=a, in1=b_, op=op), eng)

                    def ld(name, src, shape, tok=None):
                        t = sbt(es, name, shape, F32)
                        S.dma("sp", lambda e: e.dma_start(out=t[:], in_=src), writes=[tok if tok is not None else cur["tk"]])
                        return t

                    def compute_a(tag, lr, li, dtl, n):
                        mk = lambda nm: sbt(es, f"{tag}_{nm}", [128, n], F32)
                        dtv, x1, mg, th, u, r, sn_, cs_, ar, ai = [mk(k) for k in ("dt", "x1", "mg", "th", "u", "r", "sn", "cs", "ar", "ai")]
                        ui = sbt(es, f"{tag}_ui", [128, n], I32)
                        vop(lambda e: e.activation(out=dtv[:], in_=dtl[:], func=AF.Exp), "act")
                        tt(x1[:], lr[:], dtv[:], ALU.mult)
                        vop(lambda e: e.activation(out=mg[:], in_=x1[:], func=AF.Exp), "act")
                        tt(th[:], li[:], dtv[:], ALU.mult)
                        for shift, dst in ((0.0, sn_), (math.pi / 2, cs_)):
                            vop(lambda e, shift=shift: e.tensor_scalar(out=u[:], in0=th[:], scalar1=shift, scalar2=1.0 / (2 * math.pi), op0=ALU.add, op1=ALU.mult))
                            vop(lambda e: e.tensor_copy(out=ui[:], in_=u[:]), "dve")
                            vop(lambda e: e.tensor_copy(out=u[:], in_=ui[:]), "dve")
                            vop(lambda e: e.tensor_scalar(out=u[:], in0=u[:], scalar1=-2 * math.pi, scalar2=None, op0=ALU.mult))
                            tt(r[:], u[:], th[:], ALU.add)
                            vop(lambda e, shift=shift: e.tensor_scalar(out=r[:], in0=r[:], scalar1=shift, scalar2=None, op0=ALU.add))
                            vop(lambda e, dst=dst: e.activation(out=dst[:], in_=r[:], func=AF.Sin), "act")
                        tt(ar[:], mg[:], cs_[:], ALU.mult)
                        tt(ai[:], mg[:], sn_[:], ALU.mult)
                        return ar, ai

                    lrA = ld("lrA", o_lamre_A[:, :], [128, 256])
                    liA = ld("liA", o_lamim_A[:, :], [128, 256])
                    dtA = ld("dtA", o_dt_A[:, :], [128, 256])
                    brA = ld("brA", o_bre_A[:, :], [128, 256])
                    biA = ld("biA", o_bim_A[:, :], [128, 256])
                    mA = ld("mA", cd["maskA"][:, :], [128, 2], t_m)
                    mB = ld("mB", cd["maskB"][:, :], [128, 2], t_m)
                    arA, aiA = compute_a("A", lrA, liA, dtA, 256)
                    mk = lambda nm, n=256: sbt(es, nm, [128, n], F32)
                    nr, den, t1, t2, fr, fi = [mk(k) for k in ("nr", "den", "t1", "t2", "fr", "fi")]
                    vop(lambda e: e.tensor_scalar(out=nr[:], in0=arA[:], scalar1=-1.0, scalar2=None, op0=ALU.add))
                    tt(t1[:], lrA[:], lrA[:], ALU.mult)
                    tt(t2[:], liA[:], liA[:], ALU.mult)
                    tt(den[:], t1[:], t2[:], ALU.add)
                    vop(lambda e: e.reciprocal(out=den[:], in_=den[:]), "dve")
                    tt(t1[:], nr[:], lrA[:], ALU.mult)
                    tt(t2[:], aiA[:], liA[:], ALU.mult)
                    tt(t1[:], t1[:], t2[:], ALU.add)
                    tt(fr[:], t1[:], den[:], ALU.mult)
                    tt(t1[:], aiA[:], lrA[:], ALU.mult)
                    tt(t2[:], nr[:], liA[:], ALU.mult)
                    tt(t1[:], t1[:], t2[:], ALU.subtract)
                    tt(fi[:], t1[:], den[:], ALU.mult)
                    Gall = sbt(es, "Gall", [128, 8, 2, 256], F32)
                    tt(t1[:], fr[:], brA[:], ALU.mult)
                    tt(t2[:], fi[:], biA[:], ALU.mult)
                    tt(Gall[:, 0, 0, :], t1[:], t2[:], ALU.subtract)
                    tt(t1[:], fr[:], biA[:], ALU.mult)
                    tt(t2[:], fi[:], brA[:], ALU.mult)
                    tt(Gall[:, 0, 1, :], t1[:], t2[:], ALU.add)
                    for m in range(7):
                        tt(t1[:], Gall[:, m, 0, :], arA[:], ALU.mult)
                        tt(t2[:], Gall[:, m, 1, :], aiA[:], ALU.mult)
                        tt(Gall[:, m + 1, 0, :], t1[:], t2[:], ALU.subtract)
                        tt(t1[:], Gall[:, m, 0, :], aiA[:], ALU.mult)
                        tt(t2[:], Gall[:, m, 1, :], arA[:], ALU.mult)
                        tt(Gall[:, m + 1, 1, :], t1[:], t2[:], ALU.add)
                    for s in range(8):
                        for ri in range(2):
                            for gi in range(2):
                                pop(lambda e, s=s, ri=ri, gi=gi: e.tensor_scalar(
                                    out=wv_sb[:, :, s, ri, gi * 64:(gi + 1) * 64],
                                    in0=Gall[:, 7 - s, ri, :].rearrange("p (c n) -> p c n", c=4),
                                    scalar1=mA[:, gi:gi + 1], scalar2=None, op0=ALU.mult))
                    Kall = sbt(es, "Kall", [128, 4, 8, 16], F32)
                    cur["eng"], cur["tk"] = "dve", tkB
                    lrB = ld("lrB", o_lamre_B[:, :], [128, 16])
                    liB = ld("liB", o_lamim_B[:, :], [128, 16])
                    dtB = ld("dtB", o_dt_B[:, :], [128, 16])
                    crB = ld("crB", o_cre_B[:, :], [128, 256])
                    ciB = ld("ciB", o_cim_B[:, :], [128, 256])
                    arB, aiB = compute_a("B", lrB, liB, dtB, 16)
                    PB = sbt(es, "PB", [128, 8, 2, 16], F32)
                    s1 = sbt(es, "s1", [128, 16], F32)
                    s2 = sbt(es, "s2", [128, 16], F32)
                    vop(lambda e: e.tensor_copy(out=PB[:, 0, 0, :], in_=arB[:]))
                    vop(lambda e: e.tensor_copy(out=PB[:, 0, 1, :], in_=aiB[:]))
                    for r in range(7):
                        tt(s1[:], PB[:, r, 0, :], arB[:], ALU.mult)
                        tt(s2[:], PB[:, r, 1, :], aiB[:], ALU.mult)
                        tt(PB[:, r + 1, 0, :], s1[:], s2[:], ALU.subtract)
                        tt(s1[:], PB[:, r, 0, :], aiB[:], ALU.mult)
                        tt(s2[:], PB[:, r, 1, :], arB[:], ALU.mult)
                        tt(PB[:, r + 1, 1, :], s1[:], s2[:], ALU.add)
                    brB = ld("brB", o_bre_B[:, :], [128, 256])
                    biB = ld("biB", o_bim_B[:, :], [128, 256])
                    mkb = lambda nm: sbt(es, nm, [128, 16], F32)
                    nrB, denB, x1B, x2B, frB, fiB = [mkb(k) for k in ("nrB", "denB", "x1B", "x2B", "frB", "fiB")]
                    vop(lambda e: e.tensor_scalar(out=nrB[:], in0=arB[:], scalar1=-1.0, scalar2=None, op0=ALU.add))
                    tt(x1B[:], lrB[:], lrB[:], ALU.mult)
                    tt(x2B[:], liB[:], liB[:], ALU.mult)
                    tt(denB[:], x1B[:], x2B[:], ALU.add)
                    vop(lambda e: e.reciprocal(out=denB[:], in_=denB[:]), "dve")
                    tt(x1B[:], nrB[:], lrB[:], ALU.mult)
                    tt(x2B[:], aiB[:], liB[:], ALU.mult)
                    tt(x1B[:], x1B[:], x2B[:], ALU.add)
                    tt(frB[:], x1B[:], denB[:], ALU.mult)
                    tt(x1B[:], aiB[:], lrB[:], ALU.mult)
                    tt(x2B[:], nrB[:], liB[:], ALU.mult)
                    tt(x1B[:], x1B[:], x2B[:], ALU.subtract)
                    tt(fiB[:], x1B[:], denB[:], ALU.mult)
                    w1 = sbt(es, "w1", [128, 256], F32)
                    w2 = sbt(es, "w2", [128, 256], F32)
                    bbr = sbt(es, "bbrB", [128, 256], F32)
                    bbi = sbt(es, "bbiB", [128, 256], F32)
                    v3 = lambda t: t[:, :].rearrange("p (a i) -> p a i", a=16)
                    bc = lambda t: t[:, :].unsqueeze(2).broadcast_to([128, 16, 16])
                    tt(v3(w1), v3(brB), bc(frB), ALU.mult)
                    tt(v3(w2), v3(biB), bc(fiB), ALU.mult)
                    tt(bbr[:], w1[:], w2[:], ALU.subtract)
                    tt(v3(w1), v3(biB), bc(frB), ALU.mult)
                    tt(v3(w2), v3(brB), bc(fiB), ALU.mult)
                    tt(bbi[:], w1[:], w2[:], ALU.add)
                    Bmr = sbt(es, "Bmr", [128, 16, 128], F32)
                    Bmi = sbt(es, "Bmi", [128, 16, 128], F32)
                    vop(lambda e: e.memset(Bmr[:], 0.0), "pool")
                    vop(lambda e: e.memset(Bmi[:], 0.0), "pool")
                    for q in range(4):
                        for gi in range(2):
                            c0 = 32 * q + 16 * gi
                            vop(lambda e, q=q, gi=gi, c0=c0: e.tensor_scalar(
                                out=Bmr[:, :, :].rearrange("p (c q) m -> p c q m", q=4)[:, :, q, c0:c0 + 16],
                                in0=bbr[:, :].rearrange("p (c q j) -> p c q j", q=4, j=16)[:, :, q, :],
                                scalar1=mB[:, gi:gi + 1], scalar2=None, op0=ALU.mult))
                            vop(lambda e, q=q, gi=gi, c0=c0: e.tensor_scalar(
                                out=Bmi[:, :, :].rearrange("p (c q) m -> p c q m", q=4)[:, :, q, c0:c0 + 16],
                                in0=bbi[:, :].rearrange("p (c q j) -> p c q j", q=4, j=16)[:, :, q, :],
                                scalar1=mB[:, gi:gi + 1], scalar2=-1.0, op0=ALU.mult, op1=ALU.mult))
                    CAr = sbt(es, "CAr", [128, 9, 16, 16], F32)
                    CAi = sbt(es, "CAi", [128, 9, 16, 16], F32)
                    vop(lambda e: e.tensor_copy(out=CAr[:, 0, :, :], in_=v3(crB)))
                    vop(lambda e: e.tensor_copy(out=CAi[:, 0, :, :], in_=v3(ciB)))
                    for r in range(8):
                        pre = PB[:, r, 0, :].unsqueeze(2).broadcast_to([128, 16, 16])
                        pim = PB[:, r, 1, :].unsqueeze(2).broadcast_to([128, 16, 16])
                        tt(v3(w1), v3(crB), pre, ALU.mult)
                        tt(v3(w2), v3(ciB), pim, ALU.mult)
                        tt(CAr[:, r + 1, :, :], v3(w1), v3(w2), ALU.subtract)
                        tt(v3(w1), v3(crB), pim, ALU.mult)
                        tt(v3(w2), v3(ciB), pre, ALU.mult)
                        tt(CAi[:, r + 1, :, :], v3(w1), v3(w2), ALU.add)
                        for gi in range(2):
                            pop(lambda e, r=r, gi=gi: e.tensor_scalar(out=w3_sb[:, :, 0, gi * 128 + r * 16:gi * 128 + r * 16 + 16], in0=CAr[:, r + 1, :, :],
                                                                    scalar1=mB[:, gi:gi + 1], scalar2=None, op0=ALU.mult))
                            pop(lambda e, r=r, gi=gi: e.tensor_scalar(out=w3_sb[:, :, 1, gi * 128 + r * 16:gi * 128 + r * 16 + 16], in0=CAi[:, r + 1, :, :],
                                                                    scalar1=mB[:, gi:gi + 1], scalar2=-1.0, op0=ALU.mult, op1=ALU.mult))
                    for ct in range(4):
                        for q in range(4):
                            p = 4 * ct + q
                            S.op("pe", lambda e, p=p, q=q: e.matmul(psf[0][:, 0:128], lhsT=Bmr[:, p, :], rhs=CAr[:, 0:8, p, :], start=(q == 0), stop=False),
                                 reads=[tkB], writes=[pb[0]], inc=False)
                            S.op("pe", lambda e, p=p, q=q: e.matmul(psf[0][:, 0:128], lhsT=Bmi[:, p, :], rhs=CAi[:, 0:8, p, :], start=False, stop=(q == 3)),
                                 reads=[tkB], writes=[pb[0]], inc=(q == 3))
                        S.op("dve", lambda e, ct=ct: e.tensor_copy(out=Kall[:, ct, :, :], in_=psf[0][:, 0:128].rearrange("p (t i) -> p t i", t=8)),
                             reads=[pb[0], tkB], writes=[tkB])
                    vop(lambda e: e.memset(toep_sb[:], 0.0), "pool")
                    for s in range(8):
                        for gi in range(2):
                            vop(lambda e, s=s, gi=gi: e.tensor_scalar(
                                out=toep_sb[:, :, s, gi * 128 + s * 16:gi * 128 + 128],
                                in0=Kall[:, :, 0:8 - s, :].rearrange("p c t i -> p c (t i)"),
                                scalar1=mA[:, gi:gi + 1], scalar2=None, op0=ALU.mult))
                    qr = sbt(es, "qr", [128, 16], F32)
                    qi = sbt(es, "qi", [128, 16], F32)
                    vop(lambda e: e.tensor_copy(out=qr[:], in_=PB[:, 7, 0, :]))
                    vop(lambda e: e.tensor_copy(out=qi[:], in_=PB[:, 7, 1, :]))
                    for m in range(8):
                        pop(lambda e, m=m: e.tensor_copy(out=pw_tab[:, :, m, 0], in_=qr[:]))
                        pop(lambda e, m=m: e.tensor_copy(out=pw_tab[:, :, m, 1], in_=qi[:]))
                        pop(lambda e, m=m: e.tensor_scalar(out=pw_tab[:, :, m, 2], in0=qi[:], scalar1=-1.0, scalar2=None, op0=ALU.mult))
                        if m < 7:
                            tt(s1[:], qr[:], qr[:], ALU.mult)
                            tt(s2[:], qi[:], qi[:], ALU.mult)
                            tt(s2[:], s1[:], s2[:], ALU.subtract)
                            tt(s1[:], qr[:], qi[:], ALU.mult)
                            vop(lambda e: e.tensor_scalar(out=qi[:], in0=s1[:], scalar1=2.0, scalar2=None, op0=ALU.mult))
                            vop(lambda e: e.tensor_copy(out=qr[:], in_=s2[:]))
                    S.flush()

                for b in range(nb):
                    layer1(b, wv_sb, toep_sb, w3_sb, pw_tab, dA, glub, lng, lnb, dwT, ones512, t_par)

        def layer1(b, wv_sb, toep_sb, w3_sb, pw_tab, dA, glub, lng, lnb, dwT, ones512, t_par):
            with ExitStack() as L:
                suT = sbt(L, "suT", [128, 4, SEQ], BF16)
                gcT = sbt(L, "gcT", [128, 4, SEQ], BF16)
                gdT = sbt(L, "gdT", [128, 4, SEQ], BF16)
                gpad = sbt(L, "gpad", [128, 4, SEQ + 32], BF16)
                t_su = [TL(4) for _ in range(4)]
                t_gc = [TL(4) for _ in range(4)]
                t_gd = [TL(4) for _ in range(4)]
                t_gp = TL(4)
                wo1 = sbt(L, "wo1", [128, 8, D], BF16)
                t_wo1 = T()
                pwsb = sbt(L, "pwsb", [128, 4, 512], BF16)
                t_pw = T()
                with ExitStack() as es:
                    hT = sbt(es, "hT1", [128, 8, SEQ], BF16)
                    t_hT = TL(16)
                    phase_norm_T(es, out_d, b, pre_g[1:2, :], hT, t_hT, nxb=2)
                    wb = [sbt(es, f"wb1_{i}", [128, 8, 512], BF16) for i in range(2)]
                    t_wb = TL(2)
                    sg = [sbt(es, f"sg{i}", [128, 512], BF16) for i in range(2)]
                    t_sg = TL(2)
                    order = [0, 1, 4, 2, 3]

                    def loadw(oi):
                        gi = order[oi]
                        S.dma("pool", lambda e: e.dma_start(out=wb[oi % 2][:], in_=o_w_in[:, gi * 512:(gi + 1) * 512].rearrange("(k p) n -> p k n", p=128)),
                              writes=[t_wb[oi % 2]])
                    loadw(0)
                    loadw(1)
                    S.op("pool", lambda e: e.memset(gpad[:, :, 0:32], 0.0), writes=t_gp)
                    cnt = [0]

                    def proj(wbi, t_wbi, m, c):
                        bi = cnt[0] % 3
                        cnt[0] += 1
                        for k in range(8):
                            S.op("pe", lambda e, k=k: e.matmul(psf[bi][:], lhsT=wbi[:, k, m * 128:(m + 1) * 128], rhs=hT[:, k, c * 512:(c + 1) * 512],
                                                               start=(k == 0), stop=(k == 7)),
                                 reads=[t_wbi] + t_hT[4 * c:4 * c + 4], writes=[pb[bi]], inc=(k == 7))
                        return bi
                    for oi in range(3):
                        gi = order[oi]
                        dst, t_dst, fn = ((suT, t_su, AF.Copy), (gcT, t_gc, AF.Silu), None, None, (gdT, t_gd, AF.Silu))[gi]
                        for m in range(4):
                            for c in range(4):
                                bi = proj(wb[oi % 2], t_wb[oi % 2], m, c)
                                S.op("act", lambda e, bi=bi, m=m, c=c, dst=dst, fn=fn: e.activation(out=dst[:, m, c * 512:(c + 1) * 512], in_=psf[bi][:], func=fn),
                                     reads=[pb[bi]], writes=[t_dst[m][c]])
                        if oi + 2 < 5:
                            loadw(oi + 2)
                    si = 0
                    for m in range(4):
                        for c in range(4):
                            bi = proj(wb[0], t_wb[0], m, c)
                            i = si % 2
                            si += 1
                            S.op("act", lambda e, bi=bi, i=i: e.activation(out=sg[i][:], in_=psf[bi][:], func=AF.Sigmoid), reads=[pb[bi]], writes=[t_sg[i]])
                            bi2 = proj(wb[1], t_wb[1], m, c)
                            S.op("dve", lambda e, bi2=bi2, i=i, m=m, c=c: e.tensor_tensor(out=gpad[:, m, 32 + c * 512:32 + (c + 1) * 512], in0=psf[bi2][:], in1=sg[i][:], op=ALU.mult),
                                 reads=[pb[bi2], t_sg[i]], writes=[t_gp[m]])
                    S.flush()
                if os.environ.get("KSTOP") == "AB1":
                    return

                with ExitStack() as es:
                    Sb = [[[sbt(es, f"S{sl}{pi}{pp}", [128, 2, 512], F32) for pp in range(2)] for pi in range(2)] for sl in range(2)]
                    t_S = [[[T() for pp in range(2)] for pi in range(2)] for sl in range(2)]
                    Sp = [[[sbt(es, f"Sp{sl}{pi}{ri}", [128, 256], BF16) for ri in range(2)] for pi in range(2)] for sl in range(2)]
                    t_Sp = [[T() for pi in range(2)] for sl in range(2)]
                    for sl in range(2):
                        for pi in range(2):
                            for ri in range(2):
                                S.op("pool", lambda e, sl=sl, pi=pi, ri=ri: e.memset(Sp[sl][pi][ri][:, 0:1], 0.0), writes=[t_Sp[sl][pi]])
                                for pp in range(2):
                                    S.op("pool", lambda e, sl=sl, pi=pi, ri=ri, pp=pp: e.memset(Sb[sl][pi][pp][:, ri, 0:256], 0.0), writes=[t_S[sl][pi][pp]])
                    ysb = [sbt(es, f"ysb{i}", [128, 2, 8, 128], BF16) for i in range(2)]
                    t_ysb = [T(), T()]
                    gluw = sbt(es, "gluw", [128, 4, 1024], BF16)
                    t_gw = T()
                    for hf in range(2):
                        S.dma("pool", lambda e, hf=hf: e.dma_start(out=gluw[:, :, hf * 512:(hf + 1) * 512], in_=o_glu_w[:, hf * 512:(hf + 1) * 512].rearrange("(k p) n -> p k n", p=128)),
                              writes=[t_gw])
                    load_wo(wo1, t_wo1, o_w_out)
                    S.dma("pool", lambda e: e.dma_start(out=pwsb[:], in_=o_pw.rearrange("(k p) n -> p k n", p=128)), writes=[t_pw])
                    couples = [(ct, q0) for ct in range(4) for q0 in (0, 2)]

                    def emit_V(ci):
                        ct, q0 = couples[ci]
                        sl = ci % 2
                        for pi in range(2):
                            q = q0 + pi
                            rows = slice(32 * q, 32 * q + 32)
                            tp = (32 * q, 0)
                            for ri in range(2):
                                bk = (0, 1, 4, 5)[pi * 2 + ri]
                                for s in range(8):
                                    S.op("pe", lambda e, s=s, ri=ri, bk=bk, rows=rows, ct=ct, tp=tp: e.matmul(
                                        psf[bk][:, 0:256], lhsT=wv_sb[rows, ct, s, ri, :],
                                        rhs=suT[rows, ct, :].rearrange("p (k s) -> p s k", s=8)[:, s, :], start=(s == 0), stop=(s == 7), tile_position=tp),
                                        reads=t_su[ct] + [t_par], writes=[pb[bk]], inc=(s == 7))
                                S.op("act", lambda e, ri=ri, bk=bk, sl=sl, pi=pi: e.activation(out=Sb[sl][pi][0][:, ri, 256:512], in_=psf[bk][:, 0:256], func=AF.Copy),
                                     reads=[pb[bk]], writes=[t_S[sl][pi][0]])

                    def emit_scan(ci):
                        ct, q0 = couples[ci]
                        sl = ci % 2
                        for m in range(8):
                            sh = 1 << m
                            a, d_ = m % 2, (m + 1) % 2
                            for stage in range(3):
                                for pi in range(2):
                                    p = ct * 4 + q0 + pi
                                    src, dst = Sb[sl][pi][a], Sb[sl][pi][d_]
                                    ts, td = t_S[sl][pi][a], t_S[sl][pi][d_]
                                    pr = pw_tab[:, p, m, 0:1]
                                    pim = pw_tab[:, p, m, 1:2]
                                    npi = pw_tab[:, p, m, 2:3]
                                    if stage == 0:
                                        S.op("dve", lambda e, src=src, dst=dst, sh=sh, pr=pr: e.scalar_tensor_tensor(out=dst[:, :, 256:512], in0=src[:, :, 256 - sh:512 - sh], scalar=pr, in1=src[:, :, 256:512], op0=ALU.mult, op1=ALU.add),
                                             reads=[ts, t_par], writes=[td])
                                    elif stage == 1:
                                        S.op("dve", lambda e, src=src, dst=dst, sh=sh, npi=npi: e.scalar_tensor_tensor(out=dst[:, 0, 256:512], in0=src[:, 1, 256 - sh:512 - sh], scalar=npi, in1=dst[:, 0, 256:512], op0=ALU.mult, op1=ALU.add),
                                             reads=[ts, td, t_par], writes=[td])
                                    else:
                                        S.op("dve", lambda e, src=src, dst=dst, sh=sh, pim=pim: e.scalar_tensor_tensor(out=dst[:, 1, 256:512], in0=src[:, 0, 256 - sh:512 - sh], scalar=pim, in1=dst[:, 1, 256:512], op0=ALU.mult, op1=ALU.add),
                                             reads=[ts, td, t_par], writes=[td])

                    def emit_y(ci):
                        ct, q0 = couples[ci]
                        sl = ci % 2
                        yb_ = ysb[ct % 2]
                        t_y = t_ysb[ct % 2]
                        for pi in range(2):
                            q = q0 + pi
                            p = ct * 4 + q
                            rows = slice(32 * q, 32 * q + 32)
                            tp = (32 * q, 0)
                            for ri in range(2):
                                S.op("act", lambda e, ri=ri, sl=sl, pi=pi: e.activation(out=Sp[sl][pi][ri][:, 1:256], in_=Sb[sl][pi][0][:, ri, 256:511], func=AF.Copy),
                                     reads=[t_S[sl][pi][0]], writes=[t_Sp[sl][pi]])
                            for kt2 in range(2):
                                bk = 2 + kt2
                                for s in range(8):
                                    S.op("pe", lambda e, s=s, kt2=kt2, bk=bk, rows=rows, ct=ct, tp=tp: e.matmul(
                                        psf[bk][:, 0:256],
                                        lhsT=suT[rows, ct, kt2 * 1024:(kt2 + 1) * 1024].rearrange("p (k s) -> p s k", s=8)[:, s, :],
                                        rhs=toep_sb[rows, ct, s, :], start=(s == 0), stop=False, tile_position=tp),
                                        reads=t_su[ct] + [t_par], writes=[pb[bk]], inc=False)
                                for ri in range(2):
                                    S.op("pe", lambda e, ri=ri, kt2=kt2, bk=bk, sl=sl, pi=pi, p=p: e.matmul(
                                        psf[bk][:, 0:256], lhsT=Sp[sl][pi][ri][:, kt2 * 128:(kt2 + 1) * 128], rhs=w3_sb[:, p, ri, :], start=False, stop=(ri == 1)),
                                        reads=[t_Sp[sl][pi], t_par], writes=[pb[bk]], inc=(ri == 1))
                                S.op("act", lambda e, kt2=kt2, bk=bk, q=q, yb_=yb_: e.activation(
                                    out=yb_[:, kt2, :, q * 32:(q + 1) * 32].rearrange("p r (g i) -> p g r i", g=2),
                                    in_=psf[bk][:, 0:256].rearrange("p (g r i) -> p g r i", g=2, r=8), func=AF.Copy),
                                    reads=[pb[bk]], writes=[t_y])

                    def emit_T(ct):
                        yb_ = ysb[ct % 2]
                        t_y = t_ysb[ct % 2]
                        for kt2 in range(2):
                            for r in range(8):
                                S.op("pe", lambda e, kt2=kt2, r=r, yb_=yb_: e.transpose(out=psb[:, r * 128:(r + 1) * 128], in_=yb_[:, kt2, r, :], identity=identb[:]),
                                     reads=[t_y, t_const], writes=[pb[7]], inc=(r == 7))
                            S.op("dve", lambda e, kt2=kt2, ct=ct: e.scalar_tensor_tensor(
                                out=suT[:, ct, kt2 * 1024:(kt2 + 1) * 1024].rearrange("p (k r) -> p r k", r=8),
                                in0=suT[:, ct, kt2 * 1024:(kt2 + 1) * 1024].rearrange("p (k r) -> p r k", r=8),
                                scalar=dA[:, ct:ct + 1],
                                in1=psb[:, :].rearrange("p (r k) -> p r k", r=8), op0=ALU.mult, op1=ALU.add),
                                reads=[pb[7], t_par] + t_su[ct], writes=t_su[ct])

                    emit_V(0)
                    for ci in range(8):
                        if ci + 1 < 8:
                            emit_V(ci + 1)
                        emit_scan(ci)
                        emit_y(ci)
                        if ci % 2 == 1:
                            emit_T(couples[ci][0])
                    if dbg_d is not None and b == 0:
                        S.dma("pool", lambda e: e.dma_start(out=dbg_d[:, :, :], in_=suT[:]), reads=[t for tl in t_su for t in tl])
                    sgf = [sbt(es, f"sgf{i}", [128, 512], F32) for i in range(2)]
                    tgf = [sbt(es, f"tgf{i}", [128, 512], F32) for i in range(2)]
                    t_sgf, t_tgf = TL(2), TL(2)
                    gi_ = 0
                    for c in range(4):
                        for mt in range(4):
                            i = gi_ % 2
                            gi_ += 1
                            ba, bb = 4 + i, 4 + (1 - i)
                            bka = 4 + i
                            bkb = i
                            for ct in range(4):
                                S.op("pe", lambda e, ct=ct, mt=mt, c=c, bka=bka: e.matmul(psf[bka][:], lhsT=gluw[:, ct, mt * 128:(mt + 1) * 128], rhs=suT[:, ct, c * 512:(c + 1) * 512], start=(ct == 0), stop=(ct == 3)),
                                     reads=[t_gw] + [t_su[ct][c]], writes=[pb[bka]], inc=(ct == 3))
                            for ct in range(4):
                                S.op("pe", lambda e, ct=ct, mt=mt, c=c, bkb=bkb: e.matmul(psf[bkb][:], lhsT=gluw[:, ct, (4 + mt) * 128:(5 + mt) * 128], rhs=suT[:, ct, c * 512:(c + 1) * 512], start=(ct == 0), stop=(ct == 3)),
                                     reads=[t_gw] + [t_su[ct][c]], writes=[pb[bkb]], inc=(ct == 3))
                            S.op("act", lambda e, i=i, mt=mt, bkb=bkb: e.activation(out=sgf[i][:], in_=psf[bkb][:], func=AF.Sigmoid, bias=glub[:, 4 + mt:5 + mt]),
                                 reads=[pb[bkb], t_par], writes=[t_sgf[i]])
                            S.op("dve", lambda e, i=i, mt=mt, bka=bka: e.scalar_tensor_tensor(out=tgf[i][:], in0=psf[bka][:], scalar=glub[:, mt:mt + 1], in1=sgf[i][:], op0=ALU.add, op1=ALU.mult),
                                 reads=[pb[bka], t_sgf[i], t_par], writes=[t_tgf[i]])
                            S.op("pool", lambda e, i=i, mt=mt, c=c: e.tensor_tensor(out=gcT[:, mt, c * 512:(c + 1) * 512], in0=tgf[i][:], in1=gcT[:, mt, c * 512:(c + 1) * 512], op=ALU.mult),
                                 reads=[t_tgf[i], t_gc[mt][c]], writes=[t_gc[mt][c]])
                    S.flush()
                if os.environ.get("KSTOP") == "S5":
                    return

                with ExitStack() as es:
                    diag = [sbt(es, f"diag{i}", [128, 31, 128], BF16) for i in range(4)]
                    t_dg = TL(4)
                    for ct in range(4):
                        S.op("dve", lambda e, ct=ct: e.tensor_tensor(out=diag[ct][:], in0=identf[:, :].unsqueeze(1).broadcast_to([128, 31, 128]),
                                                                   in1=dwT[:, ct, :].unsqueeze(2).broadcast_to([128, 31, 128]), op=ALU.mult),
                             reads=[t_const, t_par], writes=[t_dg[ct]])
                    cf2 = None
                    c162 = [sbt(es, f"c16{i}", [128, 4, 512], BF16) for i in range(2)]
                    c22 = [sbt(es, f"c2{i}", [128, 4, 512], BF16) for i in range(2)]
                    sn2 = [sbt(es, f"sn{i}", [128, 4, 512], BF16) for i in range(2)]
                    t_cf2, t_c162, t_c22, t_sn2 = [TL(4), TL(4)], [TL(4), TL(4)], [TL(4), TL(4)], [TL(4), TL(4)]
                    mean2 = [sbt(es, "mean_sb0", [128, 512], F32)] * 2
                    m22 = [sbt(es, "m2_0", [128, 512], F32)] * 2
                    rstd2 = [sbt(es, "rstd0", [128, 512], F32)] * 2
                    t_mean2, t_m22, t_rstd2 = [T()] * 2, [T()] * 2, [T()] * 2
                    u1 = [sbt(es, f"u1_{i}", [128, 512], F32) for i in range(2)]
                    t_u1 = TL(2)
                    ui_box = [0]
                    def conv_part(c):
                            cf, c16, c2, sn = c162[c % 2], c162[c % 2], c22[c % 2], sn2[c % 2]
                            t_cf, t_c16, t_c2, t_sn = t_c162[c % 2], t_c162[c % 2], t_c22[c % 2], t_sn2[c % 2]
                            mean_sb, m2, rstd = mean2[c % 2], m22[c % 2], rstd2[c % 2]
                            t_mean, t_m2, t_rstd = t_mean2[c % 2], t_m22[c % 2], t_rstd2[c % 2]
                            for ct in range(4):
                                bk = ct % 2
                                for k in range(31):
                                    S.op("pe", lambda e, cf=cf, c16=c16, c2=c2, sn=sn, mean_sb=mean_sb, m2=m2, rstd=rstd, k=k, ct=ct, c=c, bk=bk: e.matmul(psf[bk][:], lhsT=diag[ct][:, k, :], rhs=gpad[:, ct, 2 + c * 512 + k:2 + c * 512 + k + 512],
                                                                                       start=(k == 0), stop=(k == 30)),
                                         reads=[t_dg[ct], t_gp[ct]], writes=[pb[bk]], inc=(k == 30))
                                pass
                                S.op("act", lambda e, cf=cf, c16=c16, c2=c2, sn=sn, mean_sb=mean_sb, m2=m2, rstd=rstd, ct=ct, bk=bk: e.activation(out=c16[:, ct, :], in_=psf[bk][:], func=AF.Copy), reads=[pb[bk]], writes=[t_c16[ct]])
                                S.op("act", lambda e, cf=cf, c16=c16, c2=c2, sn=sn, mean_sb=mean_sb, m2=m2, rstd=rstd, ct=ct, bk=bk: e.activation(out=c2[:, ct, :], in_=psf[bk][:], func=AF.Square), reads=[pb[bk]], writes=[t_c2[ct]])

                    def ln_part(c):
                            cf, c16, c2, sn = c162[c % 2], c162[c % 2], c22[c % 2], sn2[c % 2]
                            t_cf, t_c16, t_c2, t_sn = t_c162[c % 2], t_c162[c % 2], t_c22[c % 2], t_sn2[c % 2]
                            mean_sb, m2, rstd = mean2[c % 2], m22[c % 2], rstd2[c % 2]
                            t_mean, t_m2, t_rstd = t_mean2[c % 2], t_m22[c % 2], t_rstd2[c % 2]
                            for ct in range(4):
                                S.op("pe", lambda e, cf=cf, c16=c16, c2=c2, sn=sn, mean_sb=mean_sb, m2=m2, rstd=rstd, ct=ct: e.matmul(psf[2][:], lhsT=ones512[:], rhs=c16[:, ct, :], start=(ct == 0), stop=(ct == 3)),
                                     reads=[t_c16[ct], t_par], writes=[pb[2]], inc=(ct == 3))
                            for ct in range(4):
                                S.op("pe", lambda e, cf=cf, c16=c16, c2=c2, sn=sn, mean_sb=mean_sb, m2=m2, rstd=rstd, ct=ct: e.matmul(psf[3][:], lhsT=ones512[:], rhs=c2[:, ct, :], start=(ct == 0), stop=(ct == 3)),
                                     reads=[t_c2[ct], t_par], writes=[pb[3]], inc=(ct == 3))
                            S.op("act", lambda e, cf=cf, c16=c16, c2=c2, sn=sn, mean_sb=mean_sb, m2=m2, rstd=rstd: e.activation(out=mean_sb[:], in_=psf[2][:], func=AF.Copy), reads=[pb[2]], writes=[t_mean])
                            S.op("dve", lambda e, cf=cf, c16=c16, c2=c2, sn=sn, mean_sb=mean_sb, m2=m2, rstd=rstd: e.tensor_tensor(out=m2[:], in0=mean_sb[:], in1=mean_sb[:], op=ALU.mult), reads=[t_mean], writes=[t_m2])
                            S.op("dve", lambda e, cf=cf, c16=c16, c2=c2, sn=sn, mean_sb=mean_sb, m2=m2, rstd=rstd: e.tensor_tensor(out=m2[:], in0=psf[3][:], in1=m2[:], op=ALU.subtract), reads=[pb[3], t_m2], writes=[t_m2])
                            S.op("act", lambda e, cf=cf, c16=c16, c2=c2, sn=sn, mean_sb=mean_sb, m2=m2, rstd=rstd: e.activation(out=m2[:], in_=m2[:], func=AF.Ln, bias=EPS), reads=[t_m2], writes=[t_m2])
                            S.op("act", lambda e, cf=cf, c16=c16, c2=c2, sn=sn, mean_sb=mean_sb, m2=m2, rstd=rstd: e.activation(out=rstd[:], in_=m2[:], func=AF.Exp, scale=-0.5), reads=[t_m2], writes=[t_rstd])
                            for ct in range(4):
                                i = ui_box[0] % 2
                                ui_box[0] += 1
                                S.op("dve", lambda e, cf=cf, c16=c16, c2=c2, sn=sn, mean_sb=mean_sb, m2=m2, rstd=rstd, ct=ct, i=i: e.tensor_tensor(out=u1[i][:], in0=cf[:, ct, :], in1=mean_sb[:], op=ALU.subtract), reads=[t_cf[ct], t_mean], writes=[t_u1[i]])
                                S.op("dve", lambda e, cf=cf, c16=c16, c2=c2, sn=sn, mean_sb=mean_sb, m2=m2, rstd=rstd, i=i: e.tensor_tensor(out=u1[i][:], in0=u1[i][:], in1=rstd[:], op=ALU.mult), reads=[t_u1[i], t_rstd], writes=[t_u1[i]])
                                S.op("act", lambda e, cf=cf, c16=c16, c2=c2, sn=sn, mean_sb=mean_sb, m2=m2, rstd=rstd, ct=ct, i=i: e.activation(out=sn[:, ct, :], in_=u1[i][:], func=AF.Silu, scale=lng[:, ct:ct + 1], bias=lnb[:, ct:ct + 1]),
                                     reads=[t_u1[i], t_par], writes=[t_sn[ct]])

                    def pw_part(c):
                            cf, c16, c2, sn = c162[c % 2], c162[c % 2], c22[c % 2], sn2[c % 2]
                            t_cf, t_c16, t_c2, t_sn = t_c162[c % 2], t_c162[c % 2], t_c22[c % 2], t_sn2[c % 2]
                            mean_sb, m2, rstd = mean2[c % 2], m22[c % 2], rstd2[c % 2]
                            t_mean, t_m2, t_rstd = t_mean2[c % 2], t_m22[c % 2], t_rstd2[c % 2]
                            for mt in range(4):
                                bk = 4 + mt % 2
                                for ct in range(4):
                                    S.op("pe", lambda e, cf=cf, c16=c16, c2=c2, sn=sn, mean_sb=mean_sb, m2=m2, rstd=rstd, ct=ct, mt=mt, bk=bk: e.matmul(psf[bk][:], lhsT=pwsb[:, ct, mt * 128:(mt + 1) * 128], rhs=sn[:, ct, :], start=(ct == 0), stop=(ct == 3)),
                                         reads=[t_pw, t_sn[ct]], writes=[pb[bk]], inc=(ct == 3))
                                S.op("dve", lambda e, cf=cf, c16=c16, c2=c2, sn=sn, mean_sb=mean_sb, m2=m2, rstd=rstd, mt=mt, c=c, bk=bk: e.tensor_tensor(out=gdT[:, mt, c * 512:(c + 1) * 512], in0=psf[bk][:], in1=gdT[:, mt, c * 512:(c + 1) * 512], op=ALU.mult),
                                     reads=[pb[bk], t_gd[mt][c]], writes=[t_gd[mt][c]])

                    conv_part(0)
                    for c in range(4):
                        ln_part(c)
                        if c + 1 < 4:
                            conv_part(c + 1)
                        pw_part(c)
                    S.flush()
                if os.environ.get("KSTOP") == "CV":
                    return
                with ExitStack() as es:
                    phase_outproj(es, o_w_out, post_g[1:2, :], out_d, b, gcT, gdT, TL(4), TL(4), wo1, t_wo1)
                    S.flush()

        nbr = int(os.environ.get("KNB", NB))
        for b in range(nbr):
            layer0(b)
        if n_layers >= 2 and os.environ.get("KLAYERS", "2") == "2":
            layer1_all(nbr)

    return nc


def layer1_host_layouts(inputs, f):
    g = lambda k: np.asarray(inputs[k][0], dtype=np.float32)
    m = {}
    m["o_w_in"] = f(g("o_w_in"))

    def A_gn(a):
        a = np.repeat(a[:, None, :], 16, axis=1).reshape(4, 8, 16, 64)
        return f(a.transpose(1, 2, 0, 3).reshape(128, 256))
    m["o_lamre_A"] = A_gn(g("o_lam_re"))
    m["o_lamim_A"] = A_gn(g("o_lam_im"))
    m["o_dt_A"] = A_gn(np.repeat(g("o_log_dt")[:, None], 64, axis=1))

    def A_b(bm):
        a = bm.transpose(0, 2, 1).reshape(4, 8, 16, 64)
        return f(a.transpose(1, 2, 0, 3).reshape(128, 256))
    m["o_bre_A"] = A_b(g("o_b_re"))
    m["o_bim_A"] = A_b(g("o_b_im"))

    def B_gn(a):
        return f(a.reshape(16, 2, 64).transpose(1, 2, 0).reshape(128, 16))
    m["o_lamre_B"] = B_gn(g("o_lam_re"))
    m["o_lamim_B"] = B_gn(g("o_lam_im"))
    m["o_dt_B"] = B_gn(np.repeat(g("o_log_dt")[:, None], 64, axis=1))

    def B_c(cm):
        return f(cm.reshape(16, 2, 16, 64).transpose(1, 3, 0, 2).reshape(128, 256))
    def B_b(bm):
        return f(bm.reshape(16, 2, 64, 16).transpose(1, 2, 0, 3).reshape(128, 256))
    m["o_bre_B"] = B_b(g("o_b_re"))
    m["o_bim_B"] = B_b(g("o_b_im"))
    m["o_cre_B"] = B_c(g("o_c_re"))
    m["o_cim_B"] = B_c(g("o_c_im"))
    m["o_dA"] = f(g("o_d").reshape(4, 128).T)
    m["o_glu_w"] = f(g("o_glu_w"))
    m["o_glub"] = f(g("o_glu_b").reshape(8, 128).T)
    m["o_dwT"] = f(g("o_dw").reshape(31, 4, 128).transpose(2, 1, 0))
    m["o_lng"] = f(g("o_ln_g").reshape(4, 128).T)
    m["o_lnb"] = f(g("o_ln_b").reshape(4, 128).T)
    m["o_pw"] = f(g("o_pw"))
    m["o_w_out"] = f(g("o_w_out"))
    return m


def make_in_maps(inputs):
    n = 8
    x = np.ascontiguousarray(inputs["x"], dtype=np.float32)
    consts = host_consts()
    f = lambda a: np.ascontiguousarray(a, dtype=np.float32)
    l1maps = layer1_host_layouts(inputs, f)
    in_maps = []
    for c in range(n):
        m = {"x": x[NB * c:NB * (c + 1)]}
        for k in ("pre_norm_g", "post_norm_g"):
            m[k] = f(inputs[k])
        m["e_w_in"] = f(inputs["e_w_in"][0])
        m["e_pool_w"] = f(inputs["e_pool_w"][0])
        m["e_pool_scale"] = f(np.asarray(inputs["e_pool_scale"][0], dtype=np.float32).reshape(4, 128).T)
        m["e_w_out"] = f(inputs["e_w_out"][0])
        m.update(l1maps)
        for k, v in consts.items():
            m["c_" + k] = v
        in_maps.append(m)
    return in_maps


def kernel(**inputs):
    nc = build()
    in_maps = make_in_maps(inputs)
    res = run_bass_kernel_spmd(nc, in_maps, core_ids=list(range(8)))
    return np.concatenate([r["out"] for r in res.results], axis=0)
```

```python
import math
import os
from contextlib import ExitStack
import numpy as np
import concourse.bass as bass
import concourse.mybir as mybir
from concourse.bass_utils import run_bass_kernel_spmd

F32 = mybir.dt.float32
BF16 = mybir.dt.bfloat16
I32 = mybir.dt.int32
ALU = mybir.AluOpType
AF = mybir.ActivationFunctionType
AX = mybir.AxisListType

D = 1024
SEQ = 2048
NB = 2
HD = 128
EPS = 1e-6
NEGB = -30000.0
S5L = 8
NCH = SEQ // S5L


class T:
    __slots__ = ("w", "r")

    def __init__(self):
        self.w = None
        self.r = {}


def TL(n):
    return [T() for _ in range(n)]


class Sched:
    ENG = ("pe", "act", "dve", "pool", "sp")

    def __init__(self, nc, es, n_dma=24):
        self.nc = nc
        self.ops = {e: [] for e in self.ENG}
        self.cnt = {e: 0 for e in self.ENG}
        self.seen = {e: {} for e in self.ENG}
        self.n_dma = n_dma
        self.dma_cnt = [0] * n_dma
        self.dma_rr2 = {"sp": 0, "pool": 0}
        self.semh = {}
        for e in self.ENG:
            self.semh[("c", e)] = es.enter_context(nc.semaphore(f"s_{e}"))
        for i in range(n_dma):
            self.semh[("d", i)] = es.enter_context(nc.semaphore(f"s_d{i}"))

    def _deps(self, eng, reads, writes):
        need = {}
        for t in reads:
            if t.w is not None:
                k, v = t.w
                if need.get(k, 0) < v:
                    need[k] = v
        for t in writes:
            if t.w is not None:
                k, v = t.w
                if need.get(k, 0) < v:
                    need[k] = v
            for k, v in t.r.items():
                if need.get(k, 0) < v:
                    need[k] = v
        waits = []
        sn = self.seen[eng]
        for k, v in need.items():
            if eng == "pe" and k == ("c", "pe"):
                continue
            if sn.get(k, 0) < v:
                waits.append((k, v))
                sn[k] = v
        return waits

    def _record(self, key, v, reads, writes):
        for t in reads:
            if t.r.get(key, 0) < v:
                t.r[key] = v
        for t in writes:
            t.w = (key, v)
            t.r = {}

    def op(self, eng, fn, reads=(), writes=(), inc=True):
        waits = self._deps(eng, reads, writes)
        key = ("c", eng)
        if inc:
            self.cnt[eng] += 1
            v = self.cnt[eng]
        else:
            v = self.cnt[eng] + 1
        self.ops[eng].append([waits, fn, key, 1 if inc else 0])
        self._record(key, v, reads, writes)

    def dma(self, eng, fn, reads=(), writes=()):
        half = self.n_dma // 2
        base = 0 if eng == "sp" else half
        j = self.dma_rr2[eng]
        self.dma_rr2[eng] = (j + 1) % half
        i = base + j
        waits = self._deps(eng, reads, writes)
        key = ("d", i)
        prev = self.dma_cnt[i]
        if prev > 0 and self.seen[eng].get(key, 0) < prev:
            waits.append((key, prev))
            self.seen[eng][key] = prev
        self.dma_cnt[i] += 16
        self.ops[eng].append([waits, fn, key, 16])
        self._record(key, self.dma_cnt[i], reads, writes)

    def flush(self):
        nc = self.nc
        for e in ("pe", "act", "dve", "pool"):
            if self.ops[e] and self.ops[e][-1][3] == 0:
                self.ops[e][-1][3] = 1
                self.cnt[e] += 1
        fin = [(("d", i), self.dma_cnt[i]) for i in range(self.n_dma) if self.dma_cnt[i] > 0]
        fin += [(("c", e), self.cnt[e]) for e in ("pe", "act", "dve", "pool") if self.cnt[e] > 0]
        ops = self.ops
        semh = self.semh
        with nc.Block() as block:
            def run(engname, eng):
                for waits, fn, key, inc in ops[engname]:
                    for k, v in waits:
                        eng.wait_ge(semh[k], v)
                    ins = fn(eng)
                    if inc:
                        ins.then_inc(semh[key], inc)
                if engname == "sp":
                    for k, v in fin:
                        eng.wait_ge(semh[k], v)

            @block.tensor
            def _(e):
                run("pe", e)

            @block.scalar
            def _(e):
                run("act", e)

            @block.vector
            def _(e):
                run("dve", e)

            @block.gpsimd
            def _(e):
                run("pool", e)

            @block.sync
            def _(e):
                run("sp", e)
        self.ops = {e: [] for e in self.ENG}
        for e in self.ENG:
            for k, v in fin:
                self.seen[e][k] = v


def host_consts():
    c = {}
    c["ident"] = np.eye(128, dtype=np.float32)
    c["ones"] = np.ones((128, 128), np.float32)
    kk = np.arange(128)[:, None]
    qq = np.arange(128)[None, :]
    c["trineg"] = np.where(kk <= qq, 0.0, NEGB).astype(np.float32)
    e8 = np.zeros((8, 8, 128), np.float32)
    for n in range(8):
        e8[n, n, :] = 1.0
    c["e8"] = e8.reshape(8, 1024)
    p32 = np.zeros((128, 128), np.float32)
    for m in range(32):
        p32[(m + 16) % 32, m] = 1.0
    c["p32"] = p32
    inv = np.power(500000.0, -np.arange(0, 32, 2, dtype=np.float32) / 32.0).astype(np.float32)
    pos = np.arange(SEQ, dtype=np.float32)
    ang = (pos[None, :] * inv[:, None]).astype(np.float32)
    cs = np.cos(ang).astype(np.float32)
    sn = np.sin(ang).astype(np.float32)
    c["ropeC"] = np.concatenate([cs, cs, np.ones((96, SEQ), np.float32)], 0)
    c["ropeS"] = np.concatenate([-sn, sn, np.zeros((96, SEQ), np.float32)], 0)
    negm = np.zeros((128, 8, 8), np.float32)
    bfix = np.zeros((128, 8, 8), np.float32)
    for i in range(8):
        B = 4 + i // 2
        for n in range(8):
            negm[:, i, n] = 0.0 if n < B else -1e30
            bfix[:, i, n] = 0.0 if n == B else NEGB
    c["negmask"] = negm.reshape(128, 64)
    c["biasfix"] = bfix.reshape(128, 64)
    rc = np.zeros((128, 4, 16), np.float32)
    for g, w in enumerate((2, 4, 8, 16)):
        for t in range(16):
            rc[:, g, t] = 1.0 / min(t + 1, w)
    c["rcnt"] = rc.reshape(128, 64)
    pp = np.arange(128)
    c["maskA"] = np.stack([((pp // 16) % 2 == 0), ((pp // 16) % 2 == 1)], 1).astype(np.float32)
    c["maskB"] = np.stack([(pp // 64 == 0), (pp // 64 == 1)], 1).astype(np.float32)
    return c


CONST_SHAPES = {k: v.shape for k, v in host_consts().items()}


def build(n_layers=2):
    nc = bass.Bass("TRN2", target_bir_lowering=False)

    def dr(name, shape, kind="ExternalInput", dt=F32):
        return nc.dram_tensor(name, list(shape), dt, kind=kind).ap()

    x_d = dr("x", [NB, SEQ, D])
    out_d = dr("out", [NB, SEQ, D], "ExternalOutput")
    pre_g = dr("pre_norm_g", [2, D])
    post_g = dr("post_norm_g", [2, D])
    e_w_in = dr("e_w_in", [D, 3072])
    e_pool_w = dr("e_pool_w", [4, 128, 128])
    e_pool_scale = dr("e_pool_scale", [128, 4])
    e_w_out = dr("e_w_out", [D, D])
    o_w_in = dr("o_w_in", [D, 2560])
    o_lamre_A = dr("o_lamre_A", [128, 256])
    o_lamim_A = dr("o_lamim_A", [128, 256])
    o_dt_A = dr("o_dt_A", [128, 256])
    o_bre_A = dr("o_bre_A", [128, 256])
    o_bim_A = dr("o_bim_A", [128, 256])
    o_bre_B = dr("o_bre_B", [128, 256])
    o_bim_B = dr("o_bim_B", [128, 256])
    o_lamre_B = dr("o_lamre_B", [128, 16])
    o_lamim_B = dr("o_lamim_B", [128, 16])
    o_dt_B = dr("o_dt_B", [128, 16])
    o_cre_B = dr("o_cre_B", [128, 256])
    o_cim_B = dr("o_cim_B", [128, 256])
    o_dA = dr("o_dA", [128, 4])
    o_glu_w = dr("o_glu_w", [512, 1024])
    o_glub = dr("o_glub", [128, 8])
    o_dwT = dr("o_dwT", [128, 4, 31])
    o_lng = dr("o_lng", [128, 4])
    o_lnb = dr("o_lnb", [128, 4])
    o_pw = dr("o_pw", [512, 512])
    o_w_out = dr("o_w_out", [D, D])
    dbg_d = dr("dbg", [128, 4, SEQ], "ExternalOutput") if os.environ.get("KDBG") else None
    cd = {k: dr("c_" + k, list(s)) for k, s in CONST_SHAPES.items()}

    with ExitStack() as top:
        S = Sched(nc, top)
        uid = [0]

        def sbt(es, name, shape, dt):
            uid[0] += 1
            return es.enter_context(nc.sbuf_tensor(f"{name}_{uid[0]}", list(shape), dt))
        psf = [top.enter_context(nc.psum_tensor(f"psf{i}", [128, 512], F32)) for i in range(7)]
        psb = top.enter_context(nc.psum_tensor("psb", [128, 1024], BF16))
        pb = TL(8)

        identb = sbt(top, "identb", [128, 128], BF16)
        identf = sbt(top, "identf", [128, 128], F32)
        onesb = sbt(top, "onesb", [128, 128], BF16)
        t_const = T()
        S.dma("pool", lambda e: e.dma_start(out=identb[:], in_=cd["ident"][:, :]), writes=[t_const])
        S.dma("sp", lambda e: e.dma_start(out=identf[:], in_=cd["ident"][:, :]), writes=[t_const])
        S.dma("pool", lambda e: e.dma_start(out=onesb[:], in_=cd["ones"][:, :]), writes=[t_const])

        def rms_stats(es_tag, src_aps, st, col0, reads, t_st, junk):
            for i, ap in enumerate(src_aps):
                S.op("act", lambda e, ap=ap, i=i: e.activation(out=junk[:, 0:ap.shape[1]], in_=ap, func=AF.Square,
                                                             accum_out=st[:, col0 + i:col0 + i + 1]),
                     reads=reads, writes=[t_st])
            if len(src_aps) == 2:
                S.op("dve", lambda e: e.tensor_tensor(out=st[:, col0:col0 + 1], in0=st[:, col0:col0 + 1],
                                                      in1=st[:, col0 + 1:col0 + 2], op=ALU.add), reads=[t_st], writes=[t_st])
            S.op("act", lambda e: e.activation(out=st[:, col0 + 2:col0 + 3], in_=st[:, col0:col0 + 1], func=AF.Sqrt,
                                               scale=1.0 / D, bias=EPS), reads=[t_st], writes=[t_st])
            S.op("dve", lambda e: e.reciprocal(out=st[:, col0 + 3:col0 + 4], in_=st[:, col0 + 2:col0 + 3]),
                 reads=[t_st], writes=[t_st])

        def phase_norm_T(es, src_d, b, gvec_d, hT, t_hT, nxb=2):
            xb = [sbt(es, f"xb{i}", [128, D], F32) for i in range(nxb)]
            hb = [sbt(es, f"hb{i}", [128, D], BF16) for i in range(2)]
            junk = sbt(es, "junkA", [128, D], BF16)
            gt = sbt(es, "gtA", [128, D], F32)
            st = sbt(es, "stA", [128, 16 * 4], F32)
            t_xb, t_hb, t_g = TL(nxb), TL(2), T()
            t_st = TL(16)
            S.dma("sp", lambda e: e.dma_start(out=gt[:], in_=gvec_d.partition_broadcast(128)), writes=[t_g])
            def stats(tt):
                i = tt % 2
                xi = tt % nxb
                S.dma("sp", lambda e: e.dma_start(out=xb[xi][:], in_=src_d[b, tt * 128:(tt + 1) * 128, :]),
                      writes=[t_xb[xi]])
                rms_stats(es, [xb[xi][:]], st, tt * 4, [t_xb[xi]], t_st[tt], junk)
                S.op("dve", lambda e: e.scalar_tensor_tensor(out=hb[i][:], in0=xb[xi][:], scalar=st[:, tt * 4 + 3:tt * 4 + 4],
                                                             in1=gt[:], op0=ALU.mult, op1=ALU.mult),
                     reads=[t_xb[xi], t_st[tt], t_g], writes=[t_hb[i]])

            def trans(tt):
                i = tt % 2
                for k in range(8):
                    S.op("pe", lambda e, k=k: e.transpose(out=psb[:, k * 128:(k + 1) * 128], in_=hb[i][:, k * 128:(k + 1) * 128],
                                                          identity=identb[:]),
                         reads=[t_hb[i], t_const], writes=[pb[7]], inc=(k == 7))
                S.op("act", lambda e: e.activation(out=hT[:, :, tt * 128:(tt + 1) * 128],
                                                   in_=psb[:, :].rearrange("p (k t) -> p k t", k=8), func=AF.Copy),
                     reads=[pb[7]], writes=[t_hT[tt]])

            stats(0)
            for tt in range(16):
                if tt + 1 < 16:
                    stats(tt + 1)
                trans(tt)

        def load_wo(wo, t_wo, w_out_d):
            for hf in range(2):
                S.dma("pool", lambda e, hf=hf: e.dma_start(out=wo[:, :, hf * 512:(hf + 1) * 512],
                                                          in_=w_out_d[:, hf * 512:(hf + 1) * 512].rearrange("(k p) n -> p k n", p=128)),
                      writes=[t_wo])

        def phase_outproj(es, w_out_d, gvec_d, res_d, b, gA, gB, t_gA, t_gB, wo, t_wo):
            gt = sbt(es, "gtF", [128, D], F32)
            xr = [sbt(es, f"xr{i}", [128, D], F32) for i in range(2)]
            tm = [sbt(es, f"tmF{i}", [128, D], F32) for i in range(2)]
            junk = sbt(es, "junkF", [128, 512], BF16)
            st = sbt(es, "stF", [128, 16 * 4], F32)
            t_g = T()
            t_xr, t_tm, t_st = TL(2), TL(2), TL(16)
            S.dma("sp", lambda e: e.dma_start(out=gt[:], in_=gvec_d.partition_broadcast(128)), writes=[t_g])
            for tt in range(16):
                i = tt % 2
                bk = [psf[2 * i], psf[2 * i + 1]]
                tbk = [pb[2 * i], pb[2 * i + 1]]
                S.dma("sp", lambda e, tt=tt, i=i: e.dma_start(out=xr[i][:], in_=res_d[b, tt * 128:(tt + 1) * 128, :]),
                      writes=[t_xr[i]])
                for hf in range(2):
                    for k in range(8):
                        g_ap = (gA if k < 4 else gB)
                        S.op("pe", lambda e, hf=hf, k=k, g_ap=g_ap, tt=tt, bk=bk: e.matmul(
                            bk[hf][:], lhsT=g_ap[:, k % 4, tt * 128:(tt + 1) * 128], rhs=wo[:, k, hf * 512:(hf + 1) * 512],
                            start=(k == 0), stop=(k == 7)),
                            reads=[(t_gA if k < 4 else t_gB)[tt // 4], t_wo], writes=[tbk[hf]], inc=(k == 7))
                rms_stats(es, [bk[0][:], bk[1][:]], st, tt * 4, tbk, t_st[tt], junk)
                for hf in range(2):
                    S.op("dve", lambda e, hf=hf, tt=tt, i=i, bk=bk: e.scalar_tensor_tensor(
                        out=tm[i][:, hf * 512:(hf + 1) * 512], in0=bk[hf][:], scalar=st[:, tt * 4 + 3:tt * 4 + 4],
                        in1=gt[:, hf * 512:(hf + 1) * 512], op0=ALU.mult, op1=ALU.mult),
                        reads=[tbk[hf], t_st[tt], t_g], writes=[t_tm[i]])
                S.op("pool", lambda e, i=i: e.tensor_tensor(out=tm[i][:], in0=tm[i][:], in1=xr[i][:], op=ALU.add),
                     reads=[t_tm[i], t_xr[i]], writes=[t_tm[i]])
                S.dma("sp", lambda e, tt=tt, i=i: e.dma_start(out=out_d[b, tt * 128:(tt + 1) * 128, :], in_=tm[i][:]),
                      reads=[t_tm[i]])

        def layer0(b):
            with ExitStack() as L:
                qkT = sbt(L, "qkT", [128, 8, SEQ], BF16)
                Vt = sbt(L, "Vt", [128, 16, 512], BF16)
                gaT = sbt(L, "gaT", [128, 4, SEQ], BF16)
                gbT = sbt(L, "gbT", [128, 4, SEQ], BF16)
                t_qk = [TL(4) for _ in range(8)]
                t_V = TL(16)
                t_ga = [TL(4) for _ in range(4)]
                t_gb = [TL(4) for _ in range(4)]
                wo0 = sbt(L, "wo0", [128, 8, D], BF16)
                t_wo0 = T()

                with ExitStack() as es:
                    hT = sbt(es, "hT", [128, 8, SEQ], BF16)
                    t_hT = TL(16)
                    phase_norm_T(es, x_d, b, pre_g[0:1, :], hT, t_hT, nxb=4)
                    wb = [sbt(es, f"wb{i}", [128, 8, 512], BF16) for i in range(2)]
                    t_wb = TL(2)
                    ropeC = sbt(es, "ropeC", [128, SEQ], F32)
                    ropeS = sbt(es, "ropeS", [128, SEQ], F32)
                    p32 = sbt(es, "p32", [128, 128], BF16)
                    poolw = sbt(es, "poolw", [128, 4, 128], BF16)
                    pscale = sbt(es, "pscale", [128, 4], F32)
                    rcnt = sbt(es, "rcnt", [128, 64], F32)
                    t_c2 = T()
                    S.dma("sp", lambda e: e.dma_start(out=ropeC[:], in_=cd["ropeC"][:, :]), writes=[t_c2])
                    S.dma("sp", lambda e: e.dma_start(out=ropeS[:], in_=cd["ropeS"][:, :]), writes=[t_c2])
                    S.dma("pool", lambda e: e.dma_start(out=p32[:], in_=cd["p32"][:, :]), writes=[t_c2])
                    S.dma("pool", lambda e: e.dma_start(out=poolw[:], in_=e_pool_w.rearrange("g c d -> c g d")), writes=[t_c2])
                    S.dma("sp", lambda e: e.dma_start(out=pscale[:], in_=e_pool_scale[:, :]), writes=[t_c2])
                    S.dma("sp", lambda e: e.dma_start(out=rcnt[:], in_=cd["rcnt"][:, :]), writes=[t_c2])
                    r1 = [sbt(es, f"r1_{i}", [128, 512], F32) for i in range(2)]
                    r2 = [sbt(es, f"r2_{i}", [128, 512], F32) for i in range(2)]
                    t_r1, t_r2 = TL(2), TL(2)
                    ub = [sbt(es, f"ub{i}", [128, 528], F32) for i in range(2)]
                    sa = sbt(es, "sa", [128, 528], F32)
                    sb_ = sbt(es, "sb", [128, 528], F32)
                    mt = [sbt(es, f"mt{i}", [128, 512], BF16) for i in range(2)]
                    t_ub, t_mt = TL(2), TL(2)
                    t_sa, t_sb = T(), T()

                    order = [0, 1, 2, 3, 5, 4]
                    cnt = [0]
                    for oi in range(2):
                        S.dma("pool", lambda e, oi=oi: e.dma_start(out=wb[oi][:], in_=e_w_in[:, order[oi] * 512:(order[oi] + 1) * 512].rearrange("(k p) n -> p k n", p=128)),
                              writes=[t_wb[oi]])
                    ri = [0]
                    ksub = os.environ.get("KSUB", "")
                    for oi, gi in enumerate(order):
                        if ksub and oi >= int(ksub):
                            break
                        wbi = wb[oi % 2]
                        t_wbi = t_wb[oi % 2]
                        if gi in (0, 1):
                            pend = [None]

                            def rope_tail(j, c, r, pbk):
                                def f():
                                    S.op("pe", lambda e: e.matmul(psf[pbk][:], lhsT=p32[:], rhs=qkT[:, j, c * 512:(c + 1) * 512], start=True, stop=True),
                                         reads=[t_qk[j][c], t_c2], writes=[pb[pbk]])
                                    S.op("dve", lambda e: e.tensor_tensor(out=r2[r][:], in0=psf[pbk][:], in1=ropeS[:, c * 512:(c + 1) * 512], op=ALU.mult),
                                         reads=[pb[pbk], t_c2], writes=[t_r2[r]])
                                    S.op("dve", lambda e: e.tensor_tensor(out=qkT[:, j, c * 512:(c + 1) * 512], in0=r1[r][:], in1=r2[r][:], op=ALU.add),
                                         reads=[t_r1[r], t_r2[r]], writes=[t_qk[j][c]])
                                return f
                            for m in range(4):
                                j = gi * 4 + m
                                for c in range(4):
                                    bi = cnt[0] % 3
                                    cnt[0] += 1
                                    for k in range(8):
                                        S.op("pe", lambda e, k=k, bi=bi, m=m, c=c, wbi=wbi: e.matmul(
                                            psf[bi][:], lhsT=wbi[:, k, m * 128:(m + 1) * 128], rhs=hT[:, k, c * 512:(c + 1) * 512],
                                            start=(k == 0), stop=(k == 7)),
                                            reads=[t_wbi] + t_hT[4 * c:4 * c + 4], writes=[pb[bi]], inc=(k == 7))
                                    if pend[0] is not None:
                                        pend[0]()
                                    S.op("act", lambda e, bi=bi, j=j, c=c: e.activation(out=qkT[:, j, c * 512:(c + 1) * 512], in_=psf[bi][:], func=AF.Copy),
                                         reads=[pb[bi]], writes=[t_qk[j][c]])
                                    r = ri[0] % 2
                                    ri[0] += 1
                                    pbk = 3 + r
                                    S.op("dve", lambda e, bi=bi, c=c, r=r: e.tensor_tensor(out=r1[r][:], in0=psf[bi][:], in1=ropeC[:, c * 512:(c + 1) * 512], op=ALU.mult),
                                         reads=[pb[bi], t_c2, t_qk[j][c]], writes=[t_r1[r]])
                                    pend[0] = rope_tail(j, c, r, pbk)
                            pend[0]()
                        elif gi == 2:
                            for tt in range(16):
                                bi = cnt[0] % 3
                                cnt[0] += 1
                                for k in range(8):
                                    S.op("pe", lambda e, k=k, bi=bi, tt=tt, wbi=wbi: e.matmul(
                                        psf[bi][:], lhsT=hT[:, k, tt * 128:(tt + 1) * 128], rhs=wbi[:, k, :], start=(k == 0), stop=(k == 7)),
                                        reads=[t_wbi, t_hT[tt]], writes=[pb[bi]], inc=(k == 7))
                                S.op("dve", lambda e, bi=bi, tt=tt: e.tensor_copy(out=Vt[:, tt, :], in_=psf[bi][:]), reads=[pb[bi]], writes=[t_V[tt]])
                        elif gi in (3, 5):
                            dst, t_dst = (gaT, t_ga) if gi == 3 else (gbT, t_gb)
                            for m in range(4):
                                for c in range(4):
                                    bi = cnt[0] % 3
                                    cnt[0] += 1
                                    for k in range(8):
                                        S.op("pe", lambda e, k=k, bi=bi, m=m, c=c, wbi=wbi: e.matmul(
                                            psf[bi][:], lhsT=wbi[:, k, m * 128:(m + 1) * 128], rhs=hT[:, k, c * 512:(c + 1) * 512],
                                            start=(k == 0), stop=(k == 7)),
                                            reads=[t_wbi] + t_hT[4 * c:4 * c + 4], writes=[pb[bi]], inc=(k == 7))
                                    S.op("act", lambda e, bi=bi, m=m, c=c, dst=dst: e.activation(out=dst[:, m, c * 512:(c + 1) * 512], in_=psf[bi][:], func=AF.Silu),
                                         reads=[pb[bi]], writes=[t_dst[m][c]])
                        else:
                            ppend = [None]
                            for g in range(4):
                                w = (2, 4, 8, 16)[g]
                                nlev = g + 1
                                for c in range(4):
                                    bi = cnt[0] % 3
                                    cnt[0] += 1
                                    for k in range(8):
                                        S.op("pe", lambda e, k=k, bi=bi, g=g, c=c, wbi=wbi: e.matmul(
                                            psf[bi][:], lhsT=wbi[:, k, g * 128:(g + 1) * 128], rhs=hT[:, k, c * 512:(c + 1) * 512],
                                            start=(k == 0), stop=(k == 7)),
                                            reads=[t_wbi] + t_hT[4 * c:4 * c + 4], writes=[pb[bi]], inc=(k == 7))
                                    if ppend[0] is not None:
                                        ppend[0]()
                                        ppend[0] = None
                                    u = ub[c % 2]
                                    up = ub[(c + 1) % 2]
                                    if c == 0:
                                        S.op("pool", lambda e, u=u: e.memset(u[:, 0:16], 0.0), writes=[t_ub[c % 2]])
                                    else:
                                        S.op("pool", lambda e, u=u, up=up: e.tensor_copy(out=u[:, 0:16], in_=up[:, 512:528]),
                                             reads=[t_ub[(c + 1) % 2]], writes=[t_ub[c % 2]])
                                    S.op("act", lambda e, bi=bi, u=u: e.activation(out=u[:, 16:528], in_=psf[bi][:], func=AF.Copy),
                                         reads=[pb[bi]], writes=[t_ub[c % 2]])
                                    src, t_src = u, t_ub[c % 2]
                                    for lv in range(nlev):
                                        sh = 1 << lv
                                        lo = 2 * sh
                                        dstb, t_d = (sa, t_sa) if lv % 2 == 0 else (sb_, t_sb)
                                        S.op("dve", lambda e, src=src, dstb=dstb, lo=lo, sh=sh: e.tensor_tensor(
                                            out=dstb[:, lo:528], in0=src[:, lo:528], in1=src[:, lo - sh:528 - sh], op=ALU.add),
                                            reads=[t_src], writes=[t_d])
                                        src, t_src = dstb, t_d
                                    mi = c % 2
                                    S.op("dve", lambda e, src=src, u=u, mi=mi, w=w: e.scalar_tensor_tensor(
                                        out=mt[mi][:], in0=src[:, 16:528], scalar=1.0 / w, in1=u[:, 16:528], op0=ALU.mult, op1=ALU.subtract),
                                        reads=[t_src, t_ub[c % 2]], writes=[t_mt[mi]])
                                    if c == 0:
                                        S.op("dve", lambda e, src=src, g=g: e.tensor_tensor(out=src[:, 0:16], in0=src[:, 16:32], in1=rcnt[:, g * 16:(g + 1) * 16], op=ALU.mult),
                                             reads=[t_src, t_c2], writes=[t_src])
                                        S.op("dve", lambda e, src=src, u=u, mi=mi: e.tensor_tensor(out=mt[mi][:, 0:16], in0=src[:, 0:16], in1=u[:, 16:32], op=ALU.subtract),
                                             reads=[t_src, t_ub[c % 2]], writes=[t_mt[mi]])
                                    pbk = 3 + (c % 2)

                                    def pool_tail(g=g, c=c, mi=mi, pbk=pbk):
                                        S.op("pe", lambda e: e.matmul(psf[pbk][:], lhsT=poolw[:, g, :], rhs=mt[mi][:], start=True, stop=True),
                                             reads=[t_mt[mi], t_c2], writes=[pb[pbk]])
                                        S.op("dve", lambda e: e.scalar_tensor_tensor(
                                            out=gbT[:, g, c * 512:(c + 1) * 512], in0=psf[pbk][:], scalar=pscale[:, g:g + 1],
                                            in1=gbT[:, g, c * 512:(c + 1) * 512], op0=ALU.mult, op1=ALU.mult),
                                            reads=[pb[pbk], t_c2, t_gb[g][c]], writes=[t_gb[g][c]])
                                    ppend[0] = pool_tail
                        if gi == 4 and ppend[0] is not None:
                            ppend[0]()
                            ppend[0] = None
                        if oi + 2 < len(order) and not ksub:
                            gn = order[oi + 2]
                            S.dma("pool", lambda e, gn=gn, wbi=wbi: e.dma_start(out=wbi[:], in_=e_w_in[:, gn * 512:(gn + 1) * 512].rearrange("(k p) n -> p k n", p=128)),
                                  writes=[t_wbi])
                    S.flush()
                if os.environ.get("KSTOP") == "AB":
                    return

                with ExitStack() as es:
                    load_wo(wo0, t_wo0, e_w_out)
                    e8 = sbt(es, "e8", [8, 1024], BF16)
                    trineg = sbt(es, "trineg", [128, 128], BF16)
                    negmask = sbt(es, "negmask", [128, 64], F32)
                    biasfix = sbt(es, "biasfix", [128, 64], F32)
                    t_c3 = T()
                    S.dma("pool", lambda e: e.dma_start(out=e8[:], in_=cd["e8"][:, :]), writes=[t_c3])
                    S.dma("pool", lambda e: e.dma_start(out=trineg[:], in_=cd["trineg"][:, :]), writes=[t_c3])
                    S.dma("sp", lambda e: e.dma_start(out=negmask[:], in_=cd["negmask"][:, :]), writes=[t_c3])
                    S.dma("sp", lambda e: e.dma_start(out=biasfix[:], in_=cd["biasfix"][:, :]), writes=[t_c3])
                    Mrow = sbt(es, "Mrow", [8, 4, SEQ], BF16)
                    t_M = TL(4)
                    stab = sbt(es, "stab", [8, SEQ], F32)
                    sqt = [sbt(es, f"sqt{i}", [128, 512], BF16) for i in range(2)]
                    t_sq = TL(2)
                    kmx = sbt(es, "kmx", [8, 8], F32)
                    kb32 = sbt(es, "kb32", [128, 8], F32)
                    kbar = sbt(es, "kbar", [128, 8], BF16)
                    gm = sbt(es, "gm", [128, 64], F32)
                    top8 = sbt(es, "top8", [128, 64], F32)
                    sel = sbt(es, "sel", [128, 64], F32)
                    t_stab, t_kmx, t_kb, t_gm, t_top, t_sel = T(), T(), T(), T(), T(), T()
                    si = [0]
                    for h in range(4):
                        for c in range(4):
                            i = si[0] % 2
                            si[0] += 1
                            S.op("act", lambda e, i=i, h=h, c=c: e.activation(out=sqt[i][:], in_=qkT[:, 4 + h, c * 512:(c + 1) * 512], func=AF.Square),
                                 reads=[t_qk[4 + h][c]], writes=[t_sq[i]])
                            S.op("pe", lambda e, i=i: e.matmul(psf[6][0:8, :], lhsT=onesb[:, 0:8], rhs=sqt[i][:], start=True, stop=True),
                                 reads=[t_sq[i], t_const], writes=[pb[6]])
                            S.op("dve", lambda e, c=c: e.tensor_reduce(out=kmx[:, c:c + 1], in_=psf[6][0:8, :], axis=AX.X, op=ALU.max),
                                 reads=[pb[6]], writes=[t_kmx])
                        S.op("dve", lambda e: e.tensor_reduce(out=kmx[:, 4:5], in_=kmx[:, 0:4], axis=AX.X, op=ALU.max), reads=[t_kmx], writes=[t_kmx])
                        for c in range(4):
                            i = si[0] % 2
                            si[0] += 1
                            S.op("act", lambda e, i=i, h=h, c=c: e.activation(out=sqt[i][:], in_=qkT[:, h, c * 512:(c + 1) * 512], func=AF.Square),
                                 reads=[t_qk[h][c]], writes=[t_sq[i]])
                            S.op("pe", lambda e, i=i: e.matmul(psf[6][0:8, :], lhsT=onesb[:, 0:8], rhs=sqt[i][:], start=True, stop=True),
                                 reads=[t_sq[i], t_const], writes=[pb[6]])
                            S.op("act", lambda e, c=c: e.activation(out=stab[:, c * 512:(c + 1) * 512], in_=psf[6][0:8, :], func=AF.Sqrt, scale=kmx[:, 4:5]),
                                 reads=[pb[6], t_kmx], writes=[t_stab])
                        S.op("dve", lambda e, h=h: e.tensor_reduce(out=kb32[:], in_=qkT[:, 4 + h, :].rearrange("p (n s) -> p n s", s=256), axis=AX.X, op=ALU.add),
                             reads=t_qk[4 + h], writes=[t_kb])
                        S.op("dve", lambda e: e.tensor_scalar(out=kbar[:], in0=kb32[:], scalar1=1.0 / 256, scalar2=None, op0=ALU.mult),
                             reads=[t_kb], writes=[t_kb])
                        for i8 in range(8):
                            S.op("pe", lambda e, h=h, i8=i8: e.matmul(psf[5][:, i8 * 8:(i8 + 1) * 8], lhsT=qkT[:, h, (8 + i8) * 128:(9 + i8) * 128], rhs=kbar[:], start=True, stop=True),
                                 reads=[t_qk[h][2 + i8 // 4], t_kb], writes=[pb[5]])
                        S.op("dve", lambda e: e.tensor_tensor(out=gm[:], in0=psf[5][:, 0:64], in1=negmask[:], op=ALU.add), reads=[pb[5], t_c3], writes=[t_gm])
                        for i8 in range(8):
                            S.op("dve", lambda e, i8=i8: e.max(out=top8[:, i8 * 8:(i8 + 1) * 8], in_=gm[:, i8 * 8:(i8 + 1) * 8]), reads=[t_gm], writes=[t_top])
                        for i8 in range(8):
                            S.op("dve", lambda e, i8=i8: e.tensor_scalar(out=sel[:, i8 * 8:(i8 + 1) * 8], in0=gm[:, i8 * 8:(i8 + 1) * 8],
                                                                       scalar1=top8[:, i8 * 8 + 2:i8 * 8 + 3], scalar2=None, op0=ALU.is_ge),
                                 reads=[t_gm, t_top], writes=[t_sel])
                        S.op("dve", lambda e: e.scalar_tensor_tensor(out=sel[:], in0=sel[:], scalar=-NEGB, in1=biasfix[:], op0=ALU.mult, op1=ALU.add),
                             reads=[t_sel, t_c3], writes=[t_sel])
                        for i8 in range(8):
                            bk = 3 + i8 // 4
                            S.op("pe", lambda e, i8=i8, bk=bk: e.transpose(out=psf[bk][0:8, (i8 % 4) * 128:(i8 % 4 + 1) * 128], in_=sel[:, i8 * 8:(i8 + 1) * 8], identity=identf[:]),
                                 reads=[t_sel, t_const], writes=[pb[bk]])
                        S.op("act", lambda e, h=h: e.activation(out=Mrow[:, h, 0:1024], in_=stab[:, 0:1024], func=AF.Copy, scale=-1.0),
                             reads=[t_stab], writes=[t_M[h]])
                        for hf in range(2):
                            S.op("dve", lambda e, h=h, hf=hf: e.tensor_tensor(out=Mrow[:, h, 1024 + hf * 512:1536 + hf * 512], in0=psf[3 + hf][0:8, :],
                                                                            in1=stab[:, 1024 + hf * 512:1536 + hf * 512], op=ALU.subtract),
                                 reads=[pb[3 + hf], t_stab], writes=[t_M[h]])

                    PT = [sbt(es, f"PT{i}", [128, 512], BF16) for i in range(3)]
                    t_PT = TL(3)
                    lns = sbt(es, "lns", [128, 512], F32)
                    rinv = sbt(es, "rinv", [128, 512], F32)
                    ot = sbt(es, "ot", [128, 512], F32)
                    t_lns, t_rinv, t_ot = T(), T(), T()
                    scale = 1.0 / math.sqrt(HD)
                    items = [(h, qc, kt) for h in range(4) for qc in range(4) for kt in range(4 * qc + 4)]

                    def emit_S(idx):
                        h, qc, kt = items[idx]
                        sb_i = idx % 2
                        off = max(0, kt * 128 - qc * 512)
                        q0 = qc * 512 + off
                        q1 = (qc + 1) * 512
                        n = kt // 2
                        diag = kt >= 4 * qc
                        S.op("pe", lambda e: e.matmul(psf[sb_i][:, off:512], lhsT=qkT[:, 4 + h, kt * 128:(kt + 1) * 128], rhs=qkT[:, h, q0:q1], start=True, stop=False),
                             reads=[t_qk[4 + h][kt // 4], t_qk[h][qc]], writes=[pb[sb_i]])
                        S.op("pe", lambda e: e.matmul(psf[sb_i][:, off:512], lhsT=e8[:, n * 128:(n + 1) * 128], rhs=Mrow[:, h, q0:q1], start=False, stop=(not diag)),
                             reads=[t_M[h], t_c3], writes=[pb[sb_i]])
                        if diag:
                            S.op("pe", lambda e: e.matmul(psf[sb_i][:, off:off + 128], lhsT=identb[:], rhs=trineg[:], start=False, stop=True),
                                 reads=[t_c3, t_const], writes=[pb[sb_i]])
                        pi = idx % 3
                        S.op("act", lambda e: e.activation(out=PT[pi][:, off:512], in_=psf[sb_i][:, off:512], func=AF.Exp, scale=scale),
                             reads=[pb[sb_i]], writes=[t_PT[pi]])

                    def emit_PV(idx):
                        h, qc, kt = items[idx]
                        off = max(0, kt * 128 - qc * 512)
                        pi = idx % 3
                        par = (h * 4 + qc) % 2
                        ob, sbk = 2 + par, 4 + par
                        last = (kt == 4 * qc + 3)
                        S.op("pe", lambda e: e.matmul(psf[ob][:, off:512], lhsT=Vt[:, kt, h * 128:(h + 1) * 128], rhs=PT[pi][:, off:512], start=(kt == 0), stop=last),
                             reads=[t_V[kt], t_PT[pi]], writes=[pb[ob]])
                        S.op("pe", lambda e: e.matmul(psf[sbk][:, off:512], lhsT=onesb[:], rhs=PT[pi][:, off:512], start=(kt == 0), stop=last),
                             reads=[t_const, t_PT[pi]], writes=[pb[sbk]])
                        if last:
                            S.op("act", lambda e: e.activation(out=lns[:], in_=psf[sbk][:], func=AF.Ln), reads=[pb[sbk]], writes=[t_lns])
                            S.op("act", lambda e: e.activation(out=rinv[:], in_=lns[:], func=AF.Exp, scale=-1.0), reads=[t_lns], writes=[t_rinv])
                            S.op("dve", lambda e: e.tensor_tensor(out=ot[:], in0=psf[ob][:], in1=rinv[:], op=ALU.mult), reads=[pb[ob], t_rinv], writes=[t_ot])
                            S.op("pool", lambda e: e.tensor_tensor(out=gaT[:, h, qc * 512:(qc + 1) * 512], in0=ot[:], in1=gaT[:, h, qc * 512:(qc + 1) * 512], op=ALU.mult),
                                 reads=[t_ot, t_ga[h][qc]], writes=[t_ga[h][qc]])

                    emit_S(0)
                    for idx in range(len(items)):
                        if idx + 1 < len(items):
                            emit_S(idx + 1)
                        emit_PV(idx)
                    S.flush()
                if os.environ.get("KSTOP") == "CD":
                    return

                with ExitStack() as es:
                    t_gA = [T() for _ in range(4)]
                    t_gB = [T() for _ in range(4)]
                    phase_outproj(es, e_w_out, post_g[0:1, :], x_d, b, gaT, gbT, t_gA, t_gB, wo0, t_wo0)
                    S.flush()

        L1 = top

        def layer1_all(nb):
            with ExitStack() as P1:
                wv_sb = sbt(P1, "wv_sb", [128, 4, 8, 2, 128], BF16)
                toep_sb = sbt(P1, "toep_sb", [128, 4, 8, 256], BF16)
                w3_sb = sbt(P1, "w3_sb", [128, 16, 2, 256], BF16)
                pw_tab = sbt(P1, "pw_tab", [128, 16, 8, 3], F32)
                dA = sbt(P1, "dA", [128, 4], F32)
                glub = sbt(P1, "glub", [128, 8], F32)
                lng = sbt(P1, "lng", [128, 4], F32)
                lnb = sbt(P1, "lnb", [128, 4], F32)
                dwT = sbt(P1, "dwT", [128, 4, 31], F32)
                ones512 = sbt(P1, "ones512", [128, 128], BF16)
                t_par = T()
                for dst, src in ((dA, o_dA), (glub, o_glub), (lng, o_lng), (lnb, o_lnb)):
                    S.dma("sp", lambda e, dst=dst, src=src: e.dma_start(out=dst[:], in_=src[:, :]), writes=[t_par])
                S.dma("sp", lambda e: e.dma_start(out=dwT[:], in_=o_dwT[:, :, :]), writes=[t_par])
                S.op("act", lambda e: e.activation(out=ones512[:], in_=onesb[:], func=AF.Copy, scale=1.0 / 512), reads=[t_const], writes=[t_par])

                with ExitStack() as es:
                    tkA, tkB, t_m = T(), T(), T()
                    cur = {"eng": "dve", "tk": tkA}

                    def vop(fn, eng=None):
                        S.op(eng or cur["eng"], fn, reads=[cur["tk"], t_par, t_m], writes=[cur["tk"]])

                    def pop(fn, eng=None):
                        S.op(eng or cur["eng"], fn, reads=[cur["tk"], t_par, t_m], writes=[T()])

                    def tt(o, a, b_, op, eng=None):
                        vop(lambda e: e.tensor_tensor(out=o, in0=a, in1=b_, op=op), eng)

                    def ld(name, src, shape, tok=None):
                        t = sbt(es, name, shape, F32)
                        S.dma("sp", lambda e: e.dma_start(out=t[:], in_=src), writes=[tok if tok is not None else cur["tk"]])
                        return t

                    def compute_a(tag, lr, li, dtl, n):
                        mk = lambda nm: sbt(es, f"{tag}_{nm}", [128, n], F32)
                        dtv, x1, mg, th, u, r, sn_, cs_, ar, ai = [mk(k) for k in ("dt", "x1", "mg", "th", "u", "r", "sn", "cs", "ar", "ai")]
                        ui = sbt(es, f"{tag}_ui", [128, n], I32)
                        vop(lambda e: e.activation(out=dtv[:], in_=dtl[:], func=AF.Exp), "act")
                        tt(x1[:], lr[:], dtv[:], ALU.mult)
                        vop(lambda e: e.activation(out=mg[:], in_=x1[:], func=AF.Exp), "act")
                        tt(th[:], li[:], dtv[:], ALU.mult)
                        for shift, dst in ((0.0, sn_), (math.pi / 2, cs_)):
                            vop(lambda e, shift=shift: e.tensor_scalar(out=u[:], in0=th[:], scalar1=shift, scalar2=1.0 / (2 * math.pi), op0=ALU.add, op1=ALU.mult))
                            vop(lambda e: e.tensor_copy(out=ui[:], in_=u[:]), "dve")
                            vop(lambda e: e.tensor_copy(out=u[:], in_=ui[:]), "dve")
                            vop(lambda e: e.tensor_scalar(out=u[:], in0=u[:], scalar1=-2 * math.pi, scalar2=None, op0=ALU.mult))
                            tt(r[:], u[:], th[:], ALU.add)
                            vop(lambda e, shift=shift: e.tensor_scalar(out=r[:], in0=r[:], scalar1=shift, scalar2=None, op0=ALU.add))
                            vop(lambda e, dst=dst: e.activation(out=dst[:], in_=r[:], func=AF.Sin), "act")
                        tt(ar[:], mg[:], cs_[:], ALU.mult)
                        tt(ai[:], mg[:], sn_[:], ALU.mult)
                        return ar, ai

                    lrA = ld("lrA", o_lamre_A[:, :], [128, 256])
                    liA = ld("liA", o_lamim_A[:, :], [128, 256])
                    dtA = ld("dtA", o_dt_A[:, :], [128, 256])
                    brA = ld("brA", o_bre_A[:, :], [128, 256])
                    biA = ld("biA", o_bim_A[:, :], [128, 256])
                    mA = ld("mA", cd["maskA"][:, :], [128, 2], t_m)
                    mB = ld("mB", cd["maskB"][:, :], [128, 2], t_m)
                    arA, aiA = compute_a("A", lrA, liA, dtA, 256)
                    mk = lambda nm, n=256: sbt(es, nm, [128, n], F32)
                    nr, den, t1, t2, fr, fi = [mk(k) for k in ("nr", "den", "t1", "t2", "fr", "fi")]
                    vop(lambda e: e.tensor_scalar(out=nr[:], in0=arA[:], scalar1=-1.0, scalar2=None, op0=ALU.add))
                    tt(t1[:], lrA[:], lrA[:], ALU.mult)
                    tt(t2[:], liA[:], liA[:], ALU.mult)
                    tt(den[:], t1[:], t2[:], ALU.add)
                    vop(lambda e: e.reciprocal(out=den[:], in_=den[:]), "dve")
                    tt(t1[:], nr[:], lrA[:], ALU.mult)
                    tt(t2[:], aiA[:], liA[:], ALU.mult)
                    tt(t1[:], t1[:], t2[:], ALU.add)
                    tt(fr[:], t1[:], den[:], ALU.mult)
                    tt(t1[:], aiA[:], lrA[:], ALU.mult)
                    tt(t2[:], nr[:], liA[:], ALU.mult)
                    tt(t1[:], t1[:], t2[:], ALU.subtract)
                    tt(fi[:], t1[:], den[:], ALU.mult)
                    Gall = sbt(es, "Gall", [128, 8, 2, 256], F32)
                    tt(t1[:], fr[:], brA[:], ALU.mult)
                    tt(t2[:], fi[:], biA[:], ALU.mult)
                    tt(Gall[:, 0, 0, :], t1[:], t2[:], ALU.subtract)
                    tt(t1[:], fr[:], biA[:], ALU.mult)
                    tt(t2[:], fi[:], brA[:], ALU.mult)
                    tt(Gall[:, 0, 1, :], t1[:], t2[:], ALU.add)
                    for m in range(7):
                        tt(t1[:], Gall[:, m, 0, :], arA[:], ALU.mult)
                        tt(t2[:], Gall[:, m, 1, :], aiA[:], ALU.mult)
                        tt(Gall[:, m + 1, 0, :], t1[:], t2[:], ALU.subtract)
                        tt(t1[:], Gall[:, m, 0, :], aiA[:], ALU.mult)
                        tt(t2[:], Gall[:, m, 1, :], arA[:], ALU.mult)
                        tt(Gall[:, m + 1, 1, :], t1[:], t2[:], ALU.add)
                    for s in range(8):
                        for ri in range(2):
                            for gi in range(2):
                                pop(lambda e, s=s, ri=ri, gi=gi: e.tensor_scalar(
                                    out=wv_sb[:, :, s, ri, gi * 64:(gi + 1) * 64],
                                    in0=Gall[:, 7 - s, ri, :].rearrange("p (c n) -> p c n", c=4),
                                    scalar1=mA[:, gi:gi + 1], scalar2=None, op0=ALU.mult))
                    Kall = sbt(es, "Kall", [128, 4, 8, 16], F32)
                    cur["eng"], cur["tk"] = "dve", tkB
                    lrB = ld("lrB", o_lamre_B[:, :], [128, 16])
                    liB = ld("liB", o_lamim_B[:, :], [128, 16])
                    dtB = ld("dtB", o_dt_B[:, :], [128, 16])
                    crB = ld("crB", o_cre_B[:, :], [128, 256])
                    ciB = ld("ciB", o_cim_B[:, :], [128, 256])
                    arB, aiB = compute_a("B", lrB, liB, dtB, 16)
                    PB = sbt(es, "PB", [128, 8, 2, 16], F32)
                    s1 = sbt(es, "s1", [128, 16], F32)
                    s2 = sbt(es, "s2", [128, 16], F32)
                    vop(lambda e: e.tensor_copy(out=PB[:, 0, 0, :], in_=arB[:]))
                    vop(lambda e: e.tensor_copy(out=PB[:, 0, 1, :], in_=aiB[:]))
                    for r in range(7):
                        tt(s1[:], PB[:, r, 0, :], arB[:], ALU.mult)
                        tt(s2[:], PB[:, r, 1, :], aiB[:], ALU.mult)
                        tt(PB[:, r + 1, 0, :], s1[:], s2[:], ALU.subtract)
                        tt(s1[:], PB[:, r, 0, :], aiB[:], ALU.mult)
                        tt(s2[:], PB[:, r, 1, :], arB[:], ALU.mult)
                        tt(PB[:, r + 1, 1, :], s1[:], s2[:], ALU.add)
                    brB = ld("brB", o_bre_B[:, :], [128, 256])
                    biB = ld("biB", o_bim_B[:, :], [128, 256])
                    mkb = lambda nm: sbt(es, nm, [128, 16], F32)
                    nrB, denB, x1B, x2B, frB, fiB = [mkb(k) for k in ("nrB", "denB", "x1B", "x2B", "frB", "fiB")]
                    vop(lambda e: e.tensor_scalar(out=nrB[:], in0=arB[:], scalar1=-1.0, scalar2=None, op0=ALU.add))
                    tt(x1B[:], lrB[:], lrB[:], ALU.mult)
                    tt(x2B[:], liB[:], liB[:], ALU.mult)
                    tt(denB[:], x1B[:], x2B[:], ALU.add)
                    vop(lambda e: e.reciprocal(out=denB[:], in_=denB[:]), "dve")
                    tt(x1B[:], nrB[:], lrB[:], ALU.mult)
                    tt(x2B[:], aiB[:], liB[:], ALU.mult)
                    tt(x1B[:], x1B[:], x2B[:], ALU.add)
                    tt(frB[:], x1B[:], denB[:], ALU.mult)
                    tt(x1B[:], aiB[:], lrB[:], ALU.mult)
                    tt(x2B[:], nrB[:], liB[:], ALU.mult)
                    tt(x1B[:], x1B[:], x2B[:], ALU.subtract)
                    tt(fiB[:], x1B[:], denB[:], ALU.mult)
                    w1 = sbt(es, "w1", [128, 256], F32)
                    w2 = sbt(es, "w2", [128, 256], F32)
                    bbr = sbt(es, "bbrB", [128, 256], F32)
                    bbi = sbt(es, "bbiB", [128, 256], F32)
                    v3 = lambda t: t[:, :].rearrange("p (a i) -> p a i", a=16)
                    bc = lambda t: t[:, :].unsqueeze(2).broadcast_to([128, 16, 16])
                    tt(v3(w1), v3(brB), bc(frB), ALU.mult)
                    tt(v3(w2), v3(biB), bc(fiB), ALU.mult)
                    tt(bbr[:], w1[:], w2[:], ALU.subtract)
                    tt(v3(w1), v3(biB), bc(frB), ALU.mult)
                    tt(v3(w2), v3(brB), bc(fiB), ALU.mult)
                    tt(bbi[:], w1[:], w2[:], ALU.add)
                    Bmr = sbt(es, "Bmr", [128, 16, 128], F32)
                    Bmi = sbt(es, "Bmi", [128, 16, 128], F32)
                    vop(lambda e: e.memset(Bmr[:], 0.0), "pool")
                    vop(lambda e: e.memset(Bmi[:], 0.0), "pool")
                    for q in range(4):
                        for gi in range(2):
                            c0 = 32 * q + 16 * gi
                            vop(lambda e, q=q, gi=gi, c0=c0: e.tensor_scalar(
                                out=Bmr[:, :, :].rearrange("p (c q) m -> p c q m", q=4)[:, :, q, c0:c0 + 16],
                                in0=bbr[:, :].rearrange("p (c q j) -> p c q j", q=4, j=16)[:, :, q, :],
                                scalar1=mB[:, gi:gi + 1], scalar2=None, op0=ALU.mult))
                            vop(lambda e, q=q, gi=gi, c0=c0: e.tensor_scalar(
                                out=Bmi[:, :, :].rearrange("p (c q) m -> p c q m", q=4)[:, :, q, c0:c0 + 16],
                                in0=bbi[:, :].rearrange("p (c q j) -> p c q j", q=4, j=16)[:, :, q, :],
                                scalar1=mB[:, gi:gi + 1], scalar2=-1.0, op0=ALU.mult, op1=ALU.mult))
                    CAr = sbt(es, "CAr", [128, 9, 16, 16], F32)
                    CAi = sbt(es, "CAi", [128, 9, 16, 16], F32)
                    vop(lambda e: e.tensor_copy(out=CAr[:, 0, :, :], in_=v3(crB)))
                    vop(lambda e: e.tensor_copy(out=CAi[:, 0, :, :], in_=v3(ciB)))
                    for r in range(8):
                        pre = PB[:, r, 0, :].unsqueeze(2).broadcast_to([128, 16, 16])
                        pim = PB[:, r, 1, :].unsqueeze(2).broadcast_to([128, 16, 16])
                        tt(v3(w1), v3(crB), pre, ALU.mult)
                        tt(v3(w2), v3(ciB), pim, ALU.mult)
                        tt(CAr[:, r + 1, :, :], v3(w1), v3(w2), ALU.subtract)
                        tt(v3(w1), v3(crB), pim, ALU.mult)
                        tt(v3(w2), v3(ciB), pre, ALU.mult)
                        tt(CAi[:, r + 1, :, :], v3(w1), v3(w2), ALU.add)
                        for gi in range(2):
                            pop(lambda e, r=r, gi=gi: e.tensor_scalar(out=w3_sb[:, :, 0, gi * 128 + r * 16:gi * 128 + r * 16 + 16], in0=CAr[:, r + 1, :, :],
                                                                    scalar1=mB[:, gi:gi + 1], scalar2=None, op0=ALU.mult))
                            pop(lambda e, r=r, gi=gi: e.tensor_scalar(out=w3_sb[:, :, 1, gi * 128 + r * 16:gi * 128 + r * 16 + 16], in0=CAi[:, r + 1, :, :],
                                                                    scalar1=mB[:, gi:gi + 1], scalar2=-1.0, op0=ALU.mult, op1=ALU.mult))
                    for ct in range(4):
                        for q in range(4):
                            p = 4 * ct + q
                            S.op("pe", lambda e, p=p, q=q: e.matmul(psf[0][:, 0:128], lhsT=Bmr[:, p, :], rhs=CAr[:, 0:8, p, :], start=(q == 0), stop=False),
                                 reads=[tkB], writes=[pb[0]], inc=False)
                            S.op("pe", lambda e, p=p, q=q: e.matmul(psf[0][:, 0:128], lhsT=Bmi[:, p, :], rhs=CAi[:, 0:8, p, :], start=False, stop=(q == 3)),
                                 reads=[tkB], writes=[pb[0]], inc=(q == 3))
                        S.op("dve", lambda e, ct=ct: e.tensor_copy(out=Kall[:, ct, :, :], in_=psf[0][:, 0:128].rearrange("p (t i) -> p t i", t=8)),
                             reads=[pb[0], tkB], writes=[tkB])
                    vop(lambda e: e.memset(toep_sb[:], 0.0), "pool")
                    for s in range(8):
                        for gi in range(2):
                            vop(lambda e, s=s, gi=gi: e.tensor_scalar(
                                out=toep_sb[:, :, s, gi * 128 + s * 16:gi * 128 + 128],
                                in0=Kall[:, :, 0:8 - s, :].rearrange("p c t i -> p c (t i)"),
                                scalar1=mA[:, gi:gi + 1], scalar2=None, op0=ALU.mult))
                    qr = sbt(es, "qr", [128, 16], F32)
                    qi = sbt(es, "qi", [128, 16], F32)
                    vop(lambda e: e.tensor_copy(out=qr[:], in_=PB[:, 7, 0, :]))
                    vop(lambda e: e.tensor_copy(out=qi[:], in_=PB[:, 7, 1, :]))
                    for m in range(8):
                        pop(lambda e, m=m: e.tensor_copy(out=pw_tab[:, :, m, 0], in_=qr[:]))
                        pop(lambda e, m=m: e.tensor_copy(out=pw_tab[:, :, m, 1], in_=qi[:]))
                        pop(lambda e, m=m: e.tensor_scalar(out=pw_tab[:, :, m, 2], in0=qi[:], scalar1=-1.0, scalar2=None, op0=ALU.mult))
                        if m < 7:
                            tt(s1[:], qr[:], qr[:], ALU.mult)
                            tt(s2[:], qi[:], qi[:], ALU.mult)
                            tt(s2[:], s1[:], s2[:], ALU.subtract)
                            tt(s1[:], qr[:], qi[:], ALU.mult)
                            vop(lambda e: e.tensor_scalar(out=qi[:], in0=s1[:], scalar1=2.0, scalar2=None, op0=ALU.mult))
                            vop(lambda e: e.tensor_copy(out=qr[:], in_=s2[:]))
                    S.flush()

                for b in range(nb):
                    layer1(b, wv_sb, toep_sb, w3_sb, pw_tab, dA, glub, lng, lnb, dwT, ones512, t_par)

        def layer1(b, wv_sb, toep_sb, w3_sb, pw_tab, dA, glub, lng, lnb, dwT, ones512, t_par):
            with ExitStack() as L:
                suT = sbt(L, "suT", [128, 4, SEQ], BF16)
                gcT = sbt(L, "gcT", [128, 4, SEQ], BF16)
                gdT = sbt(L, "gdT", [128, 4, SEQ], BF16)
                gpad = sbt(L, "gpad", [128, 4, SEQ + 32], BF16)
                t_su = [TL(4) for _ in range(4)]
                t_gc = [TL(4) for _ in range(4)]
                t_gd = [TL(4) for _ in range(4)]
                t_gp = TL(4)
                wo1 = sbt(L, "wo1", [128, 8, D], BF16)
                t_wo1 = T()
                pwsb = sbt(L, "pwsb", [128, 4, 512], BF16)
                t_pw = T()
                with ExitStack() as es:
                    hT = sbt(es, "hT1", [128, 8, SEQ], BF16)
                    t_hT = TL(16)
                    phase_norm_T(es, out_d, b, pre_g[1:2, :], hT, t_hT, nxb=2)
                    wb = [sbt(es, f"wb1_{i}", [128, 8, 512], BF16) for i in range(2)]
                    t_wb = TL(2)
                    sg = [sbt(es, f"sg{i}", [128, 512], BF16) for i in range(2)]
                    t_sg = TL(2)
                    order = [0, 1, 4, 2, 3]

                    def loadw(oi):
                        gi = order[oi]
                        S.dma("pool", lambda e: e.dma_start(out=wb[oi % 2][:], in_=o_w_in[:, gi * 512:(gi + 1) * 512].rearrange("(k p) n -> p k n", p=128)),
                              writes=[t_wb[oi % 2]])
                    loadw(0)
                    loadw(1)
                    S.op("pool", lambda e: e.memset(gpad[:, :, 0:32], 0.0), writes=t_gp)
                    cnt = [0]

                    def proj(wbi, t_wbi, m, c):
                        bi = cnt[0] % 3
                        cnt[0] += 1
                        for k in range(8):
                            S.op("pe", lambda e, k=k: e.matmul(psf[bi][:], lhsT=wbi[:, k, m * 128:(m + 1) * 128], rhs=hT[:, k, c * 512:(c + 1) * 512],
                                                               start=(k == 0), stop=(k == 7)),
                                 reads=[t_wbi] + t_hT[4 * c:4 * c + 4], writes=[pb[bi]], inc=(k == 7))
                        return bi
                    for oi in range(3):
                        gi = order[oi]
                        dst, t_dst, fn = ((suT, t_su, AF.Copy), (gcT, t_gc, AF.Silu), None, None, (gdT, t_gd, AF.Silu))[gi]
                        for m in range(4):
                            for c in range(4):
                                bi = proj(wb[oi % 2], t_wb[oi % 2], m, c)
                                S.op("act", lambda e, bi=bi, m=m, c=c, dst=dst, fn=fn: e.activation(out=dst[:, m, c * 512:(c + 1) * 512], in_=psf[bi][:], func=fn),
                                     reads=[pb[bi]], writes=[t_dst[m][c]])
                        if oi + 2 < 5:
                            loadw(oi + 2)
                    si = 0
                    for m in range(4):
                        for c in range(4):
                            bi = proj(wb[0], t_wb[0], m, c)
                            i = si % 2
                            si += 1
                            S.op("act", lambda e, bi=bi, i=i: e.activation(out=sg[i][:], in_=psf[bi][:], func=AF.Sigmoid), reads=[pb[bi]], writes=[t_sg[i]])
                            bi2 = proj(wb[1], t_wb[1], m, c)
                            S.op("dve", lambda e, bi2=bi2, i=i, m=m, c=c: e.tensor_tensor(out=gpad[:, m, 32 + c * 512:32 + (c + 1) * 512], in0=psf[bi2][:], in1=sg[i][:], op=ALU.mult),
                                 reads=[pb[bi2], t_sg[i]], writes=[t_gp[m]])
                    S.flush()
                if os.environ.get("KSTOP") == "AB1":
                    return

                with ExitStack() as es:
                    Sb = [[[sbt(es, f"S{sl}{pi}{pp}", [128, 2, 512], F32) for pp in range(2)] for pi in range(2)] for sl in range(2)]
                    t_S = [[[T() for pp in range(2)] for pi in range(2)] for sl in range(2)]
                    Sp = [[[sbt(es, f"Sp{sl}{pi}{ri}", [128, 256], BF16) for ri in range(2)] for pi in range(2)] for sl in range(2)]
                    t_Sp = [[T() for pi in range(2)] for sl in range(2)]
                    for sl in range(2):
                        for pi in range(2):
                            for ri in range(2):
                                S.op("pool", lambda e, sl=sl, pi=pi, ri=ri: e.memset(Sp[sl][pi][ri][:, 0:1], 0.0), writes=[t_Sp[sl][pi]])
                                for pp in range(2):
                                    S.op("pool", lambda e, sl=sl, pi=pi, ri=ri, pp=pp: e.memset(Sb[sl][pi][pp][:, ri, 0:256], 0.0), writes=[t_S[sl][pi][pp]])
                    ysb = [sbt(es, f"ysb{i}", [128, 2, 8, 128], BF16) for i in range(2)]
                    t_ysb = [T(), T()]
                    gluw = sbt(es, "gluw", [128, 4, 1024], BF16)
                    t_gw = T()
                    for hf in range(2):
                        S.dma("pool", lambda e, hf=hf: e.dma_start(out=gluw[:, :, hf * 512:(hf + 1) * 512], in_=o_glu_w[:, hf * 512:(hf + 1) * 512].rearrange("(k p) n -> p k n", p=128)),
                              writes=[t_gw])
                    load_wo(wo1, t_wo1, o_w_out)
                    S.dma("pool", lambda e: e.dma_start(out=pwsb[:], in_=o_pw.rearrange("(k p) n -> p k n", p=128)), writes=[t_pw])
                    couples = [(ct, q0) for ct in range(4) for q0 in (0, 2)]

                    def emit_V(ci):
                        ct, q0 = couples[ci]
                        sl = ci % 2
                        for pi in range(2):
                            q = q0 + pi
                            rows = slice(32 * q, 32 * q + 32)
                            tp = (32 * q, 0)
                            for ri in range(2):
                                bk = (0, 1, 4, 5)[pi * 2 + ri]
                                for s in range(8):
                                    S.op("pe", lambda e, s=s, ri=ri, bk=bk, rows=rows, ct=ct, tp=tp: e.matmul(
                                        psf[bk][:, 0:256], lhsT=wv_sb[rows, ct, s, ri, :],
                                        rhs=suT[rows, ct, :].rearrange("p (k s) -> p s k", s=8)[:, s, :], start=(s == 0), stop=(s == 7), tile_position=tp),
                                        reads=t_su[ct] + [t_par], writes=[pb[bk]], inc=(s == 7))
                                S.op("act", lambda e, ri=ri, bk=bk, sl=sl, pi=pi: e.activation(out=Sb[sl][pi][0][:, ri, 256:512], in_=psf[bk][:, 0:256], func=AF.Copy),
                                     reads=[pb[bk]], writes=[t_S[sl][pi][0]])

                    def emit_scan(ci):
                        ct, q0 = couples[ci]
                        sl = ci % 2
                        for m in range(8):
                            sh = 1 << m
                            a, d_ = m % 2, (m + 1) % 2
                            for stage in range(3):
                                for pi in range(2):
                                    p = ct * 4 + q0 + pi
                                    src, dst = Sb[sl][pi][a], Sb[sl][pi][d_]
                                    ts, td = t_S[sl][pi][a], t_S[sl][pi][d_]
                                    pr = pw_tab[:, p, m, 0:1]
                                    pim = pw_tab[:, p, m, 1:2]
                                    npi = pw_tab[:, p, m, 2:3]
                                    if stage == 0:
                                        S.op("dve", lambda e, src=src, dst=dst, sh=sh, pr=pr: e.scalar_tensor_tensor(out=dst[:, :, 256:512], in0=src[:, :, 256 - sh:512 - sh], scalar=pr, in1=src[:, :, 256:512], op0=ALU.mult, op1=ALU.add),
                                             reads=[ts, t_par], writes=[td])
                                    elif stage == 1:
                                        S.op("dve", lambda e, src=src, dst=dst, sh=sh, npi=npi: e.scalar_tensor_tensor(out=dst[:, 0, 256:512], in0=src[:, 1, 256 - sh:512 - sh], scalar=npi, in1=dst[:, 0, 256:512], op0=ALU.mult, op1=ALU.add),
                                             reads=[ts, td, t_par], writes=[td])
                                    else:
                                        S.op("dve", lambda e, src=src, dst=dst, sh=sh, pim=pim: e.scalar_tensor_tensor(out=dst[:, 1, 256:512], in0=src[:, 0, 256 - sh:512 - sh], scalar=pim, in1=dst[:, 1, 256:512], op0=ALU.mult, op1=ALU.add),
                                             reads=[ts, td, t_par], writes=[td])

                    def emit_y(ci):
                        ct, q0 = couples[ci]
                        sl = ci % 2
                        yb_ = ysb[ct % 2]
                        t_y = t_ysb[ct % 2]
                        for pi in range(2):
                            q = q0 + pi
                            p = ct * 4 + q
                            rows = slice(32 * q, 32 * q + 32)
                            tp = (32 * q, 0)
                            for ri in range(2):
                                S.op("act", lambda e, ri=ri, sl=sl, pi=pi: e.activation(out=Sp[sl][pi][ri][:, 1:256], in_=Sb[sl][pi][0][:, ri, 256:511], func=AF.Copy),
                                     reads=[t_S[sl][pi][0]], writes=[t_Sp[sl][pi]])
                            for kt2 in range(2):
                                bk = 2 + kt2
                                for s in range(8):
                                    S.op("pe", lambda e, s=s, kt2=kt2, bk=bk, rows=rows, ct=ct, tp=tp: e.matmul(
                                        psf[bk][:, 0:256],
                                        lhsT=suT[rows, ct, kt2 * 1024:(kt2 + 1) * 1024].rearrange("p (k s) -> p s k", s=8)[:, s, :],
                                        rhs=toep_sb[rows, ct, s, :], start=(s == 0), stop=False, tile_position=tp),
                                        reads=t_su[ct] + [t_par], writes=[pb[bk]], inc=False)
                                for ri in range(2):
                                    S.op("pe", lambda e, ri=ri, kt2=kt2, bk=bk, sl=sl, pi=pi, p=p: e.matmul(
                                        psf[bk][:, 0:256], lhsT=Sp[sl][pi][ri][:, kt2 * 128:(kt2 + 1) * 128], rhs=w3_sb[:, p, ri, :], start=False, stop=(ri == 1)),
                                        reads=[t_Sp[sl][pi], t_par], writes=[pb[bk]], inc=(ri == 1))
                                S.op("act", lambda e, kt2=kt2, bk=bk, q=q, yb_=yb_: e.activation(
                                    out=yb_[:, kt2, :, q * 32:(q + 1) * 32].rearrange("p r (g i) -> p g r i", g=2),
                                    in_=psf[bk][:, 0:256].rearrange("p (g r i) -> p g r i", g=2, r=8), func=AF.Copy),
                                    reads=[pb[bk]], writes=[t_y])

                    def emit_T(ct):
                        yb_ = ysb[ct % 2]
                        t_y = t_ysb[ct % 2]
                        for kt2 in range(2):
                            for r in range(8):
                                S.op("pe", lambda e, kt2=kt2, r=r, yb_=yb_: e.transpose(out=psb[:, r * 128:(r + 1) * 128], in_=yb_[:, kt2, r, :], identity=identb[:]),
                                     reads=[t_y, t_const], writes=[pb[7]], inc=(r == 7))
                            S.op("dve", lambda e, kt2=kt2, ct=ct: e.scalar_tensor_tensor(
                                out=suT[:, ct, kt2 * 1024:(kt2 + 1) * 1024].rearrange("p (k r) -> p r k", r=8),
                                in0=suT[:, ct, kt2 * 1024:(kt2 + 1) * 1024].rearrange("p (k r) -> p r k", r=8),
                                scalar=dA[:, ct:ct + 1],
                                in1=psb[:, :].rearrange("p (r k) -> p r k", r=8), op0=ALU.mult, op1=ALU.add),
                                reads=[pb[7], t_par] + t_su[ct], writes=t_su[ct])

                    emit_V(0)
                    for ci in range(8):
                        if ci + 1 < 8:
                            emit_V(ci + 1)
                        emit_scan(ci)
                        emit_y(ci)
                        if ci % 2 == 1:
                            emit_T(couples[ci][0])
                    if dbg_d is not None and b == 0:
                        S.dma("pool", lambda e: e.dma_start(out=dbg_d[:, :, :], in_=suT[:]), reads=[t for tl in t_su for t in tl])
                    sgf = [sbt(es, f"sgf{i}", [128, 512], F32) for i in range(2)]
                    tgf = [sbt(es, f"tgf{i}", [128, 512], F32) for i in range(2)]
                    t_sgf, t_tgf = TL(2), TL(2)
                    gi_ = 0
                    for c in range(4):
                        for mt in range(4):
                            i = gi_ % 2
                            gi_ += 1
                            ba, bb = 4 + i, 4 + (1 - i)
                            bka = 4 + i
                            bkb = i
                            for ct in range(4):
                                S.op("pe", lambda e, ct=ct, mt=mt, c=c, bka=bka: e.matmul(psf[bka][:], lhsT=gluw[:, ct, mt * 128:(mt + 1) * 128], rhs=suT[:, ct, c * 512:(c + 1) * 512], start=(ct == 0), stop=(ct == 3)),
                                     reads=[t_gw] + [t_su[ct][c]], writes=[pb[bka]], inc=(ct == 3))
                            for ct in range(4):
                                S.op("pe", lambda e, ct=ct, mt=mt, c=c, bkb=bkb: e.matmul(psf[bkb][:], lhsT=gluw[:, ct, (4 + mt) * 128:(5 + mt) * 128], rhs=suT[:, ct, c * 512:(c + 1) * 512], start=(ct == 0), stop=(ct == 3)),
                                     reads=[t_gw] + [t_su[ct][c]], writes=[pb[bkb]], inc=(ct == 3))
                            S.op("act", lambda e, i=i, mt=mt, bkb=bkb: e.activation(out=sgf[i][:], in_=psf[bkb][:], func=AF.Sigmoid, bias=glub[:, 4 + mt:5 + mt]),
                                 reads=[pb[bkb], t_par], writes=[t_sgf[i]])
                            S.op("dve", lambda e, i=i, mt=mt, bka=bka: e.scalar_tensor_tensor(out=tgf[i][:], in0=psf[bka][:], scalar=glub[:, mt:mt + 1], in1=sgf[i][:], op0=ALU.add, op1=ALU.mult),
                                 reads=[pb[bka], t_sgf[i], t_par], writes=[t_tgf[i]])
                            S.op("pool", lambda e, i=i, mt=mt, c=c: e.tensor_tensor(out=gcT[:, mt, c * 512:(c + 1) * 512], in0=tgf[i][:], in1=gcT[:, mt, c * 512:(c + 1) * 512], op=ALU.mult),
                                 reads=[t_tgf[i], t_gc[mt][c]], writes=[t_gc[mt][c]])
                    S.flush()
                if os.environ.get("KSTOP") == "S5":
                    return

                with ExitStack() as es:
                    diag = [sbt(es, f"diag{i}", [128, 31, 128], BF16) for i in range(4)]
                    t_dg = TL(4)
                    for ct in range(4):
                        S.op("dve", lambda e, ct=ct: e.tensor_tensor(out=diag[ct][:], in0=identf[:, :].unsqueeze(1).broadcast_to([128, 31, 128]),
                                                                   in1=dwT[:, ct, :].unsqueeze(2).broadcast_to([128, 31, 128]), op=ALU.mult),
                             reads=[t_const, t_par], writes=[t_dg[ct]])
                    cf2 = None
                    c162 = [sbt(es, f"c16{i}", [128, 4, 512], BF16) for i in range(2)]
                    c22 = [sbt(es, f"c2{i}", [128, 4, 512], BF16) for i in range(2)]
                    sn2 = [sbt(es, f"sn{i}", [128, 4, 512], BF16) for i in range(2)]
                    t_cf2, t_c162, t_c22, t_sn2 = [TL(4), TL(4)], [TL(4), TL(4)], [TL(4), TL(4)], [TL(4), TL(4)]
                    mean2 = [sbt(es, "mean_sb0", [128, 512], F32)] * 2
                    m22 = [sbt(es, "m2_0", [128, 512], F32)] * 2
                    rstd2 = [sbt(es, "rstd0", [128, 512], F32)] * 2
                    t_mean2, t_m22, t_rstd2 = [T()] * 2, [T()] * 2, [T()] * 2
                    u1 = [sbt(es, f"u1_{i}", [128, 512], F32) for i in range(2)]
                    t_u1 = TL(2)
                    ui_box = [0]
                    def conv_part(c):
                            cf, c16, c2, sn = c162[c % 2], c162[c % 2], c22[c % 2], sn2[c % 2]
                            t_cf, t_c16, t_c2, t_sn = t_c162[c % 2], t_c162[c % 2], t_c22[c % 2], t_sn2[c % 2]
                            mean_sb, m2, rstd = mean2[c % 2], m22[c % 2], rstd2[c % 2]
                            t_mean, t_m2, t_rstd = t_mean2[c % 2], t_m22[c % 2], t_rstd2[c % 2]
                            for ct in range(4):
                                bk = ct % 2
                                for k in range(31):
                                    S.op("pe", lambda e, cf=cf, c16=c16, c2=c2, sn=sn, mean_sb=mean_sb, m2=m2, rstd=rstd, k=k, ct=ct, c=c, bk=bk: e.matmul(psf[bk][:], lhsT=diag[ct][:, k, :], rhs=gpad[:, ct, 2 + c * 512 + k:2 + c * 512 + k + 512],
                                                                                       start=(k == 0), stop=(k == 30)),
                                         reads=[t_dg[ct], t_gp[ct]], writes=[pb[bk]], inc=(k == 30))
                                pass
                                S.op("act", lambda e, cf=cf, c16=c16, c2=c2, sn=sn, mean_sb=mean_sb, m2=m2, rstd=rstd, ct=ct, bk=bk: e.activation(out=c16[:, ct, :], in_=psf[bk][:], func=AF.Copy), reads=[pb[bk]], writes=[t_c16[ct]])
                                S.op("act", lambda e, cf=cf, c16=c16, c2=c2, sn=sn, mean_sb=mean_sb, m2=m2, rstd=rstd, ct=ct, bk=bk: e.activation(out=c2[:, ct, :], in_=psf[bk][:], func=AF.Square), reads=[pb[bk]], writes=[t_c2[ct]])

                    def ln_part(c):
                            cf, c16, c2, sn = c162[c % 2], c162[c % 2], c22[c % 2], sn2[c % 2]
                            t_cf, t_c16, t_c2, t_sn = t_c162[c % 2], t_c162[c % 2], t_c22[c % 2], t_sn2[c % 2]
                            mean_sb, m2, rstd = mean2[c % 2], m22[c % 2], rstd2[c % 2]
                            t_mean, t_m2, t_rstd = t_mean2[c % 2], t_m22[c % 2], t_rstd2[c % 2]
                            for ct in range(4):
                                S.op("pe", lambda e, cf=cf, c16=c16, c2=c2, sn=sn, mean_sb=mean_sb, m2=m2, rstd=rstd, ct=ct: e.matmul(psf[2][:], lhsT=ones512[:], rhs=c16[:, ct, :], start=(ct == 0), stop=(ct == 3)),
                                     reads=[t_c16[ct], t_par], writes=[pb[2]], inc=(ct == 3))
                            for ct in range(4):
                                S.op("pe", lambda e, cf=cf, c16=c16, c2=c2, sn=sn, mean_sb=mean_sb, m2=m2, rstd=rstd, ct=ct: e.matmul(psf[3][:], lhsT=ones512[:], rhs=c2[:, ct, :], start=(ct == 0), stop=(ct == 3)),
                                     reads=[t_c2[ct], t_par], writes=[pb[3]], inc=(ct == 3))
                            S.op("act", lambda e, cf=cf, c16=c16, c2=c2, sn=sn, mean_sb=mean_sb, m2=m2, rstd=rstd: e.activation(out=mean_sb[:], in_=psf[2][:], func=AF.Copy), reads=[pb[2]], writes=[t_mean])
                            S.op("dve", lambda e, cf=cf, c16=c16, c2=c2, sn=sn, mean_sb=mean_sb, m2=m2, rstd=rstd: e.tensor_tensor(out=m2[:], in0=mean_sb[:], in1=mean_sb[:], op=ALU.mult), reads=[t_mean], writes=[t_m2])
                            S.op("dve", lambda e, cf=cf, c16=c16, c2=c2, sn=sn, mean_sb=mean_sb, m2=m2, rstd=rstd: e.tensor_tensor(out=m2[:], in0=psf[3][:], in1=m2[:], op=ALU.subtract), reads=[pb[3], t_m2], writes=[t_m2])
                            S.op("act", lambda e, cf=cf, c16=c16, c2=c2, sn=sn, mean_sb=mean_sb, m2=m2, rstd=rstd: e.activation(out=m2[:], in_=m2[:], func=AF.Ln, bias=EPS), reads=[t_m2], writes=[t_m2])
                            S.op("act", lambda e, cf=cf, c16=c16, c2=c2, sn=sn, mean_sb=mean_sb, m2=m2, rstd=rstd: e.activation(out=rstd[:], in_=m2[:], func=AF.Exp, scale=-0.5), reads=[t_m2], writes=[t_rstd])
                            for ct in range(4):
                                i = ui_box[0] % 2
                                ui_box[0] += 1
                                S.op("dve", lambda e, cf=cf, c16=c16, c2=c2, sn=sn, mean_sb=mean_sb, m2=m2, rstd=rstd, ct=ct, i=i: e.tensor_tensor(out=u1[i][:], in0=cf[:, ct, :], in1=mean_sb[:], op=ALU.subtract), reads=[t_cf[ct], t_mean], writes=[t_u1[i]])
                                S.op("dve", lambda e, cf=cf, c16=c16, c2=c2, sn=sn, mean_sb=mean_sb, m2=m2, rstd=rstd, i=i: e.tensor_tensor(out=u1[i][:], in0=u1[i][:], in1=rstd[:], op=ALU.mult), reads=[t_u1[i], t_rstd], writes=[t_u1[i]])
                                S.op("act", lambda e, cf=cf, c16=c16, c2=c2, sn=sn, mean_sb=mean_sb, m2=m2, rstd=rstd, ct=ct, i=i: e.activation(out=sn[:, ct, :], in_=u1[i][:], func=AF.Silu, scale=lng[:, ct:ct + 1], bias=lnb[:, ct:ct + 1]),
                                     reads=[t_u1[i], t_par], writes=[t_sn[ct]])

                    def pw_part(c):
                            cf, c16, c2, sn = c162[c % 2], c162[c % 2], c22[c % 2], sn2[c % 2]
                            t_cf, t_c16, t_c2, t_sn = t_c162[c % 2], t_c162[c % 2], t_c22[c % 2], t_sn2[c % 2]
                            mean_sb, m2, rstd = mean2[c % 2], m22[c % 2], rstd2[c % 2]
                            t_mean, t_m2, t_rstd = t_mean2[c % 2], t_m22[c % 2], t_rstd2[c % 2]
                            for mt in range(4):
                                bk = 4 + mt % 2
                                for ct in range(4):
                                    S.op("pe", lambda e, cf=cf, c16=c16, c2=c2, sn=sn, mean_sb=mean_sb, m2=m2, rstd=rstd, ct=ct, mt=mt, bk=bk: e.matmul(psf[bk][:], lhsT=pwsb[:, ct, mt * 128:(mt + 1) * 128], rhs=sn[:, ct, :], start=(ct == 0), stop=(ct == 3)),
                                         reads=[t_pw, t_sn[ct]], writes=[pb[bk]], inc=(ct == 3))
                                S.op("dve", lambda e, cf=cf, c16=c16, c2=c2, sn=sn, mean_sb=mean_sb, m2=m2, rstd=rstd, mt=mt, c=c, bk=bk: e.tensor_tensor(out=gdT[:, mt, c * 512:(c + 1) * 512], in0=psf[bk][:], in1=gdT[:, mt, c * 512:(c + 1) * 512], op=ALU.mult),
                                     reads=[pb[bk], t_gd[mt][c]], writes=[t_gd[mt][c]])

                    conv_part(0)
                    for c in range(4):
                        ln_part(c)
                        if c + 1 < 4:
                            conv_part(c + 1)
                        pw_part(c)
                    S.flush()
                if os.environ.get("KSTOP") == "CV":
                    return
                with ExitStack() as es:
                    phase_outproj(es, o_w_out, post_g[1:2, :], out_d, b, gcT, gdT, TL(4), TL(4), wo1, t_wo1)
                    S.flush()

        nbr = int(os.environ.get("KNB", NB))
        for b in range(nbr):
            layer0(b)
        if n_layers >= 2 and os.environ.get("KLAYERS", "2") == "2":
            layer1_all(nbr)

    return nc


def layer1_host_layouts(inputs, f):
    g = lambda k: np.asarray(inputs[k][0], dtype=np.float32)
    m = {}
    m["o_w_in"] = f(g("o_w_in"))

    def A_gn(a):
        a = np.repeat(a[:, None, :], 16, axis=1).reshape(4, 8, 16, 64)
        return f(a.transpose(1, 2, 0, 3).reshape(128, 256))
    m["o_lamre_A"] = A_gn(g("o_lam_re"))
    m["o_lamim_A"] = A_gn(g("o_lam_im"))
    m["o_dt_A"] = A_gn(np.repeat(g("o_log_dt")[:, None], 64, axis=1))

    def A_b(bm):
        a = bm.transpose(0, 2, 1).reshape(4, 8, 16, 64)
        return f(a.transpose(1, 2, 0, 3).reshape(128, 256))
    m["o_bre_A"] = A_b(g("o_b_re"))
    m["o_bim_A"] = A_b(g("o_b_im"))

    def B_gn(a):
        return f(a.reshape(16, 2, 64).transpose(1, 2, 0).reshape(128, 16))
    m["o_lamre_B"] = B_gn(g("o_lam_re"))
    m["o_lamim_B"] = B_gn(g("o_lam_im"))
    m["o_dt_B"] = B_gn(np.repeat(g("o_log_dt")[:, None], 64, axis=1))

    def B_c(cm):
        return f(cm.reshape(16, 2, 16, 64).transpose(1, 3, 0, 2).reshape(128, 256))
    def B_b(bm):
        return f(bm.reshape(16, 2, 64, 16).transpose(1, 2, 0, 3).reshape(128, 256))
    m["o_bre_B"] = B_b(g("o_b_re"))
    m["o_bim_B"] = B_b(g("o_b_im"))
    m["o_cre_B"] = B_c(g("o_c_re"))
    m["o_cim_B"] = B_c(g("o_c_im"))
    m["o_dA"] = f(g("o_d").reshape(4, 128).T)
    m["o_glu_w"] = f(g("o_glu_w"))
    m["o_glub"] = f(g("o_glu_b").reshape(8, 128).T)
    m["o_dwT"] = f(g("o_dw").reshape(31, 4, 128).transpose(2, 1, 0))
    m["o_lng"] = f(g("o_ln_g").reshape(4, 128).T)
    m["o_lnb"] = f(g("o_ln_b").reshape(4, 128).T)
    m["o_pw"] = f(g("o_pw"))
    m["o_w_out"] = f(g("o_w_out"))
    return m


def make_in_maps(inputs):
    n = 8
    x = np.ascontiguousarray(inputs["x"], dtype=np.float32)
    consts = host_consts()
    f = lambda a: np.ascontiguousarray(a, dtype=np.float32)
    l1maps = layer1_host_layouts(inputs, f)
    in_maps = []
    for c in range(n):
        m = {"x": x[NB * c:NB * (c + 1)]}
        for k in ("pre_norm_g", "post_norm_g"):
            m[k] = f(inputs[k])
        m["e_w_in"] = f(inputs["e_w_in"][0])
        m["e_pool_w"] = f(inputs["e_pool_w"][0])
        m["e_pool_scale"] = f(np.asarray(inputs["e_pool_scale"][0], dtype=np.float32).reshape(4, 128).T)
        m["e_w_out"] = f(inputs["e_w_out"][0])
        m.update(l1maps)
        for k, v in consts.items():
            m["c_" + k] = v
        in_maps.append(m)
    return in_maps


def kernel(**inputs):
    nc = build()
    in_maps = make_in_maps(inputs)
    res = run_bass_kernel_spmd(nc, in_maps, core_ids=list(range(8)))
    return np.concatenate([r["out"] for r in res.results], axis=0)
```

```python
import math
import os
from contextlib import ExitStack
import numpy as np
import concourse.bass as bass
import concourse.mybir as mybir
from concourse.bass_utils import run_bass_kernel_spmd

F32 = mybir.dt.float32
BF16 = mybir.dt.bfloat16
I32 = mybir.dt.int32
ALU = mybir.AluOpType
AF = mybir.ActivationFunctionType
AX = mybir.AxisListType

D = 1024
SEQ = 2048
NB = 2
HD = 128
EPS = 1e-6
NEGB = -30000.0
S5L = 8
NCH = SEQ // S5L


class T:
    __slots__ = ("w", "r")

    def __init__(self):
        self.w = None
        self.r = {}


def TL(n):
    return [T() for _ in range(n)]


class Sched:
    ENG = ("pe", "act", "dve", "pool", "sp")

    def __init__(self, nc, es, n_dma=24):
        self.nc = nc
        self.ops = {e: [] for e in self.ENG}
        self.cnt = {e: 0 for e in self.ENG}
        self.seen = {e: {} for e in self.ENG}
        self.n_dma = n_dma
        self.dma_cnt = [0] * n_dma
        self.dma_rr2 = {"sp": 0, "pool": 0}
        self.semh = {}
        for e in self.ENG:
            self.semh[("c", e)] = es.enter_context(nc.semaphore(f"s_{e}"))
        for i in range(n_dma):
            self.semh[("d", i)] = es.enter_context(nc.semaphore(f"s_d{i}"))

    def _deps(self, eng, reads, writes):
        need = {}
        for t in reads:
            if t.w is not None:
                k, v = t.w
                if need.get(k, 0) < v:
                    need[k] = v
        for t in writes:
            if t.w is not None:
                k, v = t.w
                if need.get(k, 0) < v:
                    need[k] = v
            for k, v in t.r.items():
                if need.get(k, 0) < v:
                    need[k] = v
        waits = []
        sn = self.seen[eng]
        for k, v in need.items():
            if eng == "pe" and k == ("c", "pe"):
                continue
            if sn.get(k, 0) < v:
                waits.append((k, v))
                sn[k] = v
        return waits

    def _record(self, key, v, reads, writes):
        for t in reads:
            if t.r.get(key, 0) < v:
                t.r[key] = v
        for t in writes:
            t.w = (key, v)
            t.r = {}

    def op(self, eng, fn, reads=(), writes=(), inc=True):
        waits = self._deps(eng, reads, writes)
        key = ("c", eng)
        if inc:
            self.cnt[eng] += 1
            v = self.cnt[eng]
        else:
            v = self.cnt[eng] + 1
        self.ops[eng].append([waits, fn, key, 1 if inc else 0])
        self._record(key, v, reads, writes)

    def dma(self, eng, fn, reads=(), writes=()):
        half = self.n_dma // 2
        base = 0 if eng == "sp" else half
        j = self.dma_rr2[eng]
        self.dma_rr2[eng] = (j + 1) % half
        i = base + j
        waits = self._deps(eng, reads, writes)
        key = ("d", i)
        prev = self.dma_cnt[i]
        if prev > 0 and self.seen[eng].get(key, 0) < prev:
            waits.append((key, prev))
            self.seen[eng][key] = prev
        self.dma_cnt[i] += 16
        self.ops[eng].append([waits, fn, key, 16])
        self._record(key, self.dma_cnt[i], reads, writes)

    def flush(self):
        nc = self.nc
        for e in ("pe", "act", "dve", "pool"):
            if self.ops[e] and self.ops[e][-1][3] == 0:
                self.ops[e][-1][3] = 1
                self.cnt[e] += 1
        fin = [(("d", i), self.dma_cnt[i]) for i in range(self.n_dma) if self.dma_cnt[i] > 0]
        fin += [(("c", e), self.cnt[e]) for e in ("pe", "act", "dve", "pool") if self.cnt[e] > 0]
        ops = self.ops
        semh = self.semh
        with nc.Block() as block:
            def run(engname, eng):
                for waits, fn, key, inc in ops[engname]:
                    for k, v in waits:
                        eng.wait_ge(semh[k], v)
                    ins = fn(eng)
                    if inc:
                        ins.then_inc(semh[key], inc)
                if engname == "sp":
                    for k, v in fin:
                        eng.wait_ge(semh[k], v)

            @block.tensor
            def _(e):
                run("pe", e)

            @block.scalar
            def _(e):
                run("act", e)

            @block.vector
            def _(e):
                run("dve", e)

            @block.gpsimd
            def _(e):
                run("pool", e)

            @block.sync
            def _(e):
                run("sp", e)
        self.ops = {e: [] for e in self.ENG}
        for e in self.ENG:
            for k, v in fin:
                self.seen[e][k] = v


def host_consts():
    c = {}
    c["ident"] = np.eye(128, dtype=np.float32)
    c["ones"] = np.ones((128, 128), np.float32)
    kk = np.arange(128)[:, None]
    qq = np.arange(128)[None, :]
    c["trineg"] = np.where(kk <= qq, 0.0, NEGB).astype(np.float32)
    e8 = np.zeros((8, 8, 128), np.float32)
    for n in range(8):
        e8[n, n, :] = 1.0
    c["e8"] = e8.reshape(8, 1024)
    p32 = np.zeros((128, 128), np.float32)
    for m in range(32):
        p32[(m + 16) % 32, m] = 1.0
    c["p32"] = p32
    inv = np.power(500000.0, -np.arange(0, 32, 2, dtype=np.float32) / 32.0).astype(np.float32)
    pos = np.arange(SEQ, dtype=np.float32)
    ang = (pos[None, :] * inv[:, None]).astype(np.float32)
    cs = np.cos(ang).astype(np.float32)
    sn = np.sin(ang).astype(np.float32)
    c["ropeC"] = np.concatenate([cs, cs, np.ones((96, SEQ), np.float32)], 0)
    c["ropeS"] = np.concatenate([-sn, sn, np.zeros((96, SEQ), np.float32)], 0)
    negm = np.zeros((128, 8, 8), np.float32)
    bfix = np.zeros((128, 8, 8), np.float32)
    for i in range(8):
        B = 4 + i // 2
        for n in range(8):
            negm[:, i, n] = 0.0 if n < B else -1e30
            bfix[:, i, n] = 0.0 if n == B else NEGB
    c["negmask"] = negm.reshape(128, 64)
    c["biasfix"] = bfix.reshape(128, 64)
    rc = np.zeros((128, 4, 16), np.float32)
    for g, w in enumerate((2, 4, 8, 16)):
        for t in range(16):
            rc[:, g, t] = 1.0 / min(t + 1, w)
    c["rcnt"] = rc.reshape(128, 64)
    pp = np.arange(128)
    c["maskA"] = np.stack([((pp // 16) % 2 == 0), ((pp // 16) % 2 == 1)], 1).astype(np.float32)
    c["maskB"] = np.stack([(pp // 64 == 0), (pp // 64 == 1)], 1).astype(np.float32)
    return c


CONST_SHAPES = {k: v.shape for k, v in host_consts().items()}


def build(n_layers=2):
    nc = bass.Bass("TRN2", target_bir_lowering=False)

    def dr(name, shape, kind="ExternalInput", dt=F32):
        return nc.dram_tensor(name, list(shape), dt, kind=kind).ap()

    x_d = dr("x", [NB, SEQ, D])
    out_d = dr("out", [NB, SEQ, D], "ExternalOutput")
    pre_g = dr("pre_norm_g", [2, D])
    post_g = dr("post_norm_g", [2, D])
    e_w_in = dr("e_w_in", [D, 3072])
    e_pool_w = dr("e_pool_w", [4, 128, 128])
    e_pool_scale = dr("e_pool_scale", [128, 4])
    e_w_out = dr("e_w_out", [D, D])
    o_w_in = dr("o_w_in", [D, 2560])
    o_lamre_A = dr("o_lamre_A", [128, 256])
    o_lamim_A = dr("o_lamim_A", [128, 256])
    o_dt_A = dr("o_dt_A", [128, 256])
    o_bre_A = dr("o_bre_A", [128, 256])
    o_bim_A = dr("o_bim_A", [128, 256])
    o_bre_B = dr("o_bre_B", [128, 256])
    o_bim_B = dr("o_bim_B", [128, 256])
    o_lamre_B = dr("o_lamre_B", [128, 16])
    o_lamim_B = dr("o_lamim_B", [128, 16])
    o_dt_B = dr("o_dt_B", [128, 16])
    o_cre_B = dr("o_cre_B", [128, 256])
    o_cim_B = dr("o_cim_B", [128, 256])
    o_dA = dr("o_dA", [128, 4])
    o_glu_w = dr("o_glu_w", [512, 1024])
    o_glub = dr("o_glub", [128, 8])
    o_dwT = dr("o_dwT", [128, 4, 31])
    o_lng = dr("o_lng", [128, 4])
    o_lnb = dr("o_lnb", [128, 4])
    o_pw = dr("o_pw", [512, 512])
    o_w_out = dr("o_w_out", [D, D])
    dbg_d = dr("dbg", [128, 4, SEQ], "ExternalOutput") if os.environ.get("KDBG") else None
    cd = {k: dr("c_" + k, list(s)) for k, s in CONST_SHAPES.items()}

    with ExitStack() as top:
        S = Sched(nc, top)
        uid = [0]

        def sbt(es, name, shape, dt):
            uid[0] += 1
            return es.enter_context(nc.sbuf_tensor(f"{name}_{uid[0]}", list(shape), dt))
        psf = [top.enter_context(nc.psum_tensor(f"psf{i}", [128, 512], F32)) for i in range(7)]
        psb = top.enter_context(nc.psum_tensor("psb", [128, 1024], BF16))
        pb = TL(8)

        identb = sbt(top, "identb", [128, 128], BF16)
        identf = sbt(top, "identf", [128, 128], F32)
        onesb = sbt(top, "onesb", [128, 128], BF16)
        t_const = T()
        S.dma("pool", lambda e: e.dma_start(out=identb[:], in_=cd["ident"][:, :]), writes=[t_const])
        S.dma("sp", lambda e: e.dma_start(out=identf[:], in_=cd["ident"][:, :]), writes=[t_const])
        S.dma("pool", lambda e: e.dma_start(out=onesb[:], in_=cd["ones"][:, :]), writes=[t_const])

        def rms_stats(es_tag, src_aps, st, col0, reads, t_st, junk):
            for i, ap in enumerate(src_aps):
                S.op("act", lambda e, ap=ap, i=i: e.activation(out=junk[:, 0:ap.shape[1]], in_=ap, func=AF.Square,
                                                             accum_out=st[:, col0 + i:col0 + i + 1]),
                     reads=reads, writes=[t_st])
            if len(src_aps) == 2:
                S.op("dve", lambda e: e.tensor_tensor(out=st[:, col0:col0 + 1], in0=st[:, col0:col0 + 1],
                                                      in1=st[:, col0 + 1:col0 + 2], op=ALU.add), reads=[t_st], writes=[t_st])
            S.op("act", lambda e: e.activation(out=st[:, col0 + 2:col0 + 3], in_=st[:, col0:col0 + 1], func=AF.Sqrt,
                                               scale=1.0 / D, bias=EPS), reads=[t_st], writes=[t_st])
            S.op("dve", lambda e: e.reciprocal(out=st[:, col0 + 3:col0 + 4], in_=st[:, col0 + 2:col0 + 3]),
                 reads=[t_st], writes=[t_st])

        def phase_norm_T(es, src_d, b, gvec_d, hT, t_hT, nxb=2):
            xb = [sbt(es, f"xb{i}", [128, D], F32) for i in range(nxb)]
            hb = [sbt(es, f"hb{i}", [128, D], BF16) for i in range(2)]
            junk = sbt(es, "junkA", [128, D], BF16)
            gt = sbt(es, "gtA", [128, D], F32)
            st = sbt(es, "stA", [128, 16 * 4], F32)
            t_xb, t_hb, t_g = TL(nxb), TL(2), T()
            t_st = TL(16)
            S.dma("sp", lambda e: e.dma_start(out=gt[:], in_=gvec_d.partition_broadcast(128)), writes=[t_g])
            def stats(tt):
                i = tt % 2
                xi = tt % nxb
                S.dma("sp", lambda e: e.dma_start(out=xb[xi][:], in_=src_d[b, tt * 128:(tt + 1) * 128, :]),
                      writes=[t_xb[xi]])
                rms_stats(es, [xb[xi][:]], st, tt * 4, [t_xb[xi]], t_st[tt], junk)
                S.op("dve", lambda e: e.scalar_tensor_tensor(out=hb[i][:], in0=xb[xi][:], scalar=st[:, tt * 4 + 3:tt * 4 + 4],
                                                             in1=gt[:], op0=ALU.mult, op1=ALU.mult),
                     reads=[t_xb[xi], t_st[tt], t_g], writes=[t_hb[i]])

            def trans(tt):
                i = tt % 2
                for k in range(8):
                    S.op("pe", lambda e, k=k: e.transpose(out=psb[:, k * 128:(k + 1) * 128], in_=hb[i][:, k * 128:(k + 1) * 128],
                                                          identity=identb[:]),
                         reads=[t_hb[i], t_const], writes=[pb[7]], inc=(k == 7))
                S.op("act", lambda e: e.activation(out=hT[:, :, tt * 128:(tt + 1) * 128],
                                                   in_=psb[:, :].rearrange("p (k t) -> p k t", k=8), func=AF.Copy),
                     reads=[pb[7]], writes=[t_hT[tt]])

            stats(0)
            for tt in range(16):
                if tt + 1 < 16:
                    stats(tt + 1)
                trans(tt)

        def load_wo(wo, t_wo, w_out_d):
            for hf in range(2):
                S.dma("pool", lambda e, hf=hf: e.dma_start(out=wo[:, :, hf * 512:(hf + 1) * 512],
                                                          in_=w_out_d[:, hf * 512:(hf + 1) * 512].rearrange("(k p) n -> p k n", p=128)),
                      writes=[t_wo])

        def phase_outproj(es, w_out_d, gvec_d, res_d, b, gA, gB, t_gA, t_gB, wo, t_wo):
            gt = sbt(es, "gtF", [128, D], F32)
            xr = [sbt(es, f"xr{i}", [128, D], F32) for i in range(2)]
            tm = [sbt(es, f"tmF{i}", [128, D], F32) for i in range(2)]
            junk = sbt(es, "junkF", [128, 512], BF16)
            st = sbt(es, "stF", [128, 16 * 4], F32)
            t_g = T()
            t_xr, t_tm, t_st = TL(2), TL(2), TL(16)
            S.dma("sp", lambda e: e.dma_start(out=gt[:], in_=gvec_d.partition_broadcast(128)), writes=[t_g])
            for tt in range(16):
                i = tt % 2
                bk = [psf[2 * i], psf[2 * i + 1]]
                tbk = [pb[2 * i], pb[2 * i + 1]]
                S.dma("sp", lambda e, tt=tt, i=i: e.dma_start(out=xr[i][:], in_=res_d[b, tt * 128:(tt + 1) * 128, :]),
                      writes=[t_xr[i]])
                for hf in range(2):
                    for k in range(8):
                        g_ap = (gA if k < 4 else gB)
                        S.op("pe", lambda e, hf=hf, k=k, g_ap=g_ap, tt=tt, bk=bk: e.matmul(
                            bk[hf][:], lhsT=g_ap[:, k % 4, tt * 128:(tt + 1) * 128], rhs=wo[:, k, hf * 512:(hf + 1) * 512],
                            start=(k == 0), stop=(k == 7)),
                            reads=[(t_gA if k < 4 else t_gB)[tt // 4], t_wo], writes=[tbk[hf]], inc=(k == 7))
                rms_stats(es, [bk[0][:], bk[1][:]], st, tt * 4, tbk, t_st[tt], junk)
                for hf in range(2):
                    S.op("dve", lambda e, hf=hf, tt=tt, i=i, bk=bk: e.scalar_tensor_tensor(
                        out=tm[i][:, hf * 512:(hf + 1) * 512], in0=bk[hf][:], scalar=st[:, tt * 4 + 3:tt * 4 + 4],
                        in1=gt[:, hf * 512:(hf + 1) * 512], op0=ALU.mult, op1=ALU.mult),
                        reads=[tbk[hf], t_st[tt], t_g], writes=[t_tm[i]])
                S.op("pool", lambda e, i=i: e.tensor_tensor(out=tm[i][:], in0=tm[i][:], in1=xr[i][:], op=ALU.add),
                     reads=[t_tm[i], t_xr[i]], writes=[t_tm[i]])
                S.dma("sp", lambda e, tt=tt, i=i: e.dma_start(out=out_d[b, tt * 128:(tt + 1) * 128, :], in_=tm[i][:]),
                      reads=[t_tm[i]])

        def layer0(b):
            with ExitStack() as L:
                qkT = sbt(L, "qkT", [128, 8, SEQ], BF16)
                Vt = sbt(L, "Vt", [128, 16, 512], BF16)
                gaT = sbt(L, "gaT", [128, 4, SEQ], BF16)
                gbT = sbt(L, "gbT", [128, 4, SEQ], BF16)
                t_qk = [TL(4) for _ in range(8)]
                t_V = TL(16)
                t_ga = [TL(4) for _ in range(4)]
                t_gb = [TL(4) for _ in range(4)]
                wo0 = sbt(L, "wo0", [128, 8, D], BF16)
                t_wo0 = T()

                with ExitStack() as es:
                    hT = sbt(es, "hT", [128, 8, SEQ], BF16)
                    t_hT = TL(16)
                    phase_norm_T(es, x_d, b, pre_g[0:1, :], hT, t_hT, nxb=4)
                    wb = [sbt(es, f"wb{i}", [128, 8, 512], BF16) for i in range(2)]
                    t_wb = TL(2)
                    ropeC = sbt(es, "ropeC", [128, SEQ], F32)
                    ropeS = sbt(es, "ropeS", [128, SEQ], F32)
                    p32 = sbt(es, "p32", [128, 128], BF16)
                    poolw = sbt(es, "poolw", [128, 4, 128], BF16)
                    pscale = sbt(es, "pscale", [128, 4], F32)
                    rcnt = sbt(es, "rcnt", [128, 64], F32)
                    t_c2 = T()
                    S.dma("sp", lambda e: e.dma_start(out=ropeC[:], in_=cd["ropeC"][:, :]), writes=[t_c2])
                    S.dma("sp", lambda e: e.dma_start(out=ropeS[:], in_=cd["ropeS"][:, :]), writes=[t_c2])
                    S.dma("pool", lambda e: e.dma_start(out=p32[:], in_=cd["p32"][:, :]), writes=[t_c2])
                    S.dma("pool", lambda e: e.dma_start(out=poolw[:], in_=e_pool_w.rearrange("g c d -> c g d")), writes=[t_c2])
                    S.dma("sp", lambda e: e.dma_start(out=pscale[:], in_=e_pool_scale[:, :]), writes=[t_c2])
                    S.dma("sp", lambda e: e.dma_start(out=rcnt[:], in_=cd["rcnt"][:, :]), writes=[t_c2])
                    r1 = [sbt(es, f"r1_{i}", [128, 512], F32) for i in range(2)]
                    r2 = [sbt(es, f"r2_{i}", [128, 512], F32) for i in range(2)]
                    t_r1, t_r2 = TL(2), TL(2)
                    ub = [sbt(es, f"ub{i}", [128, 528], F32) for i in range(2)]
                    sa = sbt(es, "sa", [128, 528], F32)
                    sb_ = sbt(es, "sb", [128, 528], F32)
                    mt = [sbt(es, f"mt{i}", [128, 512], BF16) for i in range(2)]
                    t_ub, t_mt = TL(2), TL(2)
                    t_sa, t_sb = T(), T()

                    order = [0, 1, 2, 3, 5, 4]
                    cnt = [0]
                    for oi in range(2):
                        S.dma("pool", lambda e, oi=oi: e.dma_start(out=wb[oi][:], in_=e_w_in[:, order[oi] * 512:(order[oi] + 1) * 512].rearrange("(k p) n -> p k n", p=128)),
                              writes=[t_wb[oi]])
                    ri = [0]
                    ksub = os.environ.get("KSUB", "")
                    for oi, gi in enumerate(order):
                        if ksub and oi >= int(ksub):
                            break
                        wbi = wb[oi % 2]
                        t_wbi = t_wb[oi % 2]
                        if gi in (0, 1):
                            pend = [None]

                            def rope_tail(j, c, r, pbk):
                                def f():
                                    S.op("pe", lambda e: e.matmul(psf[pbk][:], lhsT=p32[:], rhs=qkT[:, j, c * 512:(c + 1) * 512], start=True, stop=True),
                                         reads=[t_qk[j][c], t_c2], writes=[pb[pbk]])
                                    S.op("dve", lambda e: e.tensor_tensor(out=r2[r][:], in0=psf[pbk][:], in1=ropeS[:, c * 512:(c + 1) * 512], op=ALU.mult),
                                         reads=[pb[pbk], t_c2], writes=[t_r2[r]])
                                    S.op("dve", lambda e: e.tensor_tensor(out=qkT[:, j, c * 512:(c + 1) * 512], in0=r1[r][:], in1=r2[r][:], op=ALU.add),
                                         reads=[t_r1[r], t_r2[r]], writes=[t_qk[j][c]])
                                return f
                            for m in range(4):
                                j = gi * 4 + m
                                for c in range(4):
                                    bi = cnt[0] % 3
                                    cnt[0] += 1
                                    for k in range(8):
                                        S.op("pe", lambda e, k=k, bi=bi, m=m, c=c, wbi=wbi: e.matmul(
                                            psf[bi][:], lhsT=wbi[:, k, m * 128:(m + 1) * 128], rhs=hT[:, k, c * 512:(c + 1) * 512],
                                            start=(k == 0), stop=(k == 7)),
                                            reads=[t_wbi] + t_hT[4 * c:4 * c + 4], writes=[pb[bi]], inc=(k == 7))
                                    if pend[0] is not None:
                                        pend[0]()
                                    S.op("act", lambda e, bi=bi, j=j, c=c: e.activation(out=qkT[:, j, c * 512:(c + 1) * 512], in_=psf[bi][:], func=AF.Copy),
                                         reads=[pb[bi]], writes=[t_qk[j][c]])
                                    r = ri[0] % 2
                                    ri[0] += 1
                                    pbk = 3 + r
                                    S.op("dve", lambda e, bi=bi, c=c, r=r: e.tensor_tensor(out=r1[r][:], in0=psf[bi][:], in1=ropeC[:, c * 512:(c + 1) * 512], op=ALU.mult),
                                         reads=[pb[bi], t_c2, t_qk[j][c]], writes=[t_r1[r]])
                                    pend[0] = rope_tail(j, c, r, pbk)
                            pend[0]()
                        elif gi == 2:
                            for tt in range(16):
                                bi = cnt[0] % 3
                                cnt[0] += 1
                                for k in range(8):
                                    S.op("pe", lambda e, k=k, bi=bi, tt=tt, wbi=wbi: e.matmul(
                                        psf[bi][:], lhsT=hT[:, k, tt * 128:(tt + 1) * 128], rhs=wbi[:, k, :], start=(k == 0), stop=(k == 7)),
                                        reads=[t_wbi, t_hT[tt]], writes=[pb[bi]], inc=(k == 7))
                                S.op("dve", lambda e, bi=bi, tt=tt: e.tensor_copy(out=Vt[:, tt, :], in_=psf[bi][:]), reads=[pb[bi]], writes=[t_V[tt]])
                        elif gi in (3, 5):
                            dst, t_dst = (gaT, t_ga) if gi == 3 else (gbT, t_gb)
                            for m in range(4):
                                for c in range(4):
                                    bi = cnt[0] % 3
                                    cnt[0] += 1
                                    for k in range(8):
                                        S.op("pe", lambda e, k=k, bi=bi, m=m, c=c, wbi=wbi: e.matmul(
                                            psf[bi][:], lhsT=wbi[:, k, m * 128:(m + 1) * 128], rhs=hT[:, k, c * 512:(c + 1) * 512],
                                            start=(k == 0), stop=(k == 7)),
                                            reads=[t_wbi] + t_hT[4 * c:4 * c + 4], writes=[pb[bi]], inc=(k == 7))
                                    S.op("act", lambda e, bi=bi, m=m, c=c, dst=dst: e.activation(out=dst[:, m, c * 512:(c + 1) * 512], in_=psf[bi][:], func=AF.Silu),
                                         reads=[pb[bi]], writes=[t_dst[m][c]])
                        else:
                            ppend = [None]
                            for g in range(4):
                                w = (2, 4, 8, 16)[g]
                                nlev = g + 1
                                for c in range(4):
                                    bi = cnt[0] % 3
                                    cnt[0] += 1
                                    for k in range(8):
                                        S.op("pe", lambda e, k=k, bi=bi, g=g, c=c, wbi=wbi: e.matmul(
                                            psf[bi][:], lhsT=wbi[:, k, g * 128:(g + 1) * 128], rhs=hT[:, k, c * 512:(c + 1) * 512],
                                            start=(k == 0), stop=(k == 7)),
                                            reads=[t_wbi] + t_hT[4 * c:4 * c + 4], writes=[pb[bi]], inc=(k == 7))
                                    if ppend[0] is not None:
                                        ppend[0]()
                                        ppend[0] = None
                                    u = ub[c % 2]
                                    up = ub[(c + 1) % 2]
                                    if c == 0:
                                        S.op("pool", lambda e, u=u: e.memset(u[:, 0:16], 0.0), writes=[t_ub[c % 2]])
                                    else:
                                        S.op("pool", lambda e, u=u, up=up: e.tensor_copy(out=u[:, 0:16], in_=up[:, 512:528]),
                                             reads=[t_ub[(c + 1) % 2]], writes=[t_ub[c % 2]])
                                    S.op("act", lambda e, bi=bi, u=u: e.activation(out=u[:, 16:528], in_=psf[bi][:], func=AF.Copy),
                                         reads=[pb[bi]], writes=[t_ub[c % 2]])
                                    src, t_src = u, t_ub[c % 2]
                                    for lv in range(nlev):
                                        sh = 1 << lv
                                        lo = 2 * sh
                                        dstb, t_d = (sa, t_sa) if lv % 2 == 0 else (sb_, t_sb)
                                        S.op("dve", lambda e, src=src, dstb=dstb, lo=lo, sh=sh: e.tensor_tensor(
                                            out=dstb[:, lo:528], in0=src[:, lo:528], in1=src[:, lo - sh:528 - sh], op=ALU.add),
                                            reads=[t_src], writes=[t_d])
                                        src, t_src = dstb, t_d
                                    mi = c % 2
                                    S.op("dve", lambda e, src=src, u=u, mi=mi, w=w: e.scalar_tensor_tensor(
                                        out=mt[mi][:], in0=src[:, 16:528], scalar=1.0 / w, in1=u[:, 16:528], op0=ALU.mult, op1=ALU.subtract),
                                        reads=[t_src, t_ub[c % 2]], writes=[t_mt[mi]])
                                    if c == 0:
                                        S.op("dve", lambda e, src=src, g=g: e.tensor_tensor(out=src[:, 0:16], in0=src[:, 16:32], in1=rcnt[:, g * 16:(g + 1) * 16], op=ALU.mult),
                                             reads=[t_src, t_c2], writes=[t_src])
                                        S.op("dve", lambda e, src=src, u=u, mi=mi: e.tensor_tensor(out=mt[mi][:, 0:16], in0=src[:, 0:16], in1=u[:, 16:32], op=ALU.subtract),
                                             reads=[t_src, t_ub[c % 2]], writes=[t_mt[mi]])
                                    pbk = 3 + (c % 2)

                                    def pool_tail(g=g, c=c, mi=mi, pbk=pbk):
                                        S.op("pe", lambda e: e.matmul(psf[pbk][:], lhsT=poolw[:, g, :], rhs=mt[mi][:], start=True, stop=True),
                                             reads=[t_mt[mi], t_c2], writes=[pb[pbk]])
                                        S.op("dve", lambda e: e.scalar_tensor_tensor(
                                            out=gbT[:, g, c * 512:(c + 1) * 512], in0=psf[pbk][:], scalar=pscale[:, g:g + 1],
                                            in1=gbT[:, g, c * 512:(c + 1) * 512], op0=ALU.mult, op1=ALU.mult),
                                            reads=[pb[pbk], t_c2, t_gb[g][c]], writes=[t_gb[g][c]])
                                    ppend[0] = pool_tail
                        if gi == 4 and ppend[0] is not None:
                            ppend[0]()
                            ppend[0] = None
                        if oi + 2 < len(order) and not ksub:
                            gn = order[oi + 2]
                            S.dma("pool", lambda e, gn=gn, wbi=wbi: e.dma_start(out=wbi[:], in_=e_w_in[:, gn * 512:(gn + 1) * 512].rearrange("(k p) n -> p k n", p=128)),
                                  writes=[t_wbi])
                    S.flush()
                if os.environ.get("KSTOP") == "AB":
                    return

                with ExitStack() as es:
                    load_wo(wo0, t_wo0, e_w_out)
                    e8 = sbt(es, "e8", [8, 1024], BF16)
                    trineg = sbt(es, "trineg", [128, 128], BF16)
                    negmask = sbt(es, "negmask", [128, 64], F32)
                    biasfix = sbt(es, "biasfix", [128, 64], F32)
                    t_c3 = T()
                    S.dma("pool", lambda e: e.dma_start(out=e8[:], in_=cd["e8"][:, :]), writes=[t_c3])
                    S.dma("pool", lambda e: e.dma_start(out=trineg[:], in_=cd["trineg"][:, :]), writes=[t_c3])
                    S.dma("sp", lambda e: e.dma_start(out=negmask[:], in_=cd["negmask"][:, :]), writes=[t_c3])
                    S.dma("sp", lambda e: e.dma_start(out=biasfix[:], in_=cd["biasfix"][:, :]), writes=[t_c3])
                    Mrow = sbt(es, "Mrow", [8, 4, SEQ], BF16)
                    t_M = TL(4)
                    stab4 = [sbt(es, f"stab{h}", [8, SEQ], F32) for h in range(4)]
                    sqt = [sbt(es, f"sqt{i}", [128, 512], BF16) for i in range(8)]
                    t_sq = TL(8)
                    kmx4 = [sbt(es, f"kmx{h}", [8, 8], F32) for h in range(4)]
                    kb324 = [sbt(es, f"kb32{h}", [128, 8], F32) for h in range(4)]
                    kbar4 = [sbt(es, f"kbar{h}", [128, 8], BF16) for h in range(4)]
                    gm4 = [sbt(es, f"gm{h}", [128, 64], F32) for h in range(4)]
                    top84 = [sbt(es, f"top8{h}", [128, 64], F32) for h in range(4)]
                    sel4 = [sbt(es, f"sel{h}", [128, 64], F32) for h in range(4)]
                    t_stab4, t_kmx4, t_kb4, t_gm4, t_top4, t_sel4 = TL(4), TL(4), TL(4), TL(4), TL(4), TL(4)

                    def prep_ops(h):
                        L_ = []
                        add = lambda eng, fn, reads=(), writes=(), **kw: L_.append((eng, fn, list(reads), list(writes), kw))
                        stab, kmx, kb32, kbar, gm, top8, sel = stab4[h], kmx4[h], kb324[h], kbar4[h], gm4[h], top84[h], sel4[h]
                        t_stab, t_kmx, t_kb, t_gm, t_top, t_sel = t_stab4[h], t_kmx4[h], t_kb4[h], t_gm4[h], t_top4[h], t_sel4[h]
                        nb = 3 + h
                        tb = [h % 3, (h + 1) % 3]
                        for c in range(4):
                            i = 2 * h
                            add("act", lambda e, i=i, c=c: e.activation(out=sqt[i][:], in_=qkT[:, 4 + h, c * 512:(c + 1) * 512], func=AF.Square),
                                [t_qk[4 + h][c]], [t_sq[i]])
                            add("pe", lambda e, i=i: e.matmul(psf[nb][0:8, :], lhsT=onesb[:, 0:8], rhs=sqt[i][:], start=True, stop=True),
                                [t_sq[i], t_const], [pb[nb]])
                            add("dve", lambda e, c=c: e.tensor_reduce(out=kmx[:, c:c + 1], in_=psf[nb][0:8, :], axis=AX.X, op=ALU.max),
                                [pb[nb]], [t_kmx])
                        add("dve", lambda e: e.tensor_reduce(out=kmx[:, 4:5], in_=kmx[:, 0:4], axis=AX.X, op=ALU.max), [t_kmx], [t_kmx])
                        for c in range(4):
                            i = 2 * h + 1
                            add("act", lambda e, i=i, c=c: e.activation(out=sqt[i][:], in_=qkT[:, h, c * 512:(c + 1) * 512], func=AF.Square),
                                [t_qk[h][c]], [t_sq[i]])
                            add("pe", lambda e, i=i: e.matmul(psf[nb][0:8, :], lhsT=onesb[:, 0:8], rhs=sqt[i][:], start=True, stop=True),
                                [t_sq[i], t_const], [pb[nb]])
                            add("act", lambda e, c=c: e.activation(out=stab[:, c * 512:(c + 1) * 512], in_=psf[nb][0:8, :], func=AF.Sqrt, scale=kmx[:, 4:5]),
                                [pb[nb], t_kmx], [t_stab])
                        add("dve", lambda e: e.tensor_reduce(out=kb32[:], in_=qkT[:, 4 + h, :].rearrange("p (n s) -> p n s", s=256), axis=AX.X, op=ALU.add),
                            t_qk[4 + h], [t_kb])
                        add("dve", lambda e: e.tensor_scalar(out=kbar[:], in0=kb32[:], scalar1=1.0 / 256, scalar2=None, op0=ALU.mult), [t_kb], [t_kb])
                        for i8 in range(8):
                            add("pe", lambda e, i8=i8: e.matmul(psf[nb][:, i8 * 8:(i8 + 1) * 8], lhsT=qkT[:, h, (8 + i8) * 128:(9 + i8) * 128], rhs=kbar[:], start=True, stop=True),
                                [t_qk[h][2 + i8 // 4], t_kb], [pb[nb]], inc=(i8 == 7))
                        add("dve", lambda e: e.tensor_tensor(out=gm[:], in0=psf[nb][:, 0:64], in1=negmask[:], op=ALU.add), [pb[nb], t_c3], [t_gm])
                        for i8 in range(8):
                            add("dve", lambda e, i8=i8: e.max(out=top8[:, i8 * 8:(i8 + 1) * 8], in_=gm[:, i8 * 8:(i8 + 1) * 8]), [t_gm], [t_top])
                        for i8 in range(8):
                            add("dve", lambda e, i8=i8: e.tensor_scalar(out=sel[:, i8 * 8:(i8 + 1) * 8], in0=gm[:, i8 * 8:(i8 + 1) * 8],
                                                                      scalar1=top8[:, i8 * 8 + 2:i8 * 8 + 3], scalar2=None, op0=ALU.is_ge),
                                [t_gm, t_top], [t_sel])
                        add("dve", lambda e: e.scalar_tensor_tensor(out=sel[:], in0=sel[:], scalar=-NEGB, in1=biasfix[:], op0=ALU.mult, op1=ALU.add),
                            [t_sel, t_c3], [t_sel])
                        add("act", lambda e: e.activation(out=Mrow[:, h, 0:1024], in_=stab[:, 0:1024], func=AF.Copy, scale=-1.0), [t_stab], [t_M[h]])
                        for hf in range(2):
                            for i4 in range(4):
                                i8 = hf * 4 + i4
                                add("pe", lambda e, i8=i8, i4=i4: e.transpose(out=psf[nb][0:8, i4 * 128:(i4 + 1) * 128], in_=sel[:, i8 * 8:(i8 + 1) * 8], identity=identf[:]),
                                    [t_sel, t_const], [pb[nb]], inc=(i4 == 3))
                            add("dve", lambda e, hf=hf: e.tensor_tensor(out=Mrow[:, h, 1024 + hf * 512:1536 + hf * 512], in0=psf[nb][0:8, :],
                                                                      in1=stab[:, 1024 + hf * 512:1536 + hf * 512], op=ALU.subtract),
                                [pb[nb], t_stab], [t_M[h]])
                        return L_

                    plists = [prep_ops(h) for h in range(4)]
                    for i in range(max(len(l) for l in plists)):
                        for l in plists:
                            if i < len(l):
                                eng, fn, rd, wr, kw = l[i]
                                S.op(eng, fn, reads=rd, writes=wr, **kw)

                    PT = [sbt(es, f"PT{i}", [128, 512], BF16) for i in range(3)]
                    t_PT = TL(3)
                    lns = sbt(es, "lns", [128, 512], F32)
                    rinv = sbt(es, "rinv", [128, 512], F32)
                    ot = sbt(es, "ot", [128, 512], F32)
                    t_lns, t_rinv, t_ot = T(), T(), T()
                    scale = 1.0 / math.sqrt(HD)
                    items = [(h, qc, kt) for h in range(4) for qc in range(4) for kt in range(4 * qc + 4)]

                    def emit_S(idx):
                        h, qc, kt = items[idx]
                        sb_i = idx % 2
                        off = max(0, kt * 128 - qc * 512)
                        q0 = qc * 512 + off
                        q1 = (qc + 1) * 512
                        n = kt // 2
                        diag = kt >= 4 * qc
                        S.op("pe", lambda e: e.matmul(psf[sb_i][:, off:512], lhsT=qkT[:, 4 + h, kt * 128:(kt + 1) * 128], rhs=qkT[:, h, q0:q1], start=True, stop=False),
                             reads=[t_qk[4 + h][kt // 4], t_qk[h][qc]], writes=[pb[sb_i]])
                        S.op("pe", lambda e: e.matmul(psf[sb_i][:, off:512], lhsT=e8[:, n * 128:(n + 1) * 128], rhs=Mrow[:, h, q0:q1], start=False, stop=(not diag)),
                             reads=[t_M[h], t_c3], writes=[pb[sb_i]])
                        if diag:
                            S.op("pe", lambda e: e.matmul(psf[sb_i][:, off:off + 128], lhsT=identb[:], rhs=trineg[:], start=False, stop=True),
                                 reads=[t_c3, t_const], writes=[pb[sb_i]])
                        pi = idx % 3
                        S.op("act", lambda e: e.activation(out=PT[pi][:, off:512], in_=psf[sb_i][:, off:512], func=AF.Exp, scale=scale),
                             reads=[pb[sb_i]], writes=[t_PT[pi]])

                    def emit_PV(idx):
                        h, qc, kt = items[idx]
                        off = max(0, kt * 128 - qc * 512)
                        pi = idx % 3
                        par = (h * 4 + qc) % 2
                        ob, sbk = 2 + par, 4 + par
                        last = (kt == 4 * qc + 3)
                        S.op("pe", lambda e: e.matmul(psf[ob][:, off:512], lhsT=Vt[:, kt, h * 128:(h + 1) * 128], rhs=PT[pi][:, off:512], start=(kt == 0), stop=last),
                             reads=[t_V[kt], t_PT[pi]], writes=[pb[ob]])
                        S.op("pe", lambda e: e.matmul(psf[sbk][:, off:512], lhsT=onesb[:], rhs=PT[pi][:, off:512], start=(kt == 0), stop=last),
                             reads=[t_const, t_PT[pi]], writes=[pb[sbk]])
                        if last:
                            S.op("act", lambda e: e.activation(out=lns[:], in_=psf[sbk][:], func=AF.Ln), reads=[pb[sbk]], writes=[t_lns])
                            S.op("act", lambda e: e.activation(out=rinv[:], in_=lns[:], func=AF.Exp, scale=-1.0), reads=[t_lns], writes=[t_rinv])
                            S.op("dve", lambda e: e.tensor_tensor(out=ot[:], in0=psf[ob][:], in1=rinv[:], op=ALU.mult), reads=[pb[ob], t_rinv], writes=[t_ot])
                            S.op("pool", lambda e: e.tensor_tensor(out=gaT[:, h, qc * 512:(qc + 1) * 512], in0=ot[:], in1=gaT[:, h, qc * 512:(qc + 1) * 512], op=ALU.mult),
                                 reads=[t_ot, t_ga[h][qc]], writes=[t_ga[h][qc]])

                    emit_S(0)
                    for idx in range(len(items)):
                        if idx + 1 < len(items):
                            emit_S(idx + 1)
                        emit_PV(idx)
                    S.flush()
                if os.environ.get("KSTOP") == "CD":
                    return

                with ExitStack() as es:
                    t_gA = [T() for _ in range(4)]
                    t_gB = [T() for _ in range(4)]
                    phase_outproj(es, e_w_out, post_g[0:1, :], x_d, b, gaT, gbT, t_gA, t_gB, wo0, t_wo0)
                    S.flush()

        L1 = top

        def layer1_all(nb):
            with ExitStack() as P1:
                wv_sb = sbt(P1, "wv_sb", [128, 4, 8, 2, 128], BF16)
                toep_sb = sbt(P1, "toep_sb", [128, 4, 8, 256], BF16)
                w3_sb = sbt(P1, "w3_sb", [128, 16, 2, 256], BF16)
                pw_tab = sbt(P1, "pw_tab", [128, 16, 8, 3], F32)
                dA = sbt(P1, "dA", [128, 4], F32)
                glub = sbt(P1, "glub", [128, 8], F32)
                lng = sbt(P1, "lng", [128, 4], F32)
                lnb = sbt(P1, "lnb", [128, 4], F32)
                dwT = sbt(P1, "dwT", [128, 4, 31], F32)
                ones512 = sbt(P1, "ones512", [128, 128], BF16)
                t_par = T()
                for dst, src in ((dA, o_dA), (glub, o_glub), (lng, o_lng), (lnb, o_lnb)):
                    S.dma("sp", lambda e, dst=dst, src=src: e.dma_start(out=dst[:], in_=src[:, :]), writes=[t_par])
                S.dma("sp", lambda e: e.dma_start(out=dwT[:], in_=o_dwT[:, :, :]), writes=[t_par])
                S.op("act", lambda e: e.activation(out=ones512[:], in_=onesb[:], func=AF.Copy, scale=1.0 / 512), reads=[t_const], writes=[t_par])

                with ExitStack() as es:
                    tkA, tkB, t_m = T(), T(), T()
                    cur = {"eng": "dve", "tk": tkA}

                    def vop(fn, eng=None):
                        S.op(eng or cur["eng"], fn, reads=[cur["tk"], t_par, t_m], writes=[cur["tk"]])

                    def pop(fn, eng=None):
                        S.op(eng or cur["eng"], fn, reads=[cur["tk"], t_par, t_m], writes=[T()])

                    def tt(o, a, b_, op, eng=None):
                        vop(lambda e: e.tensor_tensor(out=o, in0=a, in1=b_, op=op), eng)

                    def ld(name, src, shape, tok=None):
                        t = sbt(es, name, shape, F32)
                        S.dma("sp", lambda e: e.dma_start(out=t[:], in_=src), writes=[tok if tok is not None else cur["tk"]])
                        return t

                    def compute_a(tag, lr, li, dtl, n):
                        mk = lambda nm: sbt(es, f"{tag}_{nm}", [128, n], F32)
                        dtv, x1, mg, th, u, r, sn_, cs_, ar, ai = [mk(k) for k in ("dt", "x1", "mg", "th", "u", "r", "sn", "cs", "ar", "ai")]
                        ui = sbt(es, f"{tag}_ui", [128, n], I32)
                        vop(lambda e: e.activation(out=dtv[:], in_=dtl[:], func=AF.Exp), "act")
                        tt(x1[:], lr[:], dtv[:], ALU.mult)
                        vop(lambda e: e.activation(out=mg[:], in_=x1[:], func=AF.Exp), "act")
                        tt(th[:], li[:], dtv[:], ALU.mult)
                        for shift, dst in ((0.0, sn_), (math.pi / 2, cs_)):
                            vop(lambda e, shift=shift: e.tensor_scalar(out=u[:], in0=th[:], scalar1=shift, scalar2=1.0 / (2 * math.pi), op0=ALU.add, op1=ALU.mult))
                            vop(lambda e: e.tensor_copy(out=ui[:], in_=u[:]), "dve")
                            vop(lambda e: e.tensor_copy(out=u[:], in_=ui[:]), "dve")
                            vop(lambda e: e.tensor_scalar(out=u[:], in0=u[:], scalar1=-2 * math.pi, scalar2=None, op0=ALU.mult))
                            tt(r[:], u[:], th[:], ALU.add)
                            vop(lambda e, shift=shift: e.tensor_scalar(out=r[:], in0=r[:], scalar1=shift, scalar2=None, op0=ALU.add))
                            vop(lambda e, dst=dst: e.activation(out=dst[:], in_=r[:], func=AF.Sin), "act")
                        tt(ar[:], mg[:], cs_[:], ALU.mult)
                        tt(ai[:], mg[:], sn_[:], ALU.mult)
                        return ar, ai

                    lrA = ld("lrA", o_lamre_A[:, :], [128, 256])
                    liA = ld("liA", o_lamim_A[:, :], [128, 256])
                    dtA = ld("dtA", o_dt_A[:, :], [128, 256])
                    brA = ld("brA", o_bre_A[:, :], [128, 256])
                    biA = ld("biA", o_bim_A[:, :], [128, 256])
                    mA = ld("mA", cd["maskA"][:, :], [128, 2], t_m)
                    mB = ld("mB", cd["maskB"][:, :], [128, 2], t_m)
                    arA, aiA = compute_a("A", lrA, liA, dtA, 256)
                    mk = lambda nm, n=256: sbt(es, nm, [128, n], F32)
                    nr, den, t1, t2, fr, fi = [mk(k) for k in ("nr", "den", "t1", "t2", "fr", "fi")]
                    vop(lambda e: e.tensor_scalar(out=nr[:], in0=arA[:], scalar1=-1.0, scalar2=None, op0=ALU.add))
                    tt(t1[:], lrA[:], lrA[:], ALU.mult)
                    tt(t2[:], liA[:], liA[:], ALU.mult)
                    tt(den[:], t1[:], t2[:], ALU.add)
                    vop(lambda e: e.reciprocal(out=den[:], in_=den[:]), "dve")
                    tt(t1[:], nr[:], lrA[:], ALU.mult)
                    tt(t2[:], aiA[:], liA[:], ALU.mult)
                    tt(t1[:], t1[:], t2[:], ALU.add)
                    tt(fr[:], t1[:], den[:], ALU.mult)
                    tt(t1[:], aiA[:], lrA[:], ALU.mult)
                    tt(t2[:], nr[:], liA[:], ALU.mult)
                    tt(t1[:], t1[:], t2[:], ALU.subtract)
                    tt(fi[:], t1[:], den[:], ALU.mult)
                    Gall = sbt(es, "Gall", [128, 8, 2, 256], F32)
                    tt(t1[:], fr[:], brA[:], ALU.mult)
                    tt(t2[:], fi[:], biA[:], ALU.mult)
                    tt(Gall[:, 0, 0, :], t1[:], t2[:], ALU.subtract)
                    tt(t1[:], fr[:], biA[:], ALU.mult)
                    tt(t2[:], fi[:], brA[:], ALU.mult)
                    tt(Gall[:, 0, 1, :], t1[:], t2[:], ALU.add)
                    for m in range(7):
                        tt(t1[:], Gall[:, m, 0, :], arA[:], ALU.mult)
                        tt(t2[:], Gall[:, m, 1, :], aiA[:], ALU.mult)
                        tt(Gall[:, m + 1, 0, :], t1[:], t2[:], ALU.subtract)
                        tt(t1[:], Gall[:, m, 0, :], aiA[:], ALU.mult)
                        tt(t2[:], Gall[:, m, 1, :], arA[:], ALU.mult)
                        tt(Gall[:, m + 1, 1, :], t1[:], t2[:], ALU.add)
                    for s in range(8):
                        for ri in range(2):
                            for gi in range(2):
                                pop(lambda e, s=s, ri=ri, gi=gi: e.tensor_scalar(
                                    out=wv_sb[:, :, s, ri, gi * 64:(gi + 1) * 64],
                                    in0=Gall[:, 7 - s, ri, :].rearrange("p (c n) -> p c n", c=4),
                                    scalar1=mA[:, gi:gi + 1], scalar2=None, op0=ALU.mult))
                    Kall = sbt(es, "Kall", [128, 4, 8, 16], F32)
                    cur["eng"], cur["tk"] = "dve", tkB
                    lrB = ld("lrB", o_lamre_B[:, :], [128, 16])
                    liB = ld("liB", o_lamim_B[:, :], [128, 16])
                    dtB = ld("dtB", o_dt_B[:, :], [128, 16])
                    crB = ld("crB", o_cre_B[:, :], [128, 256])
                    ciB = ld("ciB", o_cim_B[:, :], [128, 256])
                    arB, aiB = compute_a("B", lrB, liB, dtB, 16)
                    PB = sbt(es, "PB", [128, 8, 2, 16], F32)
                    s1 = sbt(es, "s1", [128, 16], F32)
                    s2 = sbt(es, "s2", [128, 16], F32)
                    vop(lambda e: e.tensor_copy(out=PB[:, 0, 0, :], in_=arB[:]))
                    vop(lambda e: e.tensor_copy(out=PB[:, 0, 1, :], in_=aiB[:]))
                    for r in range(7):
                        tt(s1[:], PB[:, r, 0, :], arB[:], ALU.mult)
                        tt(s2[:], PB[:, r, 1, :], aiB[:], ALU.mult)
                        tt(PB[:, r + 1, 0, :], s1[:], s2[:], ALU.subtract)
                        tt(s1[:], PB[:, r, 0, :], aiB[:], ALU.mult)
                        tt(s2[:], PB[:, r, 1, :], arB[:], ALU.mult)
                        tt(PB[:, r + 1, 1, :], s1[:], s2[:], ALU.add)
                    brB = ld("brB", o_bre_B[:, :], [128, 256])
                    biB = ld("biB", o_bim_B[:, :], [128, 256])
                    mkb = lambda nm: sbt(es, nm, [128, 16], F32)
                    nrB, denB, x1B, x2B, frB, fiB = [mkb(k) for k in ("nrB", "denB", "x1B", "x2B", "frB", "fiB")]
                    vop(lambda e: e.tensor_scalar(out=nrB[:], in0=arB[:], scalar1=-1.0, scalar2=None, op0=ALU.add))
                    tt(x1B[:], lrB[:], lrB[:], ALU.mult)
                    tt(x2B[:], liB[:], liB[:], ALU.mult)
                    tt(denB[:], x1B[:], x2B[:], ALU.add)
                    vop(lambda e: e.reciprocal(out=denB[:], in_=denB[:]), "dve")
                    tt(x1B[:], nrB[:], lrB[:], ALU.mult)
                    tt(x2B[:], aiB[:], liB[:], ALU.mult)
                    tt(x1B[:], x1B[:], x2B[:], ALU.add)
                    tt(frB[:], x1B[:], denB[:], ALU.mult)
                    tt(x1B[:], aiB[:], lrB[:], ALU.mult)
                    tt(x2B[:], nrB[:], liB[:], ALU.mult)
                    tt(x1B[:], x1B[:], x2B[:], ALU.subtract)
                    tt(fiB[:], x1B[:], denB[:], ALU.mult)
                    w1 = sbt(es, "w1", [128, 256], F32)
                    w2 = sbt(es, "w2", [128, 256], F32)
                    bbr = sbt(es, "bbrB", [128, 256], F32)
                    bbi = sbt(es, "bbiB", [128, 256], F32)
                    v3 = lambda t: t[:, :].rearrange("p (a i) -> p a i", a=16)
                    bc = lambda t: t[:, :].unsqueeze(2).broadcast_to([128, 16, 16])
                    tt(v3(w1), v3(brB), bc(frB), ALU.mult)
                    tt(v3(w2), v3(biB), bc(fiB), ALU.mult)
                    tt(bbr[:], w1[:], w2[:], ALU.subtract)
                    tt(v3(w1), v3(biB), bc(frB), ALU.mult)
                    tt(v3(w2), v3(brB), bc(fiB), ALU.mult)
                    tt(bbi[:], w1[:], w2[:], ALU.add)
                    Bmr = sbt(es, "Bmr", [128, 16, 128], F32)
                    Bmi = sbt(es, "Bmi", [128, 16, 128], F32)
                    vop(lambda e: e.memset(Bmr[:], 0.0), "pool")
                    vop(lambda e: e.memset(Bmi[:], 0.0), "pool")
                    for q in range(4):
                        for gi in range(2):
                            c0 = 32 * q + 16 * gi
                            vop(lambda e, q=q, gi=gi, c0=c0: e.tensor_scalar(
                                out=Bmr[:, :, :].rearrange("p (c q) m -> p c q m", q=4)[:, :, q, c0:c0 + 16],
                                in0=bbr[:, :].rearrange("p (c q j) -> p c q j", q=4, j=16)[:, :, q, :],
                                scalar1=mB[:, gi:gi + 1], scalar2=None, op0=ALU.mult))
                            vop(lambda e, q=q, gi=gi, c0=c0: e.tensor_scalar(
                                out=Bmi[:, :, :].rearrange("p (c q) m -> p c q m", q=4)[:, :, q, c0:c0 + 16],
                                in0=bbi[:, :].rearrange("p (c q j) -> p c q j", q=4, j=16)[:, :, q, :],
                                scalar1=mB[:, gi:gi + 1], scalar2=-1.0, op0=ALU.mult, op1=ALU.mult))
                    CAr = sbt(es, "CAr", [128, 9, 16, 16], F32)
                    CAi = sbt(es, "CAi", [128, 9, 16, 16], F32)
                    vop(lambda e: e.tensor_copy(out=CAr[:, 0, :, :], in_=v3(crB)))
                    vop(lambda e: e.tensor_copy(out=CAi[:, 0, :, :], in_=v3(ciB)))
                    for r in range(8):
                        pre = PB[:, r, 0, :].unsqueeze(2).broadcast_to([128, 16, 16])
                        pim = PB[:, r, 1, :].unsqueeze(2).broadcast_to([128, 16, 16])
                        tt(v3(w1), v3(crB), pre, ALU.mult)
                        tt(v3(w2), v3(ciB), pim, ALU.mult)
                        tt(CAr[:, r + 1, :, :], v3(w1), v3(w2), ALU.subtract)
                        tt(v3(w1), v3(crB), pim, ALU.mult)
                        tt(v3(w2), v3(ciB), pre, ALU.mult)
                        tt(CAi[:, r + 1, :, :], v3(w1), v3(w2), ALU.add)
                        for gi in range(2):
                            pop(lambda e, r=r, gi=gi: e.tensor_scalar(out=w3_sb[:, :, 0, gi * 128 + r * 16:gi * 128 + r * 16 + 16], in0=CAr[:, r + 1, :, :],
                                                                    scalar1=mB[:, gi:gi + 1], scalar2=None, op0=ALU.mult))
                            pop(lambda e, r=r, gi=gi: e.tensor_scalar(out=w3_sb[:, :, 1, gi * 128 + r * 16:gi * 128 + r * 16 + 16], in0=CAi[:, r + 1, :, :],
                                                                    scalar1=mB[:, gi:gi + 1], scalar2=-1.0, op0=ALU.mult, op1=ALU.mult))
                    for ct in range(4):
                        for q in range(4):
                            p = 4 * ct + q
                            S.op("pe", lambda e, p=p, q=q: e.matmul(psf[0][:, 0:128], lhsT=Bmr[:, p, :], rhs=CAr[:, 0:8, p, :], start=(q == 0), stop=False),
                                 reads=[tkB], writes=[pb[0]], inc=False)
                            S.op("pe", lambda e, p=p, q=q: e.matmul(psf[0][:, 0:128], lhsT=Bmi[:, p, :], rhs=CAi[:, 0:8, p, :], start=False, stop=(q == 3)),
                                 reads=[tkB], writes=[pb[0]], inc=(q == 3))
                        S.op("dve", lambda e, ct=ct: e.tensor_copy(out=Kall[:, ct, :, :], in_=psf[0][:, 0:128].rearrange("p (t i) -> p t i", t=8)),
                             reads=[pb[0], tkB], writes=[tkB])
                    vop(lambda e: e.memset(toep_sb[:], 0.0), "pool")
                    for s in range(8):
                        for gi in range(2):
                            vop(lambda e, s=s, gi=gi: e.tensor_scalar(
                                out=toep_sb[:, :, s, gi * 128 + s * 16:gi * 128 + 128],
                                in0=Kall[:, :, 0:8 - s, :].rearrange("p c t i -> p c (t i)"),
                                scalar1=mA[:, gi:gi + 1], scalar2=None, op0=ALU.mult))
                    qr = sbt(es, "qr", [128, 16], F32)
                    qi = sbt(es, "qi", [128, 16], F32)
                    vop(lambda e: e.tensor_copy(out=qr[:], in_=PB[:, 7, 0, :]))
                    vop(lambda e: e.tensor_copy(out=qi[:], in_=PB[:, 7, 1, :]))
                    for m in range(8):
                        pop(lambda e, m=m: e.tensor_copy(out=pw_tab[:, :, m, 0], in_=qr[:]))
                        pop(lambda e, m=m: e.tensor_copy(out=pw_tab[:, :, m, 1], in_=qi[:]))
                        pop(lambda e, m=m: e.tensor_scalar(out=pw_tab[:, :, m, 2], in0=qi[:], scalar1=-1.0, scalar2=None, op0=ALU.mult))
                        if m < 7:
                            tt(s1[:], qr[:], qr[:], ALU.mult)
                            tt(s2[:], qi[:], qi[:], ALU.mult)
                            tt(s2[:], s1[:], s2[:], ALU.subtract)
                            tt(s1[:], qr[:], qi[:], ALU.mult)
                            vop(lambda e: e.tensor_scalar(out=qi[:], in0=s1[:], scalar1=2.0, scalar2=None, op0=ALU.mult))
                            vop(lambda e: e.tensor_copy(out=qr[:], in_=s2[:]))
                    S.flush()

                for b in range(nb):
                    layer1(b, wv_sb, toep_sb, w3_sb, pw_tab, dA, glub, lng, lnb, dwT, ones512, t_par)

        def layer1(b, wv_sb, toep_sb, w3_sb, pw_tab, dA, glub, lng, lnb, dwT, ones512, t_par):
            with ExitStack() as L:
                suT = sbt(L, "suT", [128, 4, SEQ], BF16)
                gcT = sbt(L, "gcT", [128, 4, SEQ], BF16)
                gdT = sbt(L, "gdT", [128, 4, SEQ], BF16)
                gpad = sbt(L, "gpad", [128, 4, SEQ + 32], BF16)
                t_su = [TL(4) for _ in range(4)]
                t_gc = [TL(4) for _ in range(4)]
                t_gd = [TL(4) for _ in range(4)]
                t_gp = TL(4)
                wo1 = sbt(L, "wo1", [128, 8, D], BF16)
                t_wo1 = T()
                pwsb = sbt(L, "pwsb", [128, 4, 512], BF16)
                t_pw = T()
                with ExitStack() as es:
                    hT = sbt(es, "hT1", [128, 8, SEQ], BF16)
                    t_hT = TL(16)
                    phase_norm_T(es, out_d, b, pre_g[1:2, :], hT, t_hT, nxb=2)
                    wb = [sbt(es, f"wb1_{i}", [128, 8, 512], BF16) for i in range(2)]
                    t_wb = TL(2)
                    sg = [sbt(es, f"sg{i}", [128, 512], BF16) for i in range(2)]
                    t_sg = TL(2)
                    order = [0, 1, 4, 2, 3]

                    def loadw(oi):
                        gi = order[oi]
                        S.dma("pool", lambda e: e.dma_start(out=wb[oi % 2][:], in_=o_w_in[:, gi * 512:(gi + 1) * 512].rearrange("(k p) n -> p k n", p=128)),
                              writes=[t_wb[oi % 2]])
                    loadw(0)
                    loadw(1)
                    S.op("pool", lambda e: e.memset(gpad[:, :, 0:32], 0.0), writes=t_gp)
                    cnt = [0]

                    def proj(wbi, t_wbi, m, c):
                        bi = cnt[0] % 3
                        cnt[0] += 1
                        for k in range(8):
                            S.op("pe", lambda e, k=k: e.matmul(psf[bi][:], lhsT=wbi[:, k, m * 128:(m + 1) * 128], rhs=hT[:, k, c * 512:(c + 1) * 512],
                                                               start=(k == 0), stop=(k == 7)),
                                 reads=[t_wbi] + t_hT[4 * c:4 * c + 4], writes=[pb[bi]], inc=(k == 7))
                        return bi
                    for oi in range(3):
                        gi = order[oi]
                        dst, t_dst, fn = ((suT, t_su, AF.Copy), (gcT, t_gc, AF.Silu), None, None, (gdT, t_gd, AF.Silu))[gi]
                        for m in range(4):
                            for c in range(4):
                                bi = proj(wb[oi % 2], t_wb[oi % 2], m, c)
                                S.op("act", lambda e, bi=bi, m=m, c=c, dst=dst, fn=fn: e.activation(out=dst[:, m, c * 512:(c + 1) * 512], in_=psf[bi][:], func=fn),
                                     reads=[pb[bi]], writes=[t_dst[m][c]])
                        if oi + 2 < 5:
                            loadw(oi + 2)
                    si = 0
                    for m in range(4):
                        for c in range(4):
                            bi = proj(wb[0], t_wb[0], m, c)
                            i = si % 2
                            si += 1
                            S.op("act", lambda e, bi=bi, i=i: e.activation(out=sg[i][:], in_=psf[bi][:], func=AF.Sigmoid), reads=[pb[bi]], writes=[t_sg[i]])
                            bi2 = proj(wb[1], t_wb[1], m, c)
                            S.op("dve", lambda e, bi2=bi2, i=i, m=m, c=c: e.tensor_tensor(out=gpad[:, m, 32 + c * 512:32 + (c + 1) * 512], in0=psf[bi2][:], in1=sg[i][:], op=ALU.mult),
                                 reads=[pb[bi2], t_sg[i]], writes=[t_gp[m]])
                    S.flush()
                if os.environ.get("KSTOP") == "AB1":
                    return

                with ExitStack() as es:
                    Sb = [[[sbt(es, f"S{sl}{pi}{pp}", [128, 2, 512], F32) for pp in range(2)] for pi in range(2)] for sl in range(2)]
                    t_S = [[[T() for pp in range(2)] for pi in range(2)] for sl in range(2)]
                    Sp = [[[sbt(es, f"Sp{sl}{pi}{ri}", [128, 256], BF16) for ri in range(2)] for pi in range(2)] for sl in range(2)]
                    t_Sp = [[T() for pi in range(2)] for sl in range(2)]
                    for sl in range(2):
                        for pi in range(2):
                            for ri in range(2):
                                S.op("pool", lambda e, sl=sl, pi=pi, ri=ri: e.memset(Sp[sl][pi][ri][:, 0:1], 0.0), writes=[t_Sp[sl][pi]])
                                for pp in range(2):
                                    S.op("pool", lambda e, sl=sl, pi=pi, ri=ri, pp=pp: e.memset(Sb[sl][pi][pp][:, ri, 0:256], 0.0), writes=[t_S[sl][pi][pp]])
                    ysb = [sbt(es, f"ysb{i}", [128, 2, 8, 128], BF16) for i in range(2)]
                    t_ysb = [T(), T()]
                    gluw = sbt(es, "gluw", [128, 4, 1024], BF16)
                    t_gw = T()
                    for hf in range(2):
                        S.dma("pool", lambda e, hf=hf: e.dma_start(out=gluw[:, :, hf * 512:(hf + 1) * 512], in_=o_glu_w[:, hf * 512:(hf + 1) * 512].rearrange("(k p) n -> p k n", p=128)),
                              writes=[t_gw])
                    load_wo(wo1, t_wo1, o_w_out)
                    S.dma("pool", lambda e: e.dma_start(out=pwsb[:], in_=o_pw.rearrange("(k p) n -> p k n", p=128)), writes=[t_pw])
                    couples = [(ct, q0) for ct in range(4) for q0 in (0, 2)]

                    def emit_V(ci):
                        ct, q0 = couples[ci]
                        sl = ci % 2
                        for pi in range(2):
                            q = q0 + pi
                            rows = slice(32 * q, 32 * q + 32)
                            tp = (32 * q, 0)
                            for ri in range(2):
                                bk = (0, 1, 4, 5)[pi * 2 + ri]
                                for s in range(8):
                                    S.op("pe", lambda e, s=s, ri=ri, bk=bk, rows=rows, ct=ct, tp=tp: e.matmul(
                                        psf[bk][:, 0:256], lhsT=wv_sb[rows, ct, s, ri, :],
                                        rhs=suT[rows, ct, :].rearrange("p (k s) -> p s k", s=8)[:, s, :], start=(s == 0), stop=(s == 7), tile_position=tp),
                                        reads=t_su[ct] + [t_par], writes=[pb[bk]], inc=(s == 7))
                                S.op("act", lambda e, ri=ri, bk=bk, sl=sl, pi=pi: e.activation(out=Sb[sl][pi][0][:, ri, 256:512], in_=psf[bk][:, 0:256], func=AF.Copy),
                                     reads=[pb[bk]], writes=[t_S[sl][pi][0]])

                    def emit_scan(ci):
                        ct, q0 = couples[ci]
                        sl = ci % 2
                        for m in range(8):
                            sh = 1 << m
                            a, d_ = m % 2, (m + 1) % 2
                            for stage in range(3):
                                for pi in range(2):
                                    p = ct * 4 + q0 + pi
                                    src, dst = Sb[sl][pi][a], Sb[sl][pi][d_]
                                    ts, td = t_S[sl][pi][a], t_S[sl][pi][d_]
                                    pr = pw_tab[:, p, m, 0:1]
                                    pim = pw_tab[:, p, m, 1:2]
                                    npi = pw_tab[:, p, m, 2:3]
                                    if stage == 0:
                                        S.op("dve", lambda e, src=src, dst=dst, sh=sh, pr=pr: e.scalar_tensor_tensor(out=dst[:, :, 256:512], in0=src[:, :, 256 - sh:512 - sh], scalar=pr, in1=src[:, :, 256:512], op0=ALU.mult, op1=ALU.add),
                                             reads=[ts, t_par], writes=[td])
                                    elif stage == 1:
                                        S.op("dve", lambda e, src=src, dst=dst, sh=sh, npi=npi: e.scalar_tensor_tensor(out=dst[:, 0, 256:512], in0=src[:, 1, 256 - sh:512 - sh], scalar=npi, in1=dst[:, 0, 256:512], op0=ALU.mult, op1=ALU.add),
                                             reads=[ts, td, t_par], writes=[td])
                                    else:
                                        S.op("dve", lambda e, src=src, dst=dst, sh=sh, pim=pim: e.scalar_tensor_tensor(out=dst[:, 1, 256:512], in0=src[:, 0, 256 - sh:512 - sh], scalar=pim, in1=dst[:, 1, 256:512], op0=ALU.mult, op1=ALU.add),
                                             reads=[ts, td, t_par], writes=[td])

                    def emit_y(ci):
                        ct, q0 = couples[ci]
                        sl = ci % 2
                        yb_ = ysb[ct % 2]
                        t_y = t_ysb[ct % 2]
                        for pi in range(2):
                            q = q0 + pi
                            p = ct * 4 + q
                            rows = slice(32 * q, 32 * q + 32)
                            tp = (32 * q, 0)
                            for ri in range(2):
                                S.op("act", lambda e, ri=ri, sl=sl, pi=pi: e.activation(out=Sp[sl][pi][ri][:, 1:256], in_=Sb[sl][pi][0][:, ri, 256:511], func=AF.Copy),
                                     reads=[t_S[sl][pi][0]], writes=[t_Sp[sl][pi]])
                            for kt2 in range(2):
                                bk = 2 + kt2
                                for s in range(8):
                                    S.op("pe", lambda e, s=s, kt2=kt2, bk=bk, rows=rows, ct=ct, tp=tp: e.matmul(
                                        psf[bk][:, 0:256],
                                        lhsT=suT[rows, ct, kt2 * 1024:(kt2 + 1) * 1024].rearrange("p (k s) -> p s k", s=8)[:, s, :],
                                        rhs=toep_sb[rows, ct, s, :], start=(s == 0), stop=False, tile_position=tp),
                                        reads=t_su[ct] + [t_par], writes=[pb[bk]], inc=False)
                                for ri in range(2):
                                    S.op("pe", lambda e, ri=ri, kt2=kt2, bk=bk, sl=sl, pi=pi, p=p: e.matmul(
                                        psf[bk][:, 0:256], lhsT=Sp[sl][pi][ri][:, kt2 * 128:(kt2 + 1) * 128], rhs=w3_sb[:, p, ri, :], start=False, stop=(ri == 1)),
                                        reads=[t_Sp[sl][pi], t_par], writes=[pb[bk]], inc=(ri == 1))
                                S.op("act", lambda e, kt2=kt2, bk=bk, q=q, yb_=yb_: e.activation(
                                    out=yb_[:, kt2, :, q * 32:(q + 1) * 32].rearrange("p r (g i) -> p g r i", g=2),
                                    in_=psf[bk][:, 0:256].rearrange("p (g r i) -> p g r i", g=2, r=8), func=AF.Copy),
                                    reads=[pb[bk]], writes=[t_y])

                    def emit_T(ct):
                        yb_ = ysb[ct % 2]
                        t_y = t_ysb[ct % 2]
                        for kt2 in range(2):
                            for r in range(8):
                                S.op("pe", lambda e, kt2=kt2, r=r, yb_=yb_: e.transpose(out=psb[:, r * 128:(r + 1) * 128], in_=yb_[:, kt2, r, :], identity=identb[:]),
                                     reads=[t_y, t_const], writes=[pb[7]], inc=(r == 7))
                            S.op("dve", lambda e, kt2=kt2, ct=ct: e.scalar_tensor_tensor(
                                out=suT[:, ct, kt2 * 1024:(kt2 + 1) * 1024].rearrange("p (k r) -> p r k", r=8),
                                in0=suT[:, ct, kt2 * 1024:(kt2 + 1) * 1024].rearrange("p (k r) -> p r k", r=8),
                                scalar=dA[:, ct:ct + 1],
                                in1=psb[:, :].rearrange("p (r k) -> p r k", r=8), op0=ALU.mult, op1=ALU.add),
                                reads=[pb[7], t_par] + t_su[ct], writes=t_su[ct])

                    emit_V(0)
                    for ci in range(8):
                        if ci + 1 < 8:
                            emit_V(ci + 1)
                        emit_scan(ci)
                        emit_y(ci)
                        if ci % 2 == 1:
                            emit_T(couples[ci][0])
                    if dbg_d is not None and b == 0:
                        S.dma("pool", lambda e: e.dma_start(out=dbg_d[:, :, :], in_=suT[:]), reads=[t for tl in t_su for t in tl])
                    sgf = [sbt(es, f"sgf{i}", [128, 512], F32) for i in range(2)]
                    tgf = [sbt(es, f"tgf{i}", [128, 512], F32) for i in range(2)]
                    t_sgf, t_tgf = TL(2), TL(2)
                    gi_ = 0
                    for c in range(4):
                        for mt in range(4):
                            i = gi_ % 2
                            gi_ += 1
                            ba, bb = 4 + i, 4 + (1 - i)
                            bka = 4 + i
                            bkb = i
                            for ct in range(4):
                                S.op("pe", lambda e, ct=ct, mt=mt, c=c, bka=bka: e.matmul(psf[bka][:], lhsT=gluw[:, ct, mt * 128:(mt + 1) * 128], rhs=suT[:, ct, c * 512:(c + 1) * 512], start=(ct == 0), stop=(ct == 3)),
                                     reads=[t_gw] + [t_su[ct][c]], writes=[pb[bka]], inc=(ct == 3))
                            for ct in range(4):
                                S.op("pe", lambda e, ct=ct, mt=mt, c=c, bkb=bkb: e.matmul(psf[bkb][:], lhsT=gluw[:, ct, (4 + mt) * 128:(5 + mt) * 128], rhs=suT[:, ct, c * 512:(c + 1) * 512], start=(ct == 0), stop=(ct == 3)),
                                     reads=[t_gw] + [t_su[ct][c]], writes=[pb[bkb]], inc=(ct == 3))
                            S.op("act", lambda e, i=i, mt=mt, bkb=bkb: e.activation(out=sgf[i][:], in_=psf[bkb][:], func=AF.Sigmoid, bias=glub[:, 4 + mt:5 + mt]),
                                 reads=[pb[bkb], t_par], writes=[t_sgf[i]])
                            S.op("dve", lambda e, i=i, mt=mt, bka=bka: e.scalar_tensor_tensor(out=tgf[i][:], in0=psf[bka][:], scalar=glub[:, mt:mt + 1], in1=sgf[i][:], op0=ALU.add, op1=ALU.mult),
                                 reads=[pb[bka], t_sgf[i], t_par], writes=[t_tgf[i]])
                            S.op("pool", lambda e, i=i, mt=mt, c=c: e.tensor_tensor(out=gcT[:, mt, c * 512:(c + 1) * 512], in0=tgf[i][:], in1=gcT[:, mt, c * 512:(c + 1) * 512], op=ALU.mult),
                                 reads=[t_tgf[i], t_gc[mt][c]], writes=[t_gc[mt][c]])
                    S.flush()
                if os.environ.get("KSTOP") == "S5":
                    return

                with ExitStack() as es:
                    diag = [sbt(es, f"diag{i}", [128, 31, 128], BF16) for i in range(4)]
                    t_dg = TL(4)
                    for ct in range(4):
                        S.op("dve", lambda e, ct=ct: e.tensor_tensor(out=diag[ct][:], in0=identf[:, :].unsqueeze(1).broadcast_to([128, 31, 128]),
                                                                   in1=dwT[:, ct, :].unsqueeze(2).broadcast_to([128, 31, 128]), op=ALU.mult),
                             reads=[t_const, t_par], writes=[t_dg[ct]])
                    cf2 = None
                    c162 = [sbt(es, f"c16{i}", [128, 4, 512], BF16) for i in range(2)]
                    c22 = [sbt(es, f"c2{i}", [128, 4, 512], BF16) for i in range(2)]
                    sn2 = [sbt(es, f"sn{i}", [128, 4, 512], BF16) for i in range(2)]
                    t_cf2, t_c162, t_c22, t_sn2 = [TL(4), TL(4)], [TL(4), TL(4)], [TL(4), TL(4)], [TL(4), TL(4)]
                    mean2 = [sbt(es, "mean_sb0", [128, 512], F32)] * 2
                    m22 = [sbt(es, "m2_0", [128, 512], F32)] * 2
                    rstd2 = [sbt(es, "rstd0", [128, 512], F32)] * 2
                    t_mean2, t_m22, t_rstd2 = [T()] * 2, [T()] * 2, [T()] * 2
                    u1 = [sbt(es, f"u1_{i}", [128, 512], F32) for i in range(2)]
                    t_u1 = TL(2)
                    ui_box = [0]
                    def conv_part(c):
                            cf, c16, c2, sn = c162[c % 2], c162[c % 2], c22[c % 2], sn2[c % 2]
                            t_cf, t_c16, t_c2, t_sn = t_c162[c % 2], t_c162[c % 2], t_c22[c % 2], t_sn2[c % 2]
                            mean_sb, m2, rstd = mean2[c % 2], m22[c % 2], rstd2[c % 2]
                            t_mean, t_m2, t_rstd = t_mean2[c % 2], t_m22[c % 2], t_rstd2[c % 2]
                            for ct in range(4):
                                bk = ct % 2
                                for k in range(31):
                                    S.op("pe", lambda e, cf=cf, c16=c16, c2=c2, sn=sn, mean_sb=mean_sb, m2=m2, rstd=rstd, k=k, ct=ct, c=c, bk=bk: e.matmul(psf[bk][:], lhsT=diag[ct][:, k, :], rhs=gpad[:, ct, 2 + c * 512 + k:2 + c * 512 + k + 512],
                                                                                       start=(k == 0), stop=(k == 30)),
                                         reads=[t_dg[ct], t_gp[ct]], writes=[pb[bk]], inc=(k == 30))
                                pass
                                S.op("act", lambda e, cf=cf, c16=c16, c2=c2, sn=sn, mean_sb=mean_sb, m2=m2, rstd=rstd, ct=ct, bk=bk: e.activation(out=c16[:, ct, :], in_=psf[bk][:], func=AF.Copy), reads=[pb[bk]], writes=[t_c16[ct]])
                                S.op("act", lambda e, cf=cf, c16=c16, c2=c2, sn=sn, mean_sb=mean_sb, m2=m2, rstd=rstd, ct=ct, bk=bk: e.activation(out=c2[:, ct, :], in_=psf[bk][:], func=AF.Square), reads=[pb[bk]], writes=[t_c2[ct]])

                    def ln_part(c):
                            cf, c16, c2, sn = c162[c % 2], c162[c % 2], c22[c % 2], sn2[c % 2]
                            t_cf, t_c16, t_c2, t_sn = t_c162[c % 2], t_c162[c % 2], t_c22[c % 2], t_sn2[c % 2]
                            mean_sb, m2, rstd = mean2[c % 2], m22[c % 2], rstd2[c % 2]
                            t_mean, t_m2, t_rstd = t_mean2[c % 2], t_m22[c % 2], t_rstd2[c % 2]
                            for ct in range(4):
                                S.op("pe", lambda e, cf=cf, c16=c16, c2=c2, sn=sn, mean_sb=mean_sb, m2=m2, rstd=rstd, ct=ct: e.matmul(psf[2][:], lhsT=ones512[:], rhs=c16[:, ct, :], start=(ct == 0), stop=(ct == 3)),
                                     reads=[t_c16[ct], t_par], writes=[pb[2]], inc=(ct == 3))
                            for ct in range(4):
                                S.op("pe", lambda e, cf=cf, c16=c16, c2=c2, sn=sn, mean_sb=mean_sb, m2=m2, rstd=rstd, ct=ct: e.matmul(psf[3][:], lhsT=ones512[:], rhs=c2[:, ct, :], start=(ct == 0), stop=(ct == 3)),
                                     reads=[t_c2[ct], t_par], writes=[pb[3]], inc=(ct == 3))
                            S.op("act", lambda e, cf=cf, c16=c16, c2=c2, sn=sn, mean_sb=mean_sb, m2=m2, rstd=rstd: e.activation(out=mean_sb[:], in_=psf[2][:], func=AF.Copy), reads=[pb[2]], writes=[t_mean])
                            S.op("dve", lambda e, cf=cf, c16=c16, c2=c2, sn=sn, mean_sb=mean_sb, m2=m2, rstd=rstd: e.tensor_tensor(out=m2[:], in0=mean_sb[:], in1=mean_sb[:], op=ALU.mult), reads=[t_mean], writes=[t_m2])
                            S.op("dve", lambda e, cf=cf, c16=c16, c2=c2, sn=sn, mean_sb=mean_sb, m2=m2, rstd=rstd: e.tensor_tensor(out=m2[:], in0=psf[3][:], in1=m2[:], op=ALU.subtract), reads=[pb[3], t_m2], writes=[t_m2])
                            S.op("act", lambda e, cf=cf, c16=c16, c2=c2, sn=sn, mean_sb=mean_sb, m2=m2, rstd=rstd: e.activation(out=m2[:], in_=m2[:], func=AF.Ln, bias=EPS), reads=[t_m2], writes=[t_m2])
                            S.op("act", lambda e, cf=cf, c16=c16, c2=c2, sn=sn, mean_sb=mean_sb, m2=m2, rstd=rstd: e.activation(out=rstd[:], in_=m2[:], func=AF.Exp, scale=-0.5), reads=[t_m2], writes=[t_rstd])
                            for ct in range(4):
                                i = ui_box[0] % 2
                                ui_box[0] += 1
                                S.op("dve", lambda e, cf=cf, c16=c16, c2=c2, sn=sn, mean_sb=mean_sb, m2=m2, rstd=rstd, ct=ct, i=i: e.tensor_tensor(out=u1[i][:], in0=cf[:, ct, :], in1=mean_sb[:], op=ALU.subtract), reads=[t_cf[ct], t_mean], writes=[t_u1[i]])
                                S.op("dve", lambda e, cf=cf, c16=c16, c2=c2, sn=sn, mean_sb=mean_sb, m2=m2, rstd=rstd, i=i: e.tensor_tensor(out=u1[i][:], in0=u1[i][:], in1=rstd[:], op=ALU.mult), reads=[t_u1[i], t_rstd], writes=[t_u1[i]])
                                S.op("act", lambda e, cf=cf, c16=c16, c2=c2, sn=sn, mean_sb=mean_sb, m2=m2, rstd=rstd, ct=ct, i=i: e.activation(out=sn[:, ct, :], in_=u1[i][:], func=AF.Silu, scale=lng[:, ct:ct + 1], bias=lnb[:, ct:ct + 1]),
                                     reads=[t_u1[i], t_par], writes=[t_sn[ct]])

                    def pw_part(c):
                            cf, c16, c2, sn = c162[c % 2], c162[c % 2], c22[c % 2], sn2[c % 2]
                            t_cf, t_c16, t_c2, t_sn = t_c162[c % 2], t_c162[c % 2], t_c22[c % 2], t_sn2[c % 2]
                            mean_sb, m2, rstd = mean2[c % 2], m22[c % 2], rstd2[c % 2]
                            t_mean, t_m2, t_rstd = t_mean2[c % 2], t_m22[c % 2], t_rstd2[c % 2]
                            for mt in range(4):
                                bk = 4 + mt % 2
                                for ct in range(4):
                                    S.op("pe", lambda e, cf=cf, c16=c16, c2=c2, sn=sn, mean_sb=mean_sb, m2=m2, rstd=rstd, ct=ct, mt=mt, bk=bk: e.matmul(psf[bk][:], lhsT=pwsb[:, ct, mt * 128:(mt + 1) * 128], rhs=sn[:, ct, :], start=(ct == 0), stop=(ct == 3)),
                                         reads=[t_pw, t_sn[ct]], writes=[pb[bk]], inc=(ct == 3))
                                S.op("dve", lambda e, cf=cf, c16=c16, c2=c2, sn=sn, mean_sb=mean_sb, m2=m2, rstd=rstd, mt=mt, c=c, bk=bk: e.tensor_tensor(out=gdT[:, mt, c * 512:(c + 1) * 512], in0=psf[bk][:], in1=gdT[:, mt, c * 512:(c + 1) * 512], op=ALU.mult),
                                     reads=[pb[bk], t_gd[mt][c]], writes=[t_gd[mt][c]])

                    conv_part(0)
                    for c in range(4):
                        ln_part(c)
                        if c + 1 < 4:
                            conv_part(c + 1)
                        pw_part(c)
                    S.flush()
                if os.environ.get("KSTOP") == "CV":
                    return
                with ExitStack() as es:
                    phase_outproj(es, o_w_out, post_g[1:2, :], out_d, b, gcT, gdT, TL(4), TL(4), wo1, t_wo1)
                    S.flush()

        nbr = int(os.environ.get("KNB", NB))
        for b in range(nbr):
            layer0(b)
        if n_layers >= 2 and os.environ.get("KLAYERS", "2") == "2":
            layer1_all(nbr)

    return nc


def layer1_host_layouts(inputs, f):
    g = lambda k: np.asarray(inputs[k][0], dtype=np.float32)
    m = {}
    m["o_w_in"] = f(g("o_w_in"))

    def A_gn(a):
        a = np.repeat(a[:, None, :], 16, axis=1).reshape(4, 8, 16, 64)
        return f(a.transpose(1, 2, 0, 3).reshape(128, 256))
    m["o_lamre_A"] = A_gn(g("o_lam_re"))
    m["o_lamim_A"] = A_gn(g("o_lam_im"))
    m["o_dt_A"] = A_gn(np.repeat(g("o_log_dt")[:, None], 64, axis=1))

    def A_b(bm):
        a = bm.transpose(0, 2, 1).reshape(4, 8, 16, 64)
        return f(a.transpose(1, 2, 0, 3).reshape(128, 256))
    m["o_bre_A"] = A_b(g("o_b_re"))
    m["o_bim_A"] = A_b(g("o_b_im"))

    def B_gn(a):
        return f(a.reshape(16, 2, 64).transpose(1, 2, 0).reshape(128, 16))
    m["o_lamre_B"] = B_gn(g("o_lam_re"))
    m["o_lamim_B"] = B_gn(g("o_lam_im"))
    m["o_dt_B"] = B_gn(np.repeat(g("o_log_dt")[:, None], 64, axis=1))

    def B_c(cm):
        return f(cm.reshape(16, 2, 16, 64).transpose(1, 3, 0, 2).reshape(128, 256))
    def B_b(bm):
        return f(bm.reshape(16, 2, 64, 16).transpose(1, 2, 0, 3).reshape(128, 256))
    m["o_bre_B"] = B_b(g("o_b_re"))
    m["o_bim_B"] = B_b(g("o_b_im"))
    m["o_cre_B"] = B_c(g("o_c_re"))
    m["o_cim_B"] = B_c(g("o_c_im"))
    m["o_dA"] = f(g("o_d").reshape(4, 128).T)
    m["o_glu_w"] = f(g("o_glu_w"))
    m["o_glub"] = f(g("o_glu_b").reshape(8, 128).T)
    m["o_dwT"] = f(g("o_dw").reshape(31, 4, 128).transpose(2, 1, 0))
    m["o_lng"] = f(g("o_ln_g").reshape(4, 128).T)
    m["o_lnb"] = f(g("o_ln_b").reshape(4, 128).T)
    m["o_pw"] = f(g("o_pw"))
    m["o_w_out"] = f(g("o_w_out"))
    return m


def make_in_maps(inputs):
    n = 8
    x = np.ascontiguousarray(inputs["x"], dtype=np.float32)
    consts = host_consts()
    f = lambda a: np.ascontiguousarray(a, dtype=np.float32)
    l1maps = layer1_host_layouts(inputs, f)
    in_maps = []
    for c in range(n):
        m = {"x": x[NB * c:NB * (c + 1)]}
        for k in ("pre_norm_g", "post_norm_g"):
            m[k] = f(inputs[k])
        m["e_w_in"] = f(inputs["e_w_in"][0])
        m["e_pool_w"] = f(inputs["e_pool_w"][0])
        m["e_pool_scale"] = f(np.asarray(inputs["e_pool_scale"][0], dtype=np.float32).reshape(4, 128).T)
        m["e_w_out"] = f(inputs["e_w_out"][0])
        m.update(l1maps)
        for k, v in consts.items():
            m["c_" + k] = v
        in_maps.append(m)
    return in_maps


def kernel(**inputs):
    nc = build()
    in_maps = make_in_maps(inputs)
    res = run_bass_kernel_spmd(nc, in_maps, core_ids=list(range(8)))
    return np.concatenate([r["out"] for r in res.results], axis=0)
```

```python
import math
import os
from contextlib import ExitStack
import numpy as np
import concourse.bass as bass
import concourse.mybir as mybir
from concourse.bass_utils import run_bass_kernel_spmd

F32 = mybir.dt.float32
BF16 = mybir.dt.bfloat16
I32 = mybir.dt.int32
ALU = mybir.AluOpType
AF = mybir.ActivationFunctionType
AX = mybir.AxisListType

D = 1024
SEQ = 2048
NB = 2
HD = 128
EPS = 1e-6
NEGB = -30000.0
S5L = 8
NCH = SEQ // S5L


class T:
    __slots__ = ("w", "r")

    def __init__(self):
        self.w = None
        self.r = {}


def TL(n):
    return [T() for _ in range(n)]


class Sched:
    ENG = ("pe", "act", "dve", "pool", "sp")

    def __init__(self, nc, es, n_dma=24):
        self.nc = nc
        self.ops = {e: [] for e in self.ENG}
        self.cnt = {e: 0 for e in self.ENG}
        self.seen = {e: {} for e in self.ENG}
        self.n_dma = n_dma
        self.dma_cnt = [0] * n_dma
        self.dma_rr2 = {"sp": 0, "pool": 0}
        self.semh = {}
        for e in self.ENG:
            self.semh[("c", e)] = es.enter_context(nc.semaphore(f"s_{e}"))
        for i in range(n_dma):
            self.semh[("d", i)] = es.enter_context(nc.semaphore(f"s_d{i}"))

    def _deps(self, eng, reads, writes):
        need = {}
        for t in reads:
            if t.w is not None:
                k, v = t.w
                if need.get(k, 0) < v:
                    need[k] = v
        for t in writes:
            if t.w is not None:
                k, v = t.w
                if need.get(k, 0) < v:
                    need[k] = v
            for k, v in t.r.items():
                if need.get(k, 0) < v:
                    need[k] = v
        waits = []
        sn = self.seen[eng]
        for k, v in need.items():
            if eng == "pe" and k == ("c", "pe"):
                continue
            if sn.get(k, 0) < v:
                waits.append((k, v))
                sn[k] = v
        return waits

    def _record(self, key, v, reads, writes):
        for t in reads:
            if t.r.get(key, 0) < v:
                t.r[key] = v
        for t in writes:
            t.w = (key, v)
            t.r = {}

    def op(self, eng, fn, reads=(), writes=(), inc=True):
        waits = self._deps(eng, reads, writes)
        key = ("c", eng)
        if inc:
            self.cnt[eng] += 1
            v = self.cnt[eng]
        else:
            v = self.cnt[eng] + 1
        self.ops[eng].append([waits, fn, key, 1 if inc else 0])
        self._record(key, v, reads, writes)

    def dma(self, eng, fn, reads=(), writes=()):
        half = self.n_dma // 2
        base = 0 if eng == "sp" else half
        j = self.dma_rr2[eng]
        self.dma_rr2[eng] = (j + 1) % half
        i = base + j
        waits = self._deps(eng, reads, writes)
        key = ("d", i)
        prev = self.dma_cnt[i]
        if prev > 0 and self.seen[eng].get(key, 0) < prev:
            waits.append((key, prev))
            self.seen[eng][key] = prev
        self.dma_cnt[i] += 16
        self.ops[eng].append([waits, fn, key, 16])
        self._record(key, self.dma_cnt[i], reads, writes)

    def flush(self):
        nc = self.nc
        for e in ("pe", "act", "dve", "pool"):
            if self.ops[e] and self.ops[e][-1][3] == 0:
                self.ops[e][-1][3] = 1
                self.cnt[e] += 1
        fin = [(("d", i), self.dma_cnt[i]) for i in range(self.n_dma) if self.dma_cnt[i] > 0]
        fin += [(("c", e), self.cnt[e]) for e in ("pe", "act", "dve", "pool") if self.cnt[e] > 0]
        ops = self.ops
        semh = self.semh
        with nc.Block() as block:
            def run(engname, eng):
                for waits, fn, key, inc in ops[engname]:
                    for k, v in waits:
                        eng.wait_ge(semh[k], v)
                    ins = fn(eng)
                    if inc:
                        ins.then_inc(semh[key], inc)
                if engname == "sp":
                    for k, v in fin:
                        eng.wait_ge(semh[k], v)

            @block.tensor
            def _(e):
                run("pe", e)

            @block.scalar
            def _(e):
                run("act", e)

            @block.vector
            def _(e):
                run("dve", e)

            @block.gpsimd
            def _(e):
                run("pool", e)

            @block.sync
            def _(e):
                run("sp", e)
        self.ops = {e: [] for e in self.ENG}
        for e in self.ENG:
            for k, v in fin:
                self.seen[e][k] = v


def host_consts():
    c = {}
    c["ident"] = np.eye(128, dtype=np.float32)
    c["ones"] = np.ones((128, 128), np.float32)
    kk = np.arange(128)[:, None]
    qq = np.arange(128)[None, :]
    c["trineg"] = np.where(kk <= qq, 0.0, NEGB).astype(np.float32)
    e8 = np.zeros((8, 8, 128), np.float32)
    for n in range(8):
        e8[n, n, :] = 1.0
    c["e8"] = e8.reshape(8, 1024)
    p32 = np.zeros((128, 128), np.float32)
    for m in range(32):
        p32[(m + 16) % 32, m] = 1.0
    c["p32"] = p32
    inv = np.power(500000.0, -np.arange(0, 32, 2, dtype=np.float32) / 32.0).astype(np.float32)
    pos = np.arange(SEQ, dtype=np.float32)
    ang = (pos[None, :] * inv[:, None]).astype(np.float32)
    cs = np.cos(ang).astype(np.float32)
    sn = np.sin(ang).astype(np.float32)
    c["ropeC"] = np.concatenate([cs, cs, np.ones((96, SEQ), np.float32)], 0)
    c["ropeS"] = np.concatenate([-sn, sn, np.zeros((96, SEQ), np.float32)], 0)
    negm = np.zeros((128, 8, 8), np.float32)
    bfix = np.zeros((128, 8, 8), np.float32)
    for i in range(8):
        B = 4 + i // 2
        for n in range(8):
            negm[:, i, n] = 0.0 if n < B else -1e30
            bfix[:, i, n] = 0.0 if n == B else NEGB
    c["negmask"] = negm.reshape(128, 64)
    c["biasfix"] = bfix.reshape(128, 64)
    rc = np.zeros((128, 4, 16), np.float32)
    for g, w in enumerate((2, 4, 8, 16)):
        for t in range(16):
            rc[:, g, t] = 1.0 / min(t + 1, w)
    c["rcnt"] = rc.reshape(128, 64)
    pp = np.arange(128)
    c["maskA"] = np.stack([((pp // 16) % 2 == 0), ((pp // 16) % 2 == 1)], 1).astype(np.float32)
    c["maskB"] = np.stack([(pp // 64 == 0), (pp // 64 == 1)], 1).astype(np.float32)
    return c


CONST_SHAPES = {k: v.shape for k, v in host_consts().items()}


def build(n_layers=2):
    nc = bass.Bass("TRN2", target_bir_lowering=False)

    def dr(name, shape, kind="ExternalInput", dt=F32):
        return nc.dram_tensor(name, list(shape), dt, kind=kind).ap()

    x_d = dr("x", [NB, SEQ, D])
    out_d = dr("out", [NB, SEQ, D], "ExternalOutput")
    pre_g = dr("pre_norm_g", [2, D])
    post_g = dr("post_norm_g", [2, D])
    e_w_in = dr("e_w_in", [D, 3072])
    e_pool_w = dr("e_pool_w", [4, 128, 128])
    e_pool_scale = dr("e_pool_scale", [128, 4])
    e_w_out = dr("e_w_out", [D, D])
    o_w_in = dr("o_w_in", [D, 2560])
    o_lamre_A = dr("o_lamre_A", [128, 256])
    o_lamim_A = dr("o_lamim_A", [128, 256])
    o_dt_A = dr("o_dt_A", [128, 256])
    o_bre_A = dr("o_bre_A", [128, 256])
    o_bim_A = dr("o_bim_A", [128, 256])
    o_bre_B = dr("o_bre_B", [128, 256])
    o_bim_B = dr("o_bim_B", [128, 256])
    o_lamre_B = dr("o_lamre_B", [128, 16])
    o_lamim_B = dr("o_lamim_B", [128, 16])
    o_dt_B = dr("o_dt_B", [128, 16])
    o_cre_B = dr("o_cre_B", [128, 256])
    o_cim_B = dr("o_cim_B", [128, 256])
    o_dA = dr("o_dA", [128, 4])
    o_glu_w = dr("o_glu_w", [512, 1024])
    o_glub = dr("o_glub", [128, 8])
    o_dwT = dr("o_dwT", [128, 4, 31])
    o_lng = dr("o_lng", [128, 4])
    o_lnb = dr("o_lnb", [128, 4])
    o_pw = dr("o_pw", [512, 512])
    o_w_out = dr("o_w_out", [D, D])
    dbg_d = dr("dbg", [128, 4, SEQ], "ExternalOutput") if os.environ.get("KDBG") else None
    cd = {k: dr("c_" + k, list(s)) for k, s in CONST_SHAPES.items()}

    with ExitStack() as top:
        S = Sched(nc, top)
        uid = [0]

        def sbt(es, name, shape, dt):
            uid[0] += 1
            return es.enter_context(nc.sbuf_tensor(f"{name}_{uid[0]}", list(shape), dt))
        psf = [top.enter_context(nc.psum_tensor(f"psf{i}", [128, 512], F32)) for i in range(7)]
        psb = top.enter_context(nc.psum_tensor("psb", [128, 1024], BF16))
        pb = TL(8)

        identb = sbt(top, "identb", [128, 128], BF16)
        identf = sbt(top, "identf", [128, 128], F32)
        onesb = sbt(top, "onesb", [128, 128], BF16)
        t_const = T()
        S.dma("pool", lambda e: e.dma_start(out=identb[:], in_=cd["ident"][:, :]), writes=[t_const])
        S.dma("sp", lambda e: e.dma_start(out=identf[:], in_=cd["ident"][:, :]), writes=[t_const])
        S.dma("pool", lambda e: e.dma_start(out=onesb[:], in_=cd["ones"][:, :]), writes=[t_const])

        def rms_stats(es_tag, src_aps, st, col0, reads, t_st, junk):
            for i, ap in enumerate(src_aps):
                S.op("act", lambda e, ap=ap, i=i: e.activation(out=junk[:, 0:ap.shape[1]], in_=ap, func=AF.Square,
                                                             accum_out=st[:, col0 + i:col0 + i + 1]),
                     reads=reads, writes=[t_st])
            if len(src_aps) == 2:
                S.op("dve", lambda e: e.tensor_tensor(out=st[:, col0:col0 + 1], in0=st[:, col0:col0 + 1],
                                                      in1=st[:, col0 + 1:col0 + 2], op=ALU.add), reads=[t_st], writes=[t_st])
            S.op("act", lambda e: e.activation(out=st[:, col0 + 2:col0 + 3], in_=st[:, col0:col0 + 1], func=AF.Sqrt,
                                               scale=1.0 / D, bias=EPS), reads=[t_st], writes=[t_st])
            S.op("dve", lambda e: e.reciprocal(out=st[:, col0 + 3:col0 + 4], in_=st[:, col0 + 2:col0 + 3]),
                 reads=[t_st], writes=[t_st])

        def phase_norm_T(es, src_d, b, gvec_d, hT, t_hT, nxb=2):
            xb = [sbt(es, f"xb{i}", [128, D], F32) for i in range(nxb)]
            hb = [sbt(es, f"hb{i}", [128, D], BF16) for i in range(2)]
            junk = sbt(es, "junkA", [128, D], BF16)
            gt = sbt(es, "gtA", [128, D], F32)
            st = sbt(es, "stA", [128, 16 * 4], F32)
            t_xb, t_hb, t_g = TL(nxb), TL(2), T()
            t_st = TL(16)
            S.dma("sp", lambda e: e.dma_start(out=gt[:], in_=gvec_d.partition_broadcast(128)), writes=[t_g])
            def stats(tt):
                i = tt % 2
                xi = tt % nxb
                S.dma("sp", lambda e: e.dma_start(out=xb[xi][:], in_=src_d[b, tt * 128:(tt + 1) * 128, :]),
                      writes=[t_xb[xi]])
                rms_stats(es, [xb[xi][:]], st, tt * 4, [t_xb[xi]], t_st[tt], junk)
                S.op("dve", lambda e: e.scalar_tensor_tensor(out=hb[i][:], in0=xb[xi][:], scalar=st[:, tt * 4 + 3:tt * 4 + 4],
                                                             in1=gt[:], op0=ALU.mult, op1=ALU.mult),
                     reads=[t_xb[xi], t_st[tt], t_g], writes=[t_hb[i]])

            def trans(tt):
                i = tt % 2
                for k in range(8):
                    S.op("pe", lambda e, k=k: e.transpose(out=psb[:, k * 128:(k + 1) * 128], in_=hb[i][:, k * 128:(k + 1) * 128],
                                                          identity=identb[:]),
                         reads=[t_hb[i], t_const], writes=[pb[7]], inc=(k == 7))
                S.op("act", lambda e: e.activation(out=hT[:, :, tt * 128:(tt + 1) * 128],
                                                   in_=psb[:, :].rearrange("p (k t) -> p k t", k=8), func=AF.Copy),
                     reads=[pb[7]], writes=[t_hT[tt]])

            stats(0)
            for tt in range(16):
                if tt + 1 < 16:
                    stats(tt + 1)
                trans(tt)

        def load_wo(wo, t_wo, w_out_d):
            for hf in range(2):
                S.dma("pool", lambda e, hf=hf: e.dma_start(out=wo[:, :, hf * 512:(hf + 1) * 512],
                                                          in_=w_out_d[:, hf * 512:(hf + 1) * 512].rearrange("(k p) n -> p k n", p=128)),
                      writes=[t_wo])

        def phase_outproj(es, w_out_d, gvec_d, res_d, b, gA, gB, t_gA, t_gB, wo, t_wo):
            gt = sbt(es, "gtF", [128, D], F32)
            xr = [sbt(es, f"xr{i}", [128, D], F32) for i in range(2)]
            tm = [sbt(es, f"tmF{i}", [128, D], F32) for i in range(2)]
            junk = sbt(es, "junkF", [128, 512], BF16)
            st = sbt(es, "stF", [128, 16 * 4], F32)
            t_g = T()
            t_xr, t_tm, t_st = TL(2), TL(2), TL(16)
            S.dma("sp", lambda e: e.dma_start(out=gt[:], in_=gvec_d.partition_broadcast(128)), writes=[t_g])
            for tt in range(16):
                i = tt % 2
                bk = [psf[2 * i], psf[2 * i + 1]]
                tbk = [pb[2 * i], pb[2 * i + 1]]
                S.dma("sp", lambda e, tt=tt, i=i: e.dma_start(out=xr[i][:], in_=res_d[b, tt * 128:(tt + 1) * 128, :]),
                      writes=[t_xr[i]])
                for hf in range(2):
                    for k in range(8):
                        g_ap = (gA if k < 4 else gB)
                        S.op("pe", lambda e, hf=hf, k=k, g_ap=g_ap, tt=tt, bk=bk: e.matmul(
                            bk[hf][:], lhsT=g_ap[:, k % 4, tt * 128:(tt + 1) * 128], rhs=wo[:, k, hf * 512:(hf + 1) * 512],
                            start=(k == 0), stop=(k == 7)),
                            reads=[(t_gA if k < 4 else t_gB)[tt // 4], t_wo], writes=[tbk[hf]], inc=(k == 7))
                rms_stats(es, [bk[0][:], bk[1][:]], st, tt * 4, tbk, t_st[tt], junk)
                for hf in range(2):
                    S.op("dve", lambda e, hf=hf, tt=tt, i=i, bk=bk: e.scalar_tensor_tensor(
                        out=tm[i][:, hf * 512:(hf + 1) * 512], in0=bk[hf][:], scalar=st[:, tt * 4 + 3:tt * 4 + 4],
                        in1=gt[:, hf * 512:(hf + 1) * 512], op0=ALU.mult, op1=ALU.mult),
                        reads=[tbk[hf], t_st[tt], t_g], writes=[t_tm[i]])
                S.op("pool", lambda e, i=i: e.tensor_tensor(out=tm[i][:], in0=tm[i][:], in1=xr[i][:], op=ALU.add),
                     reads=[t_tm[i], t_xr[i]], writes=[t_tm[i]])
                S.dma("sp", lambda e, tt=tt, i=i: e.dma_start(out=out_d[b, tt * 128:(tt + 1) * 128, :], in_=tm[i][:]),
                      reads=[t_tm[i]])

        def layer0(b):
            with ExitStack() as L:
                qkT = sbt(L, "qkT", [128, 8, SEQ], BF16)
                Vt = sbt(L, "Vt", [128, 16, 512], BF16)
                gaT = sbt(L, "gaT", [128, 4, SEQ], BF16)
                gbT = sbt(L, "gbT", [128, 4, SEQ], BF16)
                t_qk = [TL(4) for _ in range(8)]
                t_V = TL(16)
                t_ga = [TL(4) for _ in range(4)]
                t_gb = [TL(4) for _ in range(4)]
                wo0 = sbt(L, "wo0", [128, 8, D], BF16)
                t_wo0 = T()

                with ExitStack() as es:
                    hT = sbt(es, "hT", [128, 8, SEQ], BF16)
                    t_hT = TL(16)
                    phase_norm_T(es, x_d, b, pre_g[0:1, :], hT, t_hT, nxb=4)
                    wb = [sbt(es, f"wb{i}", [128, 8, 512], BF16) for i in range(2)]
                    t_wb = TL(2)
                    ropeC = sbt(es, "ropeC", [128, SEQ], F32)
                    ropeS = sbt(es, "ropeS", [128, SEQ], F32)
                    p32 = sbt(es, "p32", [128, 128], BF16)
                    poolw = sbt(es, "poolw", [128, 4, 128], BF16)
                    pscale = sbt(es, "pscale", [128, 4], F32)
                    rcnt = sbt(es, "rcnt", [128, 64], F32)
                    t_c2 = T()
                    S.dma("sp", lambda e: e.dma_start(out=ropeC[:], in_=cd["ropeC"][:, :]), writes=[t_c2])
                    S.dma("sp", lambda e: e.dma_start(out=ropeS[:], in_=cd["ropeS"][:, :]), writes=[t_c2])
                    S.dma("pool", lambda e: e.dma_start(out=p32[:], in_=cd["p32"][:, :]), writes=[t_c2])
                    S.dma("pool", lambda e: e.dma_start(out=poolw[:], in_=e_pool_w.rearrange("g c d -> c g d")), writes=[t_c2])
                    S.dma("sp", lambda e: e.dma_start(out=pscale[:], in_=e_pool_scale[:, :]), writes=[t_c2])
                    S.dma("sp", lambda e: e.dma_start(out=rcnt[:], in_=cd["rcnt"][:, :]), writes=[t_c2])
                    r1 = [sbt(es, f"r1_{i}", [128, 512], F32) for i in range(2)]
                    r2 = [sbt(es, f"r2_{i}", [128, 512], F32) for i in range(2)]
                    t_r1, t_r2 = TL(2), TL(2)
                    ub = [sbt(es, f"ub{i}", [128, 528], F32) for i in range(2)]
                    sa = sbt(es, "sa", [128, 528], F32)
                    sb_ = sbt(es, "sb", [128, 528], F32)
                    mt = [sbt(es, f"mt{i}", [128, 512], BF16) for i in range(2)]
                    t_ub, t_mt = TL(2), TL(2)
                    t_sa, t_sb = T(), T()

                    order = [0, 1, 2, 3, 5, 4]
                    cnt = [0]
                    for oi in range(2):
                        S.dma("pool", lambda e, oi=oi: e.dma_start(out=wb[oi][:], in_=e_w_in[:, order[oi] * 512:(order[oi] + 1) * 512].rearrange("(k p) n -> p k n", p=128)),
                              writes=[t_wb[oi]])
                    ri = [0]
                    ksub = os.environ.get("KSUB", "")
                    for oi, gi in enumerate(order):
                        if ksub and oi >= int(ksub):
                            break
                        wbi = wb[oi % 2]
                        t_wbi = t_wb[oi % 2]
                        if gi in (0, 1):
                            pend = [None]

                            def rope_tail(j, c, r, pbk):
                                def f():
                                    S.op("pe", lambda e: e.matmul(psf[pbk][:], lhsT=p32[:], rhs=qkT[:, j, c * 512:(c + 1) * 512], start=True, stop=True),
                                         reads=[t_qk[j][c], t_c2], writes=[pb[pbk]])
                                    S.op("dve", lambda e: e.tensor_tensor(out=r2[r][:], in0=psf[pbk][:], in1=ropeS[:, c * 512:(c + 1) * 512], op=ALU.mult),
                                         reads=[pb[pbk], t_c2], writes=[t_r2[r]])
                                    S.op("dve", lambda e: e.tensor_tensor(out=qkT[:, j, c * 512:(c + 1) * 512], in0=r1[r][:], in1=r2[r][:], op=ALU.add),
                                         reads=[t_r1[r], t_r2[r]], writes=[t_qk[j][c]])
                                return f
                            for m in range(4):
                                j = gi * 4 + m
                                for c in range(4):
                                    bi = cnt[0] % 3
                                    cnt[0] += 1
                                    for k in range(8):
                                        S.op("pe", lambda e, k=k, bi=bi, m=m, c=c, wbi=wbi: e.matmul(
                                            psf[bi][:], lhsT=wbi[:, k, m * 128:(m + 1) * 128], rhs=hT[:, k, c * 512:(c + 1) * 512],
                                            start=(k == 0), stop=(k == 7)),
                                            reads=[t_wbi] + t_hT[4 * c:4 * c + 4], writes=[pb[bi]], inc=(k == 7))
                                    if pend[0] is not None:
                                        pend[0]()
                                    S.op("act", lambda e, bi=bi, j=j, c=c: e.activation(out=qkT[:, j, c * 512:(c + 1) * 512], in_=psf[bi][:], func=AF.Copy),
                                         reads=[pb[bi]], writes=[t_qk[j][c]])
                                    r = ri[0] % 2
                                    ri[0] += 1
                                    pbk = 3 + r
                                    S.op("dve", lambda e, bi=bi, c=c, r=r: e.tensor_tensor(out=r1[r][:], in0=psf[bi][:], in1=ropeC[:, c * 512:(c + 1) * 512], op=ALU.mult),
                                         reads=[pb[bi], t_c2, t_qk[j][c]], writes=[t_r1[r]])
                                    pend[0] = rope_tail(j, c, r, pbk)
                            pend[0]()
                        elif gi == 2:
                            for tt in range(16):
                                bi = cnt[0] % 3
                                cnt[0] += 1
                                for k in range(8):
                                    S.op("pe", lambda e, k=k, bi=bi, tt=tt, wbi=wbi: e.matmul(
                                        psf[bi][:], lhsT=hT[:, k, tt * 128:(tt + 1) * 128], rhs=wbi[:, k, :], start=(k == 0), stop=(k == 7)),
                                        reads=[t_wbi, t_hT[tt]], writes=[pb[bi]], inc=(k == 7))
                                S.op("dve", lambda e, bi=bi, tt=tt: e.tensor_copy(out=Vt[:, tt, :], in_=psf[bi][:]), reads=[pb[bi]], writes=[t_V[tt]])
                        elif gi in (3, 5):
                            dst, t_dst = (gaT, t_ga) if gi == 3 else (gbT, t_gb)
                            for m in range(4):
                                for c in range(4):
                                    bi = cnt[0] % 3
                                    cnt[0] += 1
                                    for k in range(8):
                                        S.op("pe", lambda e, k=k, bi=bi, m=m, c=c, wbi=wbi: e.matmul(
                                            psf[bi][:], lhsT=wbi[:, k, m * 128:(m + 1) * 128], rhs=hT[:, k, c * 512:(c + 1) * 512],
                                            start=(k == 0), stop=(k == 7)),
                                            reads=[t_wbi] + t_hT[4 * c:4 * c + 4], writes=[pb[bi]], inc=(k == 7))
                                    S.op("act", lambda e, bi=bi, m=m, c=c, dst=dst: e.activation(out=dst[:, m, c * 512:(c + 1) * 512], in_=psf[bi][:], func=AF.Silu),
                                         reads=[pb[bi]], writes=[t_dst[m][c]])
                        else:
                            ppend = [None]
                            for g in range(4):
                                w = (2, 4, 8, 16)[g]
                                nlev = g + 1
                                for c in range(4):
                                    bi = cnt[0] % 3
                                    cnt[0] += 1
                                    for k in range(8):
                                        S.op("pe", lambda e, k=k, bi=bi, g=g, c=c, wbi=wbi: e.matmul(
                                            psf[bi][:], lhsT=wbi[:, k, g * 128:(g + 1) * 128], rhs=hT[:, k, c * 512:(c + 1) * 512],
                                            start=(k == 0), stop=(k == 7)),
                                            reads=[t_wbi] + t_hT[4 * c:4 * c + 4], writes=[pb[bi]], inc=(k == 7))
                                    if ppend[0] is not None:
                                        ppend[0]()
                                        ppend[0] = None
                                    u = ub[c % 2]
                                    up = ub[(c + 1) % 2]
                                    if c == 0:
                                        S.op("pool", lambda e, u=u: e.memset(u[:, 0:16], 0.0), writes=[t_ub[c % 2]])
                                    else:
                                        S.op("pool", lambda e, u=u, up=up: e.tensor_copy(out=u[:, 0:16], in_=up[:, 512:528]),
                                             reads=[t_ub[(c + 1) % 2]], writes=[t_ub[c % 2]])
                                    S.op("act", lambda e, bi=bi, u=u: e.activation(out=u[:, 16:528], in_=psf[bi][:], func=AF.Copy),
                                         reads=[pb[bi]], writes=[t_ub[c % 2]])
                                    src, t_src = u, t_ub[c % 2]
                                    for lv in range(nlev):
                                        sh = 1 << lv
                                        lo = 2 * sh
                                        dstb, t_d = (sa, t_sa) if lv % 2 == 0 else (sb_, t_sb)
                                        S.op("dve", lambda e, src=src, dstb=dstb, lo=lo, sh=sh: e.tensor_tensor(
                                            out=dstb[:, lo:528], in0=src[:, lo:528], in1=src[:, lo - sh:528 - sh], op=ALU.add),
                                            reads=[t_src], writes=[t_d])
                                        src, t_src = dstb, t_d
                                    mi = c % 2
                                    S.op("dve", lambda e, src=src, u=u, mi=mi, w=w: e.scalar_tensor_tensor(
                                        out=mt[mi][:], in0=src[:, 16:528], scalar=1.0 / w, in1=u[:, 16:528], op0=ALU.mult, op1=ALU.subtract),
                                        reads=[t_src, t_ub[c % 2]], writes=[t_mt[mi]])
                                    if c == 0:
                                        S.op("dve", lambda e, src=src, g=g: e.tensor_tensor(out=src[:, 0:16], in0=src[:, 16:32], in1=rcnt[:, g * 16:(g + 1) * 16], op=ALU.mult),
                                             reads=[t_src, t_c2], writes=[t_src])
                                        S.op("dve", lambda e, src=src, u=u, mi=mi: e.tensor_tensor(out=mt[mi][:, 0:16], in0=src[:, 0:16], in1=u[:, 16:32], op=ALU.subtract),
                                             reads=[t_src, t_ub[c % 2]], writes=[t_mt[mi]])
                                    pbk = 3 + (c % 2)

                                    def pool_tail(g=g, c=c, mi=mi, pbk=pbk):
                                        S.op("pe", lambda e: e.matmul(psf[pbk][:], lhsT=poolw[:, g, :], rhs=mt[mi][:], start=True, stop=True),
                                             reads=[t_mt[mi], t_c2], writes=[pb[pbk]])
                                        S.op("dve", lambda e: e.scalar_tensor_tensor(
                                            out=gbT[:, g, c * 512:(c + 1) * 512], in0=psf[pbk][:], scalar=pscale[:, g:g + 1],
                                            in1=gbT[:, g, c * 512:(c + 1) * 512], op0=ALU.mult, op1=ALU.mult),
                                            reads=[pb[pbk], t_c2, t_gb[g][c]], writes=[t_gb[g][c]])
                                    ppend[0] = pool_tail
                        if gi == 4 and ppend[0] is not None:
                            ppend[0]()
                            ppend[0] = None
                        if oi + 2 < len(order) and not ksub:
                            gn = order[oi + 2]
                            S.dma("pool", lambda e, gn=gn, wbi=wbi: e.dma_start(out=wbi[:], in_=e_w_in[:, gn * 512:(gn + 1) * 512].rearrange("(k p) n -> p k n", p=128)),
                                  writes=[t_wbi])
                    S.flush()
                if os.environ.get("KSTOP") == "AB":
                    return

                with ExitStack() as es:
                    load_wo(wo0, t_wo0, e_w_out)
                    e8 = sbt(es, "e8", [8, 1024], BF16)
                    trineg = sbt(es, "trineg", [128, 128], BF16)
                    negmask = sbt(es, "negmask", [128, 64], F32)
                    biasfix = sbt(es, "biasfix", [128, 64], F32)
                    t_c3 = T()
                    S.dma("pool", lambda e: e.dma_start(out=e8[:], in_=cd["e8"][:, :]), writes=[t_c3])
                    S.dma("pool", lambda e: e.dma_start(out=trineg[:], in_=cd["trineg"][:, :]), writes=[t_c3])
                    S.dma("sp", lambda e: e.dma_start(out=negmask[:], in_=cd["negmask"][:, :]), writes=[t_c3])
                    S.dma("sp", lambda e: e.dma_start(out=biasfix[:], in_=cd["biasfix"][:, :]), writes=[t_c3])
                    Mrow = sbt(es, "Mrow", [8, 4, SEQ], BF16)
                    t_M = TL(4)
                    stab4 = [sbt(es, f"stab{h}", [8, SEQ], F32) for h in range(4)]
                    sqt = [sbt(es, f"sqt{i}", [128, 512], BF16) for i in range(8)]
                    t_sq = TL(8)
                    kmx4 = [sbt(es, f"kmx{h}", [8, 8], F32) for h in range(4)]
                    kb324 = [sbt(es, f"kb32{h}", [128, 8], F32) for h in range(4)]
                    kbar4 = [sbt(es, f"kbar{h}", [128, 8], BF16) for h in range(4)]
                    gm4 = [sbt(es, f"gm{h}", [128, 64], F32) for h in range(4)]
                    top84 = [sbt(es, f"top8{h}", [128, 64], F32) for h in range(4)]
                    sel4 = [sbt(es, f"sel{h}", [128, 64], F32) for h in range(4)]
                    t_stab4, t_kmx4, t_kb4, t_gm4, t_top4, t_sel4 = TL(4), TL(4), TL(4), TL(4), TL(4), TL(4)

                    def prep_ops(h):
                        L_ = []
                        add = lambda eng, fn, reads=(), writes=(), **kw: L_.append((eng, fn, list(reads), list(writes), kw))
                        stab, kmx, kb32, kbar, gm, top8, sel = stab4[h], kmx4[h], kb324[h], kbar4[h], gm4[h], top84[h], sel4[h]
                        t_stab, t_kmx, t_kb, t_gm, t_top, t_sel = t_stab4[h], t_kmx4[h], t_kb4[h], t_gm4[h], t_top4[h], t_sel4[h]
                        nb = 3 + h
                        tb = [h % 3, (h + 1) % 3]
                        for c in range(4):
                            i = 2 * h
                            add("act", lambda e, i=i, c=c: e.activation(out=sqt[i][:], in_=qkT[:, 4 + h, c * 512:(c + 1) * 512], func=AF.Square),
                                [t_qk[4 + h][c]], [t_sq[i]])
                            add("pe", lambda e, i=i: e.matmul(psf[nb][0:8, :], lhsT=onesb[:, 0:8], rhs=sqt[i][:], start=True, stop=True),
                                [t_sq[i], t_const], [pb[nb]])
                            add("dve", lambda e, c=c: e.tensor_reduce(out=kmx[:, c:c + 1], in_=psf[nb][0:8, :], axis=AX.X, op=ALU.max),
                                [pb[nb]], [t_kmx])
                        add("dve", lambda e: e.tensor_reduce(out=kmx[:, 4:5], in_=kmx[:, 0:4], axis=AX.X, op=ALU.max), [t_kmx], [t_kmx])
                        for c in range(4):
                            i = 2 * h + 1
                            add("act", lambda e, i=i, c=c: e.activation(out=sqt[i][:], in_=qkT[:, h, c * 512:(c + 1) * 512], func=AF.Square),
                                [t_qk[h][c]], [t_sq[i]])
                            add("pe", lambda e, i=i: e.matmul(psf[nb][0:8, :], lhsT=onesb[:, 0:8], rhs=sqt[i][:], start=True, stop=True),
                                [t_sq[i], t_const], [pb[nb]])
                            add("act", lambda e, c=c: e.activation(out=stab[:, c * 512:(c + 1) * 512], in_=psf[nb][0:8, :], func=AF.Sqrt, scale=kmx[:, 4:5]),
                                [pb[nb], t_kmx], [t_stab])
                        add("dve", lambda e: e.tensor_reduce(out=kb32[:], in_=qkT[:, 4 + h, :].rearrange("p (n s) -> p n s", s=256), axis=AX.X, op=ALU.add),
                            t_qk[4 + h], [t_kb])
                        add("dve", lambda e: e.tensor_scalar(out=kbar[:], in0=kb32[:], scalar1=1.0 / 256, scalar2=None, op0=ALU.mult), [t_kb], [t_kb])
                        for i8 in range(8):
                            add("pe", lambda e, i8=i8: e.matmul(psf[nb][:, i8 * 8:(i8 + 1) * 8], lhsT=qkT[:, h, (8 + i8) * 128:(9 + i8) * 128], rhs=kbar[:], start=True, stop=True),
                                [t_qk[h][2 + i8 // 4], t_kb], [pb[nb]], inc=(i8 == 7))
                        add("dve", lambda e: e.tensor_tensor(out=gm[:], in0=psf[nb][:, 0:64], in1=negmask[:], op=ALU.add), [pb[nb], t_c3], [t_gm])
                        for i8 in range(8):
                            add("dve", lambda e, i8=i8: e.max(out=top8[:, i8 * 8:(i8 + 1) * 8], in_=gm[:, i8 * 8:(i8 + 1) * 8]), [t_gm], [t_top])
                        for i8 in range(8):
                            add("dve", lambda e, i8=i8: e.tensor_scalar(out=sel[:, i8 * 8:(i8 + 1) * 8], in0=gm[:, i8 * 8:(i8 + 1) * 8],
                                                                      scalar1=top8[:, i8 * 8 + 2:i8 * 8 + 3], scalar2=None, op0=ALU.is_ge),
                                [t_gm, t_top], [t_sel])
                        add("dve", lambda e: e.scalar_tensor_tensor(out=sel[:], in0=sel[:], scalar=-NEGB, in1=biasfix[:], op0=ALU.mult, op1=ALU.add),
                            [t_sel, t_c3], [t_sel])
                        add("act", lambda e: e.activation(out=Mrow[:, h, 0:1024], in_=stab[:, 0:1024], func=AF.Copy, scale=-1.0), [t_stab], [t_M[h]])
                        for hf in range(2):
                            for i4 in range(4):
                                i8 = hf * 4 + i4
                                add("pe", lambda e, i8=i8, i4=i4: e.transpose(out=psf[nb][0:8, i4 * 128:(i4 + 1) * 128], in_=sel[:, i8 * 8:(i8 + 1) * 8], identity=identf[:]),
                                    [t_sel, t_const], [pb[nb]], inc=(i4 == 3))
                            add("dve", lambda e, hf=hf: e.tensor_tensor(out=Mrow[:, h, 1024 + hf * 512:1536 + hf * 512], in0=psf[nb][0:8, :],
                                                                      in1=stab[:, 1024 + hf * 512:1536 + hf * 512], op=ALU.subtract),
                                [pb[nb], t_stab], [t_M[h]])
                        return L_

                    plists = [prep_ops(h) for h in range(4)]
                    for i in range(max(len(l) for l in plists)):
                        for l in plists:
                            if i < len(l):
                                eng, fn, rd, wr, kw = l[i]
                                S.op(eng, fn, reads=rd, writes=wr, **kw)

                    PT = [sbt(es, f"PT{i}", [128, 512], BF16) for i in range(3)]
                    t_PT = TL(3)
                    lns = sbt(es, "lns", [128, 512], F32)
                    rinv = sbt(es, "rinv", [128, 512], F32)
                    ot = sbt(es, "ot", [128, 512], F32)
                    t_lns, t_rinv, t_ot = T(), T(), T()
                    scale = 1.0 / math.sqrt(HD)
                    items = [(h, qc, kt) for h in range(4) for qc in range(4) for kt in range(4 * qc + 4)]

                    def emit_S(idx):
                        h, qc, kt = items[idx]
                        sb_i = idx % 2
                        off = max(0, kt * 128 - qc * 512)
                        q0 = qc * 512 + off
                        q1 = (qc + 1) * 512
                        n = kt // 2
                        diag = kt >= 4 * qc
                        S.op("pe", lambda e: e.matmul(psf[sb_i][:, off:512], lhsT=qkT[:, 4 + h, kt * 128:(kt + 1) * 128], rhs=qkT[:, h, q0:q1], start=True, stop=False),
                             reads=[t_qk[4 + h][kt // 4], t_qk[h][qc]], writes=[pb[sb_i]])
                        S.op("pe", lambda e: e.matmul(psf[sb_i][:, off:512], lhsT=e8[:, n * 128:(n + 1) * 128], rhs=Mrow[:, h, q0:q1], start=False, stop=(not diag)),
                             reads=[t_M[h], t_c3], writes=[pb[sb_i]])
                        if diag:
                            S.op("pe", lambda e: e.matmul(psf[sb_i][:, off:off + 128], lhsT=identb[:], rhs=trineg[:], start=False, stop=True),
                                 reads=[t_c3, t_const], writes=[pb[sb_i]])
                        pi = idx % 3
                        S.op("act", lambda e: e.activation(out=PT[pi][:, off:512], in_=psf[sb_i][:, off:512], func=AF.Exp, scale=scale),
                             reads=[pb[sb_i]], writes=[t_PT[pi]])

                    def emit_PV(idx):
                        h, qc, kt = items[idx]
                        off = max(0, kt * 128 - qc * 512)
                        pi = idx % 3
                        par = (h * 4 + qc) % 2
                        ob, sbk = 2 + par, 4 + par
                        last = (kt == 4 * qc + 3)
                        S.op("pe", lambda e: e.matmul(psf[ob][:, off:512], lhsT=Vt[:, kt, h * 128:(h + 1) * 128], rhs=PT[pi][:, off:512], start=(kt == 0), stop=last),
                             reads=[t_V[kt], t_PT[pi]], writes=[pb[ob]])
                        S.op("pe", lambda e: e.matmul(psf[sbk][:, off:512], lhsT=onesb[:], rhs=PT[pi][:, off:512], start=(kt == 0), stop=last),
                             reads=[t_const, t_PT[pi]], writes=[pb[sbk]])
                        if last:
                            S.op("act", lambda e: e.activation(out=lns[:], in_=psf[sbk][:], func=AF.Ln), reads=[pb[sbk]], writes=[t_lns])
                            S.op("act", lambda e: e.activation(out=rinv[:], in_=lns[:], func=AF.Exp, scale=-1.0), reads=[t_lns], writes=[t_rinv])
                            S.op("dve", lambda e: e.tensor_tensor(out=ot[:], in0=psf[ob][:], in1=rinv[:], op=ALU.mult), reads=[pb[ob], t_rinv], writes=[t_ot])
                            S.op("pool", lambda e: e.tensor_tensor(out=gaT[:, h, qc * 512:(qc + 1) * 512], in0=ot[:], in1=gaT[:, h, qc * 512:(qc + 1) * 512], op=ALU.mult),
                                 reads=[t_ot, t_ga[h][qc]], writes=[t_ga[h][qc]])

                    emit_S(0)
                    for idx in range(len(items)):
                        if idx + 1 < len(items):
                            emit_S(idx + 1)
                        emit_PV(idx)
                    S.flush()
                if os.environ.get("KSTOP") == "CD":
                    return

                with ExitStack() as es:
                    t_gA = [T() for _ in range(4)]
                    t_gB = [T() for _ in range(4)]
                    phase_outproj(es, e_w_out, post_g[0:1, :], x_d, b, gaT, gbT, t_gA, t_gB, wo0, t_wo0)
                    S.flush()

        L1 = top

        def layer1_all(nb):
            with ExitStack() as P1:
                wv_sb = sbt(P1, "wv_sb", [128, 4, 8, 2, 128], BF16)
                toep_sb = sbt(P1, "toep_sb", [128, 4, 8, 256], BF16)
                w3_sb = sbt(P1, "w3_sb", [128, 16, 2, 256], BF16)
                pw_tab = sbt(P1, "pw_tab", [128, 16, 8, 3], F32)
                dA = sbt(P1, "dA", [128, 4], F32)
                glub = sbt(P1, "glub", [128, 8], F32)
                lng = sbt(P1, "lng", [128, 4], F32)
                lnb = sbt(P1, "lnb", [128, 4], F32)
                dwT = sbt(P1, "dwT", [128, 4, 31], F32)
                ones512 = sbt(P1, "ones512", [128, 128], BF16)
                t_par = T()
                for dst, src in ((dA, o_dA), (glub, o_glub), (lng, o_lng), (lnb, o_lnb)):
                    S.dma("sp", lambda e, dst=dst, src=src: e.dma_start(out=dst[:], in_=src[:, :]), writes=[t_par])
                S.dma("sp", lambda e: e.dma_start(out=dwT[:], in_=o_dwT[:, :, :]), writes=[t_par])
                S.op("act", lambda e: e.activation(out=ones512[:], in_=onesb[:], func=AF.Copy, scale=1.0 / 512), reads=[t_const], writes=[t_par])

                with ExitStack() as es:
                    tkA, tkB, t_m = T(), T(), T()
                    listA, listB = [], []
                    cur = {"eng": "dve", "tk": tkA, "list": listA}

                    def vop(fn, eng=None):
                        cur["list"].append(("op", eng or cur["eng"], fn, [cur["tk"], t_par, t_m], [cur["tk"]], {}))

                    def pop(fn, eng=None):
                        cur["list"].append(("op", eng or cur["eng"], fn, [cur["tk"], t_par, t_m], [T()], {}))

                    def tt(o, a, b_, op, eng=None):
                        vop(lambda e: e.tensor_tensor(out=o, in0=a, in1=b_, op=op), eng)

                    def ld(name, src, shape, tok=None):
                        t = sbt(es, name, shape, F32)
                        if tok is not None:
                            S.dma("sp", lambda e: e.dma_start(out=t[:], in_=src), writes=[tok])
                        else:
                            cur["list"].append(("dma", "sp", lambda e: e.dma_start(out=t[:], in_=src), [], [cur["tk"]], {}))
                        return t

                    def compute_a(tag, lr, li, dtl, n):
                        mk = lambda nm: sbt(es, f"{tag}_{nm}", [128, n], F32)
                        dtv, x1, mg, th, u, r, sn_, cs_, ar, ai = [mk(k) for k in ("dt", "x1", "mg", "th", "u", "r", "sn", "cs", "ar", "ai")]
                        ui = sbt(es, f"{tag}_ui", [128, n], I32)
                        vop(lambda e: e.activation(out=dtv[:], in_=dtl[:], func=AF.Exp), "act")
                        tt(x1[:], lr[:], dtv[:], ALU.mult)
                        vop(lambda e: e.activation(out=mg[:], in_=x1[:], func=AF.Exp), "act")
                        tt(th[:], li[:], dtv[:], ALU.mult)
                        for shift, dst in ((0.0, sn_), (math.pi / 2, cs_)):
                            vop(lambda e, shift=shift: e.tensor_scalar(out=u[:], in0=th[:], scalar1=shift, scalar2=1.0 / (2 * math.pi), op0=ALU.add, op1=ALU.mult))
                            vop(lambda e: e.tensor_copy(out=ui[:], in_=u[:]), "dve")
                            vop(lambda e: e.tensor_copy(out=u[:], in_=ui[:]), "dve")
                            vop(lambda e: e.tensor_scalar(out=u[:], in0=u[:], scalar1=-2 * math.pi, scalar2=None, op0=ALU.mult))
                            tt(r[:], u[:], th[:], ALU.add)
                            vop(lambda e, shift=shift: e.tensor_scalar(out=r[:], in0=r[:], scalar1=shift, scalar2=None, op0=ALU.add))
                            vop(lambda e, dst=dst: e.activation(out=dst[:], in_=r[:], func=AF.Sin), "act")
                        tt(ar[:], mg[:], cs_[:], ALU.mult)
                        tt(ai[:], mg[:], sn_[:], ALU.mult)
                        return ar, ai

                    lrA = ld("lrA", o_lamre_A[:, :], [128, 256])
                    liA = ld("liA", o_lamim_A[:, :], [128, 256])
                    dtA = ld("dtA", o_dt_A[:, :], [128, 256])
                    brA = ld("brA", o_bre_A[:, :], [128, 256])
                    biA = ld("biA", o_bim_A[:, :], [128, 256])
                    mA = ld("mA", cd["maskA"][:, :], [128, 2], t_m)
                    mB = ld("mB", cd["maskB"][:, :], [128, 2], t_m)
                    arA, aiA = compute_a("A", lrA, liA, dtA, 256)
                    mk = lambda nm, n=256: sbt(es, nm, [128, n], F32)
                    nr, den, t1, t2, fr, fi = [mk(k) for k in ("nr", "den", "t1", "t2", "fr", "fi")]
                    vop(lambda e: e.tensor_scalar(out=nr[:], in0=arA[:], scalar1=-1.0, scalar2=None, op0=ALU.add))
                    tt(t1[:], lrA[:], lrA[:], ALU.mult)
                    tt(t2[:], liA[:], liA[:], ALU.mult)
                    tt(den[:], t1[:], t2[:], ALU.add)
                    vop(lambda e: e.reciprocal(out=den[:], in_=den[:]), "dve")
                    tt(t1[:], nr[:], lrA[:], ALU.mult)
                    tt(t2[:], aiA[:], liA[:], ALU.mult)
                    tt(t1[:], t1[:], t2[:], ALU.add)
                    tt(fr[:], t1[:], den[:], ALU.mult)
                    tt(t1[:], aiA[:], lrA[:], ALU.mult)
                    tt(t2[:], nr[:], liA[:], ALU.mult)
                    tt(t1[:], t1[:], t2[:], ALU.subtract)
                    tt(fi[:], t1[:], den[:], ALU.mult)
                    Gall = sbt(es, "Gall", [128, 8, 2, 256], F32)
                    tt(t1[:], fr[:], brA[:], ALU.mult)
                    tt(t2[:], fi[:], biA[:], ALU.mult)
                    tt(Gall[:, 0, 0, :], t1[:], t2[:], ALU.subtract)
                    tt(t1[:], fr[:], biA[:], ALU.mult)
                    tt(t2[:], fi[:], brA[:], ALU.mult)
                    tt(Gall[:, 0, 1, :], t1[:], t2[:], ALU.add)
                    for m in range(7):
                        tt(t1[:], Gall[:, m, 0, :], arA[:], ALU.mult)
                        tt(t2[:], Gall[:, m, 1, :], aiA[:], ALU.mult)
                        tt(Gall[:, m + 1, 0, :], t1[:], t2[:], ALU.subtract)
                        tt(t1[:], Gall[:, m, 0, :], aiA[:], ALU.mult)
                        tt(t2[:], Gall[:, m, 1, :], arA[:], ALU.mult)
                        tt(Gall[:, m + 1, 1, :], t1[:], t2[:], ALU.add)
                    for s in range(8):
                        for ri in range(2):
                            for gi in range(2):
                                pop(lambda e, s=s, ri=ri, gi=gi: e.tensor_scalar(
                                    out=wv_sb[:, :, s, ri, gi * 64:(gi + 1) * 64],
                                    in0=Gall[:, 7 - s, ri, :].rearrange("p (c n) -> p c n", c=4),
                                    scalar1=mA[:, gi:gi + 1], scalar2=None, op0=ALU.mult))
                    Kall = sbt(es, "Kall", [128, 4, 8, 16], F32)
                    cur["eng"], cur["tk"], cur["list"] = "dve", tkB, listB
                    lrB = ld("lrB", o_lamre_B[:, :], [128, 16])
                    liB = ld("liB", o_lamim_B[:, :], [128, 16])
                    dtB = ld("dtB", o_dt_B[:, :], [128, 16])
                    crB = ld("crB", o_cre_B[:, :], [128, 256])
                    ciB = ld("ciB", o_cim_B[:, :], [128, 256])
                    arB, aiB = compute_a("B", lrB, liB, dtB, 16)
                    PB = sbt(es, "PB", [128, 8, 2, 16], F32)
                    s1 = sbt(es, "s1", [128, 16], F32)
                    s2 = sbt(es, "s2", [128, 16], F32)
                    vop(lambda e: e.tensor_copy(out=PB[:, 0, 0, :], in_=arB[:]))
                    vop(lambda e: e.tensor_copy(out=PB[:, 0, 1, :], in_=aiB[:]))
                    for r in range(7):
                        tt(s1[:], PB[:, r, 0, :], arB[:], ALU.mult)
                        tt(s2[:], PB[:, r, 1, :], aiB[:], ALU.mult)
                        tt(PB[:, r + 1, 0, :], s1[:], s2[:], ALU.subtract)
                        tt(s1[:], PB[:, r, 0, :], aiB[:], ALU.mult)
                        tt(s2[:], PB[:, r, 1, :], arB[:], ALU.mult)
                        tt(PB[:, r + 1, 1, :], s1[:], s2[:], ALU.add)
                    brB = ld("brB", o_bre_B[:, :], [128, 256])
                    biB = ld("biB", o_bim_B[:, :], [128, 256])
                    mkb = lambda nm: sbt(es, nm, [128, 16], F32)
                    nrB, denB, x1B, x2B, frB, fiB = [mkb(k) for k in ("nrB", "denB", "x1B", "x2B", "frB", "fiB")]
                    vop(lambda e: e.tensor_scalar(out=nrB[:], in0=arB[:], scalar1=-1.0, scalar2=None, op0=ALU.add))
                    tt(x1B[:], lrB[:], lrB[:], ALU.mult)
                    tt(x2B[:], liB[:], liB[:], ALU.mult)
                    tt(denB[:], x1B[:], x2B[:], ALU.add)
                    vop(lambda e: e.reciprocal(out=denB[:], in_=denB[:]), "dve")
                    tt(x1B[:], nrB[:], lrB[:], ALU.mult)
                    tt(x2B[:], aiB[:], liB[:], ALU.mult)
                    tt(x1B[:], x1B[:], x2B[:], ALU.add)
                    tt(frB[:], x1B[:], denB[:], ALU.mult)
                    tt(x1B[:], aiB[:], lrB[:], ALU.mult)
                    tt(x2B[:], nrB[:], liB[:], ALU.mult)
                    tt(x1B[:], x1B[:], x2B[:], ALU.subtract)
                    tt(fiB[:], x1B[:], denB[:], ALU.mult)
                    w1 = sbt(es, "w1", [128, 256], F32)
                    w2 = sbt(es, "w2", [128, 256], F32)
                    bbr = sbt(es, "bbrB", [128, 256], F32)
                    bbi = sbt(es, "bbiB", [128, 256], F32)
                    v3 = lambda t: t[:, :].rearrange("p (a i) -> p a i", a=16)
                    bc = lambda t: t[:, :].unsqueeze(2).broadcast_to([128, 16, 16])
                    tt(v3(w1), v3(brB), bc(frB), ALU.mult)
                    tt(v3(w2), v3(biB), bc(fiB), ALU.mult)
                    tt(bbr[:], w1[:], w2[:], ALU.subtract)
                    tt(v3(w1), v3(biB), bc(frB), ALU.mult)
                    tt(v3(w2), v3(brB), bc(fiB), ALU.mult)
                    tt(bbi[:], w1[:], w2[:], ALU.add)
                    Bmr = sbt(es, "Bmr", [128, 16, 128], F32)
                    Bmi = sbt(es, "Bmi", [128, 16, 128], F32)
                    vop(lambda e: e.memset(Bmr[:], 0.0), "pool")
                    vop(lambda e: e.memset(Bmi[:], 0.0), "pool")
                    for q in range(4):
                        for gi in range(2):
                            c0 = 32 * q + 16 * gi
                            vop(lambda e, q=q, gi=gi, c0=c0: e.tensor_scalar(
                                out=Bmr[:, :, :].rearrange("p (c q) m -> p c q m", q=4)[:, :, q, c0:c0 + 16],
                                in0=bbr[:, :].rearrange("p (c q j) -> p c q j", q=4, j=16)[:, :, q, :],
                                scalar1=mB[:, gi:gi + 1], scalar2=None, op0=ALU.mult))
                            vop(lambda e, q=q, gi=gi, c0=c0: e.tensor_scalar(
                                out=Bmi[:, :, :].rearrange("p (c q) m -> p c q m", q=4)[:, :, q, c0:c0 + 16],
                                in0=bbi[:, :].rearrange("p (c q j) -> p c q j", q=4, j=16)[:, :, q, :],
                                scalar1=mB[:, gi:gi + 1], scalar2=-1.0, op0=ALU.mult, op1=ALU.mult))
                    CAr = sbt(es, "CAr", [128, 9, 16, 16], F32)
                    CAi = sbt(es, "CAi", [128, 9, 16, 16], F32)
                    vop(lambda e: e.tensor_copy(out=CAr[:, 0, :, :], in_=v3(crB)))
                    vop(lambda e: e.tensor_copy(out=CAi[:, 0, :, :], in_=v3(ciB)))
                    for r in range(8):
                        pre = PB[:, r, 0, :].unsqueeze(2).broadcast_to([128, 16, 16])
                        pim = PB[:, r, 1, :].unsqueeze(2).broadcast_to([128, 16, 16])
                        tt(v3(w1), v3(crB), pre, ALU.mult)
                        tt(v3(w2), v3(ciB), pim, ALU.mult)
                        tt(CAr[:, r + 1, :, :], v3(w1), v3(w2), ALU.subtract)
                        tt(v3(w1), v3(crB), pim, ALU.mult)
                        tt(v3(w2), v3(ciB), pre, ALU.mult)
                        tt(CAi[:, r + 1, :, :], v3(w1), v3(w2), ALU.add)
                        for gi in range(2):
                            pop(lambda e, r=r, gi=gi: e.tensor_scalar(out=w3_sb[:, :, 0, gi * 128 + r * 16:gi * 128 + r * 16 + 16], in0=CAr[:, r + 1, :, :],
                                                                    scalar1=mB[:, gi:gi + 1], scalar2=None, op0=ALU.mult))
                            pop(lambda e, r=r, gi=gi: e.tensor_scalar(out=w3_sb[:, :, 1, gi * 128 + r * 16:gi * 128 + r * 16 + 16], in0=CAi[:, r + 1, :, :],
                                                                    scalar1=mB[:, gi:gi + 1], scalar2=-1.0, op0=ALU.mult, op1=ALU.mult))
                    for ct in range(4):
                        for q in range(4):
                            p = 4 * ct + q
                            listB.append(("op", "pe", lambda e, p=p, q=q: e.matmul(psf[0][:, 0:128], lhsT=Bmr[:, p, :], rhs=CAr[:, 0:8, p, :], start=(q == 0), stop=False),
                                          [tkB], [pb[0]], {"inc": False}))
                            listB.append(("op", "pe", lambda e, p=p, q=q: e.matmul(psf[0][:, 0:128], lhsT=Bmi[:, p, :], rhs=CAi[:, 0:8, p, :], start=False, stop=(q == 3)),
                                          [tkB], [pb[0]], {"inc": (q == 3)}))
                        listB.append(("op", "dve", lambda e, ct=ct: e.tensor_copy(out=Kall[:, ct, :, :], in_=psf[0][:, 0:128].rearrange("p (t i) -> p t i", t=8)),
                                      [pb[0], tkB], [tkB], {}))
                    vop(lambda e: e.memset(toep_sb[:], 0.0), "pool")
                    for s in range(8):
                        for gi in range(2):
                            vop(lambda e, s=s, gi=gi: e.tensor_scalar(
                                out=toep_sb[:, :, s, gi * 128 + s * 16:gi * 128 + 128],
                                in0=Kall[:, :, 0:8 - s, :].rearrange("p c t i -> p c (t i)"),
                                scalar1=mA[:, gi:gi + 1], scalar2=None, op0=ALU.mult))
                    qr = sbt(es, "qr", [128, 16], F32)
                    qi = sbt(es, "qi", [128, 16], F32)
                    vop(lambda e: e.tensor_copy(out=qr[:], in_=PB[:, 7, 0, :]))
                    vop(lambda e: e.tensor_copy(out=qi[:], in_=PB[:, 7, 1, :]))
                    for m in range(8):
                        pop(lambda e, m=m: e.tensor_copy(out=pw_tab[:, :, m, 0], in_=qr[:]))
                        pop(lambda e, m=m: e.tensor_copy(out=pw_tab[:, :, m, 1], in_=qi[:]))
                        pop(lambda e, m=m: e.tensor_scalar(out=pw_tab[:, :, m, 2], in0=qi[:], scalar1=-1.0, scalar2=None, op0=ALU.mult))
                        if m < 7:
                            tt(s1[:], qr[:], qr[:], ALU.mult)
                            tt(s2[:], qi[:], qi[:], ALU.mult)
                            tt(s2[:], s1[:], s2[:], ALU.subtract)
                            tt(s1[:], qr[:], qi[:], ALU.mult)
                            vop(lambda e: e.tensor_scalar(out=qi[:], in0=s1[:], scalar1=2.0, scalar2=None, op0=ALU.mult))
                            vop(lambda e: e.tensor_copy(out=qr[:], in_=s2[:]))
                    ia = ib = 0

                    def emit_rec(rec):
                        kind, eng, fn, rd, wr, kw = rec
                        if kind == "dma":
                            S.dma(eng, fn, reads=rd, writes=wr)
                        else:
                            S.op(eng, fn, reads=rd, writes=wr, **kw)
                    while ia < len(listA) or ib < len(listB):
                        if ia < len(listA):
                            emit_rec(listA[ia])
                            ia += 1
                        for _ in range(2):
                            if ib < len(listB):
                                emit_rec(listB[ib])
                                ib += 1
                    S.flush()

                for b in range(nb):
                    layer1(b, wv_sb, toep_sb, w3_sb, pw_tab, dA, glub, lng, lnb, dwT, ones512, t_par)

        def layer1(b, wv_sb, toep_sb, w3_sb, pw_tab, dA, glub, lng, lnb, dwT, ones512, t_par):
            with ExitStack() as L:
                suT = sbt(L, "suT", [128, 4, SEQ], BF16)
                gcT = sbt(L, "gcT", [128, 4, SEQ], BF16)
                gdT = sbt(L, "gdT", [128, 4, SEQ], BF16)
                gpad = sbt(L, "gpad", [128, 4, SEQ + 32], BF16)
                t_su = [TL(4) for _ in range(4)]
                t_gc = [TL(4) for _ in range(4)]
                t_gd = [TL(4) for _ in range(4)]
                t_gp = TL(4)
                wo1 = sbt(L, "wo1", [128, 8, D], BF16)
                t_wo1 = T()
                pwsb = sbt(L, "pwsb", [128, 4, 512], BF16)
                t_pw = T()
                with ExitStack() as es:
                    hT = sbt(es, "hT1", [128, 8, SEQ], BF16)
                    t_hT = TL(16)
                    phase_norm_T(es, out_d, b, pre_g[1:2, :], hT, t_hT, nxb=2)
                    wb = [sbt(es, f"wb1_{i}", [128, 8, 512], BF16) for i in range(2)]
                    t_wb = TL(2)
                    sg = [sbt(es, f"sg{i}", [128, 512], BF16) for i in range(2)]
                    t_sg = TL(2)
                    order = [0, 1, 4, 2, 3]

                    def loadw(oi):
                        gi = order[oi]
                        S.dma("pool", lambda e: e.dma_start(out=wb[oi % 2][:], in_=o_w_in[:, gi * 512:(gi + 1) * 512].rearrange("(k p) n -> p k n", p=128)),
                              writes=[t_wb[oi % 2]])
                    loadw(0)
                    loadw(1)
                    S.op("pool", lambda e: e.memset(gpad[:, :, 0:32], 0.0), writes=t_gp)
                    cnt = [0]

                    def proj(wbi, t_wbi, m, c):
                        bi = cnt[0] % 3
                        cnt[0] += 1
                        for k in range(8):
                            S.op("pe", lambda e, k=k: e.matmul(psf[bi][:], lhsT=wbi[:, k, m * 128:(m + 1) * 128], rhs=hT[:, k, c * 512:(c + 1) * 512],
                                                               start=(k == 0), stop=(k == 7)),
                                 reads=[t_wbi] + t_hT[4 * c:4 * c + 4], writes=[pb[bi]], inc=(k == 7))
                        return bi
                    for oi in range(3):
                        gi = order[oi]
                        dst, t_dst, fn = ((suT, t_su, AF.Copy), (gcT, t_gc, AF.Silu), None, None, (gdT, t_gd, AF.Silu))[gi]
                        for m in range(4):
                            for c in range(4):
                                bi = proj(wb[oi % 2], t_wb[oi % 2], m, c)
                                S.op("act", lambda e, bi=bi, m=m, c=c, dst=dst, fn=fn: e.activation(out=dst[:, m, c * 512:(c + 1) * 512], in_=psf[bi][:], func=fn),
                                     reads=[pb[bi]], writes=[t_dst[m][c]])
                        if oi + 2 < 5:
                            loadw(oi + 2)
                    si = 0
                    for m in range(4):
                        for c in range(4):
                            bi = proj(wb[0], t_wb[0], m, c)
                            i = si % 2
                            si += 1
                            S.op("act", lambda e, bi=bi, i=i: e.activation(out=sg[i][:], in_=psf[bi][:], func=AF.Sigmoid), reads=[pb[bi]], writes=[t_sg[i]])
                            bi2 = proj(wb[1], t_wb[1], m, c)
                            S.op("dve", lambda e, bi2=bi2, i=i, m=m, c=c: e.tensor_tensor(out=gpad[:, m, 32 + c * 512:32 + (c + 1) * 512], in0=psf[bi2][:], in1=sg[i][:], op=ALU.mult),
                                 reads=[pb[bi2], t_sg[i]], writes=[t_gp[m]])
                    S.flush()
                if os.environ.get("KSTOP") == "AB1":
                    return

                with ExitStack() as es:
                    Sb = [[[sbt(es, f"S{sl}{pi}{pp}", [128, 2, 512], F32) for pp in range(2)] for pi in range(2)] for sl in range(2)]
                    t_S = [[[T() for pp in range(2)] for pi in range(2)] for sl in range(2)]
                    Sp = [[[sbt(es, f"Sp{sl}{pi}{ri}", [128, 256], BF16) for ri in range(2)] for pi in range(2)] for sl in range(2)]
                    t_Sp = [[T() for pi in range(2)] for sl in range(2)]
                    for sl in range(2):
                        for pi in range(2):
                            for ri in range(2):
                                S.op("pool", lambda e, sl=sl, pi=pi, ri=ri: e.memset(Sp[sl][pi][ri][:, 0:1], 0.0), writes=[t_Sp[sl][pi]])
                                for pp in range(2):
                                    S.op("pool", lambda e, sl=sl, pi=pi, ri=ri, pp=pp: e.memset(Sb[sl][pi][pp][:, ri, 0:256], 0.0), writes=[t_S[sl][pi][pp]])
                    ysb = [sbt(es, f"ysb{i}", [128, 2, 8, 128], BF16) for i in range(2)]
                    t_ysb = [T(), T()]
                    gluw = sbt(es, "gluw", [128, 4, 1024], BF16)
                    t_gw = T()
                    for hf in range(2):
                        S.dma("pool", lambda e, hf=hf: e.dma_start(out=gluw[:, :, hf * 512:(hf + 1) * 512], in_=o_glu_w[:, hf * 512:(hf + 1) * 512].rearrange("(k p) n -> p k n", p=128)),
                              writes=[t_gw])
                    load_wo(wo1, t_wo1, o_w_out)
                    S.dma("pool", lambda e: e.dma_start(out=pwsb[:], in_=o_pw.rearrange("(k p) n -> p k n", p=128)), writes=[t_pw])
                    couples = [(ct, q0) for ct in range(4) for q0 in (0, 2)]

                    def emit_V(ci):
                        ct, q0 = couples[ci]
                        sl = ci % 2
                        for pi in range(2):
                            q = q0 + pi
                            rows = slice(32 * q, 32 * q + 32)
                            tp = (32 * q, 0)
                            for ri in range(2):
                                bk = (0, 1, 4, 5)[pi * 2 + ri]
                                for s in range(8):
                                    S.op("pe", lambda e, s=s, ri=ri, bk=bk, rows=rows, ct=ct, tp=tp: e.matmul(
                                        psf[bk][:, 0:256], lhsT=wv_sb[rows, ct, s, ri, :],
                                        rhs=suT[rows, ct, :].rearrange("p (k s) -> p s k", s=8)[:, s, :], start=(s == 0), stop=(s == 7), tile_position=tp),
                                        reads=t_su[ct] + [t_par], writes=[pb[bk]], inc=(s == 7))
                                S.op("act", lambda e, ri=ri, bk=bk, sl=sl, pi=pi: e.activation(out=Sb[sl][pi][0][:, ri, 256:512], in_=psf[bk][:, 0:256], func=AF.Copy),
                                     reads=[pb[bk]], writes=[t_S[sl][pi][0]])

                    def emit_scan(ci):
                        ct, q0 = couples[ci]
                        sl = ci % 2
                        for m in range(8):
                            sh = 1 << m
                            a, d_ = m % 2, (m + 1) % 2
                            for stage in range(3):
                                for pi in range(2):
                                    p = ct * 4 + q0 + pi
                                    src, dst = Sb[sl][pi][a], Sb[sl][pi][d_]
                                    ts, td = t_S[sl][pi][a], t_S[sl][pi][d_]
                                    pr = pw_tab[:, p, m, 0:1]
                                    pim = pw_tab[:, p, m, 1:2]
                                    npi = pw_tab[:, p, m, 2:3]
                                    if stage == 0:
                                        S.op("dve", lambda e, src=src, dst=dst, sh=sh, pr=pr: e.scalar_tensor_tensor(out=dst[:, :, 256:512], in0=src[:, :, 256 - sh:512 - sh], scalar=pr, in1=src[:, :, 256:512], op0=ALU.mult, op1=ALU.add),
                                             reads=[ts, t_par], writes=[td])
                                    elif stage == 1:
                                        S.op("dve", lambda e, src=src, dst=dst, sh=sh, npi=npi: e.scalar_tensor_tensor(out=dst[:, 0, 256:512], in0=src[:, 1, 256 - sh:512 - sh], scalar=npi, in1=dst[:, 0, 256:512], op0=ALU.mult, op1=ALU.add),
                                             reads=[ts, td, t_par], writes=[td])
                                    else:
                                        S.op("dve", lambda e, src=src, dst=dst, sh=sh, pim=pim: e.scalar_tensor_tensor(out=dst[:, 1, 256:512], in0=src[:, 0, 256 - sh:512 - sh], scalar=pim, in1=dst[:, 1, 256:512], op0=ALU.mult, op1=ALU.add),
                                             reads=[ts, td, t_par], writes=[td])

                    def emit_y(ci):
                        ct, q0 = couples[ci]
                        sl = ci % 2
                        yb_ = ysb[ct % 2]
                        t_y = t_ysb[ct % 2]
                        for pi in range(2):
                            q = q0 + pi
                            p = ct * 4 + q
                            rows = slice(32 * q, 32 * q + 32)
                            tp = (32 * q, 0)
                            for ri in range(2):
                                S.op("act", lambda e, ri=ri, sl=sl, pi=pi: e.activation(out=Sp[sl][pi][ri][:, 1:256], in_=Sb[sl][pi][0][:, ri, 256:511], func=AF.Copy),
                                     reads=[t_S[sl][pi][0]], writes=[t_Sp[sl][pi]])
                            for kt2 in range(2):
                                bk = 2 + kt2
                                for s in range(8):
                                    S.op("pe", lambda e, s=s, kt2=kt2, bk=bk, rows=rows, ct=ct, tp=tp: e.matmul(
                                        psf[bk][:, 0:256],
                                        lhsT=suT[rows, ct, kt2 * 1024:(kt2 + 1) * 1024].rearrange("p (k s) -> p s k", s=8)[:, s, :],
                                        rhs=toep_sb[rows, ct, s, :], start=(s == 0), stop=False, tile_position=tp),
                                        reads=t_su[ct] + [t_par], writes=[pb[bk]], inc=False)
                                for ri in range(2):
                                    S.op("pe", lambda e, ri=ri, kt2=kt2, bk=bk, sl=sl, pi=pi, p=p: e.matmul(
                                        psf[bk][:, 0:256], lhsT=Sp[sl][pi][ri][:, kt2 * 128:(kt2 + 1) * 128], rhs=w3_sb[:, p, ri, :], start=False, stop=(ri == 1)),
                                        reads=[t_Sp[sl][pi], t_par], writes=[pb[bk]], inc=(ri == 1))
                                S.op("act", lambda e, kt2=kt2, bk=bk, q=q, yb_=yb_: e.activation(
                                    out=yb_[:, kt2, :, q * 32:(q + 1) * 32].rearrange("p r (g i) -> p g r i", g=2),
                                    in_=psf[bk][:, 0:256].rearrange("p (g r i) -> p g r i", g=2, r=8), func=AF.Copy),
                                    reads=[pb[bk]], writes=[t_y])

                    def emit_T(ct):
                        yb_ = ysb[ct % 2]
                        t_y = t_ysb[ct % 2]
                        for kt2 in range(2):
                            for r in range(8):
                                S.op("pe", lambda e, kt2=kt2, r=r, yb_=yb_: e.transpose(out=psb[:, r * 128:(r + 1) * 128], in_=yb_[:, kt2, r, :], identity=identb[:]),
                                     reads=[t_y, t_const], writes=[pb[7]], inc=(r == 7))
                            S.op("dve", lambda e, kt2=kt2, ct=ct: e.scalar_tensor_tensor(
                                out=suT[:, ct, kt2 * 1024:(kt2 + 1) * 1024].rearrange("p (k r) -> p r k", r=8),
                                in0=suT[:, ct, kt2 * 1024:(kt2 + 1) * 1024].rearrange("p (k r) -> p r k", r=8),
                                scalar=dA[:, ct:ct + 1],
                                in1=psb[:, :].rearrange("p (r k) -> p r k", r=8), op0=ALU.mult, op1=ALU.add),
                                reads=[pb[7], t_par] + t_su[ct], writes=t_su[ct])

                    emit_V(0)
                    for ci in range(8):
                        if ci + 1 < 8:
                            emit_V(ci + 1)
                        emit_scan(ci)
                        emit_y(ci)
                        if ci % 2 == 1:
                            emit_T(couples[ci][0])
                    if dbg_d is not None and b == 0:
                        S.dma("pool", lambda e: e.dma_start(out=dbg_d[:, :, :], in_=suT[:]), reads=[t for tl in t_su for t in tl])
                    sgf = [sbt(es, f"sgf{i}", [128, 512], F32) for i in range(2)]
                    tgf = [sbt(es, f"tgf{i}", [128, 512], F32) for i in range(2)]
                    t_sgf, t_tgf = TL(2), TL(2)
                    gi_ = 0
                    for c in range(4):
                        for mt in range(4):
                            i = gi_ % 2
                            gi_ += 1
                            ba, bb = 4 + i, 4 + (1 - i)
                            bka = 4 + i
                            bkb = i
                            for ct in range(4):
                                S.op("pe", lambda e, ct=ct, mt=mt, c=c, bka=bka: e.matmul(psf[bka][:], lhsT=gluw[:, ct, mt * 128:(mt + 1) * 128], rhs=suT[:, ct, c * 512:(c + 1) * 512], start=(ct == 0), stop=(ct == 3)),
                                     reads=[t_gw] + [t_su[ct][c]], writes=[pb[bka]], inc=(ct == 3))
                            for ct in range(4):
                                S.op("pe", lambda e, ct=ct, mt=mt, c=c, bkb=bkb: e.matmul(psf[bkb][:], lhsT=gluw[:, ct, (4 + mt) * 128:(5 + mt) * 128], rhs=suT[:, ct, c * 512:(c + 1) * 512], start=(ct == 0), stop=(ct == 3)),
                                     reads=[t_gw] + [t_su[ct][c]], writes=[pb[bkb]], inc=(ct == 3))
                            S.op("act", lambda e, i=i, mt=mt, bkb=bkb: e.activation(out=sgf[i][:], in_=psf[bkb][:], func=AF.Sigmoid, bias=glub[:, 4 + mt:5 + mt]),
                                 reads=[pb[bkb], t_par], writes=[t_sgf[i]])
                            S.op("dve", lambda e, i=i, mt=mt, bka=bka: e.scalar_tensor_tensor(out=tgf[i][:], in0=psf[bka][:], scalar=glub[:, mt:mt + 1], in1=sgf[i][:], op0=ALU.add, op1=ALU.mult),
                                 reads=[pb[bka], t_sgf[i], t_par], writes=[t_tgf[i]])
                            S.op("pool", lambda e, i=i, mt=mt, c=c: e.tensor_tensor(out=gcT[:, mt, c * 512:(c + 1) * 512], in0=tgf[i][:], in1=gcT[:, mt, c * 512:(c + 1) * 512], op=ALU.mult),
                                 reads=[t_tgf[i], t_gc[mt][c]], writes=[t_gc[mt][c]])
                    S.flush()
                if os.environ.get("KSTOP") == "S5":
                    return

                with ExitStack() as es:
                    diag = [sbt(es, f"diag{i}", [128, 31, 128], BF16) for i in range(4)]
                    t_dg = TL(4)
                    for ct in range(4):
                        S.op("dve", lambda e, ct=ct: e.tensor_tensor(out=diag[ct][:], in0=identf[:, :].unsqueeze(1).broadcast_to([128, 31, 128]),
                                                                   in1=dwT[:, ct, :].unsqueeze(2).broadcast_to([128, 31, 128]), op=ALU.mult),
                             reads=[t_const, t_par], writes=[t_dg[ct]])
                    cf2 = None
                    c162 = [sbt(es, f"c16{i}", [128, 4, 512], BF16) for i in range(2)]
                    c22 = [sbt(es, f"c2{i}", [128, 4, 512], BF16) for i in range(2)]
                    sn2 = [sbt(es, f"sn{i}", [128, 4, 512], BF16) for i in range(2)]
                    t_cf2, t_c162, t_c22, t_sn2 = [TL(4), TL(4)], [TL(4), TL(4)], [TL(4), TL(4)], [TL(4), TL(4)]
                    mean2 = [sbt(es, "mean_sb0", [128, 512], F32)] * 2
                    m22 = [sbt(es, "m2_0", [128, 512], F32)] * 2
                    rstd2 = [sbt(es, "rstd0", [128, 512], F32)] * 2
                    t_mean2, t_m22, t_rstd2 = [T()] * 2, [T()] * 2, [T()] * 2
                    u1 = [sbt(es, f"u1_{i}", [128, 512], F32) for i in range(2)]
                    t_u1 = TL(2)
                    ui_box = [0]
                    def conv_part(c):
                            cf, c16, c2, sn = c162[c % 2], c162[c % 2], c22[c % 2], sn2[c % 2]
                            t_cf, t_c16, t_c2, t_sn = t_c162[c % 2], t_c162[c % 2], t_c22[c % 2], t_sn2[c % 2]
                            mean_sb, m2, rstd = mean2[c % 2], m22[c % 2], rstd2[c % 2]
                            t_mean, t_m2, t_rstd = t_mean2[c % 2], t_m22[c % 2], t_rstd2[c % 2]
                            for ct in range(4):
                                bk = ct % 2
                                for k in range(31):
                                    S.op("pe", lambda e, cf=cf, c16=c16, c2=c2, sn=sn, mean_sb=mean_sb, m2=m2, rstd=rstd, k=k, ct=ct, c=c, bk=bk: e.matmul(psf[bk][:], lhsT=diag[ct][:, k, :], rhs=gpad[:, ct, 2 + c * 512 + k:2 + c * 512 + k + 512],
                                                                                       start=(k == 0), stop=(k == 30)),
                                         reads=[t_dg[ct], t_gp[ct]], writes=[pb[bk]], inc=(k == 30))
                                pass
                                S.op("act", lambda e, cf=cf, c16=c16, c2=c2, sn=sn, mean_sb=mean_sb, m2=m2, rstd=rstd, ct=ct, bk=bk: e.activation(out=c16[:, ct, :], in_=psf[bk][:], func=AF.Copy), reads=[pb[bk]], writes=[t_c16[ct]])
                                S.op("act", lambda e, cf=cf, c16=c16, c2=c2, sn=sn, mean_sb=mean_sb, m2=m2, rstd=rstd, ct=ct, bk=bk: e.activation(out=c2[:, ct, :], in_=psf[bk][:], func=AF.Square), reads=[pb[bk]], writes=[t_c2[ct]])

                    def ln_part(c):
                            cf, c16, c2, sn = c162[c % 2], c162[c % 2], c22[c % 2], sn2[c % 2]
                            t_cf, t_c16, t_c2, t_sn = t_c162[c % 2], t_c162[c % 2], t_c22[c % 2], t_sn2[c % 2]
                            mean_sb, m2, rstd = mean2[c % 2], m22[c % 2], rstd2[c % 2]
                            t_mean, t_m2, t_rstd = t_mean2[c % 2], t_m22[c % 2], t_rstd2[c % 2]
                            for ct in range(4):
                                S.op("pe", lambda e, cf=cf, c16=c16, c2=c2, sn=sn, mean_sb=mean_sb, m2=m2, rstd=rstd, ct=ct: e.matmul(psf[2][:], lhsT=ones512[:], rhs=c16[:, ct, :], start=(ct == 0), stop=(ct == 3)),
                                     reads=[t_c16[ct], t_par], writes=[pb[2]], inc=(ct == 3))
                            for ct in range(4):
                                S.op("pe", lambda e, cf=cf, c16=c16, c2=c2, sn=sn, mean_sb=mean_sb, m2=m2, rstd=rstd, ct=ct: e.matmul(psf[3][:], lhsT=ones512[:], rhs=c2[:, ct, :], start=(ct == 0), stop=(ct == 3)),
                                     reads=[t_c2[ct], t_par], writes=[pb[3]], inc=(ct == 3))
                            S.op("act", lambda e, cf=cf, c16=c16, c2=c2, sn=sn, mean_sb=mean_sb, m2=m2, rstd=rstd: e.activation(out=mean_sb[:], in_=psf[2][:], func=AF.Copy), reads=[pb[2]], writes=[t_mean])
                            S.op("dve", lambda e, cf=cf, c16=c16, c2=c2, sn=sn, mean_sb=mean_sb, m2=m2, rstd=rstd: e.tensor_tensor(out=m2[:], in0=mean_sb[:], in1=mean_sb[:], op=ALU.mult), reads=[t_mean], writes=[t_m2])
                            S.op("dve", lambda e, cf=cf, c16=c16, c2=c2, sn=sn, mean_sb=mean_sb, m2=m2, rstd=rstd: e.tensor_tensor(out=m2[:], in0=psf[3][:], in1=m2[:], op=ALU.subtract), reads=[pb[3], t_m2], writes=[t_m2])
                            S.op("act", lambda e, cf=cf, c16=c16, c2=c2, sn=sn, mean_sb=mean_sb, m2=m2, rstd=rstd: e.activation(out=m2[:], in_=m2[:], func=AF.Ln, bias=EPS), reads=[t_m2], writes=[t_m2])
                            S.op("act", lambda e, cf=cf, c16=c16, c2=c2, sn=sn, mean_sb=mean_sb, m2=m2, rstd=rstd: e.activation(out=rstd[:], in_=m2[:], func=AF.Exp, scale=-0.5), reads=[t_m2], writes=[t_rstd])
                            for ct in range(4):
                                i = ui_box[0] % 2
                                ui_box[0] += 1
                                S.op("dve", lambda e, cf=cf, c16=c16, c2=c2, sn=sn, mean_sb=mean_sb, m2=m2, rstd=rstd, ct=ct, i=i: e.tensor_tensor(out=u1[i][:], in0=cf[:, ct, :], in1=mean_sb[:], op=ALU.subtract), reads=[t_cf[ct], t_mean], writes=[t_u1[i]])
                                S.op("dve", lambda e, cf=cf, c16=c16, c2=c2, sn=sn, mean_sb=mean_sb, m2=m2, rstd=rstd, i=i: e.tensor_tensor(out=u1[i][:], in0=u1[i][:], in1=rstd[:], op=ALU.mult), reads=[t_u1[i], t_rstd], writes=[t_u1[i]])
                                S.op("act", lambda e, cf=cf, c16=c16, c2=c2, sn=sn, mean_sb=mean_sb, m2=m2, rstd=rstd, ct=ct, i=i: e.activation(out=sn[:, ct, :], in_=u1[i][:], func=AF.Silu, scale=lng[:, ct:ct + 1], bias=lnb[:, ct:ct + 1]),
                                     reads=[t_u1[i], t_par], writes=[t_sn[ct]])

                    def pw_part(c):
                            cf, c16, c2, sn = c162[c % 2], c162[c % 2], c22[c % 2], sn2[c % 2]
                            t_cf, t_c16, t_c2, t_sn = t_c162[c % 2], t_c162[c % 2], t_c22[c % 2], t_sn2[c % 2]
                            mean_sb, m2, rstd = mean2[c % 2], m22[c % 2], rstd2[c % 2]
                            t_mean, t_m2, t_rstd = t_mean2[c % 2], t_m22[c % 2], t_rstd2[c % 2]
                            for mt in range(4):
                                bk = 4 + mt % 2
                                for ct in range(4):
                                    S.op("pe", lambda e, cf=cf, c16=c16, c2=c2, sn=sn, mean_sb=mean_sb, m2=m2, rstd=rstd, ct=ct, mt=mt, bk=bk: e.matmul(psf[bk][:], lhsT=pwsb[:, ct, mt * 128:(mt + 1) * 128], rhs=sn[:, ct, :], start=(ct == 0), stop=(ct == 3)),
                                         reads=[t_pw, t_sn[ct]], writes=[pb[bk]], inc=(ct == 3))
                                S.op("dve", lambda e, cf=cf, c16=c16, c2=c2, sn=sn, mean_sb=mean_sb, m2=m2, rstd=rstd, mt=mt, c=c, bk=bk: e.tensor_tensor(out=gdT[:, mt, c * 512:(c + 1) * 512], in0=psf[bk][:], in1=gdT[:, mt, c * 512:(c + 1) * 512], op=ALU.mult),
                                     reads=[pb[bk], t_gd[mt][c]], writes=[t_gd[mt][c]])

                    conv_part(0)
                    for c in range(4):
                        ln_part(c)
                        if c + 1 < 4:
                            conv_part(c + 1)
                        pw_part(c)
                    S.flush()
                if os.environ.get("KSTOP") == "CV":
                    return
                with ExitStack() as es:
                    phase_outproj(es, o_w_out, post_g[1:2, :], out_d, b, gcT, gdT, TL(4), TL(4), wo1, t_wo1)
                    S.flush()

        nbr = int(os.environ.get("KNB", NB))
        for b in range(nbr):
            layer0(b)
        if n_layers >= 2 and os.environ.get("KLAYERS", "2") == "2":
            layer1_all(nbr)

    return nc


def layer1_host_layouts(inputs, f):
    g = lambda k: np.asarray(inputs[k][0], dtype=np.float32)
    m = {}
    m["o_w_in"] = f(g("o_w_in"))

    def A_gn(a):
        a = np.repeat(a[:, None, :], 16, axis=1).reshape(4, 8, 16, 64)
        return f(a.transpose(1, 2, 0, 3).reshape(128, 256))
    m["o_lamre_A"] = A_gn(g("o_lam_re"))
    m["o_lamim_A"] = A_gn(g("o_lam_im"))
    m["o_dt_A"] = A_gn(np.repeat(g("o_log_dt")[:, None], 64, axis=1))

    def A_b(bm):
        a = bm.transpose(0, 2, 1).reshape(4, 8, 16, 64)
        return f(a.transpose(1, 2, 0, 3).reshape(128, 256))
    m["o_bre_A"] = A_b(g("o_b_re"))
    m["o_bim_A"] = A_b(g("o_b_im"))

    def B_gn(a):
        return f(a.reshape(16, 2, 64).transpose(1, 2, 0).reshape(128, 16))
    m["o_lamre_B"] = B_gn(g("o_lam_re"))
    m["o_lamim_B"] = B_gn(g("o_lam_im"))
    m["o_dt_B"] = B_gn(np.repeat(g("o_log_dt")[:, None], 64, axis=1))

    def B_c(cm):
        return f(cm.reshape(16, 2, 16, 64).transpose(1, 3, 0, 2).reshape(128, 256))
    def B_b(bm):
        return f(bm.reshape(16, 2, 64, 16).transpose(1, 2, 0, 3).reshape(128, 256))
    m["o_bre_B"] = B_b(g("o_b_re"))
    m["o_bim_B"] = B_b(g("o_b_im"))
    m["o_cre_B"] = B_c(g("o_c_re"))
    m["o_cim_B"] = B_c(g("o_c_im"))
    m["o_dA"] = f(g("o_d").reshape(4, 128).T)
    m["o_glu_w"] = f(g("o_glu_w"))
    m["o_glub"] = f(g("o_glu_b").reshape(8, 128).T)
    m["o_dwT"] = f(g("o_dw").reshape(31, 4, 128).transpose(2, 1, 0))
    m["o_lng"] = f(g("o_ln_g").reshape(4, 128).T)
    m["o_lnb"] = f(g("o_ln_b").reshape(4, 128).T)
    m["o_pw"] = f(g("o_pw"))
    m["o_w_out"] = f(g("o_w_out"))
    return m


def make_in_maps(inputs):
    n = 8
    x = np.ascontiguousarray(inputs["x"], dtype=np.float32)
    consts = host_consts()
    f = lambda a: np.ascontiguousarray(a, dtype=np.float32)
    l1maps = layer1_host_layouts(inputs, f)
    in_maps = []
    for c in range(n):
        m = {"x": x[NB * c:NB * (c + 1)]}
        for k in ("pre_norm_g", "post_norm_g"):
            m[k] = f(inputs[k])
        m["e_w_in"] = f(inputs["e_w_in"][0])
        m["e_pool_w"] = f(inputs["e_pool_w"][0])
        m["e_pool_scale"] = f(np.asarray(inputs["e_pool_scale"][0], dtype=np.float32).reshape(4, 128).T)
        m["e_w_out"] = f(inputs["e_w_out"][0])
        m.update(l1maps)
        for k, v in consts.items():
            m["c_" + k] = v
        in_maps.append(m)
    return in_maps


def kernel(**inputs):
    nc = build()
    in_maps = make_in_maps(inputs)
    res = run_bass_kernel_spmd(nc, in_maps, core_ids=list(range(8)))
    return np.concatenate([r["out"] for r in res.results], axis=0)
```

```python
import math
import os
from contextlib import ExitStack
import numpy as np
import concourse.bass as bass
import concourse.mybir as mybir
from concourse.bass_utils import run_bass_kernel_spmd

F32 = mybir.dt.float32
BF16 = mybir.dt.bfloat16
I32 = mybir.dt.int32
ALU = mybir.AluOpType
AF = mybir.ActivationFunctionType
AX = mybir.AxisListType

D = 1024
SEQ = 2048
NB = 2
HD = 128
EPS = 1e-6
NEGB = -30000.0
S5L = 8
NCH = SEQ // S5L


class T:
    __slots__ = ("w", "r")

    def __init__(self):
        self.w = None
        self.r = {}


def TL(n):
    return [T() for _ in range(n)]


class Sched:
    ENG = ("pe", "act", "dve", "pool", "sp")

    def __init__(self, nc, es, n_dma=24):
        self.nc = nc
        self.ops = {e: [] for e in self.ENG}
        self.cnt = {e: 0 for e in self.ENG}
        self.seen = {e: {} for e in self.ENG}
        self.n_dma = n_dma
        self.dma_cnt = [0] * n_dma
        self.dma_rr2 = {"sp": 0, "pool": 0}
        self.semh = {}
        for e in self.ENG:
            self.semh[("c", e)] = es.enter_context(nc.semaphore(f"s_{e}"))
        for i in range(n_dma):
            self.semh[("d", i)] = es.enter_context(nc.semaphore(f"s_d{i}"))

    def _deps(self, eng, reads, writes):
        need = {}
        for t in reads:
            if t.w is not None:
                k, v = t.w
                if need.get(k, 0) < v:
                    need[k] = v
        for t in writes:
            if t.w is not None:
                k, v = t.w
                if need.get(k, 0) < v:
                    need[k] = v
            for k, v in t.r.items():
                if need.get(k, 0) < v:
                    need[k] = v
        waits = []
        sn = self.seen[eng]
        for k, v in need.items():
            if eng == "pe" and k == ("c", "pe"):
                continue
            if sn.get(k, 0) < v:
                waits.append((k, v))
                sn[k] = v
        return waits

    def _record(self, key, v, reads, writes):
        for t in reads:
            if t.r.get(key, 0) < v:
                t.r[key] = v
        for t in writes:
            t.w = (key, v)
            t.r = {}

    def op(self, eng, fn, reads=(), writes=(), inc=True):
        waits = self._deps(eng, reads, writes)
        key = ("c", eng)
        if inc:
            self.cnt[eng] += 1
            v = self.cnt[eng]
        else:
            v = self.cnt[eng] + 1
        self.ops[eng].append([waits, fn, key, 1 if inc else 0])
        self._record(key, v, reads, writes)

    def dma(self, eng, fn, reads=(), writes=()):
        half = self.n_dma // 2
        base = 0 if eng == "sp" else half
        j = self.dma_rr2[eng]
        self.dma_rr2[eng] = (j + 1) % half
        i = base + j
        waits = self._deps(eng, reads, writes)
        key = ("d", i)
        prev = self.dma_cnt[i]
        if prev > 0 and self.seen[eng].get(key, 0) < prev:
            waits.append((key, prev))
            self.seen[eng][key] = prev
        self.dma_cnt[i] += 16
        self.ops[eng].append([waits, fn, key, 16])
        self._record(key, self.dma_cnt[i], reads, writes)

    def flush(self):
        nc = self.nc
        for e in ("pe", "act", "dve", "pool"):
            if self.ops[e] and self.ops[e][-1][3] == 0:
                self.ops[e][-1][3] = 1
                self.cnt[e] += 1
        fin = [(("d", i), self.dma_cnt[i]) for i in range(self.n_dma) if self.dma_cnt[i] > 0]
        fin += [(("c", e), self.cnt[e]) for e in ("pe", "act", "dve", "pool") if self.cnt[e] > 0]
        ops = self.ops
        semh = self.semh
        with nc.Block() as block:
            def run(engname, eng):
                for waits, fn, key, inc in ops[engname]:
                    for k, v in waits:
                        eng.wait_ge(semh[k], v)
                    ins = fn(eng)
                    if inc:
                        ins.then_inc(semh[key], inc)
                if engname == "sp":
                    for k, v in fin:
                        eng.wait_ge(semh[k], v)

            @block.tensor
            def _(e):
                run("pe", e)

            @block.scalar
            def _(e):
                run("act", e)

            @block.vector
            def _(e):
                run("dve", e)

            @block.gpsimd
            def _(e):
                run("pool", e)

            @block.sync
            def _(e):
                run("sp", e)
        self.ops = {e: [] for e in self.ENG}
        for e in self.ENG:
            for k, v in fin:
                self.seen[e][k] = v


def host_consts():
    c = {}
    c["ident"] = np.eye(128, dtype=np.float32)
    c["ones"] = np.ones((128, 128), np.float32)
    kk = np.arange(128)[:, None]
    qq = np.arange(128)[None, :]
    c["trineg"] = np.where(kk <= qq, 0.0, NEGB).astype(np.float32)
    e8 = np.zeros((8, 8, 128), np.float32)
    for n in range(8):
        e8[n, n, :] = 1.0
    c["e8"] = e8.reshape(8, 1024)
    p32 = np.zeros((128, 128), np.float32)
    for m in range(32):
        p32[(m + 16) % 32, m] = 1.0
    c["p32"] = p32
    inv = np.power(500000.0, -np.arange(0, 32, 2, dtype=np.float32) / 32.0).astype(np.float32)
    pos = np.arange(SEQ, dtype=np.float32)
    ang = (pos[None, :] * inv[:, None]).astype(np.float32)
    cs = np.cos(ang).astype(np.float32)
    sn = np.sin(ang).astype(np.float32)
    c["ropeC"] = np.concatenate([cs, cs, np.ones((96, SEQ), np.float32)], 0)
    c["ropeS"] = np.concatenate([-sn, sn, np.zeros((96, SEQ), np.float32)], 0)
    negm = np.zeros((128, 8, 8), np.float32)
    bfix = np.zeros((128, 8, 8), np.float32)
    for i in range(8):
        B = 4 + i // 2
        for n in range(8):
            negm[:, i, n] = 0.0 if n < B else -1e30
            bfix[:, i, n] = 0.0 if n == B else NEGB
    c["negmask"] = negm.reshape(128, 64)
    c["biasfix"] = bfix.reshape(128, 64)
    rc = np.zeros((128, 4, 16), np.float32)
    for g, w in enumerate((2, 4, 8, 16)):
        for t in range(16):
            rc[:, g, t] = 1.0 / min(t + 1, w)
    c["rcnt"] = rc.reshape(128, 64)
    pp = np.arange(128)
    c["maskA"] = np.stack([((pp // 16) % 2 == 0), ((pp // 16) % 2 == 1)], 1).astype(np.float32)
    c["maskB"] = np.stack([(pp // 64 == 0), (pp // 64 == 1)], 1).astype(np.float32)
    return c


CONST_SHAPES = {k: v.shape for k, v in host_consts().items()}


def build(n_layers=2):
    nc = bass.Bass("TRN2", target_bir_lowering=False)

    def dr(name, shape, kind="ExternalInput", dt=F32):
        return nc.dram_tensor(name, list(shape), dt, kind=kind).ap()

    x_d = dr("x", [NB, SEQ, D])
    out_d = dr("out", [NB, SEQ, D], "ExternalOutput")
    pre_g = dr("pre_norm_g", [2, D])
    post_g = dr("post_norm_g", [2, D])
    e_w_in = dr("e_w_in", [D, 3072])
    e_pool_w = dr("e_pool_w", [4, 128, 128])
    e_pool_scale = dr("e_pool_scale", [128, 4])
    e_w_out = dr("e_w_out", [D, D])
    o_w_in = dr("o_w_in", [D, 2560])
    o_lamre_A = dr("o_lamre_A", [128, 256])
    o_lamim_A = dr("o_lamim_A", [128, 256])
    o_dt_A = dr("o_dt_A", [128, 256])
    o_bre_A = dr("o_bre_A", [128, 256])
    o_bim_A = dr("o_bim_A", [128, 256])
    o_bre_B = dr("o_bre_B", [128, 256])
    o_bim_B = dr("o_bim_B", [128, 256])
    o_lamre_B = dr("o_lamre_B", [128, 16])
    o_lamim_B = dr("o_lamim_B", [128, 16])
    o_dt_B = dr("o_dt_B", [128, 16])
    o_cre_B = dr("o_cre_B", [128, 256])
    o_cim_B = dr("o_cim_B", [128, 256])
    o_dA = dr("o_dA", [128, 4])
    o_glu_w = dr("o_glu_w", [512, 1024])
    o_glub = dr("o_glub", [128, 8])
    o_dwT = dr("o_dwT", [128, 4, 31])
    o_lng = dr("o_lng", [128, 4])
    o_lnb = dr("o_lnb", [128, 4])
    o_pw = dr("o_pw", [512, 512])
    o_w_out = dr("o_w_out", [D, D])
    dbg_d = dr("dbg", [128, 4, SEQ], "ExternalOutput") if os.environ.get("KDBG") else None
    cd = {k: dr("c_" + k, list(s)) for k, s in CONST_SHAPES.items()}

    with ExitStack() as top:
        S = Sched(nc, top)
        uid = [0]

        def sbt(es, name, shape, dt):
            uid[0] += 1
            return es.enter_context(nc.sbuf_tensor(f"{name}_{uid[0]}", list(shape), dt))
        psf = [top.enter_context(nc.psum_tensor(f"psf{i}", [128, 512], F32)) for i in range(7)]
        psb = top.enter_context(nc.psum_tensor("psb", [128, 1024], BF16))
        pb = TL(8)

        identb = sbt(top, "identb", [128, 128], BF16)
        identf = sbt(top, "identf", [128, 128], F32)
        onesb = sbt(top, "onesb", [128, 128], BF16)
        t_const = T()
        S.dma("pool", lambda e: e.dma_start(out=identb[:], in_=cd["ident"][:, :]), writes=[t_const])
        S.dma("sp", lambda e: e.dma_start(out=identf[:], in_=cd["ident"][:, :]), writes=[t_const])
        S.dma("pool", lambda e: e.dma_start(out=onesb[:], in_=cd["ones"][:, :]), writes=[t_const])

        def rms_stats(es_tag, src_aps, st, col0, reads, t_st, junk):
            for i, ap in enumerate(src_aps):
                S.op("act", lambda e, ap=ap, i=i: e.activation(out=junk[:, 0:ap.shape[1]], in_=ap, func=AF.Square,
                                                             accum_out=st[:, col0 + i:col0 + i + 1]),
                     reads=reads, writes=[t_st])
            if len(src_aps) == 2:
                S.op("dve", lambda e: e.tensor_tensor(out=st[:, col0:col0 + 1], in0=st[:, col0:col0 + 1],
                                                      in1=st[:, col0 + 1:col0 + 2], op=ALU.add), reads=[t_st], writes=[t_st])
            S.op("act", lambda e: e.activation(out=st[:, col0 + 2:col0 + 3], in_=st[:, col0:col0 + 1], func=AF.Sqrt,
                                               scale=1.0 / D, bias=EPS), reads=[t_st], writes=[t_st])
            S.op("dve", lambda e: e.reciprocal(out=st[:, col0 + 3:col0 + 4], in_=st[:, col0 + 2:col0 + 3]),
                 reads=[t_st], writes=[t_st])

        def phase_norm_T(es, src_d, b, gvec_d, hT, t_hT, nxb=2):
            xb = [sbt(es, f"xb{i}", [128, D], F32) for i in range(nxb)]
            hb = [sbt(es, f"hb{i}", [128, D], BF16) for i in range(2)]
            junk = sbt(es, "junkA", [128, D], BF16)
            gt = sbt(es, "gtA", [128, D], F32)
            st = sbt(es, "stA", [128, 16 * 4], F32)
            t_xb, t_hb, t_g = TL(nxb), TL(2), T()
            t_st = TL(16)
            S.dma("sp", lambda e: e.dma_start(out=gt[:], in_=gvec_d.partition_broadcast(128)), writes=[t_g])
            def stats(tt):
                i = tt % 2
                xi = tt % nxb
                S.dma("sp", lambda e: e.dma_start(out=xb[xi][:], in_=src_d[b, tt * 128:(tt + 1) * 128, :]),
                      writes=[t_xb[xi]])
                rms_stats(es, [xb[xi][:]], st, tt * 4, [t_xb[xi]], t_st[tt], junk)
                S.op("dve", lambda e: e.scalar_tensor_tensor(out=hb[i][:], in0=xb[xi][:], scalar=st[:, tt * 4 + 3:tt * 4 + 4],
                                                             in1=gt[:], op0=ALU.mult, op1=ALU.mult),
                     reads=[t_xb[xi], t_st[tt], t_g], writes=[t_hb[i]])

            def trans(tt):
                i = tt % 2
                for k in range(8):
                    S.op("pe", lambda e, k=k: e.transpose(out=psb[:, k * 128:(k + 1) * 128], in_=hb[i][:, k * 128:(k + 1) * 128],
                                                          identity=identb[:]),
                         reads=[t_hb[i], t_const], writes=[pb[7]], inc=(k == 7))
                S.op("act", lambda e: e.activation(out=hT[:, :, tt * 128:(tt + 1) * 128],
                                                   in_=psb[:, :].rearrange("p (k t) -> p k t", k=8), func=AF.Copy),
                     reads=[pb[7]], writes=[t_hT[tt]])

            stats(0)
            for tt in range(16):
                if tt + 1 < 16:
                    stats(tt + 1)
                trans(tt)

        def load_wo(wo, t_wo, w_out_d):
            for hf in range(2):
                S.dma("pool", lambda e, hf=hf: e.dma_start(out=wo[:, :, hf * 512:(hf + 1) * 512],
                                                          in_=w_out_d[:, hf * 512:(hf + 1) * 512].rearrange("(k p) n -> p k n", p=128)),
                      writes=[t_wo])

        def phase_outproj(es, w_out_d, gvec_d, res_d, b, gA, gB, t_gA, t_gB, wo, t_wo):
            gt = sbt(es, "gtF", [128, D], F32)
            xr = [sbt(es, f"xr{i}", [128, D], F32) for i in range(2)]
            tm = [sbt(es, f"tmF{i}", [128, D], F32) for i in range(2)]
            junk = sbt(es, "junkF", [128, 512], BF16)
            st = sbt(es, "stF", [128, 16 * 4], F32)
            t_g = T()
            t_xr, t_tm, t_st = TL(2), TL(2), TL(16)
            S.dma("sp", lambda e: e.dma_start(out=gt[:], in_=gvec_d.partition_broadcast(128)), writes=[t_g])
            for tt in range(16):
                i = tt % 2
                bk = [psf[2 * i], psf[2 * i + 1]]
                tbk = [pb[2 * i], pb[2 * i + 1]]
                S.dma("sp", lambda e, tt=tt, i=i: e.dma_start(out=xr[i][:], in_=res_d[b, tt * 128:(tt + 1) * 128, :]),
                      writes=[t_xr[i]])
                for hf in range(2):
                    for k in range(8):
                        g_ap = (gA if k < 4 else gB)
                        S.op("pe", lambda e, hf=hf, k=k, g_ap=g_ap, tt=tt, bk=bk: e.matmul(
                            bk[hf][:], lhsT=g_ap[:, k % 4, tt * 128:(tt + 1) * 128], rhs=wo[:, k, hf * 512:(hf + 1) * 512],
                            start=(k == 0), stop=(k == 7)),
                            reads=[(t_gA if k < 4 else t_gB)[tt // 4], t_wo], writes=[tbk[hf]], inc=(k == 7))
                rms_stats(es, [bk[0][:], bk[1][:]], st, tt * 4, tbk, t_st[tt], junk)
                for hf in range(2):
                    S.op("dve", lambda e, hf=hf, tt=tt, i=i, bk=bk: e.scalar_tensor_tensor(
                        out=tm[i][:, hf * 512:(hf + 1) * 512], in0=bk[hf][:], scalar=st[:, tt * 4 + 3:tt * 4 + 4],
                        in1=gt[:, hf * 512:(hf + 1) * 512], op0=ALU.mult, op1=ALU.mult),
                        reads=[tbk[hf], t_st[tt], t_g], writes=[t_tm[i]])
                S.op("pool", lambda e, i=i: e.tensor_tensor(out=tm[i][:], in0=tm[i][:], in1=xr[i][:], op=ALU.add),
                     reads=[t_tm[i], t_xr[i]], writes=[t_tm[i]])
                S.dma("sp", lambda e, tt=tt, i=i: e.dma_start(out=out_d[b, tt * 128:(tt + 1) * 128, :], in_=tm[i][:]),
                      reads=[t_tm[i]])

        def layer0(b):
            with ExitStack() as L:
                qkT = sbt(L, "qkT", [128, 8, SEQ], BF16)
                Vt = sbt(L, "Vt", [128, 16, 512], BF16)
                gaT = sbt(L, "gaT", [128, 4, SEQ], BF16)
                gbT = sbt(L, "gbT", [128, 4, SEQ], BF16)
                t_qk = [TL(4) for _ in range(8)]
                t_V = TL(16)
                t_ga = [TL(4) for _ in range(4)]
                t_gb = [TL(4) for _ in range(4)]
                wo0 = sbt(L, "wo0", [128, 8, D], BF16)
                t_wo0 = T()

                with ExitStack() as es:
                    hT = sbt(es, "hT", [128, 8, SEQ], BF16)
                    t_hT = TL(16)
                    phase_norm_T(es, x_d, b, pre_g[0:1, :], hT, t_hT, nxb=4)
                    wb = [sbt(es, f"wb{i}", [128, 8, 512], BF16) for i in range(2)]
                    t_wb = TL(2)
                    ropeC = sbt(es, "ropeC", [128, SEQ], F32)
                    ropeS = sbt(es, "ropeS", [128, SEQ], F32)
                    p32 = sbt(es, "p32", [128, 128], BF16)
                    poolw = sbt(es, "poolw", [128, 4, 128], BF16)
                    pscale = sbt(es, "pscale", [128, 4], F32)
                    rcnt = sbt(es, "rcnt", [128, 64], F32)
                    t_c2 = T()
                    S.dma("sp", lambda e: e.dma_start(out=ropeC[:], in_=cd["ropeC"][:, :]), writes=[t_c2])
                    S.dma("sp", lambda e: e.dma_start(out=ropeS[:], in_=cd["ropeS"][:, :]), writes=[t_c2])
                    S.dma("pool", lambda e: e.dma_start(out=p32[:], in_=cd["p32"][:, :]), writes=[t_c2])
                    S.dma("pool", lambda e: e.dma_start(out=poolw[:], in_=e_pool_w.rearrange("g c d -> c g d")), writes=[t_c2])
                    S.dma("sp", lambda e: e.dma_start(out=pscale[:], in_=e_pool_scale[:, :]), writes=[t_c2])
                    S.dma("sp", lambda e: e.dma_start(out=rcnt[:], in_=cd["rcnt"][:, :]), writes=[t_c2])
                    r1 = [sbt(es, f"r1_{i}", [128, 512], F32) for i in range(2)]
                    r2 = [sbt(es, f"r2_{i}", [128, 512], F32) for i in range(2)]
                    t_r1, t_r2 = TL(2), TL(2)
                    ub = [sbt(es, f"ub{i}", [128, 528], F32) for i in range(2)]
                    sa = sbt(es, "sa", [128, 528], F32)
                    sb_ = sbt(es, "sb", [128, 528], F32)
                    mt = [sbt(es, f"mt{i}", [128, 512], BF16) for i in range(2)]
                    t_ub, t_mt = TL(2), TL(2)
                    t_sa, t_sb = T(), T()

                    order = [0, 1, 2, 3, 5, 4]
                    cnt = [0]
                    for oi in range(2):
                        S.dma("pool", lambda e, oi=oi: e.dma_start(out=wb[oi][:], in_=e_w_in[:, order[oi] * 512:(order[oi] + 1) * 512].rearrange("(k p) n -> p k n", p=128)),
                              writes=[t_wb[oi]])
                    ri = [0]
                    ksub = os.environ.get("KSUB", "")
                    for oi, gi in enumerate(order):
                        if ksub and oi >= int(ksub):
                            break
                        wbi = wb[oi % 2]
                        t_wbi = t_wb[oi % 2]
                        if gi in (0, 1):
                            pend = [None]

                            def rope_tail(j, c, r, pbk):
                                def f():
                                    S.op("pe", lambda e: e.matmul(psf[pbk][:], lhsT=p32[:], rhs=qkT[:, j, c * 512:(c + 1) * 512], start=True, stop=True),
                                         reads=[t_qk[j][c], t_c2], writes=[pb[pbk]])
                                    S.op("dve", lambda e: e.tensor_tensor(out=r2[r][:], in0=psf[pbk][:], in1=ropeS[:, c * 512:(c + 1) * 512], op=ALU.mult),
                                         reads=[pb[pbk], t_c2], writes=[t_r2[r]])
                                    S.op("dve", lambda e: e.tensor_tensor(out=qkT[:, j, c * 512:(c + 1) * 512], in0=r1[r][:], in1=r2[r][:], op=ALU.add),
                                         reads=[t_r1[r], t_r2[r]], writes=[t_qk[j][c]])
                                return f
                            for m in range(4):
                                j = gi * 4 + m
                                for c in range(4):
                                    bi = cnt[0] % 3
                                    cnt[0] += 1
                                    for k in range(8):
                                        S.op("pe", lambda e, k=k, bi=bi, m=m, c=c, wbi=wbi: e.matmul(
                                            psf[bi][:], lhsT=wbi[:, k, m * 128:(m + 1) * 128], rhs=hT[:, k, c * 512:(c + 1) * 512],
                                            start=(k == 0), stop=(k == 7)),
                                            reads=[t_wbi] + t_hT[4 * c:4 * c + 4], writes=[pb[bi]], inc=(k == 7))
                                    if pend[0] is not None:
                                        pend[0]()
                                    S.op("act", lambda e, bi=bi, j=j, c=c: e.activation(out=qkT[:, j, c * 512:(c + 1) * 512], in_=psf[bi][:], func=AF.Copy),
                                         reads=[pb[bi]], writes=[t_qk[j][c]])
                                    r = ri[0] % 2
                                    ri[0] += 1
                                    pbk = 3 + r
                                    S.op("dve", lambda e, bi=bi, c=c, r=r: e.tensor_tensor(out=r1[r][:], in0=psf[bi][:], in1=ropeC[:, c * 512:(c + 1) * 512], op=ALU.mult),
                                         reads=[pb[bi], t_c2, t_qk[j][c]], writes=[t_r1[r]])
                                    pend[0] = rope_tail(j, c, r, pbk)
                            pend[0]()
                        elif gi == 2:
                            for tt in range(16):
                                bi = cnt[0] % 3
                                cnt[0] += 1
                                for k in range(8):
                                    S.op("pe", lambda e, k=k, bi=bi, tt=tt, wbi=wbi: e.matmul(
                                        psf[bi][:], lhsT=hT[:, k, tt * 128:(tt + 1) * 128], rhs=wbi[:, k, :], start=(k == 0), stop=(k == 7)),
                                        reads=[t_wbi, t_hT[tt]], writes=[pb[bi]], inc=(k == 7))
                                S.op("dve", lambda e, bi=bi, tt=tt: e.tensor_copy(out=Vt[:, tt, :], in_=psf[bi][:]), reads=[pb[bi]], writes=[t_V[tt]])
                        elif gi in (3, 5):
                            dst, t_dst = (gaT, t_ga) if gi == 3 else (gbT, t_gb)
                            for m in range(4):
                                for c in range(4):
                                    bi = cnt[0] % 3
                                    cnt[0] += 1
                                    for k in range(8):
                                        S.op("pe", lambda e, k=k, bi=bi, m=m, c=c, wbi=wbi: e.matmul(
                                            psf[bi][:], lhsT=wbi[:, k, m * 128:(m + 1) * 128], rhs=hT[:, k, c * 512:(c + 1) * 512],
                                            start=(k == 0), stop=(k == 7)),
                                            reads=[t_wbi] + t_hT[4 * c:4 * c + 4], writes=[pb[bi]], inc=(k == 7))
                                    S.op("act", lambda e, bi=bi, m=m, c=c, dst=dst: e.activation(out=dst[:, m, c * 512:(c + 1) * 512], in_=psf[bi][:], func=AF.Silu),
                                         reads=[pb[bi]], writes=[t_dst[m][c]])
                        else:
                            ppend = [None]
                            for g in range(4):
                                w = (2, 4, 8, 16)[g]
                                nlev = g + 1
                                for c in range(4):
                                    bi = cnt[0] % 3
                                    cnt[0] += 1
                                    for k in range(8):
                                        S.op("pe", lambda e, k=k, bi=bi, g=g, c=c, wbi=wbi: e.matmul(
                                            psf[bi][:], lhsT=wbi[:, k, g * 128:(g + 1) * 128], rhs=hT[:, k, c * 512:(c + 1) * 512],
                                            start=(k == 0), stop=(k == 7)),
                                            reads=[t_wbi] + t_hT[4 * c:4 * c + 4], writes=[pb[bi]], inc=(k == 7))
                                    if ppend[0] is not None:
                                        ppend[0]()
                                        ppend[0] = None
                                    u = ub[c % 2]
                                    up = ub[(c + 1) % 2]
                                    if c == 0:
                                        S.op("pool", lambda e, u=u: e.memset(u[:, 0:16], 0.0), writes=[t_ub[c % 2]])
                                    else:
                                        S.op("pool", lambda e, u=u, up=up: e.tensor_copy(out=u[:, 0:16], in_=up[:, 512:528]),
                                             reads=[t_ub[(c + 1) % 2]], writes=[t_ub[c % 2]])
                                    S.op("act", lambda e, bi=bi, u=u: e.activation(out=u[:, 16:528], in_=psf[bi][:], func=AF.Copy),
                                         reads=[pb[bi]], writes=[t_ub[c % 2]])
                                    src, t_src = u, t_ub[c % 2]
                                    for lv in range(nlev):
                                        sh = 1 << lv
                                        lo = 2 * sh
                                        dstb, t_d = (sa, t_sa) if lv % 2 == 0 else (sb_, t_sb)
                                        S.op("dve", lambda e, src=src, dstb=dstb, lo=lo, sh=sh: e.tensor_tensor(
                                            out=dstb[:, lo:528], in0=src[:, lo:528], in1=src[:, lo - sh:528 - sh], op=ALU.add),
                                            reads=[t_src], writes=[t_d])
                                        src, t_src = dstb, t_d
                                    mi = c % 2
                                    S.op("dve", lambda e, src=src, u=u, mi=mi, w=w: e.scalar_tensor_tensor(
                                        out=mt[mi][:], in0=src[:, 16:528], scalar=1.0 / w, in1=u[:, 16:528], op0=ALU.mult, op1=ALU.subtract),
                                        reads=[t_src, t_ub[c % 2]], writes=[t_mt[mi]])
                                    if c == 0:
                                        S.op("dve", lambda e, src=src, g=g: e.tensor_tensor(out=src[:, 0:16], in0=src[:, 16:32], in1=rcnt[:, g * 16:(g + 1) * 16], op=ALU.mult),
                                             reads=[t_src, t_c2], writes=[t_src])
                                        S.op("dve", lambda e, src=src, u=u, mi=mi: e.tensor_tensor(out=mt[mi][:, 0:16], in0=src[:, 0:16], in1=u[:, 16:32], op=ALU.subtract),
                                             reads=[t_src, t_ub[c % 2]], writes=[t_mt[mi]])
                                    pbk = 3 + (c % 2)

                                    def pool_tail(g=g, c=c, mi=mi, pbk=pbk):
                                        S.op("pe", lambda e: e.matmul(psf[pbk][:], lhsT=poolw[:, g, :], rhs=mt[mi][:], start=True, stop=True),
                                             reads=[t_mt[mi], t_c2], writes=[pb[pbk]])
                                        S.op("dve", lambda e: e.scalar_tensor_tensor(
                                            out=gbT[:, g, c * 512:(c + 1) * 512], in0=psf[pbk][:], scalar=pscale[:, g:g + 1],
                                            in1=gbT[:, g, c * 512:(c + 1) * 512], op0=ALU.mult, op1=ALU.mult),
                                            reads=[pb[pbk], t_c2, t_gb[g][c]], writes=[t_gb[g][c]])
                                    ppend[0] = pool_tail
                        if gi == 4 and ppend[0] is not None:
                            ppend[0]()
                            ppend[0] = None
                        if oi + 2 < len(order) and not ksub:
                            gn = order[oi + 2]
                            S.dma("pool", lambda e, gn=gn, wbi=wbi: e.dma_start(out=wbi[:], in_=e_w_in[:, gn * 512:(gn + 1) * 512].rearrange("(k p) n -> p k n", p=128)),
                                  writes=[t_wbi])
                    S.flush()
                if os.environ.get("KSTOP") == "AB":
                    return

                with ExitStack() as es:
                    load_wo(wo0, t_wo0, e_w_out)
                    e8 = sbt(es, "e8", [8, 1024], BF16)
                    trineg = sbt(es, "trineg", [128, 128], BF16)
                    negmask = sbt(es, "negmask", [128, 64], F32)
                    biasfix = sbt(es, "biasfix", [128, 64], F32)
                    t_c3 = T()
                    S.dma("pool", lambda e: e.dma_start(out=e8[:], in_=cd["e8"][:, :]), writes=[t_c3])
                    S.dma("pool", lambda e: e.dma_start(out=trineg[:], in_=cd["trineg"][:, :]), writes=[t_c3])
                    S.dma("sp", lambda e: e.dma_start(out=negmask[:], in_=cd["negmask"][:, :]), writes=[t_c3])
                    S.dma("sp", lambda e: e.dma_start(out=biasfix[:], in_=cd["biasfix"][:, :]), writes=[t_c3])
                    Mrow = sbt(es, "Mrow", [8, 4, SEQ], BF16)
                    t_M = TL(4)
                    stab4 = [sbt(es, f"stab{h}", [8, SEQ], F32) for h in range(4)]
                    sqt = [sbt(es, f"sqt{i}", [128, 512], BF16) for i in range(8)]
                    t_sq = TL(8)
                    kmx4 = [sbt(es, f"kmx{h}", [8, 8], F32) for h in range(4)]
                    kb324 = [sbt(es, f"kb32{h}", [128, 8], F32) for h in range(4)]
                    kbar4 = [sbt(es, f"kbar{h}", [128, 8], BF16) for h in range(4)]
                    gm4 = [sbt(es, f"gm{h}", [128, 64], F32) for h in range(4)]
                    top84 = [sbt(es, f"top8{h}", [128, 64], F32) for h in range(4)]
                    sel4 = [sbt(es, f"sel{h}", [128, 64], F32) for h in range(4)]
                    t_stab4, t_kmx4, t_kb4, t_gm4, t_top4, t_sel4 = TL(4), TL(4), TL(4), TL(4), TL(4), TL(4)

                    def prep_ops(h):
                        L_ = []
                        add = lambda eng, fn, reads=(), writes=(), **kw: L_.append((eng, fn, list(reads), list(writes), kw))
                        stab, kmx, kb32, kbar, gm, top8, sel = stab4[h], kmx4[h], kb324[h], kbar4[h], gm4[h], top84[h], sel4[h]
                        t_stab, t_kmx, t_kb, t_gm, t_top, t_sel = t_stab4[h], t_kmx4[h], t_kb4[h], t_gm4[h], t_top4[h], t_sel4[h]
                        nb = 3 + h
                        tb = [h % 3, (h + 1) % 3]
                        for c in range(4):
                            i = 2 * h
                            add("act", lambda e, i=i, c=c: e.activation(out=sqt[i][:], in_=qkT[:, 4 + h, c * 512:(c + 1) * 512], func=AF.Square),
                                [t_qk[4 + h][c]], [t_sq[i]])
                            add("pe", lambda e, i=i: e.matmul(psf[nb][0:8, :], lhsT=onesb[:, 0:8], rhs=sqt[i][:], start=True, stop=True),
                                [t_sq[i], t_const], [pb[nb]])
                            add("dve", lambda e, c=c: e.tensor_reduce(out=kmx[:, c:c + 1], in_=psf[nb][0:8, :], axis=AX.X, op=ALU.max),
                                [pb[nb]], [t_kmx])
                        add("dve", lambda e: e.tensor_reduce(out=kmx[:, 4:5], in_=kmx[:, 0:4], axis=AX.X, op=ALU.max), [t_kmx], [t_kmx])
                        for c in range(4):
                            i = 2 * h + 1
                            add("act", lambda e, i=i, c=c: e.activation(out=sqt[i][:], in_=qkT[:, h, c * 512:(c + 1) * 512], func=AF.Square),
                                [t_qk[h][c]], [t_sq[i]])
                            add("pe", lambda e, i=i: e.matmul(psf[nb][0:8, :], lhsT=onesb[:, 0:8], rhs=sqt[i][:], start=True, stop=True),
                                [t_sq[i], t_const], [pb[nb]])
                            add("act", lambda e, c=c: e.activation(out=stab[:, c * 512:(c + 1) * 512], in_=psf[nb][0:8, :], func=AF.Sqrt, scale=kmx[:, 4:5]),
                                [pb[nb], t_kmx], [t_stab])
                        add("dve", lambda e: e.tensor_reduce(out=kb32[:], in_=qkT[:, 4 + h, :].rearrange("p (n s) -> p n s", s=256), axis=AX.X, op=ALU.add),
                            t_qk[4 + h], [t_kb])
                        add("dve", lambda e: e.tensor_scalar(out=kbar[:], in0=kb32[:], scalar1=1.0 / 256, scalar2=None, op0=ALU.mult), [t_kb], [t_kb])
                        for i8 in range(8):
                            add("pe", lambda e, i8=i8: e.matmul(psf[nb][:, i8 * 8:(i8 + 1) * 8], lhsT=qkT[:, h, (8 + i8) * 128:(9 + i8) * 128], rhs=kbar[:], start=True, stop=True),
                                [t_qk[h][2 + i8 // 4], t_kb], [pb[nb]], inc=(i8 == 7))
                        add("dve", lambda e: e.tensor_tensor(out=gm[:], in0=psf[nb][:, 0:64], in1=negmask[:], op=ALU.add), [pb[nb], t_c3], [t_gm])
                        for i8 in range(8):
                            add("dve", lambda e, i8=i8: e.max(out=top8[:, i8 * 8:(i8 + 1) * 8], in_=gm[:, i8 * 8:(i8 + 1) * 8]), [t_gm], [t_top])
                        for i8 in range(8):
                            add("dve", lambda e, i8=i8: e.tensor_scalar(out=sel[:, i8 * 8:(i8 + 1) * 8], in0=gm[:, i8 * 8:(i8 + 1) * 8],
                                                                      scalar1=top8[:, i8 * 8 + 2:i8 * 8 + 3], scalar2=None, op0=ALU.is_ge),
                                [t_gm, t_top], [t_sel])
                        add("dve", lambda e: e.scalar_tensor_tensor(out=sel[:], in0=sel[:], scalar=-NEGB, in1=biasfix[:], op0=ALU.mult, op1=ALU.add),
                            [t_sel, t_c3], [t_sel])
                        add("act", lambda e: e.activation(out=Mrow[:, h, 0:1024], in_=stab[:, 0:1024], func=AF.Copy, scale=-1.0), [t_stab], [t_M[h]])
                        for hf in range(2):
                            for i4 in range(4):
                                i8 = hf * 4 + i4
                                add("pe", lambda e, i8=i8, i4=i4: e.transpose(out=psf[nb][0:8, i4 * 128:(i4 + 1) * 128], in_=sel[:, i8 * 8:(i8 + 1) * 8], identity=identf[:]),
                                    [t_sel, t_const], [pb[nb]], inc=(i4 == 3))
                            add("dve", lambda e, hf=hf: e.tensor_tensor(out=Mrow[:, h, 1024 + hf * 512:1536 + hf * 512], in0=psf[nb][0:8, :],
                                                                      in1=stab[:, 1024 + hf * 512:1536 + hf * 512], op=ALU.subtract),
                                [pb[nb], t_stab], [t_M[h]])
                        return L_

                    plists = [prep_ops(h) for h in range(4)]
                    for i in range(max(len(l) for l in plists)):
                        for l in plists:
                            if i < len(l):
                                eng, fn, rd, wr, kw = l[i]
                                S.op(eng, fn, reads=rd, writes=wr, **kw)

                    PT = [sbt(es, f"PT{i}", [128, 512], BF16) for i in range(3)]
                    t_PT = TL(3)
                    lns = sbt(es, "lns", [128, 512], F32)
                    rinv = sbt(es, "rinv", [128, 512], F32)
                    ot = sbt(es, "ot", [128, 512], F32)
                    t_lns, t_rinv, t_ot = T(), T(), T()
                    scale = 1.0 / math.sqrt(HD)
                    items = [(h, qc, kt) for h in range(4) for qc in range(4) for kt in range(4 * qc + 4)]

                    def emit_S(idx):
                        h, qc, kt = items[idx]
                        sb_i = idx % 2
                        off = max(0, kt * 128 - qc * 512)
                        q0 = qc * 512 + off
                        q1 = (qc + 1) * 512
                        n = kt // 2
                        diag = kt >= 4 * qc
                        S.op("pe", lambda e: e.matmul(psf[sb_i][:, off:512], lhsT=qkT[:, 4 + h, kt * 128:(kt + 1) * 128], rhs=qkT[:, h, q0:q1], start=True, stop=False),
                             reads=[t_qk[4 + h][kt // 4], t_qk[h][qc]], writes=[pb[sb_i]])
                        S.op("pe", lambda e: e.matmul(psf[sb_i][:, off:512], lhsT=e8[:, n * 128:(n + 1) * 128], rhs=Mrow[:, h, q0:q1], start=False, stop=(not diag)),
                             reads=[t_M[h], t_c3], writes=[pb[sb_i]])
                        if diag:
                            S.op("pe", lambda e: e.matmul(psf[sb_i][:, off:off + 128], lhsT=identb[:], rhs=trineg[:], start=False, stop=True),
                                 reads=[t_c3, t_const], writes=[pb[sb_i]])
                        pi = idx % 3
                        S.op("act", lambda e: e.activation(out=PT[pi][:, off:512], in_=psf[sb_i][:, off:512], func=AF.Exp, scale=scale),
                             reads=[pb[sb_i]], writes=[t_PT[pi]])

                    def emit_PV(idx):
                        h, qc, kt = items[idx]
                        off = max(0, kt * 128 - qc * 512)
                        pi = idx % 3
                        par = (h * 4 + qc) % 2
                        ob, sbk = 2 + par, 4 + par
                        last = (kt == 4 * qc + 3)
                        S.op("pe", lambda e: e.matmul(psf[ob][:, off:512], lhsT=Vt[:, kt, h * 128:(h + 1) * 128], rhs=PT[pi][:, off:512], start=(kt == 0), stop=last),
                             reads=[t_V[kt], t_PT[pi]], writes=[pb[ob]])
                        S.op("pe", lambda e: e.matmul(psf[sbk][:, off:512], lhsT=onesb[:], rhs=PT[pi][:, off:512], start=(kt == 0), stop=last),
                             reads=[t_const, t_PT[pi]], writes=[pb[sbk]])
                        if last:
                            S.op("act", lambda e: e.activation(out=lns[:], in_=psf[sbk][:], func=AF.Ln), reads=[pb[sbk]], writes=[t_lns])
                            S.op("act", lambda e: e.activation(out=rinv[:], in_=lns[:], func=AF.Exp, scale=-1.0), reads=[t_lns], writes=[t_rinv])
                            S.op("dve", lambda e: e.tensor_tensor(out=ot[:], in0=psf[ob][:], in1=rinv[:], op=ALU.mult), reads=[pb[ob], t_rinv], writes=[t_ot])
                            S.op("pool", lambda e: e.tensor_tensor(out=gaT[:, h, qc * 512:(qc + 1) * 512], in0=ot[:], in1=gaT[:, h, qc * 512:(qc + 1) * 512], op=ALU.mult),
                                 reads=[t_ot, t_ga[h][qc]], writes=[t_ga[h][qc]])

                    emit_S(0)
                    for idx in range(len(items)):
                        if idx + 1 < len(items):
                            emit_S(idx + 1)
                        emit_PV(idx)
                    S.flush()
                if os.environ.get("KSTOP") == "CD":
                    return

                with ExitStack() as es:
                    t_gA = [T() for _ in range(4)]
                    t_gB = [T() for _ in range(4)]
                    phase_outproj(es, e_w_out, post_g[0:1, :], x_d, b, gaT, gbT, t_gA, t_gB, wo0, t_wo0)
                    S.flush()

        L1 = top

        def layer1_all(nb):
            with ExitStack() as P1:
                wv_sb = sbt(P1, "wv_sb", [128, 4, 8, 2, 128], BF16)
                toep_sb = sbt(P1, "toep_sb", [128, 4, 8, 256], BF16)
                w3_sb = sbt(P1, "w3_sb", [128, 16, 2, 256], BF16)
                pw_tab = sbt(P1, "pw_tab", [128, 16, 8, 3], F32)
                dA = sbt(P1, "dA", [128, 4], F32)
                glub = sbt(P1, "glub", [128, 8], F32)
                lng = sbt(P1, "lng", [128, 4], F32)
                lnb = sbt(P1, "lnb", [128, 4], F32)
                dwT = sbt(P1, "dwT", [128, 4, 31], F32)
                ones512 = sbt(P1, "ones512", [128, 128], BF16)
                t_par = T()
                for dst, src in ((dA, o_dA), (glub, o_glub), (lng, o_lng), (lnb, o_lnb)):
                    S.dma("sp", lambda e, dst=dst, src=src: e.dma_start(out=dst[:], in_=src[:, :]), writes=[t_par])
                S.dma("sp", lambda e: e.dma_start(out=dwT[:], in_=o_dwT[:, :, :]), writes=[t_par])
                S.op("act", lambda e: e.activation(out=ones512[:], in_=onesb[:], func=AF.Copy, scale=1.0 / 512), reads=[t_const], writes=[t_par])

                with ExitStack() as es:
                    tkA, tkB, t_m = T(), T(), T()
                    listA, listB = [], []
                    cur = {"eng": "dve", "tk": tkA, "list": listA}

                    def vop(fn, eng=None):
                        cur["list"].append(("op", eng or cur["eng"], fn, [cur["tk"], t_par, t_m], [cur["tk"]], {}))

                    def pop(fn, eng=None):
                        cur["list"].append(("op", eng or cur["eng"], fn, [cur["tk"], t_par, t_m], [T()], {}))

                    def tt(o, a, b_, op, eng=None):
                        vop(lambda e: e.tensor_tensor(out=o, in0=a, in1=b_, op=op), eng)

                    def ld(name, src, shape, tok=None):
                        t = sbt(es, name, shape, F32)
                        if tok is not None:
                            S.dma("sp", lambda e: e.dma_start(out=t[:], in_=src), writes=[tok])
                        else:
                            cur["list"].append(("dma", "sp", lambda e: e.dma_start(out=t[:], in_=src), [], [cur["tk"]], {}))
                        return t

                    def compute_a(tag, lr, li, dtl, n):
                        mk = lambda nm: sbt(es, f"{tag}_{nm}", [128, n], F32)
                        dtv, x1, mg, th, u, r, sn_, cs_, ar, ai = [mk(k) for k in ("dt", "x1", "mg", "th", "u", "r", "sn", "cs", "ar", "ai")]
                        ui = sbt(es, f"{tag}_ui", [128, n], I32)
                        vop(lambda e: e.activation(out=dtv[:], in_=dtl[:], func=AF.Exp), "act")
                        tt(x1[:], lr[:], dtv[:], ALU.mult)
                        vop(lambda e: e.activation(out=mg[:], in_=x1[:], func=AF.Exp), "act")
                        tt(th[:], li[:], dtv[:], ALU.mult)
                        for shift, dst in ((0.0, sn_), (math.pi / 2, cs_)):
                            vop(lambda e, shift=shift: e.tensor_scalar(out=r[:], in0=th[:], scalar1=shift, scalar2=None, op0=ALU.add))
                            vop(lambda e: e.tensor_copy(out=x1[:], in_=r[:]))
                            for jj in range(1, 6):
                                vop(lambda e, jj=jj: e.tensor_scalar(out=u[:], in0=x1[:], scalar1=(2 * jj - 1) * math.pi, scalar2=-2 * math.pi, op0=ALU.is_ge, op1=ALU.mult))
                                tt(r[:], r[:], u[:], ALU.add)
                            vop(lambda e, dst=dst: e.activation(out=dst[:], in_=r[:], func=AF.Sin), "act")
                        tt(ar[:], mg[:], cs_[:], ALU.mult)
                        tt(ai[:], mg[:], sn_[:], ALU.mult)
                        return ar, ai

                    lrA = ld("lrA", o_lamre_A[:, :], [128, 256])
                    liA = ld("liA", o_lamim_A[:, :], [128, 256])
                    dtA = ld("dtA", o_dt_A[:, :], [128, 256])
                    brA = ld("brA", o_bre_A[:, :], [128, 256])
                    biA = ld("biA", o_bim_A[:, :], [128, 256])
                    mA = ld("mA", cd["maskA"][:, :], [128, 2], t_m)
                    mB = ld("mB", cd["maskB"][:, :], [128, 2], t_m)
                    arA, aiA = compute_a("A", lrA, liA, dtA, 256)
                    mk = lambda nm, n=256: sbt(es, nm, [128, n], F32)
                    nr, den, t1, t2, fr, fi = [mk(k) for k in ("nr", "den", "t1", "t2", "fr", "fi")]
                    vop(lambda e: e.tensor_scalar(out=nr[:], in0=arA[:], scalar1=-1.0, scalar2=None, op0=ALU.add))
                    tt(t1[:], lrA[:], lrA[:], ALU.mult)
                    tt(t2[:], liA[:], liA[:], ALU.mult)
                    tt(den[:], t1[:], t2[:], ALU.add)
                    vop(lambda e: e.reciprocal(out=den[:], in_=den[:]), "dve")
                    tt(t1[:], nr[:], lrA[:], ALU.mult)
                    tt(t2[:], aiA[:], liA[:], ALU.mult)
                    tt(t1[:], t1[:], t2[:], ALU.add)
                    tt(fr[:], t1[:], den[:], ALU.mult)
                    tt(t1[:], aiA[:], lrA[:], ALU.mult)
                    tt(t2[:], nr[:], liA[:], ALU.mult)
                    tt(t1[:], t1[:], t2[:], ALU.subtract)
                    tt(fi[:], t1[:], den[:], ALU.mult)
                    Gall = sbt(es, "Gall", [128, 8, 2, 256], F32)
                    tt(t1[:], fr[:], brA[:], ALU.mult)
                    tt(t2[:], fi[:], biA[:], ALU.mult)
                    tt(Gall[:, 0, 0, :], t1[:], t2[:], ALU.subtract)
                    tt(t1[:], fr[:], biA[:], ALU.mult)
                    tt(t2[:], fi[:], brA[:], ALU.mult)
                    tt(Gall[:, 0, 1, :], t1[:], t2[:], ALU.add)
                    for m in range(7):
                        tt(t1[:], Gall[:, m, 0, :], arA[:], ALU.mult)
                        tt(t2[:], Gall[:, m, 1, :], aiA[:], ALU.mult)
                        tt(Gall[:, m + 1, 0, :], t1[:], t2[:], ALU.subtract)
                        tt(t1[:], Gall[:, m, 0, :], aiA[:], ALU.mult)
                        tt(t2[:], Gall[:, m, 1, :], arA[:], ALU.mult)
                        tt(Gall[:, m + 1, 1, :], t1[:], t2[:], ALU.add)
                    for s in range(8):
                        for ri in range(2):
                            for gi in range(2):
                                pop(lambda e, s=s, ri=ri, gi=gi: e.tensor_scalar(
                                    out=wv_sb[:, :, s, ri, gi * 64:(gi + 1) * 64],
                                    in0=Gall[:, 7 - s, ri, :].rearrange("p (c n) -> p c n", c=4),
                                    scalar1=mA[:, gi:gi + 1], scalar2=None, op0=ALU.mult))
                    Kall = sbt(es, "Kall", [128, 4, 8, 16], F32)
                    cur["eng"], cur["tk"], cur["list"] = "dve", tkB, listB
                    lrB = ld("lrB", o_lamre_B[:, :], [128, 16])
                    liB = ld("liB", o_lamim_B[:, :], [128, 16])
                    dtB = ld("dtB", o_dt_B[:, :], [128, 16])
                    crB = ld("crB", o_cre_B[:, :], [128, 256])
                    ciB = ld("ciB", o_cim_B[:, :], [128, 256])
                    arB, aiB = compute_a("B", lrB, liB, dtB, 16)
                    PB = sbt(es, "PB", [128, 8, 2, 16], F32)
                    s1 = sbt(es, "s1", [128, 16], F32)
                    s2 = sbt(es, "s2", [128, 16], F32)
                    vop(lambda e: e.tensor_copy(out=PB[:, 0, 0, :], in_=arB[:]))
                    vop(lambda e: e.tensor_copy(out=PB[:, 0, 1, :], in_=aiB[:]))
                    for r in range(7):
                        tt(s1[:], PB[:, r, 0, :], arB[:], ALU.mult)
                        tt(s2[:], PB[:, r, 1, :], aiB[:], ALU.mult)
                        tt(PB[:, r + 1, 0, :], s1[:], s2[:], ALU.subtract)
                        tt(s1[:], PB[:, r, 0, :], aiB[:], ALU.mult)
                        tt(s2[:], PB[:, r, 1, :], arB[:], ALU.mult)
                        tt(PB[:, r + 1, 1, :], s1[:], s2[:], ALU.add)
                    brB = ld("brB", o_bre_B[:, :], [128, 256])
                    biB = ld("biB", o_bim_B[:, :], [128, 256])
                    mkb = lambda nm: sbt(es, nm, [128, 16], F32)
                    nrB, denB, x1B, x2B, frB, fiB = [mkb(k) for k in ("nrB", "denB", "x1B", "x2B", "frB", "fiB")]
                    vop(lambda e: e.tensor_scalar(out=nrB[:], in0=arB[:], scalar1=-1.0, scalar2=None, op0=ALU.add))
                    tt(x1B[:], lrB[:], lrB[:], ALU.mult)
                    tt(x2B[:], liB[:], liB[:], ALU.mult)
                    tt(denB[:], x1B[:], x2B[:], ALU.add)
                    vop(lambda e: e.reciprocal(out=denB[:], in_=denB[:]), "dve")
                    tt(x1B[:], nrB[:], lrB[:], ALU.mult)
                    tt(x2B[:], aiB[:], liB[:], ALU.mult)
                    tt(x1B[:], x1B[:], x2B[:], ALU.add)
                    tt(frB[:], x1B[:], denB[:], ALU.mult)
                    tt(x1B[:], aiB[:], lrB[:], ALU.mult)
                    tt(x2B[:], nrB[:], liB[:], ALU.mult)
                    tt(x1B[:], x1B[:], x2B[:], ALU.subtract)
                    tt(fiB[:], x1B[:], denB[:], ALU.mult)
                    w1 = sbt(es, "w1", [128, 256], F32)
                    w2 = sbt(es, "w2", [128, 256], F32)
                    bbr = sbt(es, "bbrB", [128, 256], F32)
                    bbi = sbt(es, "bbiB", [128, 256], F32)
                    v3 = lambda t: t[:, :].rearrange("p (a i) -> p a i", a=16)
                    bc = lambda t: t[:, :].unsqueeze(2).broadcast_to([128, 16, 16])
                    tt(v3(w1), v3(brB), bc(frB), ALU.mult)
                    tt(v3(w2), v3(biB), bc(fiB), ALU.mult)
                    tt(bbr[:], w1[:], w2[:], ALU.subtract)
                    tt(v3(w1), v3(biB), bc(frB), ALU.mult)
                    tt(v3(w2), v3(brB), bc(fiB), ALU.mult)
                    tt(bbi[:], w1[:], w2[:], ALU.add)
                    Bmr = sbt(es, "Bmr", [128, 16, 128], F32)
                    Bmi = sbt(es, "Bmi", [128, 16, 128], F32)
                    vop(lambda e: e.memset(Bmr[:], 0.0), "pool")
                    vop(lambda e: e.memset(Bmi[:], 0.0), "pool")
                    for q in range(4):
                        for gi in range(2):
                            c0 = 32 * q + 16 * gi
                            vop(lambda e, q=q, gi=gi, c0=c0: e.tensor_scalar(
                                out=Bmr[:, :, :].rearrange("p (c q) m -> p c q m", q=4)[:, :, q, c0:c0 + 16],
                                in0=bbr[:, :].rearrange("p (c q j) -> p c q j", q=4, j=16)[:, :, q, :],
                                scalar1=mB[:, gi:gi + 1], scalar2=None, op0=ALU.mult))
                            vop(lambda e, q=q, gi=gi, c0=c0: e.tensor_scalar(
                                out=Bmi[:, :, :].rearrange("p (c q) m -> p c q m", q=4)[:, :, q, c0:c0 + 16],
                                in0=bbi[:, :].rearrange("p (c q j) -> p c q j", q=4, j=16)[:, :, q, :],
                                scalar1=mB[:, gi:gi + 1], scalar2=-1.0, op0=ALU.mult, op1=ALU.mult))
                    CAr = sbt(es, "CAr", [128, 9, 16, 16], F32)
                    CAi = sbt(es, "CAi", [128, 9, 16, 16], F32)
                    vop(lambda e: e.tensor_copy(out=CAr[:, 0, :, :], in_=v3(crB)))
                    vop(lambda e: e.tensor_copy(out=CAi[:, 0, :, :], in_=v3(ciB)))
                    for r in range(8):
                        pre = PB[:, r, 0, :].unsqueeze(2).broadcast_to([128, 16, 16])
                        pim = PB[:, r, 1, :].unsqueeze(2).broadcast_to([128, 16, 16])
                        tt(v3(w1), v3(crB), pre, ALU.mult)
                        tt(v3(w2), v3(ciB), pim, ALU.mult)
                        tt(CAr[:, r + 1, :, :], v3(w1), v3(w2), ALU.subtract)
                        tt(v3(w1), v3(crB), pim, ALU.mult)
                        tt(v3(w2), v3(ciB), pre, ALU.mult)
                        tt(CAi[:, r + 1, :, :], v3(w1), v3(w2), ALU.add)
                        for gi in range(2):
                            pop(lambda e, r=r, gi=gi: e.tensor_scalar(out=w3_sb[:, :, 0, gi * 128 + r * 16:gi * 128 + r * 16 + 16], in0=CAr[:, r + 1, :, :],
                                                                    scalar1=mB[:, gi:gi + 1], scalar2=None, op0=ALU.mult))
                            pop(lambda e, r=r, gi=gi: e.tensor_scalar(out=w3_sb[:, :, 1, gi * 128 + r * 16:gi * 128 + r * 16 + 16], in0=CAi[:, r + 1, :, :],
                                                                    scalar1=mB[:, gi:gi + 1], scalar2=-1.0, op0=ALU.mult, op1=ALU.mult))
                    for ct in range(4):
                        for q in range(4):
                            p = 4 * ct + q
                            listB.append(("op", "pe", lambda e, p=p, q=q: e.matmul(psf[0][:, 0:128], lhsT=Bmr[:, p, :], rhs=CAr[:, 0:8, p, :], start=(q == 0), stop=False),
                                          [tkB], [pb[0]], {"inc": False}))
                            listB.append(("op", "pe", lambda e, p=p, q=q: e.matmul(psf[0][:, 0:128], lhsT=Bmi[:, p, :], rhs=CAi[:, 0:8, p, :], start=False, stop=(q == 3)),
                                          [tkB], [pb[0]], {"inc": (q == 3)}))
                        listB.append(("op", "dve", lambda e, ct=ct: e.tensor_copy(out=Kall[:, ct, :, :], in_=psf[0][:, 0:128].rearrange("p (t i) -> p t i", t=8)),
                                      [pb[0], tkB], [tkB], {}))
                    vop(lambda e: e.memset(toep_sb[:], 0.0), "pool")
                    for s in range(8):
                        for gi in range(2):
                            vop(lambda e, s=s, gi=gi: e.tensor_scalar(
                                out=toep_sb[:, :, s, gi * 128 + s * 16:gi * 128 + 128],
                                in0=Kall[:, :, 0:8 - s, :].rearrange("p c t i -> p c (t i)"),
                                scalar1=mA[:, gi:gi + 1], scalar2=None, op0=ALU.mult))
                    qr = sbt(es, "qr", [128, 16], F32)
                    qi = sbt(es, "qi", [128, 16], F32)
                    vop(lambda e: e.tensor_copy(out=qr[:], in_=PB[:, 7, 0, :]))
                    vop(lambda e: e.tensor_copy(out=qi[:], in_=PB[:, 7, 1, :]))
                    for m in range(8):
                        pop(lambda e, m=m: e.tensor_copy(out=pw_tab[:, :, m, 0], in_=qr[:]))
                        pop(lambda e, m=m: e.tensor_copy(out=pw_tab[:, :, m, 1], in_=qi[:]))
                        pop(lambda e, m=m: e.tensor_scalar(out=pw_tab[:, :, m, 2], in0=qi[:], scalar1=-1.0, scalar2=None, op0=ALU.mult))
                        if m < 7:
                            tt(s1[:], qr[:], qr[:], ALU.mult)
                            tt(s2[:], qi[:], qi[:], ALU.mult)
                            tt(s2[:], s1[:], s2[:], ALU.subtract)
                            tt(s1[:], qr[:], qi[:], ALU.mult)
                            vop(lambda e: e.tensor_scalar(out=qi[:], in0=s1[:], scalar1=2.0, scalar2=None, op0=ALU.mult))
                            vop(lambda e: e.tensor_copy(out=qr[:], in_=s2[:]))
                    ia = ib = 0

                    def emit_rec(rec):
                        kind, eng, fn, rd, wr, kw = rec
                        if kind == "dma":
                            S.dma(eng, fn, reads=rd, writes=wr)
                        else:
                            S.op(eng, fn, reads=rd, writes=wr, **kw)
                    while ia < len(listA) or ib < len(listB):
                        if ia < len(listA):
                            emit_rec(listA[ia])
                            ia += 1
                        for _ in range(2):
                            if ib < len(listB):
                                emit_rec(listB[ib])
                                ib += 1
                    S.flush()

                for b in range(nb):
                    layer1(b, wv_sb, toep_sb, w3_sb, pw_tab, dA, glub, lng, lnb, dwT, ones512, t_par)

        def layer1(b, wv_sb, toep_sb, w3_sb, pw_tab, dA, glub, lng, lnb, dwT, ones512, t_par):
            with ExitStack() as L:
                suT = sbt(L, "suT", [128, 4, SEQ], BF16)
                gcT = sbt(L, "gcT", [128, 4, SEQ], BF16)
                gdT = sbt(L, "gdT", [128, 4, SEQ], BF16)
                gpad = sbt(L, "gpad", [128, 4, SEQ + 32], BF16)
                t_su = [TL(4) for _ in range(4)]
                t_gc = [TL(4) for _ in range(4)]
                t_gd = [TL(4) for _ in range(4)]
                t_gp = TL(4)
                wo1 = sbt(L, "wo1", [128, 8, D], BF16)
                t_wo1 = T()
                pwsb = sbt(L, "pwsb", [128, 4, 512], BF16)
                t_pw = T()
                with ExitStack() as es:
                    hT = sbt(es, "hT1", [128, 8, SEQ], BF16)
                    t_hT = TL(16)
                    phase_norm_T(es, out_d, b, pre_g[1:2, :], hT, t_hT, nxb=2)
                    wb = [sbt(es, f"wb1_{i}", [128, 8, 512], BF16) for i in range(2)]
                    t_wb = TL(2)
                    sg = [sbt(es, f"sg{i}", [128, 512], BF16) for i in range(2)]
                    t_sg = TL(2)
                    order = [0, 1, 4, 2, 3]

                    def loadw(oi):
                        gi = order[oi]
                        S.dma("pool", lambda e: e.dma_start(out=wb[oi % 2][:], in_=o_w_in[:, gi * 512:(gi + 1) * 512].rearrange("(k p) n -> p k n", p=128)),
                              writes=[t_wb[oi % 2]])
                    loadw(0)
                    loadw(1)
                    S.op("pool", lambda e: e.memset(gpad[:, :, 0:32], 0.0), writes=t_gp)
                    cnt = [0]

                    def proj(wbi, t_wbi, m, c):
                        bi = cnt[0] % 3
                        cnt[0] += 1
                        for k in range(8):
                            S.op("pe", lambda e, k=k: e.matmul(psf[bi][:], lhsT=wbi[:, k, m * 128:(m + 1) * 128], rhs=hT[:, k, c * 512:(c + 1) * 512],
                                                               start=(k == 0), stop=(k == 7)),
                                 reads=[t_wbi] + t_hT[4 * c:4 * c + 4], writes=[pb[bi]], inc=(k == 7))
                        return bi
                    for oi in range(3):
                        gi = order[oi]
                        dst, t_dst, fn = ((suT, t_su, AF.Copy), (gcT, t_gc, AF.Silu), None, None, (gdT, t_gd, AF.Silu))[gi]
                        for m in range(4):
                            for c in range(4):
                                bi = proj(wb[oi % 2], t_wb[oi % 2], m, c)
                                S.op("act", lambda e, bi=bi, m=m, c=c, dst=dst, fn=fn: e.activation(out=dst[:, m, c * 512:(c + 1) * 512], in_=psf[bi][:], func=fn),
                                     reads=[pb[bi]], writes=[t_dst[m][c]])
                        if oi + 2 < 5:
                            loadw(oi + 2)
                    si = 0
                    for m in range(4):
                        for c in range(4):
                            bi = proj(wb[0], t_wb[0], m, c)
                            i = si % 2
                            si += 1
                            S.op("act", lambda e, bi=bi, i=i: e.activation(out=sg[i][:], in_=psf[bi][:], func=AF.Sigmoid), reads=[pb[bi]], writes=[t_sg[i]])
                            bi2 = proj(wb[1], t_wb[1], m, c)
                            S.op("dve", lambda e, bi2=bi2, i=i, m=m, c=c: e.tensor_tensor(out=gpad[:, m, 32 + c * 512:32 + (c + 1) * 512], in0=psf[bi2][:], in1=sg[i][:], op=ALU.mult),
                                 reads=[pb[bi2], t_sg[i]], writes=[t_gp[m]])
                    S.flush()
                if os.environ.get("KSTOP") == "AB1":
                    return

                with ExitStack() as es:
                    Sb = [[[sbt(es, f"S{sl}{pi}{pp}", [128, 2, 512], F32) for pp in range(2)] for pi in range(2)] for sl in range(2)]
                    t_S = [[[T() for pp in range(2)] for pi in range(2)] for sl in range(2)]
                    Sp = [[[sbt(es, f"Sp{sl}{pi}{ri}", [128, 256], BF16) for ri in range(2)] for pi in range(2)] for sl in range(2)]
                    t_Sp = [[T() for pi in range(2)] for sl in range(2)]
                    for sl in range(2):
                        for pi in range(2):
                            for ri in range(2):
                                S.op("pool", lambda e, sl=sl, pi=pi, ri=ri: e.memset(Sp[sl][pi][ri][:, 0:1], 0.0), writes=[t_Sp[sl][pi]])
                                for pp in range(2):
                                    S.op("pool", lambda e, sl=sl, pi=pi, ri=ri, pp=pp: e.memset(Sb[sl][pi][pp][:, ri, 0:256], 0.0), writes=[t_S[sl][pi][pp]])
                    ysb = [sbt(es, f"ysb{i}", [128, 2, 8, 128], BF16) for i in range(2)]
                    t_ysb = [T(), T()]
                    gluw = sbt(es, "gluw", [128, 4, 1024], BF16)
                    t_gw = T()
                    for hf in range(2):
                        S.dma("pool", lambda e, hf=hf: e.dma_start(out=gluw[:, :, hf * 512:(hf + 1) * 512], in_=o_glu_w[:, hf * 512:(hf + 1) * 512].rearrange("(k p) n -> p k n", p=128)),
                              writes=[t_gw])
                    load_wo(wo1, t_wo1, o_w_out)
                    S.dma("pool", lambda e: e.dma_start(out=pwsb[:], in_=o_pw.rearrange("(k p) n -> p k n", p=128)), writes=[t_pw])
                    couples = [(ct, q0) for ct in range(4) for q0 in (0, 2)]

                    def emit_V(ci):
                        ct, q0 = couples[ci]
                        sl = ci % 2
                        for pi in range(2):
                            q = q0 + pi
                            rows = slice(32 * q, 32 * q + 32)
                            tp = (32 * q, 0)
                            for ri in range(2):
                                bk = (0, 1, 4, 5)[pi * 2 + ri]
                                for s in range(8):
                                    S.op("pe", lambda e, s=s, ri=ri, bk=bk, rows=rows, ct=ct, tp=tp: e.matmul(
                                        psf[bk][:, 0:256], lhsT=wv_sb[rows, ct, s, ri, :],
                                        rhs=suT[rows, ct, :].rearrange("p (k s) -> p s k", s=8)[:, s, :], start=(s == 0), stop=(s == 7), tile_position=tp),
                                        reads=t_su[ct] + [t_par], writes=[pb[bk]], inc=(s == 7))
                                S.op("act", lambda e, ri=ri, bk=bk, sl=sl, pi=pi: e.activation(out=Sb[sl][pi][0][:, ri, 256:512], in_=psf[bk][:, 0:256], func=AF.Copy),
                                     reads=[pb[bk]], writes=[t_S[sl][pi][0]])

                    def emit_scan(ci):
                        ct, q0 = couples[ci]
                        sl = ci % 2
                        for m in range(8):
                            sh = 1 << m
                            a, d_ = m % 2, (m + 1) % 2
                            for stage in range(3):
                                for pi in range(2):
                                    p = ct * 4 + q0 + pi
                                    src, dst = Sb[sl][pi][a], Sb[sl][pi][d_]
                                    ts, td = t_S[sl][pi][a], t_S[sl][pi][d_]
                                    pr = pw_tab[:, p, m, 0:1]
                                    pim = pw_tab[:, p, m, 1:2]
                                    npi = pw_tab[:, p, m, 2:3]
                                    if stage == 0:
                                        S.op("dve", lambda e, src=src, dst=dst, sh=sh, pr=pr: e.scalar_tensor_tensor(out=dst[:, :, 256:512], in0=src[:, :, 256 - sh:512 - sh], scalar=pr, in1=src[:, :, 256:512], op0=ALU.mult, op1=ALU.add),
                                             reads=[ts, t_par], writes=[td])
                                    elif stage == 1:
                                        S.op("dve", lambda e, src=src, dst=dst, sh=sh, npi=npi: e.scalar_tensor_tensor(out=dst[:, 0, 256:512], in0=src[:, 1, 256 - sh:512 - sh], scalar=npi, in1=dst[:, 0, 256:512], op0=ALU.mult, op1=ALU.add),
                                             reads=[ts, td, t_par], writes=[td])
                                    else:
                                        S.op("dve", lambda e, src=src, dst=dst, sh=sh, pim=pim: e.scalar_tensor_tensor(out=dst[:, 1, 256:512], in0=src[:, 0, 256 - sh:512 - sh], scalar=pim, in1=dst[:, 1, 256:512], op0=ALU.mult, op1=ALU.add),
                                             reads=[ts, td, t_par], writes=[td])

                    def emit_y(ci):
                        ct, q0 = couples[ci]
                        sl = ci % 2
                        yb_ = ysb[ct % 2]
                        t_y = t_ysb[ct % 2]
                        for pi in range(2):
                            q = q0 + pi
                            p = ct * 4 + q
                            rows = slice(32 * q, 32 * q + 32)
                            tp = (32 * q, 0)
                            for ri in range(2):
                                S.op("act", lambda e, ri=ri, sl=sl, pi=pi: e.activation(out=Sp[sl][pi][ri][:, 1:256], in_=Sb[sl][pi][0][:, ri, 256:511], func=AF.Copy),
                                     reads=[t_S[sl][pi][0]], writes=[t_Sp[sl][pi]])
                            for kt2 in range(2):
                                bk = 2 + kt2
                                for s in range(8):
                                    S.op("pe", lambda e, s=s, kt2=kt2, bk=bk, rows=rows, ct=ct, tp=tp: e.matmul(
                                        psf[bk][:, 0:256],
                                        lhsT=suT[rows, ct, kt2 * 1024:(kt2 + 1) * 1024].rearrange("p (k s) -> p s k", s=8)[:, s, :],
                                        rhs=toep_sb[rows, ct, s, :], start=(s == 0), stop=False, tile_position=tp),
                                        reads=t_su[ct] + [t_par], writes=[pb[bk]], inc=False)
                                for ri in range(2):
                                    S.op("pe", lambda e, ri=ri, kt2=kt2, bk=bk, sl=sl, pi=pi, p=p: e.matmul(
                                        psf[bk][:, 0:256], lhsT=Sp[sl][pi][ri][:, kt2 * 128:(kt2 + 1) * 128], rhs=w3_sb[:, p, ri, :], start=False, stop=(ri == 1)),
                                        reads=[t_Sp[sl][pi], t_par], writes=[pb[bk]], inc=(ri == 1))
                                S.op("act", lambda e, kt2=kt2, bk=bk, q=q, yb_=yb_: e.activation(
                                    out=yb_[:, kt2, :, q * 32:(q + 1) * 32].rearrange("p r (g i) -> p g r i", g=2),
                                    in_=psf[bk][:, 0:256].rearrange("p (g r i) -> p g r i", g=2, r=8), func=AF.Copy),
                                    reads=[pb[bk]], writes=[t_y])

                    def emit_T(ct):
                        yb_ = ysb[ct % 2]
                        t_y = t_ysb[ct % 2]
                        for kt2 in range(2):
                            for r in range(8):
                                S.op("pe", lambda e, kt2=kt2, r=r, yb_=yb_: e.transpose(out=psb[:, r * 128:(r + 1) * 128], in_=yb_[:, kt2, r, :], identity=identb[:]),
                                     reads=[t_y, t_const], writes=[pb[7]], inc=(r == 7))
                            S.op("dve", lambda e, kt2=kt2, ct=ct: e.scalar_tensor_tensor(
                                out=suT[:, ct, kt2 * 1024:(kt2 + 1) * 1024].rearrange("p (k r) -> p r k", r=8),
                                in0=suT[:, ct, kt2 * 1024:(kt2 + 1) * 1024].rearrange("p (k r) -> p r k", r=8),
                                scalar=dA[:, ct:ct + 1],
                                in1=psb[:, :].rearrange("p (r k) -> p r k", r=8), op0=ALU.mult, op1=ALU.add),
                                reads=[pb[7], t_par] + t_su[ct], writes=t_su[ct])

                    emit_V(0)
                    for ci in range(8):
                        if ci + 1 < 8:
                            emit_V(ci + 1)
                        emit_scan(ci)
                        emit_y(ci)
                        if ci % 2 == 1:
                            emit_T(couples[ci][0])
                    if dbg_d is not None and b == 0:
                        S.dma("pool", lambda e: e.dma_start(out=dbg_d[:, :, :], in_=suT[:]), reads=[t for tl in t_su for t in tl])
                    sgf = [sbt(es, f"sgf{i}", [128, 512], F32) for i in range(2)]
                    tgf = [sbt(es, f"tgf{i}", [128, 512], F32) for i in range(2)]
                    t_sgf, t_tgf = TL(2), TL(2)
                    gi_ = 0
                    for c in range(4):
                        for mt in range(4):
                            i = gi_ % 2
                            gi_ += 1
                            ba, bb = 4 + i, 4 + (1 - i)
                            bka = 4 + i
                            bkb = i
                            for ct in range(4):
                                S.op("pe", lambda e, ct=ct, mt=mt, c=c, bka=bka: e.matmul(psf[bka][:], lhsT=gluw[:, ct, mt * 128:(mt + 1) * 128], rhs=suT[:, ct, c * 512:(c + 1) * 512], start=(ct == 0), stop=(ct == 3)),
                                     reads=[t_gw] + [t_su[ct][c]], writes=[pb[bka]], inc=(ct == 3))
                            for ct in range(4):
                                S.op("pe", lambda e, ct=ct, mt=mt, c=c, bkb=bkb: e.matmul(psf[bkb][:], lhsT=gluw[:, ct, (4 + mt) * 128:(5 + mt) * 128], rhs=suT[:, ct, c * 512:(c + 1) * 512], start=(ct == 0), stop=(ct == 3)),
                                     reads=[t_gw] + [t_su[ct][c]], writes=[pb[bkb]], inc=(ct == 3))
                            S.op("act", lambda e, i=i, mt=mt, bkb=bkb: e.activation(out=sgf[i][:], in_=psf[bkb][:], func=AF.Sigmoid, bias=glub[:, 4 + mt:5 + mt]),
                                 reads=[pb[bkb], t_par], writes=[t_sgf[i]])
                            S.op("dve", lambda e, i=i, mt=mt, bka=bka: e.scalar_tensor_tensor(out=tgf[i][:], in0=psf[bka][:], scalar=glub[:, mt:mt + 1], in1=sgf[i][:], op0=ALU.add, op1=ALU.mult),
                                 reads=[pb[bka], t_sgf[i], t_par], writes=[t_tgf[i]])
                            S.op("pool", lambda e, i=i, mt=mt, c=c: e.tensor_tensor(out=gcT[:, mt, c * 512:(c + 1) * 512], in0=tgf[i][:], in1=gcT[:, mt, c * 512:(c + 1) * 512], op=ALU.mult),
                                 reads=[t_tgf[i], t_gc[mt][c]], writes=[t_gc[mt][c]])
                    S.flush()
                if os.environ.get("KSTOP") == "S5":
                    return

                with ExitStack() as es:
                    diag = [sbt(es, f"diag{i}", [128, 31, 128], BF16) for i in range(4)]
                    t_dg = TL(4)
                    for ct in range(4):
                        S.op("dve", lambda e, ct=ct: e.tensor_tensor(out=diag[ct][:], in0=identf[:, :].unsqueeze(1).broadcast_to([128, 31, 128]),
                                                                   in1=dwT[:, ct, :].unsqueeze(2).broadcast_to([128, 31, 128]), op=ALU.mult),
                             reads=[t_const, t_par], writes=[t_dg[ct]])
                    cf2 = None
                    c162 = [sbt(es, f"c16{i}", [128, 4, 512], BF16) for i in range(2)]
                    c22 = [sbt(es, f"c2{i}", [128, 4, 512], BF16) for i in range(2)]
                    sn2 = [sbt(es, f"sn{i}", [128, 4, 512], BF16) for i in range(2)]
                    t_cf2, t_c162, t_c22, t_sn2 = [TL(4), TL(4)], [TL(4), TL(4)], [TL(4), TL(4)], [TL(4), TL(4)]
                    mean2 = [sbt(es, "mean_sb0", [128, 512], F32)] * 2
                    m22 = [sbt(es, "m2_0", [128, 512], F32)] * 2
                    rstd2 = [sbt(es, "rstd0", [128, 512], F32)] * 2
                    t_mean2, t_m22, t_rstd2 = [T()] * 2, [T()] * 2, [T()] * 2
                    u1 = [sbt(es, f"u1_{i}", [128, 512], F32) for i in range(2)]
                    t_u1 = TL(2)
                    ui_box = [0]
                    def conv_part(c):
                            cf, c16, c2, sn = c162[c % 2], c162[c % 2], c22[c % 2], sn2[c % 2]
                            t_cf, t_c16, t_c2, t_sn = t_c162[c % 2], t_c162[c % 2], t_c22[c % 2], t_sn2[c % 2]
                            mean_sb, m2, rstd = mean2[c % 2], m22[c % 2], rstd2[c % 2]
                            t_mean, t_m2, t_rstd = t_mean2[c % 2], t_m22[c % 2], t_rstd2[c % 2]
                            for ct in range(4):
                                bk = ct % 2
                                for k in range(31):
                                    S.op("pe", lambda e, cf=cf, c16=c16, c2=c2, sn=sn, mean_sb=mean_sb, m2=m2, rstd=rstd, k=k, ct=ct, c=c, bk=bk: e.matmul(psf[bk][:], lhsT=diag[ct][:, k, :], rhs=gpad[:, ct, 2 + c * 512 + k:2 + c * 512 + k + 512],
                                                                                       start=(k == 0), stop=(k == 30)),
                                         reads=[t_dg[ct], t_gp[ct]], writes=[pb[bk]], inc=(k == 30))
                                pass
                                S.op("act", lambda e, cf=cf, c16=c16, c2=c2, sn=sn, mean_sb=mean_sb, m2=m2, rstd=rstd, ct=ct, bk=bk: e.activation(out=c16[:, ct, :], in_=psf[bk][:], func=AF.Copy), reads=[pb[bk]], writes=[t_c16[ct]])
                                S.op("act", lambda e, cf=cf, c16=c16, c2=c2, sn=sn, mean_sb=mean_sb, m2=m2, rstd=rstd, ct=ct, bk=bk: e.activation(out=c2[:, ct, :], in_=psf[bk][:], func=AF.Square), reads=[pb[bk]], writes=[t_c2[ct]])

                    def ln_part(c):
                            cf, c16, c2, sn = c162[c % 2], c162[c % 2], c22[c % 2], sn2[c % 2]
                            t_cf, t_c16, t_c2, t_sn = t_c162[c % 2], t_c162[c % 2], t_c22[c % 2], t_sn2[c % 2]
                            mean_sb, m2, rstd = mean2[c % 2], m22[c % 2], rstd2[c % 2]
                            t_mean, t_m2, t_rstd = t_mean2[c % 2], t_m22[c % 2], t_rstd2[c % 2]
                            for ct in range(4):
                                S.op("pe", lambda e, cf=cf, c16=c16, c2=c2, sn=sn, mean_sb=mean_sb, m2=m2, rstd=rstd, ct=ct: e.matmul(psf[2][:], lhsT=ones512[:], rhs=c16[:, ct, :], start=(ct == 0), stop=(ct == 3)),
                                     reads=[t_c16[ct], t_par], writes=[pb[2]], inc=(ct == 3))
                            for ct in range(4):
                                S.op("pe", lambda e, cf=cf, c16=c16, c2=c2, sn=sn, mean_sb=mean_sb, m2=m2, rstd=rstd, ct=ct: e.matmul(psf[3][:], lhsT=ones512[:], rhs=c2[:, ct, :], start=(ct == 0), stop=(ct == 3)),
                                     reads=[t_c2[ct], t_par], writes=[pb[3]], inc=(ct == 3))
                            S.op("act", lambda e, cf=cf, c16=c16, c2=c2, sn=sn, mean_sb=mean_sb, m2=m2, rstd=rstd: e.activation(out=mean_sb[:], in_=psf[2][:], func=AF.Copy), reads=[pb[2]], writes=[t_mean])
                            S.op("dve", lambda e, cf=cf, c16=c16, c2=c2, sn=sn, mean_sb=mean_sb, m2=m2, rstd=rstd: e.tensor_tensor(out=m2[:], in0=mean_sb[:], in1=mean_sb[:], op=ALU.mult), reads=[t_mean], writes=[t_m2])
                            S.op("dve", lambda e, cf=cf, c16=c16, c2=c2, sn=sn, mean_sb=mean_sb, m2=m2, rstd=rstd: e.tensor_tensor(out=m2[:], in0=psf[3][:], in1=m2[:], op=ALU.subtract), reads=[pb[3], t_m2], writes=[t_m2])
                            S.op("act", lambda e, cf=cf, c16=c16, c2=c2, sn=sn, mean_sb=mean_sb, m2=m2, rstd=rstd: e.activation(out=m2[:], in_=m2[:], func=AF.Ln, bias=EPS), reads=[t_m2], writes=[t_m2])
                            S.op("act", lambda e, cf=cf, c16=c16, c2=c2, sn=sn, mean_sb=mean_sb, m2=m2, rstd=rstd: e.activation(out=rstd[:], in_=m2[:], func=AF.Exp, scale=-0.5), reads=[t_m2], writes=[t_rstd])
                            for ct in range(4):
                                i = ui_box[0] % 2
                                ui_box[0] += 1
                                S.op("dve", lambda e, cf=cf, c16=c16, c2=c2, sn=sn, mean_sb=mean_sb, m2=m2, rstd=rstd, ct=ct, i=i: e.tensor_tensor(out=u1[i][:], in0=cf[:, ct, :], in1=mean_sb[:], op=ALU.subtract), reads=[t_cf[ct], t_mean], writes=[t_u1[i]])
                                S.op("dve", lambda e, cf=cf, c16=c16, c2=c2, sn=sn, mean_sb=mean_sb, m2=m2, rstd=rstd, i=i: e.tensor_tensor(out=u1[i][:], in0=u1[i][:], in1=rstd[:], op=ALU.mult), reads=[t_u1[i], t_rstd], writes=[t_u1[i]])
                                S.op("act", lambda e, cf=cf, c16=c16, c2=c2, sn=sn, mean_sb=mean_sb, m2=m2, rstd=rstd, ct=ct, i=i: e.activation(out=sn[:, ct, :], in_=u1[i][:], func=AF.Silu, scale=lng[:, ct:ct + 1], bias=lnb[:, ct:ct + 1]),
                                     reads=[t_u1[i], t_par], writes=[t_sn[ct]])

                    def pw_part(c):
                            cf, c16, c2, sn = c162[c % 2], c162[c % 2], c22[c % 2], sn2[c % 2]
                            t_cf, t_c16, t_c2, t_sn = t_c162[c % 2], t_c162[c % 2], t_c22[c % 2], t_sn2[c % 2]
                            mean_sb, m2, rstd = mean2[c % 2], m22[c % 2], rstd2[c % 2]
                            t_mean, t_m2, t_rstd = t_mean2[c % 2], t_m22[c % 2], t_rstd2[c % 2]
                            for mt in range(4):
                                bk = 4 + mt % 2
                                for ct in range(4):
                                    S.op("pe", lambda e, cf=cf, c16=c16, c2=c2, sn=sn, mean_sb=mean_sb, m2=m2, rstd=rstd, ct=ct, mt=mt, bk=bk: e.matmul(psf[bk][:], lhsT=pwsb[:, ct, mt * 128:(mt + 1) * 128], rhs=sn[:, ct, :], start=(ct == 0), stop=(ct == 3)),
                                         reads=[t_pw, t_sn[ct]], writes=[pb[bk]], inc=(ct == 3))
                                S.op("dve", lambda e, cf=cf, c16=c16, c2=c2, sn=sn, mean_sb=mean_sb, m2=m2, rstd=rstd, mt=mt, c=c, bk=bk: e.tensor_tensor(out=gdT[:, mt, c * 512:(c + 1) * 512], in0=psf[bk][:], in1=gdT[:, mt, c * 512:(c + 1) * 512], op=ALU.mult),
                                     reads=[pb[bk], t_gd[mt][c]], writes=[t_gd[mt][c]])

                    conv_part(0)
                    for c in range(4):
                        ln_part(c)
                        if c + 1 < 4:
                            conv_part(c + 1)
                        pw_part(c)
                    S.flush()
                if os.environ.get("KSTOP") == "CV":
                    return
                with ExitStack() as es:
                    phase_outproj(es, o_w_out, post_g[1:2, :], out_d, b, gcT, gdT, TL(4), TL(4), wo1, t_wo1)
                    S.flush()

        nbr = int(os.environ.get("KNB", NB))
        for b in range(nbr):
            layer0(b)
        if n_layers >= 2 and os.environ.get("KLAYERS", "2") == "2":
            layer1_all(nbr)

    return nc


def layer1_host_layouts(inputs, f):
    g = lambda k: np.asarray(inputs[k][0], dtype=np.float32)
    m = {}
    m["o_w_in"] = f(g("o_w_in"))

    def A_gn(a):
        a = np.repeat(a[:, None, :], 16, axis=1).reshape(4, 8, 16, 64)
        return f(a.transpose(1, 2, 0, 3).reshape(128, 256))
    m["o_lamre_A"] = A_gn(g("o_lam_re"))
    m["o_lamim_A"] = A_gn(g("o_lam_im"))
    m["o_dt_A"] = A_gn(np.repeat(g("o_log_dt")[:, None], 64, axis=1))

    def A_b(bm):
        a = bm.transpose(0, 2, 1).reshape(4, 8, 16, 64)
        return f(a.transpose(1, 2, 0, 3).reshape(128, 256))
    m["o_bre_A"] = A_b(g("o_b_re"))
    m["o_bim_A"] = A_b(g("o_b_im"))

    def B_gn(a):
        return f(a.reshape(16, 2, 64).transpose(1, 2, 0).reshape(128, 16))
    m["o_lamre_B"] = B_gn(g("o_lam_re"))
    m["o_lamim_B"] = B_gn(g("o_lam_im"))
    m["o_dt_B"] = B_gn(np.repeat(g("o_log_dt")[:, None], 64, axis=1))

    def B_c(cm):
        return f(cm.reshape(16, 2, 16, 64).transpose(1, 3, 0, 2).reshape(128, 256))
    def B_b(bm):
        return f(bm.reshape(16, 2, 64, 16).transpose(1, 2, 0, 3).reshape(128, 256))
    m["o_bre_B"] = B_b(g("o_b_re"))
    m["o_bim_B"] = B_b(g("o_b_im"))
    m["o_cre_B"] = B_c(g("o_c_re"))
    m["o_cim_B"] = B_c(g("o_c_im"))
    m["o_dA"] = f(g("o_d").reshape(4, 128).T)
    m["o_glu_w"] = f(g("o_glu_w"))
    m["o_glub"] = f(g("o_glu_b").reshape(8, 128).T)
    m["o_dwT"] = f(g("o_dw").reshape(31, 4, 128).transpose(2, 1, 0))
    m["o_lng"] = f(g("o_ln_g").reshape(4, 128).T)
    m["o_lnb"] = f(g("o_ln_b").reshape(4, 128).T)
    m["o_pw"] = f(g("o_pw"))
    m["o_w_out"] = f(g("o_w_out"))
    return m


def make_in_maps(inputs):
    n = 8
    x = np.ascontiguousarray(inputs["x"], dtype=np.float32)
    consts = host_consts()
    f = lambda a: np.ascontiguousarray(a, dtype=np.float32)
    l1maps = layer1_host_layouts(inputs, f)
    in_maps = []
    for c in range(n):
        m = {"x": x[NB * c:NB * (c + 1)]}
        for k in ("pre_norm_g", "post_norm_g"):
            m[k] = f(inputs[k])
        m["e_w_in"] = f(inputs["e_w_in"][0])
        m["e_pool_w"] = f(inputs["e_pool_w"][0])
        m["e_pool_scale"] = f(np.asarray(inputs["e_pool_scale"][0], dtype=np.float32).reshape(4, 128).T)
        m["e_w_out"] = f(inputs["e_w_out"][0])
        m.update(l1maps)
        for k, v in consts.items():
            m["c_" + k] = v
        in_maps.append(m)
    return in_maps


def kernel(**inputs):
    nc = build()
    in_maps = make_in_maps(inputs)
    res = run_bass_kernel_spmd(nc, in_maps, core_ids=list(range(8)))
    return np.concatenate([r["out"] for r in res.results], axis=0)
```

```python
import math
import os
from contextlib import ExitStack
import numpy as np
import concourse.bass as bass
import concourse.mybir as mybir
from concourse.bass_utils import run_bass_kernel_spmd

F32 = mybir.dt.float32
BF16 = mybir.dt.bfloat16
I32 = mybir.dt.int32
ALU = mybir.AluOpType
AF = mybir.ActivationFunctionType
AX = mybir.AxisListType

D = 1024
SEQ = 2048
NB = 2
HD = 128
EPS = 1e-6
NEGB = -30000.0
S5L = 8
NCH = SEQ // S5L


class T:
    __slots__ = ("w", "r")

    def __init__(self):
        self.w = None
        self.r = {}


def TL(n):
    return [T() for _ in range(n)]


class Sched:
    ENG = ("pe", "act", "dve", "pool", "sp")

    def __init__(self, nc, es, n_dma=24):
        self.nc = nc
        self.ops = {e: [] for e in self.ENG}
        self.cnt = {e: 0 for e in self.ENG}
        self.seen = {e: {} for e in self.ENG}
        self.n_dma = n_dma
        self.dma_cnt = [0] * n_dma
        self.dma_rr2 = {"sp": 0, "pool": 0}
        self.semh = {}
        for e in self.ENG:
            self.semh[("c", e)] = es.enter_context(nc.semaphore(f"s_{e}"))
        for i in range(n_dma):
            self.semh[("d", i)] = es.enter_context(nc.semaphore(f"s_d{i}"))

    def _deps(self, eng, reads, writes):
        need = {}
        for t in reads:
            if t.w is not None:
                k, v = t.w
                if need.get(k, 0) < v:
                    need[k] = v
        for t in writes:
            if t.w is not None:
                k, v = t.w
                if need.get(k, 0) < v:
                    need[k] = v
            for k, v in t.r.items():
                if need.get(k, 0) < v:
                    need[k] = v
        waits = []
        sn = self.seen[eng]
        for k, v in need.items():
            if eng == "pe" and k == ("c", "pe"):
                continue
            if sn.get(k, 0) < v:
                waits.append((k, v))
                sn[k] = v
        return waits

    def _record(self, key, v, reads, writes):
        for t in reads:
            if t.r.get(key, 0) < v:
                t.r[key] = v
        for t in writes:
            t.w = (key, v)
            t.r = {}

    def op(self, eng, fn, reads=(), writes=(), inc=True):
        waits = self._deps(eng, reads, writes)
        key = ("c", eng)
        if inc:
            self.cnt[eng] += 1
            v = self.cnt[eng]
        else:
            v = self.cnt[eng] + 1
        self.ops[eng].append([waits, fn, key, 1 if inc else 0])
        self._record(key, v, reads, writes)

    def dma(self, eng, fn, reads=(), writes=()):
        half = self.n_dma // 2
        base = 0 if eng == "sp" else half
        j = self.dma_rr2[eng]
        self.dma_rr2[eng] = (j + 1) % half
        i = base + j
        waits = self._deps(eng, reads, writes)
        key = ("d", i)
        prev = self.dma_cnt[i]
        if prev > 0 and self.seen[eng].get(key, 0) < prev:
            waits.append((key, prev))
            self.seen[eng][key] = prev
        self.dma_cnt[i] += 16
        self.ops[eng].append([waits, fn, key, 16])
        self._record(key, self.dma_cnt[i], reads, writes)

    def flush(self):
        nc = self.nc
        for e in ("pe", "act", "dve", "pool"):
            if self.ops[e] and self.ops[e][-1][3] == 0:
                self.ops[e][-1][3] = 1
                self.cnt[e] += 1
        fin = [(("d", i), self.dma_cnt[i]) for i in range(self.n_dma) if self.dma_cnt[i] > 0]
        fin += [(("c", e), self.cnt[e]) for e in ("pe", "act", "dve", "pool") if self.cnt[e] > 0]
        ops = self.ops
        semh = self.semh
        with nc.Block() as block:
            def run(engname, eng):
                for waits, fn, key, inc in ops[engname]:
                    for k, v in waits:
                        eng.wait_ge(semh[k], v)
                    ins = fn(eng)
                    if inc:
                        ins.then_inc(semh[key], inc)
                if engname == "sp":
                    for k, v in fin:
                        eng.wait_ge(semh[k], v)

            @block.tensor
            def _(e):
                run("pe", e)

            @block.scalar
            def _(e):
                run("act", e)

            @block.vector
            def _(e):
                run("dve", e)

            @block.gpsimd
            def _(e):
                run("pool", e)

            @block.sync
            def _(e):
                run("sp", e)
        self.ops = {e: [] for e in self.ENG}
        for e in self.ENG:
            for k, v in fin:
                self.seen[e][k] = v


def host_consts():
    c = {}
    c["ident"] = np.eye(128, dtype=np.float32)
    c["ones"] = np.ones((128, 128), np.float32)
    kk = np.arange(128)[:, None]
    qq = np.arange(128)[None, :]
    c["trineg"] = np.where(kk <= qq, 0.0, NEGB).astype(np.float32)
    e8 = np.zeros((8, 8, 128), np.float32)
    for n in range(8):
        e8[n, n, :] = 1.0
    c["e8"] = e8.reshape(8, 1024)
    p32 = np.zeros((128, 128), np.float32)
    for m in range(32):
        p32[(m + 16) % 32, m] = 1.0
    c["p32"] = p32
    inv = np.power(500000.0, -np.arange(0, 32, 2, dtype=np.float32) / 32.0).astype(np.float32)
    pos = np.arange(SEQ, dtype=np.float32)
    ang = (pos[None, :] * inv[:, None]).astype(np.float32)
    cs = np.cos(ang).astype(np.float32)
    sn = np.sin(ang).astype(np.float32)
    c["ropeC"] = np.concatenate([cs, cs, np.ones((96, SEQ), np.float32)], 0)
    c["ropeS"] = np.concatenate([-sn, sn, np.zeros((96, SEQ), np.float32)], 0)
    negm = np.zeros((128, 8, 8), np.float32)
    bfix = np.zeros((128, 8, 8), np.float32)
    for i in range(8):
        B = 4 + i // 2
        for n in range(8):
            negm[:, i, n] = 0.0 if n < B else -1e30
            bfix[:, i, n] = 0.0 if n == B else NEGB
    c["negmask"] = negm.reshape(128, 64)
    c["biasfix"] = bfix.reshape(128, 64)
    rc = np.zeros((128, 4, 16), np.float32)
    for g, w in enumerate((2, 4, 8, 16)):
        for t in range(16):
            rc[:, g, t] = 1.0 / min(t + 1, w)
    c["rcnt"] = rc.reshape(128, 64)
    pp = np.arange(128)
    c["maskA"] = np.stack([((pp // 16) % 2 == 0), ((pp // 16) % 2 == 1)], 1).astype(np.float32)
    c["maskB"] = np.stack([(pp // 64 == 0), (pp // 64 == 1)], 1).astype(np.float32)
    return c


CONST_SHAPES = {k: v.shape for k, v in host_consts().items()}


def build(n_layers=2):
    nc = bass.Bass("TRN2", target_bir_lowering=False)

    def dr(name, shape, kind="ExternalInput", dt=F32):
        return nc.dram_tensor(name, list(shape), dt, kind=kind).ap()

    x_d = dr("x", [NB, SEQ, D])
    out_d = dr("out", [NB, SEQ, D], "ExternalOutput")
    pre_g = dr("pre_norm_g", [2, D])
    post_g = dr("post_norm_g", [2, D])
    e_w_in = dr("e_w_in", [D, 3072])
    e_pool_w = dr("e_pool_w", [4, 128, 128])
    e_pool_scale = dr("e_pool_scale", [128, 4])
    e_w_out = dr("e_w_out", [D, D])
    o_w_in = dr("o_w_in", [D, 2560])
    o_lamre_A = dr("o_lamre_A", [128, 256])
    o_lamim_A = dr("o_lamim_A", [128, 256])
    o_dt_A = dr("o_dt_A", [128, 256])
    o_bre_A = dr("o_bre_A", [128, 256])
    o_bim_A = dr("o_bim_A", [128, 256])
    o_bre_B = dr("o_bre_B", [128, 256])
    o_bim_B = dr("o_bim_B", [128, 256])
    o_lamre_B = dr("o_lamre_B", [128, 16])
    o_lamim_B = dr("o_lamim_B", [128, 16])
    o_dt_B = dr("o_dt_B", [128, 16])
    o_cre_B = dr("o_cre_B", [128, 256])
    o_cim_B = dr("o_cim_B", [128, 256])
    o_dA = dr("o_dA", [128, 4])
    o_glu_w = dr("o_glu_w", [512, 1024])
    o_glub = dr("o_glub", [128, 8])
    o_dwT = dr("o_dwT", [128, 4, 31])
    o_lng = dr("o_lng", [128, 4])
    o_lnb = dr("o_lnb", [128, 4])
    o_pw = dr("o_pw", [512, 512])
    o_w_out = dr("o_w_out", [D, D])
    dbg_d = dr("dbg", [128, 4, SEQ], "ExternalOutput") if os.environ.get("KDBG") else None
    cd = {k: dr("c_" + k, list(s)) for k, s in CONST_SHAPES.items()}

    with ExitStack() as top:
        S = Sched(nc, top)
        uid = [0]

        def sbt(es, name, shape, dt):
            uid[0] += 1
            return es.enter_context(nc.sbuf_tensor(f"{name}_{uid[0]}", list(shape), dt))
        psf = [top.enter_context(nc.psum_tensor(f"psf{i}", [128, 512], F32)) for i in range(7)]
        psb = top.enter_context(nc.psum_tensor("psb", [128, 1024], BF16))
        pb = TL(8)

        identb = sbt(top, "identb", [128, 128], BF16)
        identf = sbt(top, "identf", [128, 128], F32)
        onesb = sbt(top, "onesb", [128, 128], BF16)
        t_const = T()
        S.dma("pool", lambda e: e.dma_start(out=identb[:], in_=cd["ident"][:, :]), writes=[t_const])
        S.dma("sp", lambda e: e.dma_start(out=identf[:], in_=cd["ident"][:, :]), writes=[t_const])
        S.dma("pool", lambda e: e.dma_start(out=onesb[:], in_=cd["ones"][:, :]), writes=[t_const])

        def rms_stats(es_tag, src_aps, st, col0, reads, t_st, junk):
            for i, ap in enumerate(src_aps):
                S.op("act", lambda e, ap=ap, i=i: e.activation(out=junk[:, 0:ap.shape[1]], in_=ap, func=AF.Square,
                                                             accum_out=st[:, col0 + i:col0 + i + 1]),
                     reads=reads, writes=[t_st])
            if len(src_aps) == 2:
                S.op("dve", lambda e: e.tensor_tensor(out=st[:, col0:col0 + 1], in0=st[:, col0:col0 + 1],
                                                      in1=st[:, col0 + 1:col0 + 2], op=ALU.add), reads=[t_st], writes=[t_st])
            S.op("act", lambda e: e.activation(out=st[:, col0 + 2:col0 + 3], in_=st[:, col0:col0 + 1], func=AF.Sqrt,
                                               scale=1.0 / D, bias=EPS), reads=[t_st], writes=[t_st])
            S.op("dve", lambda e: e.reciprocal(out=st[:, col0 + 3:col0 + 4], in_=st[:, col0 + 2:col0 + 3]),
                 reads=[t_st], writes=[t_st])

        def phase_norm_T(es, src_d, b, gvec_d, hT, t_hT, nxb=2):
            xb = [sbt(es, f"xb{i}", [128, D], F32) for i in range(nxb)]
            hb = [sbt(es, f"hb{i}", [128, D], BF16) for i in range(2)]
            junk = sbt(es, "junkA", [128, D], BF16)
            gt = sbt(es, "gtA", [128, D], F32)
            st = sbt(es, "stA", [128, 16 * 4], F32)
            t_xb, t_hb, t_g = TL(nxb), TL(2), T()
            t_st = TL(16)
            S.dma("sp", lambda e: e.dma_start(out=gt[:], in_=gvec_d.partition_broadcast(128)), writes=[t_g])
            def stats(tt):
                i = tt % 2
                xi = tt % nxb
                S.dma("sp", lambda e: e.dma_start(out=xb[xi][:], in_=src_d[b, tt * 128:(tt + 1) * 128, :]),
                      writes=[t_xb[xi]])
                rms_stats(es, [xb[xi][:]], st, tt * 4, [t_xb[xi]], t_st[tt], junk)
                S.op("dve", lambda e: e.scalar_tensor_tensor(out=hb[i][:], in0=xb[xi][:], scalar=st[:, tt * 4 + 3:tt * 4 + 4],
                                                             in1=gt[:], op0=ALU.mult, op1=ALU.mult),
                     reads=[t_xb[xi], t_st[tt], t_g], writes=[t_hb[i]])

            def trans(tt):
                i = tt % 2
                for k in range(8):
                    S.op("pe", lambda e, k=k: e.transpose(out=psb[:, k * 128:(k + 1) * 128], in_=hb[i][:, k * 128:(k + 1) * 128],
                                                          identity=identb[:]),
                         reads=[t_hb[i], t_const], writes=[pb[7]], inc=(k == 7))
                S.op("act", lambda e: e.activation(out=hT[:, :, tt * 128:(tt + 1) * 128],
                                                   in_=psb[:, :].rearrange("p (k t) -> p k t", k=8), func=AF.Copy),
                     reads=[pb[7]], writes=[t_hT[tt]])

            stats(0)
            for tt in range(16):
                if tt + 1 < 16:
                    stats(tt + 1)
                trans(tt)

        def load_wo(wo, t_wo, w_out_d):
            for hf in range(2):
                S.dma("pool", lambda e, hf=hf: e.dma_start(out=wo[:, :, hf * 512:(hf + 1) * 512],
                                                          in_=w_out_d[:, hf * 512:(hf + 1) * 512].rearrange("(k p) n -> p k n", p=128)),
                      writes=[t_wo])

        def phase_outproj(es, w_out_d, gvec_d, res_d, b, gA, gB, t_gA, t_gB, wo, t_wo, tokfn=None):
            gt = sbt(es, "gtF", [128, D], F32)
            xr = [sbt(es, f"xr{i}", [128, D], F32) for i in range(3)]
            tm = [sbt(es, f"tmF{i}", [128, D], F32) for i in range(2)]
            junk = sbt(es, "junkF", [128, 512], BF16)
            st = sbt(es, "stF", [128, 16 * 4], F32)
            t_g = T()
            t_xr, t_tm, t_st = TL(3), TL(2), TL(16)
            S.dma("sp", lambda e: e.dma_start(out=gt[:], in_=gvec_d.partition_broadcast(128)), writes=[t_g])

            def load_xr(tt):
                j = tt % 3
                S.dma("sp", lambda e: e.dma_start(out=xr[j][:], in_=res_d[b, tt * 128:(tt + 1) * 128, :]), writes=[t_xr[j]])
            load_xr(0)
            load_xr(1)
            for tt in range(16):
                i = tt % 2
                xj = tt % 3
                bk = [psf[2 * i], psf[2 * i + 1]]
                tbk = [pb[2 * i], pb[2 * i + 1]]
                for hf in range(2):
                    for k in range(8):
                        g_ap = (gA if k < 4 else gB)
                        S.op("pe", lambda e, hf=hf, k=k, g_ap=g_ap, tt=tt, bk=bk: e.matmul(
                            bk[hf][:], lhsT=g_ap[:, k % 4, tt * 128:(tt + 1) * 128], rhs=wo[:, k, hf * 512:(hf + 1) * 512],
                            start=(k == 0), stop=(k == 7)),
                            reads=[(tokfn(k, tt // 4) if tokfn is not None else (t_gA if k < 4 else t_gB)[tt // 4]), t_wo], writes=[tbk[hf]], inc=(k == 7))
                if tt + 2 < 16:
                    load_xr(tt + 2)
                rms_stats(es, [bk[0][:], bk[1][:]], st, tt * 4, tbk, t_st[tt], junk)
                for hf in range(2):
                    S.op("dve", lambda e, hf=hf, tt=tt, i=i, bk=bk: e.scalar_tensor_tensor(
                        out=tm[i][:, hf * 512:(hf + 1) * 512], in0=bk[hf][:], scalar=st[:, tt * 4 + 3:tt * 4 + 4],
                        in1=gt[:, hf * 512:(hf + 1) * 512], op0=ALU.mult, op1=ALU.mult),
                        reads=[tbk[hf], t_st[tt], t_g], writes=[t_tm[i]])
                S.op("pool", lambda e, i=i, xj=xj: e.tensor_tensor(out=tm[i][:], in0=tm[i][:], in1=xr[xj][:], op=ALU.add),
                     reads=[t_tm[i], t_xr[xj]], writes=[t_tm[i]])
                S.dma("sp", lambda e, tt=tt, i=i: e.dma_start(out=out_d[b, tt * 128:(tt + 1) * 128, :], in_=tm[i][:]),
                      reads=[t_tm[i]])

        def layer0(b):
            with ExitStack() as L:
                qkT = sbt(L, "qkT", [128, 8, SEQ], BF16)
                Vt = sbt(L, "Vt", [128, 16, 512], BF16)
                gaT = sbt(L, "gaT", [128, 4, SEQ], BF16)
                gbT = sbt(L, "gbT", [128, 4, SEQ], BF16)
                t_qk = [TL(4) for _ in range(8)]
                t_V = TL(16)
                t_ga = [TL(4) for _ in range(4)]
                t_gb = [TL(4) for _ in range(4)]
                wo0 = sbt(L, "wo0", [128, 8, D], BF16)
                t_wo0 = T()

                with ExitStack() as es:
                    hT = sbt(es, "hT", [128, 8, SEQ], BF16)
                    t_hT = TL(16)
                    phase_norm_T(es, x_d, b, pre_g[0:1, :], hT, t_hT, nxb=4)
                    wb = [sbt(es, f"wb{i}", [128, 8, 512], BF16) for i in range(2)]
                    t_wb = TL(2)
                    ropeC = sbt(es, "ropeC", [128, SEQ], F32)
                    ropeS = sbt(es, "ropeS", [128, SEQ], F32)
                    p32 = sbt(es, "p32", [128, 128], BF16)
                    poolw = sbt(es, "poolw", [128, 4, 128], BF16)
                    pscale = sbt(es, "pscale", [128, 4], F32)
                    rcnt = sbt(es, "rcnt", [128, 64], F32)
                    t_c2 = T()
                    S.dma("sp", lambda e: e.dma_start(out=ropeC[:], in_=cd["ropeC"][:, :]), writes=[t_c2])
                    S.dma("sp", lambda e: e.dma_start(out=ropeS[:], in_=cd["ropeS"][:, :]), writes=[t_c2])
                    S.dma("pool", lambda e: e.dma_start(out=p32[:], in_=cd["p32"][:, :]), writes=[t_c2])
                    S.dma("pool", lambda e: e.dma_start(out=poolw[:], in_=e_pool_w.rearrange("g c d -> c g d")), writes=[t_c2])
                    S.dma("sp", lambda e: e.dma_start(out=pscale[:], in_=e_pool_scale[:, :]), writes=[t_c2])
                    S.dma("sp", lambda e: e.dma_start(out=rcnt[:], in_=cd["rcnt"][:, :]), writes=[t_c2])
                    r1 = [sbt(es, f"r1_{i}", [128, 512], F32) for i in range(2)]
                    r2 = [sbt(es, f"r2_{i}", [128, 512], F32) for i in range(2)]
                    t_r1, t_r2 = TL(2), TL(2)
                    ub = [sbt(es, f"ub{i}", [128, 528], F32) for i in range(2)]
                    sa = sbt(es, "sa", [128, 528], F32)
                    sb_ = sbt(es, "sb", [128, 528], F32)
                    mt = [sbt(es, f"mt{i}", [128, 512], BF16) for i in range(2)]
                    t_ub, t_mt = TL(2), TL(2)
                    t_sa, t_sb = T(), T()

                    order = [0, 1, 2, 3, 5, 4]
                    cnt = [0]
                    for oi in range(2):
                        S.dma("pool", lambda e, oi=oi: e.dma_start(out=wb[oi][:], in_=e_w_in[:, order[oi] * 512:(order[oi] + 1) * 512].rearrange("(k p) n -> p k n", p=128)),
                              writes=[t_wb[oi]])
                    ri = [0]
                    ksub = os.environ.get("KSUB", "")
                    for oi, gi in enumerate(order):
                        if ksub and oi >= int(ksub):
                            break
                        wbi = wb[oi % 2]
                        t_wbi = t_wb[oi % 2]
                        if gi in (0, 1):
                            pend = [None]

                            def rope_tail(j, c, r, pbk):
                                def f():
                                    S.op("pe", lambda e: e.matmul(psf[pbk][:], lhsT=p32[:], rhs=qkT[:, j, c * 512:(c + 1) * 512], start=True, stop=True),
                                         reads=[t_qk[j][c], t_c2], writes=[pb[pbk]])
                                    S.op("dve", lambda e: e.tensor_tensor(out=r2[r][:], in0=psf[pbk][:], in1=ropeS[:, c * 512:(c + 1) * 512], op=ALU.mult),
                                         reads=[pb[pbk], t_c2], writes=[t_r2[r]])
                                    S.op("dve", lambda e: e.tensor_tensor(out=qkT[:, j, c * 512:(c + 1) * 512], in0=r1[r][:], in1=r2[r][:], op=ALU.add),
                                         reads=[t_r1[r], t_r2[r]], writes=[t_qk[j][c]])
                                return f
                            for m in range(4):
                                j = gi * 4 + m
                                for c in range(4):
                                    bi = cnt[0] % 3
                                    cnt[0] += 1
                                    for k in range(8):
                                        S.op("pe", lambda e, k=k, bi=bi, m=m, c=c, wbi=wbi: e.matmul(
                                            psf[bi][:], lhsT=wbi[:, k, m * 128:(m + 1) * 128], rhs=hT[:, k, c * 512:(c + 1) * 512],
                                            start=(k == 0), stop=(k == 7)),
                                            reads=[t_wbi] + t_hT[4 * c:4 * c + 4], writes=[pb[bi]], inc=(k == 7))
                                    if pend[0] is not None:
                                        pend[0]()
                                    S.op("act", lambda e, bi=bi, j=j, c=c: e.activation(out=qkT[:, j, c * 512:(c + 1) * 512], in_=psf[bi][:], func=AF.Copy),
                                         reads=[pb[bi]], writes=[t_qk[j][c]])
                                    r = ri[0] % 2
                                    ri[0] += 1
                                    pbk = 3 + r
                                    S.op("dve", lambda e, bi=bi, c=c, r=r: e.tensor_tensor(out=r1[r][:], in0=psf[bi][:], in1=ropeC[:, c * 512:(c + 1) * 512], op=ALU.mult),
                                         reads=[pb[bi], t_c2, t_qk[j][c]], writes=[t_r1[r]])
                                    pend[0] = rope_tail(j, c, r, pbk)
                            pend[0]()
                        elif gi == 2:
                            for tt in range(16):
                                bi = cnt[0] % 3
                                cnt[0] += 1
                                for k in range(8):
                                    S.op("pe", lambda e, k=k, bi=bi, tt=tt, wbi=wbi: e.matmul(
                                        psf[bi][:], lhsT=hT[:, k, tt * 128:(tt + 1) * 128], rhs=wbi[:, k, :], start=(k == 0), stop=(k == 7)),
                                        reads=[t_wbi, t_hT[tt]], writes=[pb[bi]], inc=(k == 7))
                                S.op("dve", lambda e, bi=bi, tt=tt: e.tensor_copy(out=Vt[:, tt, :], in_=psf[bi][:]), reads=[pb[bi]], writes=[t_V[tt]])
                        elif gi in (3, 5):
                            dst, t_dst = (gaT, t_ga) if gi == 3 else (gbT, t_gb)
                            for m in range(4):
                                for c in range(4):
                                    bi = cnt[0] % 3
                                    cnt[0] += 1
                                    for k in range(8):
                                        S.op("pe", lambda e, k=k, bi=bi, m=m, c=c, wbi=wbi: e.matmul(
                                            psf[bi][:], lhsT=wbi[:, k, m * 128:(m + 1) * 128], rhs=hT[:, k, c * 512:(c + 1) * 512],
                                            start=(k == 0), stop=(k == 7)),
                                            reads=[t_wbi] + t_hT[4 * c:4 * c + 4], writes=[pb[bi]], inc=(k == 7))
                                    S.op("act", lambda e, bi=bi, m=m, c=c, dst=dst: e.activation(out=dst[:, m, c * 512:(c + 1) * 512], in_=psf[bi][:], func=AF.Silu),
                                         reads=[pb[bi]], writes=[t_dst[m][c]])
                        else:
                            ppend = [None]
                            for g in range(4):
                                w = (2, 4, 8, 16)[g]
                                nlev = g + 1
                                for c in range(4):
                                    bi = cnt[0] % 3
                                    cnt[0] += 1
                                    for k in range(8):
                                        S.op("pe", lambda e, k=k, bi=bi, g=g, c=c, wbi=wbi: e.matmul(
                                            psf[bi][:], lhsT=wbi[:, k, g * 128:(g + 1) * 128], rhs=hT[:, k, c * 512:(c + 1) * 512],
                                            start=(k == 0), stop=(k == 7)),
                                            reads=[t_wbi] + t_hT[4 * c:4 * c + 4], writes=[pb[bi]], inc=(k == 7))
                                    if ppend[0] is not None:
                                        ppend[0]()
                                        ppend[0] = None
                                    u = ub[c % 2]
                                    up = ub[(c + 1) % 2]
                                    if c == 0:
                                        S.op("pool", lambda e, u=u: e.memset(u[:, 0:16], 0.0), writes=[t_ub[c % 2]])
                                    else:
                                        S.op("pool", lambda e, u=u, up=up: e.tensor_copy(out=u[:, 0:16], in_=up[:, 512:528]),
                                             reads=[t_ub[(c + 1) % 2]], writes=[t_ub[c % 2]])
                                    S.op("act", lambda e, bi=bi, u=u: e.activation(out=u[:, 16:528], in_=psf[bi][:], func=AF.Copy),
                                         reads=[pb[bi]], writes=[t_ub[c % 2]])
                                    src, t_src = u, t_ub[c % 2]
                                    for lv in range(nlev):
                                        sh = 1 << lv
                                        lo = 2 * sh
                                        dstb, t_d = (sa, t_sa) if lv % 2 == 0 else (sb_, t_sb)
                                        S.op("dve", lambda e, src=src, dstb=dstb, lo=lo, sh=sh: e.tensor_tensor(
                                            out=dstb[:, lo:528], in0=src[:, lo:528], in1=src[:, lo - sh:528 - sh], op=ALU.add),
                                            reads=[t_src], writes=[t_d])
                                        src, t_src = dstb, t_d
                                    mi = c % 2
                                    S.op("dve", lambda e, src=src, u=u, mi=mi, w=w: e.scalar_tensor_tensor(
                                        out=mt[mi][:], in0=src[:, 16:528], scalar=1.0 / w, in1=u[:, 16:528], op0=ALU.mult, op1=ALU.subtract),
                                        reads=[t_src, t_ub[c % 2]], writes=[t_mt[mi]])
                                    if c == 0:
                                        S.op("dve", lambda e, src=src, g=g: e.tensor_tensor(out=src[:, 0:16], in0=src[:, 16:32], in1=rcnt[:, g * 16:(g + 1) * 16], op=ALU.mult),
                                             reads=[t_src, t_c2], writes=[t_src])
                                        S.op("dve", lambda e, src=src, u=u, mi=mi: e.tensor_tensor(out=mt[mi][:, 0:16], in0=src[:, 0:16], in1=u[:, 16:32], op=ALU.subtract),
                                             reads=[t_src, t_ub[c % 2]], writes=[t_mt[mi]])
                                    pbk = 3 + (c % 2)

                                    def pool_tail(g=g, c=c, mi=mi, pbk=pbk):
                                        S.op("pe", lambda e: e.matmul(psf[pbk][:], lhsT=poolw[:, g, :], rhs=mt[mi][:], start=True, stop=True),
                                             reads=[t_mt[mi], t_c2], writes=[pb[pbk]])
                                        S.op("dve", lambda e: e.scalar_tensor_tensor(
                                            out=gbT[:, g, c * 512:(c + 1) * 512], in0=psf[pbk][:], scalar=pscale[:, g:g + 1],
                                            in1=gbT[:, g, c * 512:(c + 1) * 512], op0=ALU.mult, op1=ALU.mult),
                                            reads=[pb[pbk], t_c2, t_gb[g][c]], writes=[t_gb[g][c]])
                                    ppend[0] = pool_tail
                        if gi == 4 and ppend[0] is not None:
                            ppend[0]()
                            ppend[0] = None
                        if oi + 2 < len(order) and not ksub:
                            gn = order[oi + 2]
                            S.dma("pool", lambda e, gn=gn, wbi=wbi: e.dma_start(out=wbi[:], in_=e_w_in[:, gn * 512:(gn + 1) * 512].rearrange("(k p) n -> p k n", p=128)),
                                  writes=[t_wbi])
                    S.flush()
                if os.environ.get("KSTOP") == "AB":
                    return

                with ExitStack() as es:
                    load_wo(wo0, t_wo0, e_w_out)
                    e8 = sbt(es, "e8", [8, 1024], BF16)
                    trineg = sbt(es, "trineg", [128, 128], BF16)
                    negmask = sbt(es, "negmask", [128, 64], F32)
                    biasfix = sbt(es, "biasfix", [128, 64], F32)
                    t_c3 = T()
                    S.dma("pool", lambda e: e.dma_start(out=e8[:], in_=cd["e8"][:, :]), writes=[t_c3])
                    S.dma("pool", lambda e: e.dma_start(out=trineg[:], in_=cd["trineg"][:, :]), writes=[t_c3])
                    S.dma("sp", lambda e: e.dma_start(out=negmask[:], in_=cd["negmask"][:, :]), writes=[t_c3])
                    S.dma("sp", lambda e: e.dma_start(out=biasfix[:], in_=cd["biasfix"][:, :]), writes=[t_c3])
                    Mrow = sbt(es, "Mrow", [8, 4, SEQ], BF16)
                    t_M = TL(4)
                    stab4 = [sbt(es, f"stab{h}", [8, SEQ], F32) for h in range(4)]
                    sqt = [sbt(es, f"sqt{i}", [128, 512], BF16) for i in range(8)]
                    t_sq = TL(8)
                    kmx4 = [sbt(es, f"kmx{h}", [8, 8], F32) for h in range(4)]
                    kb324 = [sbt(es, f"kb32{h}", [128, 8], F32) for h in range(4)]
                    kbar4 = [sbt(es, f"kbar{h}", [128, 8], BF16) for h in range(4)]
                    gm4 = [sbt(es, f"gm{h}", [128, 64], F32) for h in range(4)]
                    top84 = [sbt(es, f"top8{h}", [128, 64], F32) for h in range(4)]
                    sel4 = [sbt(es, f"sel{h}", [128, 64], F32) for h in range(4)]
                    t_stab4, t_kmx4, t_kb4, t_gm4, t_top4, t_sel4 = TL(4), TL(4), TL(4), TL(4), TL(4), TL(4)

                    def prep_ops(h):
                        L_ = []
                        add = lambda eng, fn, reads=(), writes=(), **kw: L_.append((eng, fn, list(reads), list(writes), kw))
                        stab, kmx, kb32, kbar, gm, top8, sel = stab4[h], kmx4[h], kb324[h], kbar4[h], gm4[h], top84[h], sel4[h]
                        t_stab, t_kmx, t_kb, t_gm, t_top, t_sel = t_stab4[h], t_kmx4[h], t_kb4[h], t_gm4[h], t_top4[h], t_sel4[h]
                        nb = 3 + h
                        tb = [h % 3, (h + 1) % 3]
                        for c in range(4):
                            i = 2 * h
                            add("act", lambda e, i=i, c=c: e.activation(out=sqt[i][:], in_=qkT[:, 4 + h, c * 512:(c + 1) * 512], func=AF.Square),
                                [t_qk[4 + h][c]], [t_sq[i]])
                            add("pe", lambda e, i=i: e.matmul(psf[nb][0:8, :], lhsT=onesb[:, 0:8], rhs=sqt[i][:], start=True, stop=True),
                                [t_sq[i], t_const], [pb[nb]])
                            add("dve", lambda e, c=c: e.tensor_reduce(out=kmx[:, c:c + 1], in_=psf[nb][0:8, :], axis=AX.X, op=ALU.max),
                                [pb[nb]], [t_kmx])
                        add("dve", lambda e: e.tensor_reduce(out=kmx[:, 4:5], in_=kmx[:, 0:4], axis=AX.X, op=ALU.max), [t_kmx], [t_kmx])
                        for c in range(4):
                            i = 2 * h + 1
                            add("act", lambda e, i=i, c=c: e.activation(out=sqt[i][:], in_=qkT[:, h, c * 512:(c + 1) * 512], func=AF.Square),
                                [t_qk[h][c]], [t_sq[i]])
                            add("pe", lambda e, i=i: e.matmul(psf[nb][0:8, :], lhsT=onesb[:, 0:8], rhs=sqt[i][:], start=True, stop=True),
                                [t_sq[i], t_const], [pb[nb]])
                            add("act", lambda e, c=c: e.activation(out=stab[:, c * 512:(c + 1) * 512], in_=psf[nb][0:8, :], func=AF.Sqrt, scale=kmx[:, 4:5]),
                                [pb[nb], t_kmx], [t_stab])
                        add("dve", lambda e: e.tensor_reduce(out=kb32[:], in_=qkT[:, 4 + h, :].rearrange("p (n s) -> p n s", s=256), axis=AX.X, op=ALU.add),
                            t_qk[4 + h], [t_kb])
                        add("dve", lambda e: e.tensor_scalar(out=kbar[:], in0=kb32[:], scalar1=1.0 / 256, scalar2=None, op0=ALU.mult), [t_kb], [t_kb])
                        for i8 in range(8):
                            add("pe", lambda e, i8=i8: e.matmul(psf[nb][:, i8 * 8:(i8 + 1) * 8], lhsT=qkT[:, h, (8 + i8) * 128:(9 + i8) * 128], rhs=kbar[:], start=True, stop=True),
                                [t_qk[h][2 + i8 // 4], t_kb], [pb[nb]], inc=(i8 == 7))
                        add("dve", lambda e: e.tensor_tensor(out=gm[:], in0=psf[nb][:, 0:64], in1=negmask[:], op=ALU.add), [pb[nb], t_c3], [t_gm])
                        for i8 in range(8):
                            add("dve", lambda e, i8=i8: e.max(out=top8[:, i8 * 8:(i8 + 1) * 8], in_=gm[:, i8 * 8:(i8 + 1) * 8]), [t_gm], [t_top])
                        for i8 in range(8):
                            add("dve", lambda e, i8=i8: e.tensor_scalar(out=sel[:, i8 * 8:(i8 + 1) * 8], in0=gm[:, i8 * 8:(i8 + 1) * 8],
                                                                      scalar1=top8[:, i8 * 8 + 2:i8 * 8 + 3], scalar2=None, op0=ALU.is_ge),
                                [t_gm, t_top], [t_sel])
                        add("dve", lambda e: e.scalar_tensor_tensor(out=sel[:], in0=sel[:], scalar=-NEGB, in1=biasfix[:], op0=ALU.mult, op1=ALU.add),
                            [t_sel, t_c3], [t_sel])
                        add("act", lambda e: e.activation(out=Mrow[:, h, 0:1024], in_=stab[:, 0:1024], func=AF.Copy, scale=-1.0), [t_stab], [t_M[h]])
                        for hf in range(2):
                            for i4 in range(4):
                                i8 = hf * 4 + i4
                                add("pe", lambda e, i8=i8, i4=i4: e.transpose(out=psf[nb][0:8, i4 * 128:(i4 + 1) * 128], in_=sel[:, i8 * 8:(i8 + 1) * 8], identity=identf[:]),
                                    [t_sel, t_const], [pb[nb]], inc=(i4 == 3))
                            add("dve", lambda e, hf=hf: e.tensor_tensor(out=Mrow[:, h, 1024 + hf * 512:1536 + hf * 512], in0=psf[nb][0:8, :],
                                                                      in1=stab[:, 1024 + hf * 512:1536 + hf * 512], op=ALU.subtract),
                                [pb[nb], t_stab], [t_M[h]])
                        return L_

                    plists = [prep_ops(h) for h in range(4)]
                    for i in range(max(len(l) for l in plists)):
                        for l in plists:
                            if i < len(l):
                                eng, fn, rd, wr, kw = l[i]
                                S.op(eng, fn, reads=rd, writes=wr, **kw)

                    PT = [sbt(es, f"PT{i}", [128, 512], BF16) for i in range(3)]
                    t_PT = TL(3)
                    lns = sbt(es, "lns", [128, 512], F32)
                    rinv = sbt(es, "rinv", [128, 512], F32)
                    ot = sbt(es, "ot", [128, 512], F32)
                    t_lns, t_rinv, t_ot = T(), T(), T()
                    scale = 1.0 / math.sqrt(HD)
                    items = [(h, qc, kt) for h in range(4) for qc in range(4) for kt in range(4 * qc + 4)]

                    def emit_S(idx):
                        h, qc, kt = items[idx]
                        sb_i = idx % 2
                        off = max(0, kt * 128 - qc * 512)
                        q0 = qc * 512 + off
                        q1 = (qc + 1) * 512
                        n = kt // 2
                        diag = kt >= 4 * qc
                        S.op("pe", lambda e: e.matmul(psf[sb_i][:, off:512], lhsT=qkT[:, 4 + h, kt * 128:(kt + 1) * 128], rhs=qkT[:, h, q0:q1], start=True, stop=False),
                             reads=[t_qk[4 + h][kt // 4], t_qk[h][qc]], writes=[pb[sb_i]])
                        S.op("pe", lambda e: e.matmul(psf[sb_i][:, off:512], lhsT=e8[:, n * 128:(n + 1) * 128], rhs=Mrow[:, h, q0:q1], start=False, stop=(not diag)),
                             reads=[t_M[h], t_c3], writes=[pb[sb_i]])
                        if diag:
                            S.op("pe", lambda e: e.matmul(psf[sb_i][:, off:off + 128], lhsT=identb[:], rhs=trineg[:], start=False, stop=True),
                                 reads=[t_c3, t_const], writes=[pb[sb_i]])
                        pi = idx % 3
                        S.op("act", lambda e: e.activation(out=PT[pi][:, off:512], in_=psf[sb_i][:, off:512], func=AF.Exp, scale=scale),
                             reads=[pb[sb_i]], writes=[t_PT[pi]])

                    def emit_PV(idx):
                        h, qc, kt = items[idx]
                        off = max(0, kt * 128 - qc * 512)
                        pi = idx % 3
                        par = (h * 4 + qc) % 2
                        ob, sbk = 2 + par, 4 + par
                        last = (kt == 4 * qc + 3)
                        S.op("pe", lambda e: e.matmul(psf[ob][:, off:512], lhsT=Vt[:, kt, h * 128:(h + 1) * 128], rhs=PT[pi][:, off:512], start=(kt == 0), stop=last),
                             reads=[t_V[kt], t_PT[pi]], writes=[pb[ob]])
                        S.op("pe", lambda e: e.matmul(psf[sbk][:, off:512], lhsT=onesb[:], rhs=PT[pi][:, off:512], start=(kt == 0), stop=last),
                             reads=[t_const, t_PT[pi]], writes=[pb[sbk]])
                        if last:
                            S.op("act", lambda e: e.activation(out=lns[:], in_=psf[sbk][:], func=AF.Ln), reads=[pb[sbk]], writes=[t_lns])
                            S.op("act", lambda e: e.activation(out=rinv[:], in_=lns[:], func=AF.Exp, scale=-1.0), reads=[t_lns], writes=[t_rinv])
                            S.op("dve", lambda e: e.tensor_tensor(out=ot[:], in0=psf[ob][:], in1=rinv[:], op=ALU.mult), reads=[pb[ob], t_rinv], writes=[t_ot])
                            S.op("pool", lambda e: e.tensor_tensor(out=gaT[:, h, qc * 512:(qc + 1) * 512], in0=ot[:], in1=gaT[:, h, qc * 512:(qc + 1) * 512], op=ALU.mult),
                                 reads=[t_ot, t_ga[h][qc]], writes=[t_ga[h][qc]])

                    emit_S(0)
                    for idx in range(len(items)):
                        if idx + 1 < len(items):
                            emit_S(idx + 1)
                        emit_PV(idx)
                    phase_outproj(es, e_w_out, post_g[0:1, :], x_d, b, gaT, gbT, None, None, wo0, t_wo0,
                                  tokfn=lambda k, c: (t_ga[k][c] if k < 4 else t_gb[k - 4][c]))
                    S.flush()

        L1 = top

        def layer1_all(nb):
            with ExitStack() as P1:
                wv_sb = sbt(P1, "wv_sb", [128, 4, 8, 2, 128], BF16)
                toep_sb = sbt(P1, "toep_sb", [128, 4, 8, 256], BF16)
                w3_sb = sbt(P1, "w3_sb", [128, 16, 2, 256], BF16)
                pw_tab = sbt(P1, "pw_tab", [128, 16, 8, 3], F32)
                dA = sbt(P1, "dA", [128, 4], F32)
                glub = sbt(P1, "glub", [128, 8], F32)
                lng = sbt(P1, "lng", [128, 4], F32)
                lnb = sbt(P1, "lnb", [128, 4], F32)
                dwT = sbt(P1, "dwT", [128, 4, 31], F32)
                ones512 = sbt(P1, "ones512", [128, 128], BF16)
                t_par = T()
                for dst, src in ((dA, o_dA), (glub, o_glub), (lng, o_lng), (lnb, o_lnb)):
                    S.dma("sp", lambda e, dst=dst, src=src: e.dma_start(out=dst[:], in_=src[:, :]), writes=[t_par])
                S.dma("sp", lambda e: e.dma_start(out=dwT[:], in_=o_dwT[:, :, :]), writes=[t_par])
                S.op("act", lambda e: e.activation(out=ones512[:], in_=onesb[:], func=AF.Copy, scale=1.0 / 512), reads=[t_const], writes=[t_par])

                with ExitStack() as es:
                    tkA, tkB, t_m = T(), T(), T()
                    listA, listB = [], []
                    cur = {"eng": "dve", "tk": tkA, "list": listA}

                    def vop(fn, eng=None):
                        cur["list"].append(("op", eng or cur["eng"], fn, [cur["tk"], t_par, t_m], [cur["tk"]], {}))

                    def pop(fn, eng=None):
                        cur["list"].append(("op", eng or cur["eng"], fn, [cur["tk"], t_par, t_m], [T()], {}))

                    def tt(o, a, b_, op, eng=None):
                        vop(lambda e: e.tensor_tensor(out=o, in0=a, in1=b_, op=op), eng)

                    def ld(name, src, shape, tok=None):
                        t = sbt(es, name, shape, F32)
                        if tok is not None:
                            S.dma("sp", lambda e: e.dma_start(out=t[:], in_=src), writes=[tok])
                        else:
                            cur["list"].append(("dma", "sp", lambda e: e.dma_start(out=t[:], in_=src), [], [cur["tk"]], {}))
                        return t

                    def compute_a(tag, lr, li, dtl, n):
                        mk = lambda nm: sbt(es, f"{tag}_{nm}", [128, n], F32)
                        dtv, x1, mg, th, u, r, sn_, cs_, ar, ai = [mk(k) for k in ("dt", "x1", "mg", "th", "u", "r", "sn", "cs", "ar", "ai")]
                        ui = sbt(es, f"{tag}_ui", [128, n], I32)
                        vop(lambda e: e.activation(out=dtv[:], in_=dtl[:], func=AF.Exp), "act")
                        tt(x1[:], lr[:], dtv[:], ALU.mult)
                        vop(lambda e: e.activation(out=mg[:], in_=x1[:], func=AF.Exp), "act")
                        tt(th[:], li[:], dtv[:], ALU.mult)
                        for shift, dst in ((0.0, sn_), (math.pi / 2, cs_)):
                            vop(lambda e, shift=shift: e.tensor_scalar(out=r[:], in0=th[:], scalar1=shift, scalar2=None, op0=ALU.add))
                            vop(lambda e: e.tensor_copy(out=x1[:], in_=r[:]))
                            for jj in range(1, 6):
                                vop(lambda e, jj=jj: e.tensor_scalar(out=u[:], in0=x1[:], scalar1=(2 * jj - 1) * math.pi, scalar2=-2 * math.pi, op0=ALU.is_ge, op1=ALU.mult))
                                tt(r[:], r[:], u[:], ALU.add)
                            vop(lambda e, dst=dst: e.activation(out=dst[:], in_=r[:], func=AF.Sin), "act")
                        tt(ar[:], mg[:], cs_[:], ALU.mult)
                        tt(ai[:], mg[:], sn_[:], ALU.mult)
                        return ar, ai

                    lrA = ld("lrA", o_lamre_A[:, :], [128, 256])
                    liA = ld("liA", o_lamim_A[:, :], [128, 256])
                    dtA = ld("dtA", o_dt_A[:, :], [128, 256])
                    brA = ld("brA", o_bre_A[:, :], [128, 256])
                    biA = ld("biA", o_bim_A[:, :], [128, 256])
                    mA = ld("mA", cd["maskA"][:, :], [128, 2], t_m)
                    mB = ld("mB", cd["maskB"][:, :], [128, 2], t_m)
                    arA, aiA = compute_a("A", lrA, liA, dtA, 256)
                    mk = lambda nm, n=256: sbt(es, nm, [128, n], F32)
                    nr, den, t1, t2, fr, fi = [mk(k) for k in ("nr", "den", "t1", "t2", "fr", "fi")]
                    vop(lambda e: e.tensor_scalar(out=nr[:], in0=arA[:], scalar1=-1.0, scalar2=None, op0=ALU.add))
                    tt(t1[:], lrA[:], lrA[:], ALU.mult)
                    tt(t2[:], liA[:], liA[:], ALU.mult)
                    tt(den[:], t1[:], t2[:], ALU.add)
                    vop(lambda e: e.reciprocal(out=den[:], in_=den[:]), "dve")
                    tt(t1[:], nr[:], lrA[:], ALU.mult)
                    tt(t2[:], aiA[:], liA[:], ALU.mult)
                    tt(t1[:], t1[:], t2[:], ALU.add)
                    tt(fr[:], t1[:], den[:], ALU.mult)
                    tt(t1[:], aiA[:], lrA[:], ALU.mult)
                    tt(t2[:], nr[:], liA[:], ALU.mult)
                    tt(t1[:], t1[:], t2[:], ALU.subtract)
                    tt(fi[:], t1[:], den[:], ALU.mult)
                    Gall = sbt(es, "Gall", [128, 8, 2, 256], F32)
                    tt(t1[:], fr[:], brA[:], ALU.mult)
                    tt(t2[:], fi[:], biA[:], ALU.mult)
                    tt(Gall[:, 0, 0, :], t1[:], t2[:], ALU.subtract)
                    tt(t1[:], fr[:], biA[:], ALU.mult)
                    tt(t2[:], fi[:], brA[:], ALU.mult)
                    tt(Gall[:, 0, 1, :], t1[:], t2[:], ALU.add)
                    for m in range(7):
                        tt(t1[:], Gall[:, m, 0, :], arA[:], ALU.mult)
                        tt(t2[:], Gall[:, m, 1, :], aiA[:], ALU.mult)
                        tt(Gall[:, m + 1, 0, :], t1[:], t2[:], ALU.subtract)
                        tt(t1[:], Gall[:, m, 0, :], aiA[:], ALU.mult)
                        tt(t2[:], Gall[:, m, 1, :], arA[:], ALU.mult)
                        tt(Gall[:, m + 1, 1, :], t1[:], t2[:], ALU.add)
                    for s in range(8):
                        for ri in range(2):
                            for gi in range(2):
                                pop(lambda e, s=s, ri=ri, gi=gi: e.tensor_scalar(
                                    out=wv_sb[:, :, s, ri, gi * 64:(gi + 1) * 64],
                                    in0=Gall[:, 7 - s, ri, :].rearrange("p (c n) -> p c n", c=4),
                                    scalar1=mA[:, gi:gi + 1], scalar2=None, op0=ALU.mult))
                    Kall = sbt(es, "Kall", [128, 4, 8, 16], F32)
                    cur["eng"], cur["tk"], cur["list"] = "dve", tkB, listB
                    lrB = ld("lrB", o_lamre_B[:, :], [128, 16])
                    liB = ld("liB", o_lamim_B[:, :], [128, 16])
                    dtB = ld("dtB", o_dt_B[:, :], [128, 16])
                    crB = ld("crB", o_cre_B[:, :], [128, 256])
                    ciB = ld("ciB", o_cim_B[:, :], [128, 256])
                    arB, aiB = compute_a("B", lrB, liB, dtB, 16)
                    PB = sbt(es, "PB", [128, 8, 2, 16], F32)
                    s1 = sbt(es, "s1", [128, 16], F32)
                    s2 = sbt(es, "s2", [128, 16], F32)
                    vop(lambda e: e.tensor_copy(out=PB[:, 0, 0, :], in_=arB[:]))
                    vop(lambda e: e.tensor_copy(out=PB[:, 0, 1, :], in_=aiB[:]))
                    for r in range(7):
                        tt(s1[:], PB[:, r, 0, :], arB[:], ALU.mult)
                        tt(s2[:], PB[:, r, 1, :], aiB[:], ALU.mult)
                        tt(PB[:, r + 1, 0, :], s1[:], s2[:], ALU.subtract)
                        tt(s1[:], PB[:, r, 0, :], aiB[:], ALU.mult)
                        tt(s2[:], PB[:, r, 1, :], arB[:], ALU.mult)
                        tt(PB[:, r + 1, 1, :], s1[:], s2[:], ALU.add)
                    brB = ld("brB", o_bre_B[:, :], [128, 256])
                    biB = ld("biB", o_bim_B[:, :], [128, 256])
                    mkb = lambda nm: sbt(es, nm, [128, 16], F32)
                    nrB, denB, x1B, x2B, frB, fiB = [mkb(k) for k in ("nrB", "denB", "x1B", "x2B", "frB", "fiB")]
                    vop(lambda e: e.tensor_scalar(out=nrB[:], in0=arB[:], scalar1=-1.0, scalar2=None, op0=ALU.add))
                    tt(x1B[:], lrB[:], lrB[:], ALU.mult)
                    tt(x2B[:], liB[:], liB[:], ALU.mult)
                    tt(denB[:], x1B[:], x2B[:], ALU.add)
                    vop(lambda e: e.reciprocal(out=denB[:], in_=denB[:]), "dve")
                    tt(x1B[:], nrB[:], lrB[:], ALU.mult)
                    tt(x2B[:], aiB[:], liB[:], ALU.mult)
                    tt(x1B[:], x1B[:], x2B[:], ALU.add)
                    tt(frB[:], x1B[:], denB[:], ALU.mult)
                    tt(x1B[:], aiB[:], lrB[:], ALU.mult)
                    tt(x2B[:], nrB[:], liB[:], ALU.mult)
                    tt(x1B[:], x1B[:], x2B[:], ALU.subtract)
                    tt(fiB[:], x1B[:], denB[:], ALU.mult)
                    w1 = sbt(es, "w1", [128, 256], F32)
                    w2 = sbt(es, "w2", [128, 256], F32)
                    bbr = sbt(es, "bbrB", [128, 256], F32)
                    bbi = sbt(es, "bbiB", [128, 256], F32)
                    v3 = lambda t: t[:, :].rearrange("p (a i) -> p a i", a=16)
                    bc = lambda t: t[:, :].unsqueeze(2).broadcast_to([128, 16, 16])
                    tt(v3(w1), v3(brB), bc(frB), ALU.mult)
                    tt(v3(w2), v3(biB), bc(fiB), ALU.mult)
                    tt(bbr[:], w1[:], w2[:], ALU.subtract)
                    tt(v3(w1), v3(biB), bc(frB), ALU.mult)
                    tt(v3(w2), v3(brB), bc(fiB), ALU.mult)
                    tt(bbi[:], w1[:], w2[:], ALU.add)
                    Bmr = sbt(es, "Bmr", [128, 16, 128], F32)
                    Bmi = sbt(es, "Bmi", [128, 16, 128], F32)
                    vop(lambda e: e.memset(Bmr[:], 0.0), "pool")
                    vop(lambda e: e.memset(Bmi[:], 0.0), "pool")
                    for q in range(4):
                        for gi in range(2):
                            c0 = 32 * q + 16 * gi
                            vop(lambda e, q=q, gi=gi, c0=c0: e.tensor_scalar(
                                out=Bmr[:, :, :].rearrange("p (c q) m -> p c q m", q=4)[:, :, q, c0:c0 + 16],
                                in0=bbr[:, :].rearrange("p (c q j) -> p c q j", q=4, j=16)[:, :, q, :],
                                scalar1=mB[:, gi:gi + 1], scalar2=None, op0=ALU.mult))
                            vop(lambda e, q=q, gi=gi, c0=c0: e.tensor_scalar(
                                out=Bmi[:, :, :].rearrange("p (c q) m -> p c q m", q=4)[:, :, q, c0:c0 + 16],
                                in0=bbi[:, :].rearrange("p (c q j) -> p c q j", q=4, j=16)[:, :, q, :],
                                scalar1=mB[:, gi:gi + 1], scalar2=-1.0, op0=ALU.mult, op1=ALU.mult))
                    CAr = sbt(es, "CAr", [128, 9, 16, 16], F32)
                    CAi = sbt(es, "CAi", [128, 9, 16, 16], F32)
                    vop(lambda e: e.tensor_copy(out=CAr[:, 0, :, :], in_=v3(crB)))
                    vop(lambda e: e.tensor_copy(out=CAi[:, 0, :, :], in_=v3(ciB)))
                    for r in range(8):
                        pre = PB[:, r, 0, :].unsqueeze(2).broadcast_to([128, 16, 16])
                        pim = PB[:, r, 1, :].unsqueeze(2).broadcast_to([128, 16, 16])
                        tt(v3(w1), v3(crB), pre, ALU.mult)
                        tt(v3(w2), v3(ciB), pim, ALU.mult)
                        tt(CAr[:, r + 1, :, :], v3(w1), v3(w2), ALU.subtract)
                        tt(v3(w1), v3(crB), pim, ALU.mult)
                        tt(v3(w2), v3(ciB), pre, ALU.mult)
                        tt(CAi[:, r + 1, :, :], v3(w1), v3(w2), ALU.add)
                        for gi in range(2):
                            pop(lambda e, r=r, gi=gi: e.tensor_scalar(out=w3_sb[:, :, 0, gi * 128 + r * 16:gi * 128 + r * 16 + 16], in0=CAr[:, r + 1, :, :],
                                                                    scalar1=mB[:, gi:gi + 1], scalar2=None, op0=ALU.mult))
                            pop(lambda e, r=r, gi=gi: e.tensor_scalar(out=w3_sb[:, :, 1, gi * 128 + r * 16:gi * 128 + r * 16 + 16], in0=CAi[:, r + 1, :, :],
                                                                    scalar1=mB[:, gi:gi + 1], scalar2=-1.0, op0=ALU.mult, op1=ALU.mult))
                    for ct in range(4):
                        for q in range(4):
                            p = 4 * ct + q
                            listB.append(("op", "pe", lambda e, p=p, q=q: e.matmul(psf[0][:, 0:128], lhsT=Bmr[:, p, :], rhs=CAr[:, 0:8, p, :], start=(q == 0), stop=False),
                                          [tkB], [pb[0]], {"inc": False}))
                            listB.append(("op", "pe", lambda e, p=p, q=q: e.matmul(psf[0][:, 0:128], lhsT=Bmi[:, p, :], rhs=CAi[:, 0:8, p, :], start=False, stop=(q == 3)),
                                          [tkB], [pb[0]], {"inc": (q == 3)}))
                        listB.append(("op", "dve", lambda e, ct=ct: e.tensor_copy(out=Kall[:, ct, :, :], in_=psf[0][:, 0:128].rearrange("p (t i) -> p t i", t=8)),
                                      [pb[0], tkB], [tkB], {}))
                    vop(lambda e: e.memset(toep_sb[:], 0.0), "pool")
                    for s in range(8):
                        for gi in range(2):
                            vop(lambda e, s=s, gi=gi: e.tensor_scalar(
                                out=toep_sb[:, :, s, gi * 128 + s * 16:gi * 128 + 128],
                                in0=Kall[:, :, 0:8 - s, :].rearrange("p c t i -> p c (t i)"),
                                scalar1=mA[:, gi:gi + 1], scalar2=None, op0=ALU.mult))
                    qr = sbt(es, "qr", [128, 16], F32)
                    qi = sbt(es, "qi", [128, 16], F32)
                    vop(lambda e: e.tensor_copy(out=qr[:], in_=PB[:, 7, 0, :]))
                    vop(lambda e: e.tensor_copy(out=qi[:], in_=PB[:, 7, 1, :]))
                    for m in range(8):
                        pop(lambda e, m=m: e.tensor_copy(out=pw_tab[:, :, m, 0], in_=qr[:]))
                        pop(lambda e, m=m: e.tensor_copy(out=pw_tab[:, :, m, 1], in_=qi[:]))
                        pop(lambda e, m=m: e.tensor_scalar(out=pw_tab[:, :, m, 2], in0=qi[:], scalar1=-1.0, scalar2=None, op0=ALU.mult))
                        if m < 7:
                            tt(s1[:], qr[:], qr[:], ALU.mult)
                            tt(s2[:], qi[:], qi[:], ALU.mult)
                            tt(s2[:], s1[:], s2[:], ALU.subtract)
                            tt(s1[:], qr[:], qi[:], ALU.mult)
                            vop(lambda e: e.tensor_scalar(out=qi[:], in0=s1[:], scalar1=2.0, scalar2=None, op0=ALU.mult))
                            vop(lambda e: e.tensor_copy(out=qr[:], in_=s2[:]))
                    ia = ib = 0

                    def emit_rec(rec):
                        kind, eng, fn, rd, wr, kw = rec
                        if kind == "dma":
                            S.dma(eng, fn, reads=rd, writes=wr)
                        else:
                            S.op(eng, fn, reads=rd, writes=wr, **kw)
                    while ia < len(listA) or ib < len(listB):
                        if ia < len(listA):
                            emit_rec(listA[ia])
                            ia += 1
                        for _ in range(2):
                            if ib < len(listB):
                                emit_rec(listB[ib])
                                ib += 1
                    S.flush()

                for b in range(nb):
                    layer1(b, wv_sb, toep_sb, w3_sb, pw_tab, dA, glub, lng, lnb, dwT, ones512, t_par)

        def layer1(b, wv_sb, toep_sb, w3_sb, pw_tab, dA, glub, lng, lnb, dwT, ones512, t_par):
            with ExitStack() as L:
                suT = sbt(L, "suT", [128, 4, SEQ], BF16)
                gcT = sbt(L, "gcT", [128, 4, SEQ], BF16)
                gdT = sbt(L, "gdT", [128, 4, SEQ], BF16)
                gpad = sbt(L, "gpad", [128, 4, SEQ + 32], BF16)
                t_su = [TL(4) for _ in range(4)]
                t_gc = [TL(4) for _ in range(4)]
                t_gd = [TL(4) for _ in range(4)]
                t_gp = TL(4)
                wo1 = sbt(L, "wo1", [128, 8, D], BF16)
                t_wo1 = T()
                pwsb = sbt(L, "pwsb", [128, 4, 512], BF16)
                t_pw = T()
                with ExitStack() as es:
                    hT = sbt(es, "hT1", [128, 8, SEQ], BF16)
                    t_hT = TL(16)
                    phase_norm_T(es, out_d, b, pre_g[1:2, :], hT, t_hT, nxb=2)
                    wb = [sbt(es, f"wb1_{i}", [128, 8, 512], BF16) for i in range(2)]
                    t_wb = TL(2)
                    sg = [sbt(es, f"sg{i}", [128, 512], BF16) for i in range(2)]
                    t_sg = TL(2)
                    order = [0, 1, 4, 2, 3]

                    def loadw(oi):
                        gi = order[oi]
                        S.dma("pool", lambda e: e.dma_start(out=wb[oi % 2][:], in_=o_w_in[:, gi * 512:(gi + 1) * 512].rearrange("(k p) n -> p k n", p=128)),
                              writes=[t_wb[oi % 2]])
                    loadw(0)
                    loadw(1)
                    S.op("pool", lambda e: e.memset(gpad[:, :, 0:32], 0.0), writes=t_gp)
                    cnt = [0]

                    def proj(wbi, t_wbi, m, c):
                        bi = cnt[0] % 3
                        cnt[0] += 1
                        for k in range(8):
                            S.op("pe", lambda e, k=k: e.matmul(psf[bi][:], lhsT=wbi[:, k, m * 128:(m + 1) * 128], rhs=hT[:, k, c * 512:(c + 1) * 512],
                                                               start=(k == 0), stop=(k == 7)),
                                 reads=[t_wbi] + t_hT[4 * c:4 * c + 4], writes=[pb[bi]], inc=(k == 7))
                        return bi
                    for oi in range(3):
                        gi = order[oi]
                        dst, t_dst, fn = ((suT, t_su, AF.Copy), (gcT, t_gc, AF.Silu), None, None, (gdT, t_gd, AF.Silu))[gi]
                        for m in range(4):
                            for c in range(4):
                                bi = proj(wb[oi % 2], t_wb[oi % 2], m, c)
                                S.op("act", lambda e, bi=bi, m=m, c=c, dst=dst, fn=fn: e.activation(out=dst[:, m, c * 512:(c + 1) * 512], in_=psf[bi][:], func=fn),
                                     reads=[pb[bi]], writes=[t_dst[m][c]])
                        if oi + 2 < 5:
                            loadw(oi + 2)
                    si = 0
                    for m in range(4):
                        for c in range(4):
                            bi = proj(wb[0], t_wb[0], m, c)
                            i = si % 2
                            si += 1
                            S.op("act", lambda e, bi=bi, i=i: e.activation(out=sg[i][:], in_=psf[bi][:], func=AF.Sigmoid), reads=[pb[bi]], writes=[t_sg[i]])
                            bi2 = proj(wb[1], t_wb[1], m, c)
                            S.op("dve", lambda e, bi2=bi2, i=i, m=m, c=c: e.tensor_tensor(out=gpad[:, m, 32 + c * 512:32 + (c + 1) * 512], in0=psf[bi2][:], in1=sg[i][:], op=ALU.mult),
                                 reads=[pb[bi2], t_sg[i]], writes=[t_gp[m]])
                    S.flush()
                if os.environ.get("KSTOP") == "AB1":
                    return

                with ExitStack() as es:
                    Sb = [[[sbt(es, f"S{sl}{pi}{pp}", [128, 2, 512], F32) for pp in range(2)] for pi in range(2)] for sl in range(2)]
                    t_S = [[[T() for pp in range(2)] for pi in range(2)] for sl in range(2)]
                    Sp = [[[sbt(es, f"Sp{sl}{pi}{ri}", [128, 256], BF16) for ri in range(2)] for pi in range(2)] for sl in range(2)]
                    t_Sp = [[T() for pi in range(2)] for sl in range(2)]
                    for sl in range(2):
                        for pi in range(2):
                            for ri in range(2):
                                S.op("pool", lambda e, sl=sl, pi=pi, ri=ri: e.memset(Sp[sl][pi][ri][:, 0:1], 0.0), writes=[t_Sp[sl][pi]])
                                for pp in range(2):
                                    S.op("pool", lambda e, sl=sl, pi=pi, ri=ri, pp=pp: e.memset(Sb[sl][pi][pp][:, ri, 0:256], 0.0), writes=[t_S[sl][pi][pp]])
                    ysb = [sbt(es, f"ysb{i}", [128, 2, 8, 128], BF16) for i in range(2)]
                    t_ysb = [T(), T()]
                    gluw = sbt(es, "gluw", [128, 4, 1024], BF16)
                    t_gw = T()
                    for hf in range(2):
                        S.dma("pool", lambda e, hf=hf: e.dma_start(out=gluw[:, :, hf * 512:(hf + 1) * 512], in_=o_glu_w[:, hf * 512:(hf + 1) * 512].rearrange("(k p) n -> p k n", p=128)),
                              writes=[t_gw])
                    load_wo(wo1, t_wo1, o_w_out)
                    S.dma("pool", lambda e: e.dma_start(out=pwsb[:], in_=o_pw.rearrange("(k p) n -> p k n", p=128)), writes=[t_pw])
                    couples = [(ct, q0) for ct in range(4) for q0 in (0, 2)]

                    def emit_V(ci):
                        ct, q0 = couples[ci]
                        sl = ci % 2
                        for pi in range(2):
                            q = q0 + pi
                            rows = slice(32 * q, 32 * q + 32)
                            tp = (32 * q, 0)
                            for ri in range(2):
                                bk = (0, 1, 4, 5)[pi * 2 + ri]
                                for s in range(8):
                                    S.op("pe", lambda e, s=s, ri=ri, bk=bk, rows=rows, ct=ct, tp=tp: e.matmul(
                                        psf[bk][:, 0:256], lhsT=wv_sb[rows, ct, s, ri, :],
                                        rhs=suT[rows, ct, :].rearrange("p (k s) -> p s k", s=8)[:, s, :], start=(s == 0), stop=(s == 7), tile_position=tp),
                                        reads=t_su[ct] + [t_par], writes=[pb[bk]], inc=(s == 7))
                                S.op("act", lambda e, ri=ri, bk=bk, sl=sl, pi=pi: e.activation(out=Sb[sl][pi][0][:, ri, 256:512], in_=psf[bk][:, 0:256], func=AF.Copy),
                                     reads=[pb[bk]], writes=[t_S[sl][pi][0]])

                    def emit_scan(ci):
                        ct, q0 = couples[ci]
                        sl = ci % 2
                        for m in range(8):
                            sh = 1 << m
                            a, d_ = m % 2, (m + 1) % 2
                            for stage in range(3):
                                for pi in range(2):
                                    p = ct * 4 + q0 + pi
                                    src, dst = Sb[sl][pi][a], Sb[sl][pi][d_]
                                    ts, td = t_S[sl][pi][a], t_S[sl][pi][d_]
                                    pr = pw_tab[:, p, m, 0:1]
                                    pim = pw_tab[:, p, m, 1:2]
                                    npi = pw_tab[:, p, m, 2:3]
                                    if stage == 0:
                                        S.op("dve", lambda e, src=src, dst=dst, sh=sh, pr=pr: e.scalar_tensor_tensor(out=dst[:, :, 256:512], in0=src[:, :, 256 - sh:512 - sh], scalar=pr, in1=src[:, :, 256:512], op0=ALU.mult, op1=ALU.add),
                                             reads=[ts, t_par], writes=[td])
                                    elif stage == 1:
                                        S.op("dve", lambda e, src=src, dst=dst, sh=sh, npi=npi: e.scalar_tensor_tensor(out=dst[:, 0, 256:512], in0=src[:, 1, 256 - sh:512 - sh], scalar=npi, in1=dst[:, 0, 256:512], op0=ALU.mult, op1=ALU.add),
                                             reads=[ts, td, t_par], writes=[td])
                                    else:
                                        S.op("dve", lambda e, src=src, dst=dst, sh=sh, pim=pim: e.scalar_tensor_tensor(out=dst[:, 1, 256:512], in0=src[:, 0, 256 - sh:512 - sh], scalar=pim, in1=dst[:, 1, 256:512], op0=ALU.mult, op1=ALU.add),
                                             reads=[ts, td, t_par], writes=[td])

                    def emit_y(ci):
                        ct, q0 = couples[ci]
                        sl = ci % 2
                        yb_ = ysb[ct % 2]
                        t_y = t_ysb[ct % 2]
                        for pi in range(2):
                            q = q0 + pi
                            p = ct * 4 + q
                            rows = slice(32 * q, 32 * q + 32)
                            tp = (32 * q, 0)
                            for ri in range(2):
                                S.op("act", lambda e, ri=ri, sl=sl, pi=pi: e.activation(out=Sp[sl][pi][ri][:, 1:256], in_=Sb[sl][pi][0][:, ri, 256:511], func=AF.Copy),
                                     reads=[t_S[sl][pi][0]], writes=[t_Sp[sl][pi]])
                            for kt2 in range(2):
                                bk = 2 + kt2
                                for s in range(8):
                                    S.op("pe", lambda e, s=s, kt2=kt2, bk=bk, rows=rows, ct=ct, tp=tp: e.matmul(
                                        psf[bk][:, 0:256],
                                        lhsT=suT[rows, ct, kt2 * 1024:(kt2 + 1) * 1024].rearrange("p (k s) -> p s k", s=8)[:, s, :],
                                        rhs=toep_sb[rows, ct, s, :], start=(s == 0), stop=False, tile_position=tp),
                                        reads=t_su[ct] + [t_par], writes=[pb[bk]], inc=False)
                                for ri in range(2):
                                    S.op("pe", lambda e, ri=ri, kt2=kt2, bk=bk, sl=sl, pi=pi, p=p: e.matmul(
                                        psf[bk][:, 0:256], lhsT=Sp[sl][pi][ri][:, kt2 * 128:(kt2 + 1) * 128], rhs=w3_sb[:, p, ri, :], start=False, stop=(ri == 1)),
                                        reads=[t_Sp[sl][pi], t_par], writes=[pb[bk]], inc=(ri == 1))
                                S.op("act", lambda e, kt2=kt2, bk=bk, q=q, yb_=yb_: e.activation(
                                    out=yb_[:, kt2, :, q * 32:(q + 1) * 32].rearrange("p r (g i) -> p g r i", g=2),
                                    in_=psf[bk][:, 0:256].rearrange("p (g r i) -> p g r i", g=2, r=8), func=AF.Copy),
                                    reads=[pb[bk]], writes=[t_y])

                    def emit_T(ct):
                        yb_ = ysb[ct % 2]
                        t_y = t_ysb[ct % 2]
                        for kt2 in range(2):
                            for r in range(8):
                                S.op("pe", lambda e, kt2=kt2, r=r, yb_=yb_: e.transpose(out=psb[:, r * 128:(r + 1) * 128], in_=yb_[:, kt2, r, :], identity=identb[:]),
                                     reads=[t_y, t_const], writes=[pb[7]], inc=(r == 7))
                            S.op("dve", lambda e, kt2=kt2, ct=ct: e.scalar_tensor_tensor(
                                out=suT[:, ct, kt2 * 1024:(kt2 + 1) * 1024].rearrange("p (k r) -> p r k", r=8),
                                in0=suT[:, ct, kt2 * 1024:(kt2 + 1) * 1024].rearrange("p (k r) -> p r k", r=8),
                                scalar=dA[:, ct:ct + 1],
                                in1=psb[:, :].rearrange("p (r k) -> p r k", r=8), op0=ALU.mult, op1=ALU.add),
                                reads=[pb[7], t_par] + t_su[ct], writes=t_su[ct])

                    emit_V(0)
                    for ci in range(8):
                        if ci + 1 < 8:
                            emit_V(ci + 1)
                        emit_scan(ci)
                        emit_y(ci)
                        if ci % 2 == 1:
                            emit_T(couples[ci][0])
                    if dbg_d is not None and b == 0:
                        S.dma("pool", lambda e: e.dma_start(out=dbg_d[:, :, :], in_=suT[:]), reads=[t for tl in t_su for t in tl])
                    sgf = [sbt(es, f"sgf{i}", [128, 512], F32) for i in range(2)]
                    tgf = [sbt(es, f"tgf{i}", [128, 512], F32) for i in range(2)]
                    t_sgf, t_tgf = TL(2), TL(2)
                    gi_ = 0
                    for c in range(4):
                        for mt in range(4):
                            i = gi_ % 2
                            gi_ += 1
                            ba, bb = 4 + i, 4 + (1 - i)
                            bka = 4 + i
                            bkb = i
                            for ct in range(4):
                                S.op("pe", lambda e, ct=ct, mt=mt, c=c, bka=bka: e.matmul(psf[bka][:], lhsT=gluw[:, ct, mt * 128:(mt + 1) * 128], rhs=suT[:, ct, c * 512:(c + 1) * 512], start=(ct == 0), stop=(ct == 3)),
                                     reads=[t_gw] + [t_su[ct][c]], writes=[pb[bka]], inc=(ct == 3))
                            for ct in range(4):
                                S.op("pe", lambda e, ct=ct, mt=mt, c=c, bkb=bkb: e.matmul(psf[bkb][:], lhsT=gluw[:, ct, (4 + mt) * 128:(5 + mt) * 128], rhs=suT[:, ct, c * 512:(c + 1) * 512], start=(ct == 0), stop=(ct == 3)),
                                     reads=[t_gw] + [t_su[ct][c]], writes=[pb[bkb]], inc=(ct == 3))
                            S.op("act", lambda e, i=i, mt=mt, bkb=bkb: e.activation(out=sgf[i][:], in_=psf[bkb][:], func=AF.Sigmoid, bias=glub[:, 4 + mt:5 + mt]),
                                 reads=[pb[bkb], t_par], writes=[t_sgf[i]])
                            S.op("dve", lambda e, i=i, mt=mt, bka=bka: e.scalar_tensor_tensor(out=tgf[i][:], in0=psf[bka][:], scalar=glub[:, mt:mt + 1], in1=sgf[i][:], op0=ALU.add, op1=ALU.mult),
                                 reads=[pb[bka], t_sgf[i], t_par], writes=[t_tgf[i]])
                            S.op("pool", lambda e, i=i, mt=mt, c=c: e.tensor_tensor(out=gcT[:, mt, c * 512:(c + 1) * 512], in0=tgf[i][:], in1=gcT[:, mt, c * 512:(c + 1) * 512], op=ALU.mult),
                                 reads=[t_tgf[i], t_gc[mt][c]], writes=[t_gc[mt][c]])
                    S.flush()
                if os.environ.get("KSTOP") == "S5":
                    return

                with ExitStack() as es:
                    diag = [sbt(es, f"diag{i}", [128, 31, 128], BF16) for i in range(4)]
                    t_dg = TL(4)
                    for ct in range(4):
                        S.op("dve", lambda e, ct=ct: e.tensor_tensor(out=diag[ct][:], in0=identf[:, :].unsqueeze(1).broadcast_to([128, 31, 128]),
                                                                   in1=dwT[:, ct, :].unsqueeze(2).broadcast_to([128, 31, 128]), op=ALU.mult),
                             reads=[t_const, t_par], writes=[t_dg[ct]])
                    cf2 = None
                    c162 = [sbt(es, f"c16{i}", [128, 4, 512], BF16) for i in range(2)]
                    c22 = [sbt(es, f"c2{i}", [128, 4, 512], BF16) for i in range(2)]
                    sn2 = [sbt(es, f"sn{i}", [128, 4, 512], BF16) for i in range(2)]
                    t_cf2, t_c162, t_c22, t_sn2 = [TL(4), TL(4)], [TL(4), TL(4)], [TL(4), TL(4)], [TL(4), TL(4)]
                    mean2 = [sbt(es, "mean_sb0", [128, 512], F32)] * 2
                    m22 = [sbt(es, "m2_0", [128, 512], F32)] * 2
                    rstd2 = [sbt(es, "rstd0", [128, 512], F32)] * 2
                    t_mean2, t_m22, t_rstd2 = [T()] * 2, [T()] * 2, [T()] * 2
                    u1 = [sbt(es, f"u1_{i}", [128, 512], F32) for i in range(2)]
                    t_u1 = TL(2)
                    ui_box = [0]
                    def conv_part(c):
                            cf, c16, c2, sn = c162[c % 2], c162[c % 2], c22[c % 2], sn2[c % 2]
                            t_cf, t_c16, t_c2, t_sn = t_c162[c % 2], t_c162[c % 2], t_c22[c % 2], t_sn2[c % 2]
                            mean_sb, m2, rstd = mean2[c % 2], m22[c % 2], rstd2[c % 2]
                            t_mean, t_m2, t_rstd = t_mean2[c % 2], t_m22[c % 2], t_rstd2[c % 2]
                            for ct in range(4):
                                bk = ct % 2
                                for k in range(31):
                                    S.op("pe", lambda e, cf=cf, c16=c16, c2=c2, sn=sn, mean_sb=mean_sb, m2=m2, rstd=rstd, k=k, ct=ct, c=c, bk=bk: e.matmul(psf[bk][:], lhsT=diag[ct][:, k, :], rhs=gpad[:, ct, 2 + c * 512 + k:2 + c * 512 + k + 512],
                                                                                       start=(k == 0), stop=(k == 30)),
                                         reads=[t_dg[ct], t_gp[ct]], writes=[pb[bk]], inc=(k == 30))
                                pass
                                S.op("act", lambda e, cf=cf, c16=c16, c2=c2, sn=sn, mean_sb=mean_sb, m2=m2, rstd=rstd, ct=ct, bk=bk: e.activation(out=c16[:, ct, :], in_=psf[bk][:], func=AF.Copy), reads=[pb[bk]], writes=[t_c16[ct]])
                                S.op("act", lambda e, cf=cf, c16=c16, c2=c2, sn=sn, mean_sb=mean_sb, m2=m2, rstd=rstd, ct=ct, bk=bk: e.activation(out=c2[:, ct, :], in_=psf[bk][:], func=AF.Square), reads=[pb[bk]], writes=[t_c2[ct]])

                    def ln_part(c):
                            cf, c16, c2, sn = c162[c % 2], c162[c % 2], c22[c % 2], sn2[c % 2]
                            t_cf, t_c16, t_c2, t_sn = t_c162[c % 2], t_c162[c % 2], t_c22[c % 2], t_sn2[c % 2]
                            mean_sb, m2, rstd = mean2[c % 2], m22[c % 2], rstd2[c % 2]
                            t_mean, t_m2, t_rstd = t_mean2[c % 2], t_m22[c % 2], t_rstd2[c % 2]
                            for ct in range(4):
                                S.op("pe", lambda e, cf=cf, c16=c16, c2=c2, sn=sn, mean_sb=mean_sb, m2=m2, rstd=rstd, ct=ct: e.matmul(psf[2][:], lhsT=ones512[:], rhs=c16[:, ct, :], start=(ct == 0), stop=(ct == 3)),
                                     reads=[t_c16[ct], t_par], writes=[pb[2]], inc=(ct == 3))
                            for ct in range(4):
                                S.op("pe", lambda e, cf=cf, c16=c16, c2=c2, sn=sn, mean_sb=mean_sb, m2=m2, rstd=rstd, ct=ct: e.matmul(psf[3][:], lhsT=ones512[:], rhs=c2[:, ct, :], start=(ct == 0), stop=(ct == 3)),
                                     reads=[t_c2[ct], t_par], writes=[pb[3]], inc=(ct == 3))
                            S.op("act", lambda e, cf=cf, c16=c16, c2=c2, sn=sn, mean_sb=mean_sb, m2=m2, rstd=rstd: e.activation(out=mean_sb[:], in_=psf[2][:], func=AF.Copy), reads=[pb[2]], writes=[t_mean])
                            S.op("dve", lambda e, cf=cf, c16=c16, c2=c2, sn=sn, mean_sb=mean_sb, m2=m2, rstd=rstd: e.tensor_tensor(out=m2[:], in0=mean_sb[:], in1=mean_sb[:], op=ALU.mult), reads=[t_mean], writes=[t_m2])
                            S.op("dve", lambda e, cf=cf, c16=c16, c2=c2, sn=sn, mean_sb=mean_sb, m2=m2, rstd=rstd: e.tensor_tensor(out=m2[:], in0=psf[3][:], in1=m2[:], op=ALU.subtract), reads=[pb[3], t_m2], writes=[t_m2])
                            S.op("act", lambda e, cf=cf, c16=c16, c2=c2, sn=sn, mean_sb=mean_sb, m2=m2, rstd=rstd: e.activation(out=m2[:], in_=m2[:], func=AF.Ln, bias=EPS), reads=[t_m2], writes=[t_m2])
                            S.op("act", lambda e, cf=cf, c16=c16, c2=c2, sn=sn, mean_sb=mean_sb, m2=m2, rstd=rstd: e.activation(out=rstd[:], in_=m2[:], func=AF.Exp, scale=-0.5), reads=[t_m2], writes=[t_rstd])
                            for ct in range(4):
                                i = ui_box[0] % 2
                                ui_box[0] += 1
                                S.op("dve", lambda e, cf=cf, c16=c16, c2=c2, sn=sn, mean_sb=mean_sb, m2=m2, rstd=rstd, ct=ct, i=i: e.tensor_tensor(out=u1[i][:], in0=cf[:, ct, :], in1=mean_sb[:], op=ALU.subtract), reads=[t_cf[ct], t_mean], writes=[t_u1[i]])
                                S.op("dve", lambda e, cf=cf, c16=c16, c2=c2, sn=sn, mean_sb=mean_sb, m2=m2, rstd=rstd, i=i: e.tensor_tensor(out=u1[i][:], in0=u1[i][:], in1=rstd[:], op=ALU.mult), reads=[t_u1[i], t_rstd], writes=[t_u1[i]])
                                S.op("act", lambda e, cf=cf, c16=c16, c2=c2, sn=sn, mean_sb=mean_sb, m2=m2, rstd=rstd, ct=ct, i=i: e.activation(out=sn[:, ct, :], in_=u1[i][:], func=AF.Silu, scale=lng[:, ct:ct + 1], bias=lnb[:, ct:ct + 1]),
                                     reads=[t_u1[i], t_par], writes=[t_sn[ct]])

                    def pw_part(c):
                            cf, c16, c2, sn = c162[c % 2], c162[c % 2], c22[c % 2], sn2[c % 2]
                            t_cf, t_c16, t_c2, t_sn = t_c162[c % 2], t_c162[c % 2], t_c22[c % 2], t_sn2[c % 2]
                            mean_sb, m2, rstd = mean2[c % 2], m22[c % 2], rstd2[c % 2]
                            t_mean, t_m2, t_rstd = t_mean2[c % 2], t_m22[c % 2], t_rstd2[c % 2]
                            for mt in range(4):
                                bk = 4 + mt % 2
                                for ct in range(4):
                                    S.op("pe", lambda e, cf=cf, c16=c16, c2=c2, sn=sn, mean_sb=mean_sb, m2=m2, rstd=rstd, ct=ct, mt=mt, bk=bk: e.matmul(psf[bk][:], lhsT=pwsb[:, ct, mt * 128:(mt + 1) * 128], rhs=sn[:, ct, :], start=(ct == 0), stop=(ct == 3)),
                                         reads=[t_pw, t_sn[ct]], writes=[pb[bk]], inc=(ct == 3))
                                S.op("dve", lambda e, cf=cf, c16=c16, c2=c2, sn=sn, mean_sb=mean_sb, m2=m2, rstd=rstd, mt=mt, c=c, bk=bk: e.tensor_tensor(out=gdT[:, mt, c * 512:(c + 1) * 512], in0=psf[bk][:], in1=gdT[:, mt, c * 512:(c + 1) * 512], op=ALU.mult),
                                     reads=[pb[bk], t_gd[mt][c]], writes=[t_gd[mt][c]])

                    conv_part(0)
                    for c in range(4):
                        ln_part(c)
                        if c + 1 < 4:
                            conv_part(c + 1)
                        pw_part(c)
                    S.flush()
                if os.environ.get("KSTOP") == "CV":
                    return
                with ExitStack() as es:
                    phase_outproj(es, o_w_out, post_g[1:2, :], out_d, b, gcT, gdT, TL(4), TL(4), wo1, t_wo1)
                    S.flush()

        nbr = int(os.environ.get("KNB", NB))
        for b in range(nbr):
            layer0(b)
        if n_layers >= 2 and os.environ.get("KLAYERS", "2") == "2":
            layer1_all(nbr)

    return nc


def layer1_host_layouts(inputs, f):
    g = lambda k: np.asarray(inputs[k][0], dtype=np.float32)
    m = {}
    m["o_w_in"] = f(g("o_w_in"))

    def A_gn(a):
        a = np.repeat(a[:, None, :], 16, axis=1).reshape(4, 8, 16, 64)
        return f(a.transpose(1, 2, 0, 3).reshape(128, 256))
    m["o_lamre_A"] = A_gn(g("o_lam_re"))
    m["o_lamim_A"] = A_gn(g("o_lam_im"))
    m["o_dt_A"] = A_gn(np.repeat(g("o_log_dt")[:, None], 64, axis=1))

    def A_b(bm):
        a = bm.transpose(0, 2, 1).reshape(4, 8, 16, 64)
        return f(a.transpose(1, 2, 0, 3).reshape(128, 256))
    m["o_bre_A"] = A_b(g("o_b_re"))
    m["o_bim_A"] = A_b(g("o_b_im"))

    def B_gn(a):
        return f(a.reshape(16, 2, 64).transpose(1, 2, 0).reshape(128, 16))
    m["o_lamre_B"] = B_gn(g("o_lam_re"))
    m["o_lamim_B"] = B_gn(g("o_lam_im"))
    m["o_dt_B"] = B_gn(np.repeat(g("o_log_dt")[:, None], 64, axis=1))

    def B_c(cm):
        return f(cm.reshape(16, 2, 16, 64).transpose(1, 3, 0, 2).reshape(128, 256))
    def B_b(bm):
        return f(bm.reshape(16, 2, 64, 16).transpose(1, 2, 0, 3).reshape(128, 256))
    m["o_bre_B"] = B_b(g("o_b_re"))
    m["o_bim_B"] = B_b(g("o_b_im"))
    m["o_cre_B"] = B_c(g("o_c_re"))
    m["o_cim_B"] = B_c(g("o_c_im"))
    m["o_dA"] = f(g("o_d").reshape(4, 128).T)
    m["o_glu_w"] = f(g("o_glu_w"))
    m["o_glub"] = f(g("o_glu_b").reshape(8, 128).T)
    m["o_dwT"] = f(g("o_dw").reshape(31, 4, 128).transpose(2, 1, 0))
    m["o_lng"] = f(g("o_ln_g").reshape(4, 128).T)
    m["o_lnb"] = f(g("o_ln_b").reshape(4, 128).T)
    m["o_pw"] = f(g("o_pw"))
    m["o_w_out"] = f(g("o_w_out"))
    return m


def make_in_maps(inputs):
    n = 8
    x = np.ascontiguousarray(inputs["x"], dtype=np.float32)
    consts = host_consts()
    f = lambda a: np.ascontiguousarray(a, dtype=np.float32)
    l1maps = layer1_host_layouts(inputs, f)
    in_maps = []
    for c in range(n):
        m = {"x": x[NB * c:NB * (c + 1)]}
        for k in ("pre_norm_g", "post_norm_g"):
            m[k] = f(inputs[k])
        m["e_w_in"] = f(inputs["e_w_in"][0])
        m["e_pool_w"] = f(inputs["e_pool_w"][0])
        m["e_pool_scale"] = f(np.asarray(inputs["e_pool_scale"][0], dtype=np.float32).reshape(4, 128).T)
        m["e_w_out"] = f(inputs["e_w_out"][0])
        m.update(l1maps)
        for k, v in consts.items():
            m["c_" + k] = v
        in_maps.append(m)
    return in_maps


def kernel(**inputs):
    nc = build()
    in_maps = make_in_maps(inputs)
    res = run_bass_kernel_spmd(nc, in_maps, core_ids=list(range(8)))
    return np.concatenate([r["out"] for r in res.results], axis=0)
```

```python
import math
import os
from contextlib import ExitStack
import numpy as np
import concourse.bass as bass
import concourse.mybir as mybir
from concourse.bass_utils import run_bass_kernel_spmd

F32 = mybir.dt.float32
BF16 = mybir.dt.bfloat16
I32 = mybir.dt.int32
ALU = mybir.AluOpType
AF = mybir.ActivationFunctionType
AX = mybir.AxisListType

D = 1024
SEQ = 2048
NB = 2
HD = 128
EPS = 1e-6
NEGB = -30000.0
S5L = 8
NCH = SEQ // S5L


class T:
    __slots__ = ("w", "r")

    def __init__(self):
        self.w = None
        self.r = {}


def TL(n):
    return [T() for _ in range(n)]


class Sched:
    ENG = ("pe", "act", "dve", "pool", "sp")

    def __init__(self, nc, es, n_dma=24):
        self.nc = nc
        self.ops = {e: [] for e in self.ENG}
        self.cnt = {e: 0 for e in self.ENG}
        self.seen = {e: {} for e in self.ENG}
        self.n_dma = n_dma
        self.dma_cnt = [0] * n_dma
        self.dma_rr2 = {"sp": 0, "pool": 0}
        self.semh = {}
        for e in self.ENG:
            self.semh[("c", e)] = es.enter_context(nc.semaphore(f"s_{e}"))
        for i in range(n_dma):
            self.semh[("d", i)] = es.enter_context(nc.semaphore(f"s_d{i}"))

    def _deps(self, eng, reads, writes):
        need = {}
        for t in reads:
            if t.w is not None:
                k, v = t.w
                if need.get(k, 0) < v:
                    need[k] = v
        for t in writes:
            if t.w is not None:
                k, v = t.w
                if need.get(k, 0) < v:
                    need[k] = v
            for k, v in t.r.items():
                if need.get(k, 0) < v:
                    need[k] = v
        waits = []
        sn = self.seen[eng]
        for k, v in need.items():
            if eng == "pe" and k == ("c", "pe"):
                continue
            if sn.get(k, 0) < v:
                waits.append((k, v))
                sn[k] = v
        return waits

    def _record(self, key, v, reads, writes):
        for t in reads:
            if t.r.get(key, 0) < v:
                t.r[key] = v
        for t in writes:
            t.w = (key, v)
            t.r = {}

    def op(self, eng, fn, reads=(), writes=(), inc=True):
        waits = self._deps(eng, reads, writes)
        key = ("c", eng)
        if inc:
            self.cnt[eng] += 1
            v = self.cnt[eng]
        else:
            v = self.cnt[eng] + 1
        self.ops[eng].append([waits, fn, key, 1 if inc else 0])
        self._record(key, v, reads, writes)

    def dma(self, eng, fn, reads=(), writes=()):
        half = self.n_dma // 2
        base = 0 if eng == "sp" else half
        j = self.dma_rr2[eng]
        self.dma_rr2[eng] = (j + 1) % half
        i = base + j
        waits = self._deps(eng, reads, writes)
        key = ("d", i)
        prev = self.dma_cnt[i]
        if prev > 0 and self.seen[eng].get(key, 0) < prev:
            waits.append((key, prev))
            self.seen[eng][key] = prev
        self.dma_cnt[i] += 16
        self.ops[eng].append([waits, fn, key, 16])
        self._record(key, self.dma_cnt[i], reads, writes)

    def flush(self):
        nc = self.nc
        for e in ("pe", "act", "dve", "pool"):
            if self.ops[e] and self.ops[e][-1][3] == 0:
                self.ops[e][-1][3] = 1
                self.cnt[e] += 1
        fin = [(("d", i), self.dma_cnt[i]) for i in range(self.n_dma) if self.dma_cnt[i] > 0]
        fin += [(("c", e), self.cnt[e]) for e in ("pe", "act", "dve", "pool") if self.cnt[e] > 0]
        ops = self.ops
        semh = self.semh
        with nc.Block() as block:
            def run(engname, eng):
                for waits, fn, key, inc in ops[engname]:
                    for k, v in waits:
                        eng.wait_ge(semh[k], v)
                    ins = fn(eng)
                    if inc:
                        ins.then_inc(semh[key], inc)
                if engname == "sp":
                    for k, v in fin:
                        eng.wait_ge(semh[k], v)

            @block.tensor
            def _(e):
                run("pe", e)

            @block.scalar
            def _(e):
                run("act", e)

            @block.vector
            def _(e):
                run("dve", e)

            @block.gpsimd
            def _(e):
                run("pool", e)

            @block.sync
            def _(e):
                run("sp", e)
        self.ops = {e: [] for e in self.ENG}
        for e in self.ENG:
            for k, v in fin:
                self.seen[e][k] = v


def host_consts():
    c = {}
    c["ident"] = np.eye(128, dtype=np.float32)
    c["ones"] = np.ones((128, 128), np.float32)
    kk = np.arange(128)[:, None]
    qq = np.arange(128)[None, :]
    c["trineg"] = np.where(kk <= qq, 0.0, NEGB).astype(np.float32)
    e8 = np.zeros((8, 8, 128), np.float32)
    for n in range(8):
        e8[n, n, :] = 1.0
    c["e8"] = e8.reshape(8, 1024)
    p32 = np.zeros((128, 128), np.float32)
    for m in range(32):
        p32[(m + 16) % 32, m] = 1.0
    c["p32"] = p32
    inv = np.power(500000.0, -np.arange(0, 32, 2, dtype=np.float32) / 32.0).astype(np.float32)
    pos = np.arange(SEQ, dtype=np.float32)
    ang = (pos[None, :] * inv[:, None]).astype(np.float32)
    cs = np.cos(ang).astype(np.float32)
    sn = np.sin(ang).astype(np.float32)
    c["ropeC"] = np.concatenate([cs, cs, np.ones((96, SEQ), np.float32)], 0)
    c["ropeS"] = np.concatenate([-sn, sn, np.zeros((96, SEQ), np.float32)], 0)
    negm = np.zeros((128, 8, 8), np.float32)
    bfix = np.zeros((128, 8, 8), np.float32)
    for i in range(8):
        B = 4 + i // 2
        for n in range(8):
            negm[:, i, n] = 0.0 if n < B else -1e30
            bfix[:, i, n] = 0.0 if n == B else NEGB
    c["negmask"] = negm.reshape(128, 64)
    c["biasfix"] = bfix.reshape(128, 64)
    rc = np.zeros((128, 4, 16), np.float32)
    for g, w in enumerate((2, 4, 8, 16)):
        for t in range(16):
            rc[:, g, t] = 1.0 / min(t + 1, w)
    c["rcnt"] = rc.reshape(128, 64)
    pp = np.arange(128)
    c["maskA"] = np.stack([((pp // 16) % 2 == 0), ((pp // 16) % 2 == 1)], 1).astype(np.float32)
    c["maskB"] = np.stack([(pp // 64 == 0), (pp // 64 == 1)], 1).astype(np.float32)
    return c


CONST_SHAPES = {k: v.shape for k, v in host_consts().items()}


def build(n_layers=2):
    nc = bass.Bass("TRN2", target_bir_lowering=False)

    def dr(name, shape, kind="ExternalInput", dt=F32):
        return nc.dram_tensor(name, list(shape), dt, kind=kind).ap()

    x_d = dr("x", [NB, SEQ, D])
    out_d = dr("out", [NB, SEQ, D], "ExternalOutput")
    pre_g = dr("pre_norm_g", [2, D])
    post_g = dr("post_norm_g", [2, D])
    e_w_in = dr("e_w_in", [D, 3072])
    e_pool_w = dr("e_pool_w", [4, 128, 128])
    e_pool_scale = dr("e_pool_scale", [128, 4])
    e_w_out = dr("e_w_out", [D, D])
    o_w_in = dr("o_w_in", [D, 2560])
    o_lamre_A = dr("o_lamre_A", [128, 256])
    o_lamim_A = dr("o_lamim_A", [128, 256])
    o_dt_A = dr("o_dt_A", [128, 256])
    o_bre_A = dr("o_bre_A", [128, 256])
    o_bim_A = dr("o_bim_A", [128, 256])
    o_bre_B = dr("o_bre_B", [128, 256])
    o_bim_B = dr("o_bim_B", [128, 256])
    o_lamre_B = dr("o_lamre_B", [128, 16])
    o_lamim_B = dr("o_lamim_B", [128, 16])
    o_dt_B = dr("o_dt_B", [128, 16])
    o_cre_B = dr("o_cre_B", [128, 256])
    o_cim_B = dr("o_cim_B", [128, 256])
    o_dA = dr("o_dA", [128, 4])
    o_glu_w = dr("o_glu_w", [512, 1024])
    o_glub = dr("o_glub", [128, 8])
    o_dwT = dr("o_dwT", [128, 4, 31])
    o_lng = dr("o_lng", [128, 4])
    o_lnb = dr("o_lnb", [128, 4])
    o_pw = dr("o_pw", [512, 512])
    o_w_out = dr("o_w_out", [D, D])
    dbg_d = dr("dbg", [128, 4, SEQ], "ExternalOutput") if os.environ.get("KDBG") else None
    cd = {k: dr("c_" + k, list(s)) for k, s in CONST_SHAPES.items()}

    with ExitStack() as top:
        S = Sched(nc, top)
        uid = [0]

        def sbt(es, name, shape, dt):
            uid[0] += 1
            return es.enter_context(nc.sbuf_tensor(f"{name}_{uid[0]}", list(shape), dt))
        psf = [top.enter_context(nc.psum_tensor(f"psf{i}", [128, 512], F32)) for i in range(7)]
        psb = top.enter_context(nc.psum_tensor("psb", [128, 1024], BF16))
        pb = TL(8)

        identb = sbt(top, "identb", [128, 128], BF16)
        identf = sbt(top, "identf", [128, 128], F32)
        onesb = sbt(top, "onesb", [128, 128], BF16)
        t_const = T()
        S.dma("pool", lambda e: e.dma_start(out=identb[:], in_=cd["ident"][:, :]), writes=[t_const])
        S.dma("sp", lambda e: e.dma_start(out=identf[:], in_=cd["ident"][:, :]), writes=[t_const])
        S.dma("pool", lambda e: e.dma_start(out=onesb[:], in_=cd["ones"][:, :]), writes=[t_const])

        def rms_stats(es_tag, src_aps, st, col0, reads, t_st, junk):
            for i, ap in enumerate(src_aps):
                S.op("act", lambda e, ap=ap, i=i: e.activation(out=junk[:, 0:ap.shape[1]], in_=ap, func=AF.Square,
                                                             accum_out=st[:, col0 + i:col0 + i + 1]),
                     reads=reads, writes=[t_st])
            if len(src_aps) == 2:
                S.op("dve", lambda e: e.tensor_tensor(out=st[:, col0:col0 + 1], in0=st[:, col0:col0 + 1],
                                                      in1=st[:, col0 + 1:col0 + 2], op=ALU.add), reads=[t_st], writes=[t_st])
            S.op("act", lambda e: e.activation(out=st[:, col0 + 2:col0 + 3], in_=st[:, col0:col0 + 1], func=AF.Sqrt,
                                               scale=1.0 / D, bias=EPS), reads=[t_st], writes=[t_st])
            S.op("dve", lambda e: e.reciprocal(out=st[:, col0 + 3:col0 + 4], in_=st[:, col0 + 2:col0 + 3]),
                 reads=[t_st], writes=[t_st])

        def phase_norm_T(es, src_d, b, gvec_d, hT, t_hT, nxb=2):
            xb = [sbt(es, f"xb{i}", [128, D], F32) for i in range(nxb)]
            hb = [sbt(es, f"hb{i}", [128, D], BF16) for i in range(2)]
            junk = sbt(es, "junkA", [128, D], BF16)
            gt = sbt(es, "gtA", [128, D], F32)
            st = sbt(es, "stA", [128, 16 * 4], F32)
            t_xb, t_hb, t_g = TL(nxb), TL(2), T()
            t_st = TL(16)
            S.dma("sp", lambda e: e.dma_start(out=gt[:], in_=gvec_d.partition_broadcast(128)), writes=[t_g])
            def stats(tt):
                i = tt % 2
                xi = tt % nxb
                S.dma("sp", lambda e: e.dma_start(out=xb[xi][:], in_=src_d[b, tt * 128:(tt + 1) * 128, :]),
                      writes=[t_xb[xi]])
                rms_stats(es, [xb[xi][:]], st, tt * 4, [t_xb[xi]], t_st[tt], junk)
                S.op("dve", lambda e: e.scalar_tensor_tensor(out=hb[i][:], in0=xb[xi][:], scalar=st[:, tt * 4 + 3:tt * 4 + 4],
                                                             in1=gt[:], op0=ALU.mult, op1=ALU.mult),
                     reads=[t_xb[xi], t_st[tt], t_g], writes=[t_hb[i]])

            def trans(tt):
                i = tt % 2
                for k in range(8):
                    S.op("pe", lambda e, k=k: e.transpose(out=psb[:, k * 128:(k + 1) * 128], in_=hb[i][:, k * 128:(k + 1) * 128],
                                                          identity=identb[:]),
                         reads=[t_hb[i], t_const], writes=[pb[7]], inc=(k == 7))
                S.op("act", lambda e: e.activation(out=hT[:, :, tt * 128:(tt + 1) * 128],
                                                   in_=psb[:, :].rearrange("p (k t) -> p k t", k=8), func=AF.Copy),
                     reads=[pb[7]], writes=[t_hT[tt]])

            stats(0)
            for tt in range(16):
                if tt + 1 < 16:
                    stats(tt + 1)
                trans(tt)

        def load_wo(wo, t_wo, w_out_d):
            for hf in range(2):
                S.dma("pool", lambda e, hf=hf: e.dma_start(out=wo[:, :, hf * 512:(hf + 1) * 512],
                                                          in_=w_out_d[:, hf * 512:(hf + 1) * 512].rearrange("(k p) n -> p k n", p=128)),
                      writes=[t_wo])

        def phase_outproj(es, w_out_d, gvec_d, res_d, b, gA, gB, t_gA, t_gB, wo, t_wo, tokfn=None):
            gt = sbt(es, "gtF", [128, D], F32)
            xr = [sbt(es, f"xr{i}", [128, D], F32) for i in range(3)]
            tm = [sbt(es, f"tmF{i}", [128, D], F32) for i in range(3)]
            junk = sbt(es, "junkF", [128, 512], BF16)
            st = sbt(es, "stF", [128, 16 * 4], F32)
            t_g = T()
            t_xr, t_tm, t_st = TL(3), TL(3), TL(16)
            S.dma("sp", lambda e: e.dma_start(out=gt[:], in_=gvec_d.partition_broadcast(128)), writes=[t_g])

            def load_xr(tt):
                j = tt % 3
                S.dma("sp", lambda e: e.dma_start(out=xr[j][:], in_=res_d[b, tt * 128:(tt + 1) * 128, :]), writes=[t_xr[j]])
            load_xr(0)
            load_xr(1)
            for tt in range(16):
                i = tt % 2
                xj = tt % 3
                bk = [psf[2 * i], psf[2 * i + 1]]
                tbk = [pb[2 * i], pb[2 * i + 1]]
                for hf in range(2):
                    for k in range(8):
                        g_ap = (gA if k < 4 else gB)
                        S.op("pe", lambda e, hf=hf, k=k, g_ap=g_ap, tt=tt, bk=bk: e.matmul(
                            bk[hf][:], lhsT=g_ap[:, k % 4, tt * 128:(tt + 1) * 128], rhs=wo[:, k, hf * 512:(hf + 1) * 512],
                            start=(k == 0), stop=(k == 7)),
                            reads=[(tokfn(k, tt // 4) if tokfn is not None else (t_gA if k < 4 else t_gB)[tt // 4]), t_wo], writes=[tbk[hf]], inc=(k == 7))
                if tt + 2 < 16:
                    load_xr(tt + 2)
                rms_stats(es, [bk[0][:], bk[1][:]], st, tt * 4, tbk, t_st[tt], junk)
                for hf in range(2):
                    S.op("dve", lambda e, hf=hf, tt=tt, xj=xj, bk=bk: e.scalar_tensor_tensor(
                        out=tm[xj][:, hf * 512:(hf + 1) * 512], in0=bk[hf][:], scalar=st[:, tt * 4 + 3:tt * 4 + 4],
                        in1=gt[:, hf * 512:(hf + 1) * 512], op0=ALU.mult, op1=ALU.mult),
                        reads=[tbk[hf], t_st[tt], t_g], writes=[t_tm[xj]])
                S.op("pool", lambda e, xj=xj: e.tensor_tensor(out=tm[xj][:], in0=tm[xj][:], in1=xr[xj][:], op=ALU.add),
                     reads=[t_tm[xj], t_xr[xj]], writes=[t_tm[xj]])
                S.dma("sp", lambda e, tt=tt, xj=xj: e.dma_start(out=out_d[b, tt * 128:(tt + 1) * 128, :], in_=tm[xj][:]),
                      reads=[t_tm[xj]])

        def layer0(b):
            with ExitStack() as L:
                qkT = sbt(L, "qkT", [128, 8, SEQ], BF16)
                Vt = sbt(L, "Vt", [128, 16, 512], BF16)
                gaT = sbt(L, "gaT", [128, 4, SEQ], BF16)
                gbT = sbt(L, "gbT", [128, 4, SEQ], BF16)
                t_qk = [TL(4) for _ in range(8)]
                t_V = TL(16)
                t_ga = [TL(4) for _ in range(4)]
                t_gb = [TL(4) for _ in range(4)]
                wo0 = sbt(L, "wo0", [128, 8, D], BF16)
                t_wo0 = T()

                with ExitStack() as es:
                    hT = sbt(es, "hT", [128, 8, SEQ], BF16)
                    t_hT = TL(16)
                    phase_norm_T(es, x_d, b, pre_g[0:1, :], hT, t_hT, nxb=4)
                    wb = [sbt(es, f"wb{i}", [128, 8, 512], BF16) for i in range(2)]
                    t_wb = TL(2)
                    ropeC = sbt(es, "ropeC", [128, SEQ], F32)
                    ropeS = sbt(es, "ropeS", [128, SEQ], F32)
                    p32 = sbt(es, "p32", [128, 128], BF16)
                    poolw = sbt(es, "poolw", [128, 4, 128], BF16)
                    pscale = sbt(es, "pscale", [128, 4], F32)
                    rcnt = sbt(es, "rcnt", [128, 64], F32)
                    t_c2 = T()
                    S.dma("sp", lambda e: e.dma_start(out=ropeC[:], in_=cd["ropeC"][:, :]), writes=[t_c2])
                    S.dma("sp", lambda e: e.dma_start(out=ropeS[:], in_=cd["ropeS"][:, :]), writes=[t_c2])
                    S.dma("pool", lambda e: e.dma_start(out=p32[:], in_=cd["p32"][:, :]), writes=[t_c2])
                    S.dma("pool", lambda e: e.dma_start(out=poolw[:], in_=e_pool_w.rearrange("g c d -> c g d")), writes=[t_c2])
                    S.dma("sp", lambda e: e.dma_start(out=pscale[:], in_=e_pool_scale[:, :]), writes=[t_c2])
                    S.dma("sp", lambda e: e.dma_start(out=rcnt[:], in_=cd["rcnt"][:, :]), writes=[t_c2])
                    r1 = [sbt(es, f"r1_{i}", [128, 512], F32) for i in range(2)]
                    r2 = [sbt(es, f"r2_{i}", [128, 512], F32) for i in range(2)]
                    t_r1, t_r2 = TL(2), TL(2)
                    ub = [sbt(es, f"ub{i}", [128, 528], F32) for i in range(2)]
                    sa = sbt(es, "sa", [128, 528], F32)
                    sb_ = sbt(es, "sb", [128, 528], F32)
                    mt = [sbt(es, f"mt{i}", [128, 512], BF16) for i in range(2)]
                    t_ub, t_mt = TL(2), TL(2)
                    t_sa, t_sb = T(), T()

                    order = [0, 1, 2, 3, 5, 4]
                    cnt = [0]
                    for oi in range(2):
                        S.dma("pool", lambda e, oi=oi: e.dma_start(out=wb[oi][:], in_=e_w_in[:, order[oi] * 512:(order[oi] + 1) * 512].rearrange("(k p) n -> p k n", p=128)),
                              writes=[t_wb[oi]])
                    ri = [0]
                    ksub = os.environ.get("KSUB", "")
                    for oi, gi in enumerate(order):
                        if ksub and oi >= int(ksub):
                            break
                        wbi = wb[oi % 2]
                        t_wbi = t_wb[oi % 2]
                        if gi in (0, 1):
                            pend = [None]

                            def rope_tail(j, c, r, pbk):
                                def f():
                                    S.op("pe", lambda e: e.matmul(psf[pbk][:], lhsT=p32[:], rhs=qkT[:, j, c * 512:(c + 1) * 512], start=True, stop=True),
                                         reads=[t_qk[j][c], t_c2], writes=[pb[pbk]])
                                    S.op("dve", lambda e: e.tensor_tensor(out=r2[r][:], in0=psf[pbk][:], in1=ropeS[:, c * 512:(c + 1) * 512], op=ALU.mult),
                                         reads=[pb[pbk], t_c2], writes=[t_r2[r]])
                                    S.op("dve", lambda e: e.tensor_tensor(out=qkT[:, j, c * 512:(c + 1) * 512], in0=r1[r][:], in1=r2[r][:], op=ALU.add),
                                         reads=[t_r1[r], t_r2[r]], writes=[t_qk[j][c]])
                                return f
                            for m in range(4):
                                j = gi * 4 + m
                                for c in range(4):
                                    bi = cnt[0] % 3
                                    cnt[0] += 1
                                    for k in range(8):
                                        S.op("pe", lambda e, k=k, bi=bi, m=m, c=c, wbi=wbi: e.matmul(
                                            psf[bi][:], lhsT=wbi[:, k, m * 128:(m + 1) * 128], rhs=hT[:, k, c * 512:(c + 1) * 512],
                                            start=(k == 0), stop=(k == 7)),
                                            reads=[t_wbi] + t_hT[4 * c:4 * c + 4], writes=[pb[bi]], inc=(k == 7))
                                    if pend[0] is not None:
                                        pend[0]()
                                    S.op("act", lambda e, bi=bi, j=j, c=c: e.activation(out=qkT[:, j, c * 512:(c + 1) * 512], in_=psf[bi][:], func=AF.Copy),
                                         reads=[pb[bi]], writes=[t_qk[j][c]])
                                    r = ri[0] % 2
                                    ri[0] += 1
                                    pbk = 3 + r
                                    S.op("dve", lambda e, bi=bi, c=c, r=r: e.tensor_tensor(out=r1[r][:], in0=psf[bi][:], in1=ropeC[:, c * 512:(c + 1) * 512], op=ALU.mult),
                                         reads=[pb[bi], t_c2, t_qk[j][c]], writes=[t_r1[r]])
                                    pend[0] = rope_tail(j, c, r, pbk)
                            pend[0]()
                        elif gi == 2:
                            for tt in range(16):
                                bi = cnt[0] % 3
                                cnt[0] += 1
                                for k in range(8):
                                    S.op("pe", lambda e, k=k, bi=bi, tt=tt, wbi=wbi: e.matmul(
                                        psf[bi][:], lhsT=hT[:, k, tt * 128:(tt + 1) * 128], rhs=wbi[:, k, :], start=(k == 0), stop=(k == 7)),
                                        reads=[t_wbi, t_hT[tt]], writes=[pb[bi]], inc=(k == 7))
                                S.op("dve", lambda e, bi=bi, tt=tt: e.tensor_copy(out=Vt[:, tt, :], in_=psf[bi][:]), reads=[pb[bi]], writes=[t_V[tt]])
                        elif gi in (3, 5):
                            dst, t_dst = (gaT, t_ga) if gi == 3 else (gbT, t_gb)
                            for m in range(4):
                                for c in range(4):
                                    bi = cnt[0] % 3
                                    cnt[0] += 1
                                    for k in range(8):
                                        S.op("pe", lambda e, k=k, bi=bi, m=m, c=c, wbi=wbi: e.matmul(
                                            psf[bi][:], lhsT=wbi[:, k, m * 128:(m + 1) * 128], rhs=hT[:, k, c * 512:(c + 1) * 512],
                                            start=(k == 0), stop=(k == 7)),
                                            reads=[t_wbi] + t_hT[4 * c:4 * c + 4], writes=[pb[bi]], inc=(k == 7))
                                    S.op("act", lambda e, bi=bi, m=m, c=c, dst=dst: e.activation(out=dst[:, m, c * 512:(c + 1) * 512], in_=psf[bi][:], func=AF.Silu),
                                         reads=[pb[bi]], writes=[t_dst[m][c]])
                        else:
                            ppend = [None]
                            for g in range(4):
                                w = (2, 4, 8, 16)[g]
                                nlev = g + 1
                                for c in range(4):
                                    bi = cnt[0] % 3
                                    cnt[0] += 1
                                    for k in range(8):
                                        S.op("pe", lambda e, k=k, bi=bi, g=g, c=c, wbi=wbi: e.matmul(
                                            psf[bi][:], lhsT=wbi[:, k, g * 128:(g + 1) * 128], rhs=hT[:, k, c * 512:(c + 1) * 512],
                                            start=(k == 0), stop=(k == 7)),
                                            reads=[t_wbi] + t_hT[4 * c:4 * c + 4], writes=[pb[bi]], inc=(k == 7))
                                    if ppend[0] is not None:
                                        ppend[0]()
                                        ppend[0] = None
                                    u = ub[c % 2]
                                    up = ub[(c + 1) % 2]
                                    if c == 0:
                                        S.op("pool", lambda e, u=u: e.memset(u[:, 0:16], 0.0), writes=[t_ub[c % 2]])
                                    else:
                                        S.op("pool", lambda e, u=u, up=up: e.tensor_copy(out=u[:, 0:16], in_=up[:, 512:528]),
                                             reads=[t_ub[(c + 1) % 2]], writes=[t_ub[c % 2]])
                                    S.op("act", lambda e, bi=bi, u=u: e.activation(out=u[:, 16:528], in_=psf[bi][:], func=AF.Copy),
                                         reads=[pb[bi]], writes=[t_ub[c % 2]])
                                    src, t_src = u, t_ub[c % 2]
                                    for lv in range(nlev):
                                        sh = 1 << lv
                                        lo = 2 * sh
                                        dstb, t_d = (sa, t_sa) if lv % 2 == 0 else (sb_, t_sb)
                                        S.op("dve", lambda e, src=src, dstb=dstb, lo=lo, sh=sh: e.tensor_tensor(
                                            out=dstb[:, lo:528], in0=src[:, lo:528], in1=src[:, lo - sh:528 - sh], op=ALU.add),
                                            reads=[t_src], writes=[t_d])
                                        src, t_src = dstb, t_d
                                    mi = c % 2
                                    S.op("dve", lambda e, src=src, u=u, mi=mi, w=w: e.scalar_tensor_tensor(
                                        out=mt[mi][:], in0=src[:, 16:528], scalar=1.0 / w, in1=u[:, 16:528], op0=ALU.mult, op1=ALU.subtract),
                                        reads=[t_src, t_ub[c % 2]], writes=[t_mt[mi]])
                                    if c == 0:
                                        S.op("dve", lambda e, src=src, g=g: e.tensor_tensor(out=src[:, 0:16], in0=src[:, 16:32], in1=rcnt[:, g * 16:(g + 1) * 16], op=ALU.mult),
                                             reads=[t_src, t_c2], writes=[t_src])
                                        S.op("dve", lambda e, src=src, u=u, mi=mi: e.tensor_tensor(out=mt[mi][:, 0:16], in0=src[:, 0:16], in1=u[:, 16:32], op=ALU.subtract),
                                             reads=[t_src, t_ub[c % 2]], writes=[t_mt[mi]])
                                    pbk = 3 + (c % 2)

                                    def pool_tail(g=g, c=c, mi=mi, pbk=pbk):
                                        S.op("pe", lambda e: e.matmul(psf[pbk][:], lhsT=poolw[:, g, :], rhs=mt[mi][:], start=True, stop=True),
                                             reads=[t_mt[mi], t_c2], writes=[pb[pbk]])
                                        S.op("dve", lambda e: e.scalar_tensor_tensor(
                                            out=gbT[:, g, c * 512:(c + 1) * 512], in0=psf[pbk][:], scalar=pscale[:, g:g + 1],
                                            in1=gbT[:, g, c * 512:(c + 1) * 512], op0=ALU.mult, op1=ALU.mult),
                                            reads=[pb[pbk], t_c2, t_gb[g][c]], writes=[t_gb[g][c]])
                                    ppend[0] = pool_tail
                        if gi == 4 and ppend[0] is not None:
                            ppend[0]()
                            ppend[0] = None
                        if oi + 2 < len(order) and not ksub:
                            gn = order[oi + 2]
                            S.dma("pool", lambda e, gn=gn, wbi=wbi: e.dma_start(out=wbi[:], in_=e_w_in[:, gn * 512:(gn + 1) * 512].rearrange("(k p) n -> p k n", p=128)),
                                  writes=[t_wbi])
                    S.flush()
                if os.environ.get("KSTOP") == "AB":
                    return

                with ExitStack() as es:
                    load_wo(wo0, t_wo0, e_w_out)
                    e8 = sbt(es, "e8", [8, 1024], BF16)
                    trineg = sbt(es, "trineg", [128, 128], BF16)
                    negmask = sbt(es, "negmask", [128, 64], F32)
                    biasfix = sbt(es, "biasfix", [128, 64], F32)
                    t_c3 = T()
                    S.dma("pool", lambda e: e.dma_start(out=e8[:], in_=cd["e8"][:, :]), writes=[t_c3])
                    S.dma("pool", lambda e: e.dma_start(out=trineg[:], in_=cd["trineg"][:, :]), writes=[t_c3])
                    S.dma("sp", lambda e: e.dma_start(out=negmask[:], in_=cd["negmask"][:, :]), writes=[t_c3])
                    S.dma("sp", lambda e: e.dma_start(out=biasfix[:], in_=cd["biasfix"][:, :]), writes=[t_c3])
                    Mrow = sbt(es, "Mrow", [8, 4, SEQ], BF16)
                    t_M = TL(4)
                    stab4 = [sbt(es, f"stab{h}", [8, SEQ], F32) for h in range(4)]
                    sqt = [sbt(es, f"sqt{i}", [128, 512], BF16) for i in range(8)]
                    t_sq = TL(8)
                    kmx4 = [sbt(es, f"kmx{h}", [8, 8], F32) for h in range(4)]
                    kb324 = [sbt(es, f"kb32{h}", [128, 8], F32) for h in range(4)]
                    kbar4 = [sbt(es, f"kbar{h}", [128, 8], BF16) for h in range(4)]
                    gm4 = [sbt(es, f"gm{h}", [128, 64], F32) for h in range(4)]
                    top84 = [sbt(es, f"top8{h}", [128, 64], F32) for h in range(4)]
                    sel4 = [sbt(es, f"sel{h}", [128, 64], F32) for h in range(4)]
                    t_stab4, t_kmx4, t_kb4, t_gm4, t_top4, t_sel4 = TL(4), TL(4), TL(4), TL(4), TL(4), TL(4)

                    def prep_ops(h):
                        L_ = []
                        add = lambda eng, fn, reads=(), writes=(), **kw: L_.append((eng, fn, list(reads), list(writes), kw))
                        stab, kmx, kb32, kbar, gm, top8, sel = stab4[h], kmx4[h], kb324[h], kbar4[h], gm4[h], top84[h], sel4[h]
                        t_stab, t_kmx, t_kb, t_gm, t_top, t_sel = t_stab4[h], t_kmx4[h], t_kb4[h], t_gm4[h], t_top4[h], t_sel4[h]
                        nb = 3 + h
                        tb = [h % 3, (h + 1) % 3]
                        for c in range(4):
                            i = 2 * h
                            add("act", lambda e, i=i, c=c: e.activation(out=sqt[i][:], in_=qkT[:, 4 + h, c * 512:(c + 1) * 512], func=AF.Square),
                                [t_qk[4 + h][c]], [t_sq[i]])
                            add("pe", lambda e, i=i: e.matmul(psf[nb][0:8, :], lhsT=onesb[:, 0:8], rhs=sqt[i][:], start=True, stop=True),
                                [t_sq[i], t_const], [pb[nb]])
                            add("dve", lambda e, c=c: e.tensor_reduce(out=kmx[:, c:c + 1], in_=psf[nb][0:8, :], axis=AX.X, op=ALU.max),
                                [pb[nb]], [t_kmx])
                        add("dve", lambda e: e.tensor_reduce(out=kmx[:, 4:5], in_=kmx[:, 0:4], axis=AX.X, op=ALU.max), [t_kmx], [t_kmx])
                        for c in range(4):
                            i = 2 * h + 1
                            add("act", lambda e, i=i, c=c: e.activation(out=sqt[i][:], in_=qkT[:, h, c * 512:(c + 1) * 512], func=AF.Square),
                                [t_qk[h][c]], [t_sq[i]])
                            add("pe", lambda e, i=i: e.matmul(psf[nb][0:8, :], lhsT=onesb[:, 0:8], rhs=sqt[i][:], start=True, stop=True),
                                [t_sq[i], t_const], [pb[nb]])
                            add("act", lambda e, c=c: e.activation(out=stab[:, c * 512:(c + 1) * 512], in_=psf[nb][0:8, :], func=AF.Sqrt, scale=kmx[:, 4:5]),
                                [pb[nb], t_kmx], [t_stab])
                        add("dve", lambda e: e.tensor_reduce(out=kb32[:], in_=qkT[:, 4 + h, :].rearrange("p (n s) -> p n s", s=256), axis=AX.X, op=ALU.add),
                            t_qk[4 + h], [t_kb])
                        add("dve", lambda e: e.tensor_scalar(out=kbar[:], in0=kb32[:], scalar1=1.0 / 256, scalar2=None, op0=ALU.mult), [t_kb], [t_kb])
                        for i8 in range(8):
                            add("pe", lambda e, i8=i8: e.matmul(psf[nb][:, i8 * 8:(i8 + 1) * 8], lhsT=qkT[:, h, (8 + i8) * 128:(9 + i8) * 128], rhs=kbar[:], start=True, stop=True),
                                [t_qk[h][2 + i8 // 4], t_kb], [pb[nb]], inc=(i8 == 7))
                        add("dve", lambda e: e.tensor_tensor(out=gm[:], in0=psf[nb][:, 0:64], in1=negmask[:], op=ALU.add), [pb[nb], t_c3], [t_gm])
                        for i8 in range(8):
                            add("dve", lambda e, i8=i8: e.max(out=top8[:, i8 * 8:(i8 + 1) * 8], in_=gm[:, i8 * 8:(i8 + 1) * 8]), [t_gm], [t_top])
                        for i8 in range(8):
                            add("dve", lambda e, i8=i8: e.tensor_scalar(out=sel[:, i8 * 8:(i8 + 1) * 8], in0=gm[:, i8 * 8:(i8 + 1) * 8],
                                                                      scalar1=top8[:, i8 * 8 + 2:i8 * 8 + 3], scalar2=None, op0=ALU.is_ge),
                                [t_gm, t_top], [t_sel])
                        add("dve", lambda e: e.scalar_tensor_tensor(out=sel[:], in0=sel[:], scalar=-NEGB, in1=biasfix[:], op0=ALU.mult, op1=ALU.add),
                            [t_sel, t_c3], [t_sel])
                        add("act", lambda e: e.activation(out=Mrow[:, h, 0:1024], in_=stab[:, 0:1024], func=AF.Copy, scale=-1.0), [t_stab], [t_M[h]])
                        for hf in range(2):
                            for i4 in range(4):
                                i8 = hf * 4 + i4
                                add("pe", lambda e, i8=i8, i4=i4: e.transpose(out=psf[nb][0:8, i4 * 128:(i4 + 1) * 128], in_=sel[:, i8 * 8:(i8 + 1) * 8], identity=identf[:]),
                                    [t_sel, t_const], [pb[nb]], inc=(i4 == 3))
                            add("dve", lambda e, hf=hf: e.tensor_tensor(out=Mrow[:, h, 1024 + hf * 512:1536 + hf * 512], in0=psf[nb][0:8, :],
                                                                      in1=stab[:, 1024 + hf * 512:1536 + hf * 512], op=ALU.subtract),
                                [pb[nb], t_stab], [t_M[h]])
                        return L_

                    plists = [prep_ops(h) for h in range(4)]
                    for i in range(max(len(l) for l in plists)):
                        for l in plists:
                            if i < len(l):
                                eng, fn, rd, wr, kw = l[i]
                                S.op(eng, fn, reads=rd, writes=wr, **kw)

                    PT = [sbt(es, f"PT{i}", [128, 512], BF16) for i in range(3)]
                    t_PT = TL(3)
                    lns = sbt(es, "lns", [128, 512], F32)
                    rinv = sbt(es, "rinv", [128, 512], F32)
                    ot = sbt(es, "ot", [128, 512], F32)
                    t_lns, t_rinv, t_ot = T(), T(), T()
                    scale = 1.0 / math.sqrt(HD)
                    items = [(h, qc, kt) for h in range(4) for qc in range(4) for kt in range(4 * qc + 4)]

                    def emit_S(idx):
                        h, qc, kt = items[idx]
                        sb_i = idx % 2
                        off = max(0, kt * 128 - qc * 512)
                        q0 = qc * 512 + off
                        q1 = (qc + 1) * 512
                        n = kt // 2
                        diag = kt >= 4 * qc
                        S.op("pe", lambda e: e.matmul(psf[sb_i][:, off:512], lhsT=qkT[:, 4 + h, kt * 128:(kt + 1) * 128], rhs=qkT[:, h, q0:q1], start=True, stop=False),
                             reads=[t_qk[4 + h][kt // 4], t_qk[h][qc]], writes=[pb[sb_i]])
                        S.op("pe", lambda e: e.matmul(psf[sb_i][:, off:512], lhsT=e8[:, n * 128:(n + 1) * 128], rhs=Mrow[:, h, q0:q1], start=False, stop=(not diag)),
                             reads=[t_M[h], t_c3], writes=[pb[sb_i]])
                        if diag:
                            S.op("pe", lambda e: e.matmul(psf[sb_i][:, off:off + 128], lhsT=identb[:], rhs=trineg[:], start=False, stop=True),
                                 reads=[t_c3, t_const], writes=[pb[sb_i]])
                        pi = idx % 3
                        S.op("act", lambda e: e.activation(out=PT[pi][:, off:512], in_=psf[sb_i][:, off:512], func=AF.Exp, scale=scale),
                             reads=[pb[sb_i]], writes=[t_PT[pi]])

                    def emit_PV(idx):
                        h, qc, kt = items[idx]
                        off = max(0, kt * 128 - qc * 512)
                        pi = idx % 3
                        par = (h * 4 + qc) % 2
                        ob, sbk = 2 + par, 4 + par
                        last = (kt == 4 * qc + 3)
                        S.op("pe", lambda e: e.matmul(psf[ob][:, off:512], lhsT=Vt[:, kt, h * 128:(h + 1) * 128], rhs=PT[pi][:, off:512], start=(kt == 0), stop=last),
                             reads=[t_V[kt], t_PT[pi]], writes=[pb[ob]])
                        S.op("pe", lambda e: e.matmul(psf[sbk][:, off:512], lhsT=onesb[:], rhs=PT[pi][:, off:512], start=(kt == 0), stop=last),
                             reads=[t_const, t_PT[pi]], writes=[pb[sbk]])
                        if last:
                            S.op("act", lambda e: e.activation(out=lns[:], in_=psf[sbk][:], func=AF.Ln), reads=[pb[sbk]], writes=[t_lns])
                            S.op("act", lambda e: e.activation(out=rinv[:], in_=lns[:], func=AF.Exp, scale=-1.0), reads=[t_lns], writes=[t_rinv])
                            S.op("dve", lambda e: e.tensor_tensor(out=ot[:], in0=psf[ob][:], in1=rinv[:], op=ALU.mult), reads=[pb[ob], t_rinv], writes=[t_ot])
                            S.op("pool", lambda e: e.tensor_tensor(out=gaT[:, h, qc * 512:(qc + 1) * 512], in0=ot[:], in1=gaT[:, h, qc * 512:(qc + 1) * 512], op=ALU.mult),
                                 reads=[t_ot, t_ga[h][qc]], writes=[t_ga[h][qc]])

                    emit_S(0)
                    for idx in range(len(items)):
                        if idx + 1 < len(items):
                            emit_S(idx + 1)
                        emit_PV(idx)
                    phase_outproj(es, e_w_out, post_g[0:1, :], x_d, b, gaT, gbT, None, None, wo0, t_wo0,
                                  tokfn=lambda k, c: (t_ga[k][c] if k < 4 else t_gb[k - 4][c]))
                    S.flush()

        L1 = top

        def layer1_all(nb):
            with ExitStack() as P1:
                wv_sb = sbt(P1, "wv_sb", [128, 4, 8, 2, 128], BF16)
                toep_sb = sbt(P1, "toep_sb", [128, 4, 8, 256], BF16)
                w3_sb = sbt(P1, "w3_sb", [128, 16, 2, 256], BF16)
                pw_tab = sbt(P1, "pw_tab", [128, 16, 8, 3], F32)
                dA = sbt(P1, "dA", [128, 4], F32)
                glub = sbt(P1, "glub", [128, 8], F32)
                lng = sbt(P1, "lng", [128, 4], F32)
                lnb = sbt(P1, "lnb", [128, 4], F32)
                dwT = sbt(P1, "dwT", [128, 4, 31], F32)
                ones512 = sbt(P1, "ones512", [128, 128], BF16)
                t_par = T()
                for dst, src in ((dA, o_dA), (glub, o_glub), (lng, o_lng), (lnb, o_lnb)):
                    S.dma("sp", lambda e, dst=dst, src=src: e.dma_start(out=dst[:], in_=src[:, :]), writes=[t_par])
                S.dma("sp", lambda e: e.dma_start(out=dwT[:], in_=o_dwT[:, :, :]), writes=[t_par])
                S.op("act", lambda e: e.activation(out=ones512[:], in_=onesb[:], func=AF.Copy, scale=1.0 / 512), reads=[t_const], writes=[t_par])

                with ExitStack() as es:
                    tkA, tkB, t_m = T(), T(), T()
                    listA, listB = [], []
                    cur = {"eng": "dve", "tk": tkA, "list": listA}

                    def vop(fn, eng=None):
                        cur["list"].append(("op", eng or cur["eng"], fn, [cur["tk"], t_par, t_m], [cur["tk"]], {}))

                    def pop(fn, eng=None):
                        cur["list"].append(("op", eng or cur["eng"], fn, [cur["tk"], t_par, t_m], [T()], {}))

                    def tt(o, a, b_, op, eng=None):
                        vop(lambda e: e.tensor_tensor(out=o, in0=a, in1=b_, op=op), eng)

                    def ld(name, src, shape, tok=None):
                        t = sbt(es, name, shape, F32)
                        if tok is not None:
                            S.dma("sp", lambda e: e.dma_start(out=t[:], in_=src), writes=[tok])
                        else:
                            cur["list"].append(("dma", "sp", lambda e: e.dma_start(out=t[:], in_=src), [], [cur["tk"]], {}))
                        return t

                    def compute_a(tag, lr, li, dtl, n):
                        mk = lambda nm: sbt(es, f"{tag}_{nm}", [128, n], F32)
                        dtv, x1, mg, th, u, r, sn_, cs_, ar, ai = [mk(k) for k in ("dt", "x1", "mg", "th", "u", "r", "sn", "cs", "ar", "ai")]
                        ui = sbt(es, f"{tag}_ui", [128, n], I32)
                        vop(lambda e: e.activation(out=dtv[:], in_=dtl[:], func=AF.Exp), "act")
                        tt(x1[:], lr[:], dtv[:], ALU.mult)
                        vop(lambda e: e.activation(out=mg[:], in_=x1[:], func=AF.Exp), "act")
                        tt(th[:], li[:], dtv[:], ALU.mult)
                        for shift, dst in ((0.0, sn_), (math.pi / 2, cs_)):
                            vop(lambda e, shift=shift: e.tensor_scalar(out=r[:], in0=th[:], scalar1=shift, scalar2=None, op0=ALU.add))
                            vop(lambda e: e.tensor_copy(out=x1[:], in_=r[:]))
                            for jj in range(1, 6):
                                vop(lambda e, jj=jj: e.tensor_scalar(out=u[:], in0=x1[:], scalar1=(2 * jj - 1) * math.pi, scalar2=-2 * math.pi, op0=ALU.is_ge, op1=ALU.mult))
                                tt(r[:], r[:], u[:], ALU.add)
                            vop(lambda e, dst=dst: e.activation(out=dst[:], in_=r[:], func=AF.Sin), "act")
                        tt(ar[:], mg[:], cs_[:], ALU.mult)
                        tt(ai[:], mg[:], sn_[:], ALU.mult)
                        return ar, ai

                    lrA = ld("lrA", o_lamre_A[:, :], [128, 256])
                    liA = ld("liA", o_lamim_A[:, :], [128, 256])
                    dtA = ld("dtA", o_dt_A[:, :], [128, 256])
                    brA = ld("brA", o_bre_A[:, :], [128, 256])
                    biA = ld("biA", o_bim_A[:, :], [128, 256])
                    mA = ld("mA", cd["maskA"][:, :], [128, 2], t_m)
                    mB = ld("mB", cd["maskB"][:, :], [128, 2], t_m)
                    arA, aiA = compute_a("A", lrA, liA, dtA, 256)
                    mk = lambda nm, n=256: sbt(es, nm, [128, n], F32)
                    nr, den, t1, t2, fr, fi = [mk(k) for k in ("nr", "den", "t1", "t2", "fr", "fi")]
                    vop(lambda e: e.tensor_scalar(out=nr[:], in0=arA[:], scalar1=-1.0, scalar2=None, op0=ALU.add))
                    tt(t1[:], lrA[:], lrA[:], ALU.mult)
                    tt(t2[:], liA[:], liA[:], ALU.mult)
                    tt(den[:], t1[:], t2[:], ALU.add)
                    vop(lambda e: e.reciprocal(out=den[:], in_=den[:]), "dve")
                    tt(t1[:], nr[:], lrA[:], ALU.mult)
                    tt(t2[:], aiA[:], liA[:], ALU.mult)
                    tt(t1[:], t1[:], t2[:], ALU.add)
                    tt(fr[:], t1[:], den[:], ALU.mult)
                    tt(t1[:], aiA[:], lrA[:], ALU.mult)
                    tt(t2[:], nr[:], liA[:], ALU.mult)
                    tt(t1[:], t1[:], t2[:], ALU.subtract)
                    tt(fi[:], t1[:], den[:], ALU.mult)
                    Gall = sbt(es, "Gall", [128, 8, 2, 256], F32)
                    tt(t1[:], fr[:], brA[:], ALU.mult)
                    tt(t2[:], fi[:], biA[:], ALU.mult)
                    tt(Gall[:, 0, 0, :], t1[:], t2[:], ALU.subtract)
                    tt(t1[:], fr[:], biA[:], ALU.mult)
                    tt(t2[:], fi[:], brA[:], ALU.mult)
                    tt(Gall[:, 0, 1, :], t1[:], t2[:], ALU.add)
                    for m in range(7):
                        tt(t1[:], Gall[:, m, 0, :], arA[:], ALU.mult)
                        tt(t2[:], Gall[:, m, 1, :], aiA[:], ALU.mult)
                        tt(Gall[:, m + 1, 0, :], t1[:], t2[:], ALU.subtract)
                        tt(t1[:], Gall[:, m, 0, :], aiA[:], ALU.mult)
                        tt(t2[:], Gall[:, m, 1, :], arA[:], ALU.mult)
                        tt(Gall[:, m + 1, 1, :], t1[:], t2[:], ALU.add)
                    for s in range(8):
                        for ri in range(2):
                            for gi in range(2):
                                pop(lambda e, s=s, ri=ri, gi=gi: e.tensor_scalar(
                                    out=wv_sb[:, :, s, ri, gi * 64:(gi + 1) * 64],
                                    in0=Gall[:, 7 - s, ri, :].rearrange("p (c n) -> p c n", c=4),
                                    scalar1=mA[:, gi:gi + 1], scalar2=None, op0=ALU.mult))
                    Kall = sbt(es, "Kall", [128, 4, 8, 16], F32)
                    cur["eng"], cur["tk"], cur["list"] = "dve", tkB, listB
                    lrB = ld("lrB", o_lamre_B[:, :], [128, 16])
                    liB = ld("liB", o_lamim_B[:, :], [128, 16])
                    dtB = ld("dtB", o_dt_B[:, :], [128, 16])
                    crB = ld("crB", o_cre_B[:, :], [128, 256])
                    ciB = ld("ciB", o_cim_B[:, :], [128, 256])
                    arB, aiB = compute_a("B", lrB, liB, dtB, 16)
                    PB = sbt(es, "PB", [128, 8, 2, 16], F32)
                    s1 = sbt(es, "s1", [128, 16], F32)
                    s2 = sbt(es, "s2", [128, 16], F32)
                    vop(lambda e: e.tensor_copy(out=PB[:, 0, 0, :], in_=arB[:]))
                    vop(lambda e: e.tensor_copy(out=PB[:, 0, 1, :], in_=aiB[:]))
                    for r in range(7):
                        tt(s1[:], PB[:, r, 0, :], arB[:], ALU.mult)
                        tt(s2[:], PB[:, r, 1, :], aiB[:], ALU.mult)
                        tt(PB[:, r + 1, 0, :], s1[:], s2[:], ALU.subtract)
                        tt(s1[:], PB[:, r, 0, :], aiB[:], ALU.mult)
                        tt(s2[:], PB[:, r, 1, :], arB[:], ALU.mult)
                        tt(PB[:, r + 1, 1, :], s1[:], s2[:], ALU.add)
                    brB = ld("brB", o_bre_B[:, :], [128, 256])
                    biB = ld("biB", o_bim_B[:, :], [128, 256])
                    mkb = lambda nm: sbt(es, nm, [128, 16], F32)
                    nrB, denB, x1B, x2B, frB, fiB = [mkb(k) for k in ("nrB", "denB", "x1B", "x2B", "frB", "fiB")]
                    vop(lambda e: e.tensor_scalar(out=nrB[:], in0=arB[:], scalar1=-1.0, scalar2=None, op0=ALU.add))
                    tt(x1B[:], lrB[:], lrB[:], ALU.mult)
                    tt(x2B[:], liB[:], liB[:], ALU.mult)
                    tt(denB[:], x1B[:], x2B[:], ALU.add)
                    vop(lambda e: e.reciprocal(out=denB[:], in_=denB[:]), "dve")
                    tt(x1B[:], nrB[:], lrB[:], ALU.mult)
                    tt(x2B[:], aiB[:], liB[:], ALU.mult)
                    tt(x1B[:], x1B[:], x2B[:], ALU.add)
                    tt(frB[:], x1B[:], denB[:], ALU.mult)
                    tt(x1B[:], aiB[:], lrB[:], ALU.mult)
                    tt(x2B[:], nrB[:], liB[:], ALU.mult)
                    tt(x1B[:], x1B[:], x2B[:], ALU.subtract)
                    tt(fiB[:], x1B[:], denB[:], ALU.mult)
                    w1 = sbt(es, "w1", [128, 256], F32)
                    w2 = sbt(es, "w2", [128, 256], F32)
                    bbr = sbt(es, "bbrB", [128, 256], F32)
                    bbi = sbt(es, "bbiB", [128, 256], F32)
                    v3 = lambda t: t[:, :].rearrange("p (a i) -> p a i", a=16)
                    bc = lambda t: t[:, :].unsqueeze(2).broadcast_to([128, 16, 16])
                    tt(v3(w1), v3(brB), bc(frB), ALU.mult)
                    tt(v3(w2), v3(biB), bc(fiB), ALU.mult)
                    tt(bbr[:], w1[:], w2[:], ALU.subtract)
                    tt(v3(w1), v3(biB), bc(frB), ALU.mult)
                    tt(v3(w2), v3(brB), bc(fiB), ALU.mult)
                    tt(bbi[:], w1[:], w2[:], ALU.add)
                    Bmr = sbt(es, "Bmr", [128, 16, 128], F32)
                    Bmi = sbt(es, "Bmi", [128, 16, 128], F32)
                    vop(lambda e: e.memset(Bmr[:], 0.0), "pool")
                    vop(lambda e: e.memset(Bmi[:], 0.0), "pool")
                    for q in range(4):
                        for gi in range(2):
                            c0 = 32 * q + 16 * gi
                            vop(lambda e, q=q, gi=gi, c0=c0: e.tensor_scalar(
                                out=Bmr[:, :, :].rearrange("p (c q) m -> p c q m", q=4)[:, :, q, c0:c0 + 16],
                                in0=bbr[:, :].rearrange("p (c q j) -> p c q j", q=4, j=16)[:, :, q, :],
                                scalar1=mB[:, gi:gi + 1], scalar2=None, op0=ALU.mult))
                            vop(lambda e, q=q, gi=gi, c0=c0: e.tensor_scalar(
                                out=Bmi[:, :, :].rearrange("p (c q) m -> p c q m", q=4)[:, :, q, c0:c0 + 16],
                                in0=bbi[:, :].rearrange("p (c q j) -> p c q j", q=4, j=16)[:, :, q, :],
                                scalar1=mB[:, gi:gi + 1], scalar2=-1.0, op0=ALU.mult, op1=ALU.mult))
                    CAr = sbt(es, "CAr", [128, 9, 16, 16], F32)
                    CAi = sbt(es, "CAi", [128, 9, 16, 16], F32)
                    vop(lambda e: e.tensor_copy(out=CAr[:, 0, :, :], in_=v3(crB)))
                    vop(lambda e: e.tensor_copy(out=CAi[:, 0, :, :], in_=v3(ciB)))
                    for r in range(8):
                        pre = PB[:, r, 0, :].unsqueeze(2).broadcast_to([128, 16, 16])
                        pim = PB[:, r, 1, :].unsqueeze(2).broadcast_to([128, 16, 16])
                        tt(v3(w1), v3(crB), pre, ALU.mult)
                        tt(v3(w2), v3(ciB), pim, ALU.mult)
                        tt(CAr[:, r + 1, :, :], v3(w1), v3(w2), ALU.subtract)
                        tt(v3(w1), v3(crB), pim, ALU.mult)
                        tt(v3(w2), v3(ciB), pre, ALU.mult)
                        tt(CAi[:, r + 1, :, :], v3(w1), v3(w2), ALU.add)
                        for gi in range(2):
                            pop(lambda e, r=r, gi=gi: e.tensor_scalar(out=w3_sb[:, :, 0, gi * 128 + r * 16:gi * 128 + r * 16 + 16], in0=CAr[:, r + 1, :, :],
                                                                    scalar1=mB[:, gi:gi + 1], scalar2=None, op0=ALU.mult))
                            pop(lambda e, r=r, gi=gi: e.tensor_scalar(out=w3_sb[:, :, 1, gi * 128 + r * 16:gi * 128 + r * 16 + 16], in0=CAi[:, r + 1, :, :],
                                                                    scalar1=mB[:, gi:gi + 1], scalar2=-1.0, op0=ALU.mult, op1=ALU.mult))
                    for ct in range(4):
                        for q in range(4):
                            p = 4 * ct + q
                            listB.append(("op", "pe", lambda e, p=p, q=q: e.matmul(psf[0][:, 0:128], lhsT=Bmr[:, p, :], rhs=CAr[:, 0:8, p, :], start=(q == 0), stop=False),
                                          [tkB], [pb[0]], {"inc": False}))
                            listB.append(("op", "pe", lambda e, p=p, q=q: e.matmul(psf[0][:, 0:128], lhsT=Bmi[:, p, :], rhs=CAi[:, 0:8, p, :], start=False, stop=(q == 3)),
                                          [tkB], [pb[0]], {"inc": (q == 3)}))
                        listB.append(("op", "dve", lambda e, ct=ct: e.tensor_copy(out=Kall[:, ct, :, :], in_=psf[0][:, 0:128].rearrange("p (t i) -> p t i", t=8)),
                                      [pb[0], tkB], [tkB], {}))
                    vop(lambda e: e.memset(toep_sb[:], 0.0), "pool")
                    for s in range(8):
                        for gi in range(2):
                            vop(lambda e, s=s, gi=gi: e.tensor_scalar(
                                out=toep_sb[:, :, s, gi * 128 + s * 16:gi * 128 + 128],
                                in0=Kall[:, :, 0:8 - s, :].rearrange("p c t i -> p c (t i)"),
                                scalar1=mA[:, gi:gi + 1], scalar2=None, op0=ALU.mult))
                    qr = sbt(es, "qr", [128, 16], F32)
                    qi = sbt(es, "qi", [128, 16], F32)
                    vop(lambda e: e.tensor_copy(out=qr[:], in_=PB[:, 7, 0, :]))
                    vop(lambda e: e.tensor_copy(out=qi[:], in_=PB[:, 7, 1, :]))
                    for m in range(8):
                        pop(lambda e, m=m: e.tensor_copy(out=pw_tab[:, :, m, 0], in_=qr[:]))
                        pop(lambda e, m=m: e.tensor_copy(out=pw_tab[:, :, m, 1], in_=qi[:]))
                        pop(lambda e, m=m: e.tensor_scalar(out=pw_tab[:, :, m, 2], in0=qi[:], scalar1=-1.0, scalar2=None, op0=ALU.mult))
                        if m < 7:
                            tt(s1[:], qr[:], qr[:], ALU.mult)
                            tt(s2[:], qi[:], qi[:], ALU.mult)
                            tt(s2[:], s1[:], s2[:], ALU.subtract)
                            tt(s1[:], qr[:], qi[:], ALU.mult)
                            vop(lambda e: e.tensor_scalar(out=qi[:], in0=s1[:], scalar1=2.0, scalar2=None, op0=ALU.mult))
                            vop(lambda e: e.tensor_copy(out=qr[:], in_=s2[:]))
                    ia = ib = 0

                    def emit_rec(rec):
                        kind, eng, fn, rd, wr, kw = rec
                        if kind == "dma":
                            S.dma(eng, fn, reads=rd, writes=wr)
                        else:
                            S.op(eng, fn, reads=rd, writes=wr, **kw)
                    while ia < len(listA) or ib < len(listB):
                        if ia < len(listA):
                            emit_rec(listA[ia])
                            ia += 1
                        for _ in range(2):
                            if ib < len(listB):
                                emit_rec(listB[ib])
                                ib += 1
                    S.flush()

                for b in range(nb):
                    layer1(b, wv_sb, toep_sb, w3_sb, pw_tab, dA, glub, lng, lnb, dwT, ones512, t_par)

        def layer1(b, wv_sb, toep_sb, w3_sb, pw_tab, dA, glub, lng, lnb, dwT, ones512, t_par):
            with ExitStack() as L:
                suT = sbt(L, "suT", [128, 4, SEQ], BF16)
                gcT = sbt(L, "gcT", [128, 4, SEQ], BF16)
                gdT = sbt(L, "gdT", [128, 4, SEQ], BF16)
                gpad = sbt(L, "gpad", [128, 4, SEQ + 32], BF16)
                t_su = [TL(4) for _ in range(4)]
                t_gc = [TL(4) for _ in range(4)]
                t_gd = [TL(4) for _ in range(4)]
                t_gp = TL(4)
                wo1 = sbt(L, "wo1", [128, 8, D], BF16)
                t_wo1 = T()
                pwsb = sbt(L, "pwsb", [128, 4, 512], BF16)
                t_pw = T()
                with ExitStack() as es:
                    hT = sbt(es, "hT1", [128, 8, SEQ], BF16)
                    t_hT = TL(16)
                    phase_norm_T(es, out_d, b, pre_g[1:2, :], hT, t_hT, nxb=2)
                    wb = [sbt(es, f"wb1_{i}", [128, 8, 512], BF16) for i in range(2)]
                    t_wb = TL(2)
                    sg = [sbt(es, f"sg{i}", [128, 512], BF16) for i in range(2)]
                    t_sg = TL(2)
                    order = [0, 1, 4, 2, 3]

                    def loadw(oi):
                        gi = order[oi]
                        S.dma("pool", lambda e: e.dma_start(out=wb[oi % 2][:], in_=o_w_in[:, gi * 512:(gi + 1) * 512].rearrange("(k p) n -> p k n", p=128)),
                              writes=[t_wb[oi % 2]])
                    loadw(0)
                    loadw(1)
                    S.op("pool", lambda e: e.memset(gpad[:, :, 0:32], 0.0), writes=t_gp)
                    cnt = [0]

                    def proj(wbi, t_wbi, m, c):
                        bi = cnt[0] % 3
                        cnt[0] += 1
                        for k in range(8):
                            S.op("pe", lambda e, k=k: e.matmul(psf[bi][:], lhsT=wbi[:, k, m * 128:(m + 1) * 128], rhs=hT[:, k, c * 512:(c + 1) * 512],
                                                               start=(k == 0), stop=(k == 7)),
                                 reads=[t_wbi] + t_hT[4 * c:4 * c + 4], writes=[pb[bi]], inc=(k == 7))
                        return bi
                    for oi in range(3):
                        gi = order[oi]
                        dst, t_dst, fn = ((suT, t_su, AF.Copy), (gcT, t_gc, AF.Silu), None, None, (gdT, t_gd, AF.Silu))[gi]
                        for m in range(4):
                            for c in range(4):
                                bi = proj(wb[oi % 2], t_wb[oi % 2], m, c)
                                S.op("act", lambda e, bi=bi, m=m, c=c, dst=dst, fn=fn: e.activation(out=dst[:, m, c * 512:(c + 1) * 512], in_=psf[bi][:], func=fn),
                                     reads=[pb[bi]], writes=[t_dst[m][c]])
                        if oi + 2 < 5:
                            loadw(oi + 2)
                    si = 0
                    for m in range(4):
                        for c in range(4):
                            bi = proj(wb[0], t_wb[0], m, c)
                            i = si % 2
                            si += 1
                            S.op("act", lambda e, bi=bi, i=i: e.activation(out=sg[i][:], in_=psf[bi][:], func=AF.Sigmoid), reads=[pb[bi]], writes=[t_sg[i]])
                            bi2 = proj(wb[1], t_wb[1], m, c)
                            S.op("dve", lambda e, bi2=bi2, i=i, m=m, c=c: e.tensor_tensor(out=gpad[:, m, 32 + c * 512:32 + (c + 1) * 512], in0=psf[bi2][:], in1=sg[i][:], op=ALU.mult),
                                 reads=[pb[bi2], t_sg[i]], writes=[t_gp[m]])
                    S.flush()
                if os.environ.get("KSTOP") == "AB1":
                    return

                with ExitStack() as es:
                    Sb = [[[sbt(es, f"S{sl}{pi}{pp}", [128, 2, 512], F32) for pp in range(2)] for pi in range(2)] for sl in range(2)]
                    t_S = [[[T() for pp in range(2)] for pi in range(2)] for sl in range(2)]
                    Sp = [[[sbt(es, f"Sp{sl}{pi}{ri}", [128, 256], BF16) for ri in range(2)] for pi in range(2)] for sl in range(2)]
                    t_Sp = [[T() for pi in range(2)] for sl in range(2)]
                    for sl in range(2):
                        for pi in range(2):
                            for ri in range(2):
                                S.op("pool", lambda e, sl=sl, pi=pi, ri=ri: e.memset(Sp[sl][pi][ri][:, 0:1], 0.0), writes=[t_Sp[sl][pi]])
                                for pp in range(2):
                                    S.op("pool", lambda e, sl=sl, pi=pi, ri=ri, pp=pp: e.memset(Sb[sl][pi][pp][:, ri, 0:256], 0.0), writes=[t_S[sl][pi][pp]])
                    ysb = [sbt(es, f"ysb{i}", [128, 2, 8, 128], BF16) for i in range(2)]
                    t_ysb = [T(), T()]
                    gluw = sbt(es, "gluw", [128, 4, 1024], BF16)
                    t_gw = T()
                    for hf in range(2):
                        S.dma("pool", lambda e, hf=hf: e.dma_start(out=gluw[:, :, hf * 512:(hf + 1) * 512], in_=o_glu_w[:, hf * 512:(hf + 1) * 512].rearrange("(k p) n -> p k n", p=128)),
                              writes=[t_gw])
                    load_wo(wo1, t_wo1, o_w_out)
                    S.dma("pool", lambda e: e.dma_start(out=pwsb[:], in_=o_pw.rearrange("(k p) n -> p k n", p=128)), writes=[t_pw])
                    couples = [(ct, q0) for ct in range(4) for q0 in (0, 2)]

                    def emit_V(ci):
                        ct, q0 = couples[ci]
                        sl = ci % 2
                        for pi in range(2):
                            q = q0 + pi
                            rows = slice(32 * q, 32 * q + 32)
                            tp = (32 * q, 0)
                            for ri in range(2):
                                bk = (0, 1, 4, 5)[pi * 2 + ri]
                                for s in range(8):
                                    S.op("pe", lambda e, s=s, ri=ri, bk=bk, rows=rows, ct=ct, tp=tp: e.matmul(
                                        psf[bk][:, 0:256], lhsT=wv_sb[rows, ct, s, ri, :],
                                        rhs=suT[rows, ct, :].rearrange("p (k s) -> p s k", s=8)[:, s, :], start=(s == 0), stop=(s == 7), tile_position=tp),
                                        reads=t_su[ct] + [t_par], writes=[pb[bk]], inc=(s == 7))
                                S.op("act", lambda e, ri=ri, bk=bk, sl=sl, pi=pi: e.activation(out=Sb[sl][pi][0][:, ri, 256:512], in_=psf[bk][:, 0:256], func=AF.Copy),
                                     reads=[pb[bk]], writes=[t_S[sl][pi][0]])

                    def emit_scan(ci):
                        ct, q0 = couples[ci]
                        sl = ci % 2
                        for m in range(8):
                            sh = 1 << m
                            a, d_ = m % 2, (m + 1) % 2
                            for stage in range(3):
                                for pi in range(2):
                                    p = ct * 4 + q0 + pi
                                    src, dst = Sb[sl][pi][a], Sb[sl][pi][d_]
                                    ts, td = t_S[sl][pi][a], t_S[sl][pi][d_]
                                    pr = pw_tab[:, p, m, 0:1]
                                    pim = pw_tab[:, p, m, 1:2]
                                    npi = pw_tab[:, p, m, 2:3]
                                    if stage == 0:
                                        S.op("dve", lambda e, src=src, dst=dst, sh=sh, pr=pr: e.scalar_tensor_tensor(out=dst[:, :, 256:512], in0=src[:, :, 256 - sh:512 - sh], scalar=pr, in1=src[:, :, 256:512], op0=ALU.mult, op1=ALU.add),
                                             reads=[ts, t_par], writes=[td])
                                    elif stage == 1:
                                        S.op("dve", lambda e, src=src, dst=dst, sh=sh, npi=npi: e.scalar_tensor_tensor(out=dst[:, 0, 256:512], in0=src[:, 1, 256 - sh:512 - sh], scalar=npi, in1=dst[:, 0, 256:512], op0=ALU.mult, op1=ALU.add),
                                             reads=[ts, td, t_par], writes=[td])
                                    else:
                                        S.op("dve", lambda e, src=src, dst=dst, sh=sh, pim=pim: e.scalar_tensor_tensor(out=dst[:, 1, 256:512], in0=src[:, 0, 256 - sh:512 - sh], scalar=pim, in1=dst[:, 1, 256:512], op0=ALU.mult, op1=ALU.add),
                                             reads=[ts, td, t_par], writes=[td])

                    def emit_y(ci):
                        ct, q0 = couples[ci]
                        sl = ci % 2
                        yb_ = ysb[ct % 2]
                        t_y = t_ysb[ct % 2]
                        for pi in range(2):
                            q = q0 + pi
                            p = ct * 4 + q
                            rows = slice(32 * q, 32 * q + 32)
                            tp = (32 * q, 0)
                            for ri in range(2):
                                S.op("act", lambda e, ri=ri, sl=sl, pi=pi: e.activation(out=Sp[sl][pi][ri][:, 1:256], in_=Sb[sl][pi][0][:, ri, 256:511], func=AF.Copy),
                                     reads=[t_S[sl][pi][0]], writes=[t_Sp[sl][pi]])
                            for kt2 in range(2):
                                bk = 2 + kt2
                                for s in range(8):
                                    S.op("pe", lambda e, s=s, kt2=kt2, bk=bk, rows=rows, ct=ct, tp=tp: e.matmul(
                                        psf[bk][:, 0:256],
                                        lhsT=suT[rows, ct, kt2 * 1024:(kt2 + 1) * 1024].rearrange("p (k s) -> p s k", s=8)[:, s, :],
                                        rhs=toep_sb[rows, ct, s, :], start=(s == 0), stop=False, tile_position=tp),
                                        reads=t_su[ct] + [t_par], writes=[pb[bk]], inc=False)
                                for ri in range(2):
                                    S.op("pe", lambda e, ri=ri, kt2=kt2, bk=bk, sl=sl, pi=pi, p=p: e.matmul(
                                        psf[bk][:, 0:256], lhsT=Sp[sl][pi][ri][:, kt2 * 128:(kt2 + 1) * 128], rhs=w3_sb[:, p, ri, :], start=False, stop=(ri == 1)),
                                        reads=[t_Sp[sl][pi], t_par], writes=[pb[bk]], inc=(ri == 1))
                                S.op("act", lambda e, kt2=kt2, bk=bk, q=q, yb_=yb_: e.activation(
                                    out=yb_[:, kt2, :, q * 32:(q + 1) * 32].rearrange("p r (g i) -> p g r i", g=2),
                                    in_=psf[bk][:, 0:256].rearrange("p (g r i) -> p g r i", g=2, r=8), func=AF.Copy),
                                    reads=[pb[bk]], writes=[t_y])

                    def emit_T(ct):
                        yb_ = ysb[ct % 2]
                        t_y = t_ysb[ct % 2]
                        for kt2 in range(2):
                            for r in range(8):
                                S.op("pe", lambda e, kt2=kt2, r=r, yb_=yb_: e.transpose(out=psb[:, r * 128:(r + 1) * 128], in_=yb_[:, kt2, r, :], identity=identb[:]),
                                     reads=[t_y, t_const], writes=[pb[7]], inc=(r == 7))
                            S.op("dve", lambda e, kt2=kt2, ct=ct: e.scalar_tensor_tensor(
                                out=suT[:, ct, kt2 * 1024:(kt2 + 1) * 1024].rearrange("p (k r) -> p r k", r=8),
                                in0=suT[:, ct, kt2 * 1024:(kt2 + 1) * 1024].rearrange("p (k r) -> p r k", r=8),
                                scalar=dA[:, ct:ct + 1],
                                in1=psb[:, :].rearrange("p (r k) -> p r k", r=8), op0=ALU.mult, op1=ALU.add),
                                reads=[pb[7], t_par] + t_su[ct], writes=t_su[ct])

                    emit_V(0)
                    for ci in range(8):
                        if ci + 1 < 8:
                            emit_V(ci + 1)
                        emit_scan(ci)
                        emit_y(ci)
                        if ci % 2 == 1:
                            emit_T(couples[ci][0])
                    if dbg_d is not None and b == 0:
                        S.dma("pool", lambda e: e.dma_start(out=dbg_d[:, :, :], in_=suT[:]), reads=[t for tl in t_su for t in tl])
                    sgf = [sbt(es, f"sgf{i}", [128, 512], F32) for i in range(2)]
                    tgf = [sbt(es, f"tgf{i}", [128, 512], F32) for i in range(2)]
                    t_sgf, t_tgf = TL(2), TL(2)
                    gi_ = 0
                    for c in range(4):
                        for mt in range(4):
                            i = gi_ % 2
                            gi_ += 1
                            ba, bb = 4 + i, 4 + (1 - i)
                            bka = 4 + i
                            bkb = i
                            for ct in range(4):
                                S.op("pe", lambda e, ct=ct, mt=mt, c=c, bka=bka: e.matmul(psf[bka][:], lhsT=gluw[:, ct, mt * 128:(mt + 1) * 128], rhs=suT[:, ct, c * 512:(c + 1) * 512], start=(ct == 0), stop=(ct == 3)),
                                     reads=[t_gw] + [t_su[ct][c]], writes=[pb[bka]], inc=(ct == 3))
                            for ct in range(4):
                                S.op("pe", lambda e, ct=ct, mt=mt, c=c, bkb=bkb: e.matmul(psf[bkb][:], lhsT=gluw[:, ct, (4 + mt) * 128:(5 + mt) * 128], rhs=suT[:, ct, c * 512:(c + 1) * 512], start=(ct == 0), stop=(ct == 3)),
                                     reads=[t_gw] + [t_su[ct][c]], writes=[pb[bkb]], inc=(ct == 3))
                            S.op("act", lambda e, i=i, mt=mt, bkb=bkb: e.activation(out=sgf[i][:], in_=psf[bkb][:], func=AF.Sigmoid, bias=glub[:, 4 + mt:5 + mt]),
                                 reads=[pb[bkb], t_par], writes=[t_sgf[i]])
                            S.op("dve", lambda e, i=i, mt=mt, bka=bka: e.scalar_tensor_tensor(out=tgf[i][:], in0=psf[bka][:], scalar=glub[:, mt:mt + 1], in1=sgf[i][:], op0=ALU.add, op1=ALU.mult),
                                 reads=[pb[bka], t_sgf[i], t_par], writes=[t_tgf[i]])
                            S.op("pool", lambda e, i=i, mt=mt, c=c: e.tensor_tensor(out=gcT[:, mt, c * 512:(c + 1) * 512], in0=tgf[i][:], in1=gcT[:, mt, c * 512:(c + 1) * 512], op=ALU.mult),
                                 reads=[t_tgf[i], t_gc[mt][c]], writes=[t_gc[mt][c]])
                    S.flush()
                if os.environ.get("KSTOP") == "S5":
                    return

                with ExitStack() as es:
                    diag = [sbt(es, f"diag{i}", [128, 31, 128], BF16) for i in range(4)]
                    t_dg = TL(4)
                    for ct in range(4):
                        S.op("dve", lambda e, ct=ct: e.tensor_tensor(out=diag[ct][:], in0=identf[:, :].unsqueeze(1).broadcast_to([128, 31, 128]),
                                                                   in1=dwT[:, ct, :].unsqueeze(2).broadcast_to([128, 31, 128]), op=ALU.mult),
                             reads=[t_const, t_par], writes=[t_dg[ct]])
                    cf2 = None
                    c162 = [sbt(es, f"c16{i}", [128, 4, 512], BF16) for i in range(2)]
                    c22 = [sbt(es, f"c2{i}", [128, 4, 512], BF16) for i in range(2)]
                    sn2 = [sbt(es, f"sn{i}", [128, 4, 512], BF16) for i in range(2)]
                    t_cf2, t_c162, t_c22, t_sn2 = [TL(4), TL(4)], [TL(4), TL(4)], [TL(4), TL(4)], [TL(4), TL(4)]
                    mean2 = [sbt(es, "mean_sb0", [128, 512], F32)] * 2
                    m22 = [sbt(es, "m2_0", [128, 512], F32)] * 2
                    rstd2 = [sbt(es, "rstd0", [128, 512], F32)] * 2
                    t_mean2, t_m22, t_rstd2 = [T()] * 2, [T()] * 2, [T()] * 2
                    u1 = [sbt(es, f"u1_{i}", [128, 512], F32) for i in range(2)]
                    t_u1 = TL(2)
                    ui_box = [0]
                    def conv_part(c):
                            cf, c16, c2, sn = c162[c % 2], c162[c % 2], c22[c % 2], sn2[c % 2]
                            t_cf, t_c16, t_c2, t_sn = t_c162[c % 2], t_c162[c % 2], t_c22[c % 2], t_sn2[c % 2]
                            mean_sb, m2, rstd = mean2[c % 2], m22[c % 2], rstd2[c % 2]
                            t_mean, t_m2, t_rstd = t_mean2[c % 2], t_m22[c % 2], t_rstd2[c % 2]
                            for ct in range(4):
                                bk = ct % 2
                                for k in range(31):
                                    S.op("pe", lambda e, cf=cf, c16=c16, c2=c2, sn=sn, mean_sb=mean_sb, m2=m2, rstd=rstd, k=k, ct=ct, c=c, bk=bk: e.matmul(psf[bk][:], lhsT=diag[ct][:, k, :], rhs=gpad[:, ct, 2 + c * 512 + k:2 + c * 512 + k + 512],
                                                                                       start=(k == 0), stop=(k == 30)),
                                         reads=[t_dg[ct], t_gp[ct]], writes=[pb[bk]], inc=(k == 30))
                                pass
                                S.op("act", lambda e, cf=cf, c16=c16, c2=c2, sn=sn, mean_sb=mean_sb, m2=m2, rstd=rstd, ct=ct, bk=bk: e.activation(out=c16[:, ct, :], in_=psf[bk][:], func=AF.Copy), reads=[pb[bk]], writes=[t_c16[ct]])
                                S.op("act", lambda e, cf=cf, c16=c16, c2=c2, sn=sn, mean_sb=mean_sb, m2=m2, rstd=rstd, ct=ct, bk=bk: e.activation(out=c2[:, ct, :], in_=psf[bk][:], func=AF.Square), reads=[pb[bk]], writes=[t_c2[ct]])

                    def ln_part(c):
                            cf, c16, c2, sn = c162[c % 2], c162[c % 2], c22[c % 2], sn2[c % 2]
                            t_cf, t_c16, t_c2, t_sn = t_c162[c % 2], t_c162[c % 2], t_c22[c % 2], t_sn2[c % 2]
                            mean_sb, m2, rstd = mean2[c % 2], m22[c % 2], rstd2[c % 2]
                            t_mean, t_m2, t_rstd = t_mean2[c % 2], t_m22[c % 2], t_rstd2[c % 2]
                            for ct in range(4):
                                S.op("pe", lambda e, cf=cf, c16=c16, c2=c2, sn=sn, mean_sb=mean_sb, m2=m2, rstd=rstd, ct=ct: e.matmul(psf[2][:], lhsT=ones512[:], rhs=c16[:, ct, :], start=(ct == 0), stop=(ct == 3)),
                                     reads=[t_c16[ct], t_par], writes=[pb[2]], inc=(ct == 3))
                            for ct in range(4):
                                S.op("pe", lambda e, cf=cf, c16=c16, c2=c2, sn=sn, mean_sb=mean_sb, m2=m2, rstd=rstd, ct=ct: e.matmul(psf[3][:], lhsT=ones512[:], rhs=c2[:, ct, :], start=(ct == 0), stop=(ct == 3)),
                                     reads=[t_c2[ct], t_par], writes=[pb[3]], inc=(ct == 3))
                            S.op("act", lambda e, cf=cf, c16=c16, c2=c2, sn=sn, mean_sb=mean_sb, m2=m2, rstd=rstd: e.activation(out=mean_sb[:], in_=psf[2][:], func=AF.Copy), reads=[pb[2]], writes=[t_mean])
                            S.op("dve", lambda e, cf=cf, c16=c16, c2=c2, sn=sn, mean_sb=mean_sb, m2=m2, rstd=rstd: e.tensor_tensor(out=m2[:], in0=mean_sb[:], in1=mean_sb[:], op=ALU.mult), reads=[t_mean], writes=[t_m2])
                            S.op("dve", lambda e, cf=cf, c16=c16, c2=c2, sn=sn, mean_sb=mean_sb, m2=m2, rstd=rstd: e.tensor_tensor(out=m2[:], in0=psf[3][:], in1=m2[:], op=ALU.subtract), reads=[pb[3], t_m2], writes=[t_m2])
                            S.op("act", lambda e, cf=cf, c16=c16, c2=c2, sn=sn, mean_sb=mean_sb, m2=m2, rstd=rstd: e.activation(out=m2[:], in_=m2[:], func=AF.Ln, bias=EPS), reads=[t_m2], writes=[t_m2])
                            S.op("act", lambda e, cf=cf, c16=c16, c2=c2, sn=sn, mean_sb=mean_sb, m2=m2, rstd=rstd: e.activation(out=rstd[:], in_=m2[:], func=AF.Exp, scale=-0.5), reads=[t_m2], writes=[t_rstd])
                            for ct in range(4):
                                i = ui_box[0] % 2
                                ui_box[0] += 1
                                S.op("dve", lambda e, cf=cf, c16=c16, c2=c2, sn=sn, mean_sb=mean_sb, m2=m2, rstd=rstd, ct=ct, i=i: e.tensor_tensor(out=u1[i][:], in0=cf[:, ct, :], in1=mean_sb[:], op=ALU.subtract), reads=[t_cf[ct], t_mean], writes=[t_u1[i]])
                                S.op("dve", lambda e, cf=cf, c16=c16, c2=c2, sn=sn, mean_sb=mean_sb, m2=m2, rstd=rstd, i=i: e.tensor_tensor(out=u1[i][:], in0=u1[i][:], in1=rstd[:], op=ALU.mult), reads=[t_u1[i], t_rstd], writes=[t_u1[i]])
                                S.op("act", lambda e, cf=cf, c16=c16, c2=c2, sn=sn, mean_sb=mean_sb, m2=m2, rstd=rstd, ct=ct, i=i: e.activation(out=sn[:, ct, :], in_=u1[i][:], func=AF.Silu, scale=lng[:, ct:ct + 1], bias=lnb[:, ct:ct + 1]),
                                     reads=[t_u1[i], t_par], writes=[t_sn[ct]])

                    def pw_part(c):
                            cf, c16, c2, sn = c162[c % 2], c162[c % 2], c22[c % 2], sn2[c % 2]
                            t_cf, t_c16, t_c2, t_sn = t_c162[c % 2], t_c162[c % 2], t_c22[c % 2], t_sn2[c % 2]
                            mean_sb, m2, rstd = mean2[c % 2], m22[c % 2], rstd2[c % 2]
                            t_mean, t_m2, t_rstd = t_mean2[c % 2], t_m22[c % 2], t_rstd2[c % 2]
                            for mt in range(4):
                                bk = 4 + mt % 2
                                for ct in range(4):
                                    S.op("pe", lambda e, cf=cf, c16=c16, c2=c2, sn=sn, mean_sb=mean_sb, m2=m2, rstd=rstd, ct=ct, mt=mt, bk=bk: e.matmul(psf[bk][:], lhsT=pwsb[:, ct, mt * 128:(mt + 1) * 128], rhs=sn[:, ct, :], start=(ct == 0), stop=(ct == 3)),
                                         reads=[t_pw, t_sn[ct]], writes=[pb[bk]], inc=(ct == 3))
                                S.op("dve", lambda e, cf=cf, c16=c16, c2=c2, sn=sn, mean_sb=mean_sb, m2=m2, rstd=rstd, mt=mt, c=c, bk=bk: e.tensor_tensor(out=gdT[:, mt, c * 512:(c + 1) * 512], in0=psf[bk][:], in1=gdT[:, mt, c * 512:(c + 1) * 512], op=ALU.mult),
                                     reads=[pb[bk], t_gd[mt][c]], writes=[t_gd[mt][c]])

                    conv_part(0)
                    for c in range(4):
                        ln_part(c)
                        if c + 1 < 4:
                            conv_part(c + 1)
                        pw_part(c)
                    S.flush()
                if os.environ.get("KSTOP") == "CV":
                    return
                with ExitStack() as es:
                    phase_outproj(es, o_w_out, post_g[1:2, :], out_d, b, gcT, gdT, TL(4), TL(4), wo1, t_wo1)
                    S.flush()

        nbr = int(os.environ.get("KNB", NB))
        for b in range(nbr):
            layer0(b)
        if n_layers >= 2 and os.environ.get("KLAYERS", "2") == "2":
            layer1_all(nbr)

    return nc


def layer1_host_layouts(inputs, f):
    g = lambda k: np.asarray(inputs[k][0], dtype=np.float32)
    m = {}
    m["o_w_in"] = f(g("o_w_in"))

    def A_gn(a):
        a = np.repeat(a[:, None, :], 16, axis=1).reshape(4, 8, 16, 64)
        return f(a.transpose(1, 2, 0, 3).reshape(128, 256))
    m["o_lamre_A"] = A_gn(g("o_lam_re"))
    m["o_lamim_A"] = A_gn(g("o_lam_im"))
    m["o_dt_A"] = A_gn(np.repeat(g("o_log_dt")[:, None], 64, axis=1))

    def A_b(bm):
        a = bm.transpose(0, 2, 1).reshape(4, 8, 16, 64)
        return f(a.transpose(1, 2, 0, 3).reshape(128, 256))
    m["o_bre_A"] = A_b(g("o_b_re"))
    m["o_bim_A"] = A_b(g("o_b_im"))

    def B_gn(a):
        return f(a.reshape(16, 2, 64).transpose(1, 2, 0).reshape(128, 16))
    m["o_lamre_B"] = B_gn(g("o_lam_re"))
    m["o_lamim_B"] = B_gn(g("o_lam_im"))
    m["o_dt_B"] = B_gn(np.repeat(g("o_log_dt")[:, None], 64, axis=1))

    def B_c(cm):
        return f(cm.reshape(16, 2, 16, 64).transpose(1, 3, 0, 2).reshape(128, 256))
    def B_b(bm):
        return f(bm.reshape(16, 2, 64, 16).transpose(1, 2, 0, 3).reshape(128, 256))
    m["o_bre_B"] = B_b(g("o_b_re"))
    m["o_bim_B"] = B_b(g("o_b_im"))
    m["o_cre_B"] = B_c(g("o_c_re"))
    m["o_cim_B"] = B_c(g("o_c_im"))
    m["o_dA"] = f(g("o_d").reshape(4, 128).T)
    m["o_glu_w"] = f(g("o_glu_w"))
    m["o_glub"] = f(g("o_glu_b").reshape(8, 128).T)
    m["o_dwT"] = f(g("o_dw").reshape(31, 4, 128).transpose(2, 1, 0))
    m["o_lng"] = f(g("o_ln_g").reshape(4, 128).T)
    m["o_lnb"] = f(g("o_ln_b").reshape(4, 128).T)
    m["o_pw"] = f(g("o_pw"))
    m["o_w_out"] = f(g("o_w_out"))
    return m


def make_in_maps(inputs):
    n = 8
    x = np.ascontiguousarray(inputs["x"], dtype=np.float32)
    consts = host_consts()
    f = lambda a: np.ascontiguousarray(a, dtype=np.float32)
    l1maps = layer1_host_layouts(inputs, f)
    in_maps = []
    for c in range(n):
        m = {"x": x[NB * c:NB * (c + 1)]}
        for k in ("pre_norm_g", "post_norm_g"):
            m[k] = f(inputs[k])
        m["e_w_in"] = f(inputs["e_w_in"][0])
        m["e_pool_w"] = f(inputs["e_pool_w"][0])
        m["e_pool_scale"] = f(np.asarray(inputs["e_pool_scale"][0], dtype=np.float32).reshape(4, 128).T)
        m["e_w_out"] = f(inputs["e_w_out"][0])
        m.update(l1maps)
        for k, v in consts.items():
            m["c_" + k] = v
        in_maps.append(m)
    return in_maps


def kernel(**inputs):
    nc = build()
    in_maps = make_in_maps(inputs)
    res = run_bass_kernel_spmd(nc, in_maps, core_ids=list(range(8)))
    return np.concatenate([r["out"] for r in res.results], axis=0)
```
